# Optimizing a Trainium2 kernel written in Bass

```python
import math
import jax, jax.numpy as jnp
from jax import lax
import numpy as np

D_MODEL = 1024
BATCH = 8
SEQ = 2048
DEPTH = 4
DEC_BATCH = 128
DEC_SEQ = 8
PAST_LEN = 16384
PAGE_SIZE = 128

N_MIXERS = 2
N_S5_LAYERS = (DEPTH + 1) // 2
N_RWKV_LAYERS = DEPTH // 2
S5_GROUP = 16
S5_GROUPS = D_MODEL // S5_GROUP
S5_STATE = 64
RWKV_HEAD = 64
RWKV_HEADS = D_MODEL // RWKV_HEAD
DECAY_LORA = 64
AAA_LORA = 64
MV_LORA = 32
GATE_LORA = 128
RWKV_GN_EPS = 64e-5
MEM_TOKENS = 256
XA_HEADS = 4
XA_HEAD_DIM = D_MODEL // XA_HEADS
D_FF = 4 * D_MODEL
LN_EPS = 1e-5
DEEPNORM_ALPHA = (2.0 * DEPTH) ** 0.25
DEEPNORM_BETA = (8.0 * DEPTH) ** -0.25

kernel_name = "s5_rwkv7_memxattn_deepnorm_step"


def layer_norm(x, g, b):
    xf = x.astype(jnp.float32)
    mu = jnp.mean(xf, axis=-1, keepdims=True)
    var = jnp.mean(jnp.square(xf - mu), axis=-1, keepdims=True)
    return ((xf - mu) * lax.rsqrt(var + LN_EPS) * g + b).astype(x.dtype)


def _complex_affine_combine(earlier, later):
    a1r, a1i, b1r, b1i = earlier
    a2r, a2i, b2r, b2i = later
    ar = a2r * a1r - a2i * a1i
    ai = a2r * a1i + a2i * a1r
    br = a2r * b1r - a2i * b1i + b2r
    bi = a2r * b1i + a2i * b1r + b2i
    return (ar, ai, br, bi)


def s5_mixer(u, h0_re, h0_im, p, j):
    f32 = jnp.float32
    bsz, t_len, _ = u.shape
    uf = u.astype(f32)
    ug = uf.reshape(bsz, t_len, S5_GROUPS, S5_GROUP)
    lam_re = p["s5_a_re"][j].astype(f32)
    lam_im = p["s5_a_im"][j].astype(f32)
    dt = jnp.exp(p["s5_log_dt"][j].astype(f32))[:, None]
    mag = jnp.exp(lam_re * dt)
    ang = lam_im * dt
    ab_re = mag * jnp.cos(ang)
    ab_im = mag * jnp.sin(ang)
    den = lam_re * lam_re + lam_im * lam_im
    num_re = ab_re - 1.0
    f_re = (num_re * lam_re + ab_im * lam_im) / den
    f_im = (ab_im * lam_re - num_re * lam_im) / den
    br = p["s5_b_re"][j].astype(f32)
    bi = p["s5_b_im"][j].astype(f32)
    bb_re = f_re[..., None] * br - f_im[..., None] * bi
    bb_im = f_re[..., None] * bi + f_im[..., None] * br
    bu_re = jnp.einsum('btgc,gnc->tbgn', ug, bb_re)
    bu_im = jnp.einsum('btgc,gnc->tbgn', ug, bb_im)
    a_t_re = jnp.broadcast_to(ab_re, (t_len, 1, S5_GROUPS, S5_STATE))
    a_t_im = jnp.broadcast_to(ab_im, (t_len, 1, S5_GROUPS, S5_STATE))
    p_re, p_im, hs_re, hs_im = lax.associative_scan(
        _complex_affine_combine, (a_t_re, a_t_im, bu_re, bu_im), axis=0)
    h0r = h0_re.astype(f32)[None]
    h0i = h0_im.astype(f32)[None]
    h_re = hs_re + p_re * h0r - p_im * h0i
    h_im = hs_im + p_re * h0i + p_im * h0r
    cr = p["s5_c_re"][j].astype(f32)
    ci = p["s5_c_im"][j].astype(f32)
    y = jnp.einsum('tbgn,gcn->btgc', h_re, cr) - jnp.einsum('tbgn,gcn->btgc', h_im, ci)
    y = y.reshape(bsz, t_len, D_MODEL) + p["s5_d"][j].astype(f32) * uf
    z = jax.nn.gelu(y).astype(u.dtype)
    out = (z @ p["s5_w_glu_v"][j]) * jax.nn.sigmoid(z @ p["s5_w_glu_g"][j])
    return out.astype(u.dtype), h_re[-1], h_im[-1]


def _rwkv7_step(S, inp):
    r_t, w_t, k_t, v_t, kk_t, a_t = inp
    sa = jnp.einsum('bhij,bhj->bhi', S, -kk_t)
    S = (S * w_t[:, :, None, :]
         + sa[..., None] * (kk_t * a_t)[:, :, None, :]
         + v_t[..., None] * k_t[:, :, None, :])
    y = jnp.einsum('bhij,bhj->bhi', S, r_t)
    return S, y


def rwkv7_mixer(x, s0, x_prev, v_first, p, j):
    f32 = jnp.float32
    bsz, t_len, _ = x.shape
    xx = jnp.concatenate([x_prev[:, None].astype(x.dtype), x[:, :-1]], axis=1) - x
    mu = p["rwkv_mu"][j]
    xr = x + xx * mu[0]
    xw = x + xx * mu[1]
    xk = x + xx * mu[2]
    xv = x + xx * mu[3]
    xa = x + xx * mu[4]
    xg = x + xx * mu[5]
    r = xr @ p["rwkv_w_r"][j]
    k = xk @ p["rwkv_w_k"][j]
    v = xv @ p["rwkv_w_v"][j]
    w_log = -jax.nn.softplus(-(p["rwkv_w0"][j] + jnp.tanh(xw @ p["rwkv_w1"][j]) @ p["rwkv_w2"][j])) - 0.5
    if j == 0:
        v_first = v
    else:
        gate_v = jax.nn.sigmoid(p["rwkv_v0"][j - 1] + (xv @ p["rwkv_v1"][j - 1]) @ p["rwkv_v2"][j - 1])
        v = v + (v_first - v) * gate_v
    a = jax.nn.sigmoid(p["rwkv_a0"][j] + (xa @ p["rwkv_a1"][j]) @ p["rwkv_a2"][j])
    g = jax.nn.sigmoid(xg @ p["rwkv_g1"][j]) @ p["rwkv_g2"][j]
    hshape = (bsz, t_len, RWKV_HEADS, RWKV_HEAD)
    kk = (k * p["rwkv_k_k"][j]).astype(f32).reshape(hshape)
    kk = kk / jnp.maximum(jnp.sqrt(jnp.sum(kk * kk, axis=-1, keepdims=True)), 1e-12)
    k = k * (1.0 + (a - 1.0) * p["rwkv_k_a"][j])
    decay = jnp.exp(-jnp.exp(w_log.astype(f32)))
    r_h = r.astype(f32).reshape(hshape)
    k_h = k.astype(f32).reshape(hshape)
    v_h = v.astype(f32).reshape(hshape)
    a_h = a.astype(f32).reshape(hshape)
    w_h = decay.reshape(hshape)
    tm = lambda t: jnp.moveaxis(t, 1, 0)
    S_last, ys = lax.scan(_rwkv7_step, s0.astype(f32),
                          (tm(r_h), tm(w_h), tm(k_h), tm(v_h), tm(kk), tm(a_h)))
    y = jnp.moveaxis(ys, 0, 1)
    mu_y = jnp.mean(y, axis=-1, keepdims=True)
    var_y = jnp.mean(jnp.square(y - mu_y), axis=-1, keepdims=True)
    y = ((y - mu_y) * lax.rsqrt(var_y + RWKV_GN_EPS)).reshape(bsz, t_len, D_MODEL)
    y = y * p["rwkv_lnx_g"][j] + p["rwkv_lnx_b"][j]
    bonus = jnp.sum(r_h * k_h * p["rwkv_r_k"][j].astype(f32), axis=-1, keepdims=True) * v_h
    o = ((y + bonus.reshape(bsz, t_len, D_MODEL)) * g).astype(x.dtype)
    out = o @ p["rwkv_w_o"][j]
    return out.astype(x.dtype), S_last, x[:, -1], v_first


def memory_cross_attention(x, mk, mv, w_q, w_o):
    bsz, t_len, _ = x.shape
    q = (x @ w_q).reshape(bsz, t_len, XA_HEADS, XA_HEAD_DIM)
    s = jnp.einsum('bthd,bmhd->bhtm', q, mk).astype(jnp.float32) * (XA_HEAD_DIM ** -0.5)
    pr = jax.nn.softmax(s, axis=-1).astype(x.dtype)
    o = jnp.einsum('bhtm,bmhd->bthd', pr, mv).reshape(bsz, t_len, D_MODEL).astype(x.dtype)
    return (o @ w_o).astype(x.dtype)


def sq_relu_mlp(x, w1, w2):
    h = jax.nn.relu(x @ w1)
    return ((h * h) @ w2).astype(x.dtype)


def trunk(x, mem_k, mem_v, s5_re, s5_im, rwkv_s, shift, p):
    new_re, new_im, new_s, new_shift = [], [], [], []
    v_first = None
    for i in range(DEPTH):
        j = i // N_MIXERS
        if i % N_MIXERS == 0:
            h, hr, hi = s5_mixer(x, s5_re[j], s5_im[j], p, j)
            new_re.append(hr)
            new_im.append(hi)
        else:
            h, S_last, x_last, v_first = rwkv7_mixer(x, rwkv_s[j], shift[j], v_first, p, j)
            new_s.append(S_last)
            new_shift.append(x_last)
        x = layer_norm(DEEPNORM_ALPHA * x + h, p["ln_g"][i, 0], p["ln_b"][i, 0])
        h = memory_cross_attention(x, mem_k[i], mem_v[i], p["xa_w_q"][i], p["xa_w_o"][i])
        x = layer_norm(DEEPNORM_ALPHA * x + h, p["ln_g"][i, 1], p["ln_b"][i, 1])
        h = sq_relu_mlp(x, p["mlp_w1"][i], p["mlp_w2"][i])
        x = layer_norm(DEEPNORM_ALPHA * x + h, p["ln_g"][i, 2], p["ln_b"][i, 2])
    return x, jnp.stack(new_re), jnp.stack(new_im), jnp.stack(new_s), jnp.stack(new_shift)


def setup_inputs(seed: int = 0) -> dict:
    key = jax.random.key(seed)
    ks = iter(jax.random.split(key, 64))
    f32 = jnp.float32

    def nrm(shape, scale):
        return scale * jax.random.normal(next(ks), shape, f32)

    NR, NS, NV = N_RWKV_LAYERS, N_S5_LAYERS, N_RWKV_LAYERS - 1
    D = D_MODEL
    inp = {}
    inp["x_prompt"] = nrm((BATCH, SEQ, D), 1.0)
    inp["x_sample"] = nrm((DEC_BATCH, DEC_SEQ, D), 1.0)
    inp["mem_prompt"] = nrm((BATCH, MEM_TOKENS, D), 1.0)
    inp["cache_mem_k"] = nrm((DEPTH, DEC_BATCH, MEM_TOKENS, XA_HEADS, XA_HEAD_DIM), 1.0)
    inp["cache_mem_v"] = nrm((DEPTH, DEC_BATCH, MEM_TOKENS, XA_HEADS, XA_HEAD_DIM), 1.0)
    inp["state_s5_re"] = nrm((NS, DEC_BATCH, S5_GROUPS, S5_STATE), 0.5)
    inp["state_s5_im"] = nrm((NS, DEC_BATCH, S5_GROUPS, S5_STATE), 0.5)
    inp["state_rwkv"] = nrm((NR, DEC_BATCH, RWKV_HEADS, RWKV_HEAD, RWKV_HEAD), 0.2)
    inp["state_shift"] = nrm((NR, DEC_BATCH, D), 1.0)
    inp["ln_g"] = 1.0 + nrm((DEPTH, 3, D), 0.01)
    inp["ln_b"] = nrm((DEPTH, 3, D), 0.01)
    inp["s5_a_re"] = -0.5 + nrm((NS, S5_GROUPS, S5_STATE), 0.01)
    inp["s5_a_im"] = math.pi * jnp.arange(S5_STATE, dtype=f32) + nrm((NS, S5_GROUPS, S5_STATE), 0.01)
    inp["s5_log_dt"] = jax.random.uniform(next(ks), (NS, S5_GROUPS), f32, math.log(1e-3), math.log(1e-1))
    inp["s5_b_re"] = nrm((NS, S5_GROUPS, S5_STATE, S5_GROUP), (2.0 * S5_GROUP) ** -0.5)
    inp["s5_b_im"] = nrm((NS, S5_GROUPS, S5_STATE, S5_GROUP), (2.0 * S5_GROUP) ** -0.5)
    inp["s5_c_re"] = nrm((NS, S5_GROUPS, S5_GROUP, S5_STATE), (2.0 * S5_STATE) ** -0.5)
    inp["s5_c_im"] = nrm((NS, S5_GROUPS, S5_GROUP, S5_STATE), (2.0 * S5_STATE) ** -0.5)
    inp["s5_d"] = nrm((NS, D), 1.0)
    inp["s5_w_glu_v"] = nrm((NS, D, D), DEEPNORM_BETA * D ** -0.5)
    inp["s5_w_glu_g"] = nrm((NS, D, D), D ** -0.5)
    inp["rwkv_mu"] = jax.random.uniform(next(ks), (NR, 6, D), f32)
    inp["rwkv_w_r"] = nrm((NR, D, D), D ** -0.5)
    inp["rwkv_w_k"] = nrm((NR, D, D), D ** -0.5)
    inp["rwkv_w_v"] = nrm((NR, D, D), D ** -0.5)
    inp["rwkv_w_o"] = nrm((NR, D, D), DEEPNORM_BETA * D ** -0.5)
    inp["rwkv_w0"] = jnp.linspace(-6.0, -1.0, D, dtype=f32)[None] + nrm((NR, D), 0.1)
    inp["rwkv_w1"] = nrm((NR, D, DECAY_LORA), D ** -0.5)
    inp["rwkv_w2"] = nrm((NR, DECAY_LORA, D), 0.1 * DECAY_LORA ** -0.5)
    inp["rwkv_a0"] = nrm((NR, D), 0.1)
    inp["rwkv_a1"] = nrm((NR, D, AAA_LORA), D ** -0.5)
    inp["rwkv_a2"] = nrm((NR, AAA_LORA, D), 0.1 * AAA_LORA ** -0.5)
    inp["rwkv_v0"] = 1.0 + nrm((NV, D), 0.1)
    inp["rwkv_v1"] = nrm((NV, D, MV_LORA), D ** -0.5)
    inp["rwkv_v2"] = nrm((NV, MV_LORA, D), 0.1 * MV_LORA ** -0.5)
    inp["rwkv_g1"] = nrm((NR, D, GATE_LORA), D ** -0.5)
    inp["rwkv_g2"] = nrm((NR, GATE_LORA, D), GATE_LORA ** -0.5)
    inp["rwkv_k_k"] = 0.85 + nrm((NR, D), 0.05)
    inp["rwkv_k_a"] = 1.0 + nrm((NR, D), 0.05)
    inp["rwkv_r_k"] = nrm((NR, RWKV_HEADS, RWKV_HEAD), 0.1)
    inp["rwkv_lnx_g"] = 1.0 + nrm((NR, D), 0.01)
    inp["rwkv_lnx_b"] = nrm((NR, D), 0.01)
    inp["xa_w_q"] = nrm((DEPTH, D, D), D ** -0.5)
    inp["xa_w_k"] = nrm((DEPTH, D, D), D ** -0.5)
    inp["xa_w_v"] = nrm((DEPTH, D, D), D ** -0.5)
    inp["xa_w_o"] = nrm((DEPTH, D, D), DEEPNORM_BETA * D ** -0.5)
    inp["mlp_w1"] = nrm((DEPTH, D, D_FF), D ** -0.5)
    inp["mlp_w2"] = nrm((DEPTH, D_FF, D), DEEPNORM_BETA * D_FF ** -0.5)
    return inp


def reference(x_prompt, x_sample, mem_prompt, cache_mem_k, cache_mem_v, state_s5_re, state_s5_im,
              state_rwkv, state_shift, ln_g, ln_b, s5_a_re, s5_a_im, s5_log_dt, s5_b_re, s5_b_im,
              s5_c_re, s5_c_im, s5_d, s5_w_glu_v, s5_w_glu_g, rwkv_mu, rwkv_w_r, rwkv_w_k, rwkv_w_v,
              rwkv_w_o, rwkv_w0, rwkv_w1, rwkv_w2, rwkv_a0, rwkv_a1, rwkv_a2, rwkv_v0, rwkv_v1, rwkv_v2,
              rwkv_g1, rwkv_g2, rwkv_k_k, rwkv_k_a, rwkv_r_k, rwkv_lnx_g, rwkv_lnx_b,
              xa_w_q, xa_w_k, xa_w_v, xa_w_o, mlp_w1, mlp_w2):
    p = {
        "ln_g": ln_g, "ln_b": ln_b,
        "s5_a_re": s5_a_re, "s5_a_im": s5_a_im, "s5_log_dt": s5_log_dt,
        "s5_b_re": s5_b_re, "s5_b_im": s5_b_im, "s5_c_re": s5_c_re, "s5_c_im": s5_c_im,
        "s5_d": s5_d, "s5_w_glu_v": s5_w_glu_v, "s5_w_glu_g": s5_w_glu_g,
        "rwkv_mu": rwkv_mu, "rwkv_w_r": rwkv_w_r, "rwkv_w_k": rwkv_w_k, "rwkv_w_v": rwkv_w_v,
        "rwkv_w_o": rwkv_w_o, "rwkv_w0": rwkv_w0, "rwkv_w1": rwkv_w1, "rwkv_w2": rwkv_w2,
        "rwkv_a0": rwkv_a0, "rwkv_a1": rwkv_a1, "rwkv_a2": rwkv_a2,
        "rwkv_v0": rwkv_v0, "rwkv_v1": rwkv_v1, "rwkv_v2": rwkv_v2,
        "rwkv_g1": rwkv_g1, "rwkv_g2": rwkv_g2, "rwkv_k_k": rwkv_k_k, "rwkv_k_a": rwkv_k_a,
        "rwkv_r_k": rwkv_r_k, "rwkv_lnx_g": rwkv_lnx_g, "rwkv_lnx_b": rwkv_lnx_b,
        "xa_w_q": xa_w_q, "xa_w_o": xa_w_o, "mlp_w1": mlp_w1, "mlp_w2": mlp_w2,
    }
    f32 = jnp.float32
    b_p = x_prompt.shape[0]
    mk_shape = (DEPTH, b_p, mem_prompt.shape[1], XA_HEADS, XA_HEAD_DIM)
    mem_k_prompt = jnp.einsum('bmd,ldc->lbmc', mem_prompt, xa_w_k).reshape(mk_shape)
    mem_v_prompt = jnp.einsum('bmd,ldc->lbmc', mem_prompt, xa_w_v).reshape(mk_shape)
    z_s5 = jnp.zeros((N_S5_LAYERS, b_p, S5_GROUPS, S5_STATE), f32)
    z_rwkv = jnp.zeros((N_RWKV_LAYERS, b_p, RWKV_HEADS, RWKV_HEAD, RWKV_HEAD), f32)
    z_shift = jnp.zeros((N_RWKV_LAYERS, b_p, D_MODEL), x_prompt.dtype)
    y_prompt, s5_re_prompt, s5_im_prompt, rwkv_prompt, shift_prompt = trunk(
        x_prompt, mem_k_prompt, mem_v_prompt, z_s5, z_s5, z_rwkv, z_shift, p)
    y_sample, s5_re_sample, s5_im_sample, rwkv_sample, shift_sample = trunk(
        x_sample, cache_mem_k, cache_mem_v, state_s5_re, state_s5_im, state_rwkv, state_shift, p)
    return (y_prompt, y_sample, mem_k_prompt, mem_v_prompt, s5_re_prompt, s5_im_prompt,
            rwkv_prompt, shift_prompt, s5_re_sample, s5_im_sample, rwkv_sample, shift_sample)
```

```python
import math
import numpy as np
import concourse.bass as bass
import concourse.mybir as mybir
from concourse.bass_utils import run_bass_kernel_spmd

F32 = mybir.dt.float32
BF16 = mybir.dt.bfloat16
I32 = mybir.dt.int32
AF = mybir.ActivationFunctionType
ALU = mybir.AluOpType
AX = mybir.AxisListType

D = 1024
DEPTH = 4
TP = 2048
TS = 128
TT = TP + TS
NSEQ = 16
ALPHA = (2.0 * DEPTH) ** 0.25
LN_EPS = 1e-5
GN_EPS = 64e-5
TBLK = [(0, 512), (512, 512), (1024, 512), (1536, 512), (2048, 128)]
STUB_RWKV = False

ENGS = ("pe", "act", "dve", "pool", "sp")
SEM_ROLL = 20000
N_DMA_SEMS = 24


class Prog:
    def __init__(self, nc):
        self.nc = nc
        self.ops = {e: [] for e in ENGS}
        self.sems = {e: [nc.alloc_semaphore(f"s_{e}_0")] for e in ENGS}
        self.cnt = {e: 0 for e in ENGS}
        self.known = {e: {} for e in ENGS}
        self.last_w = {}
        self.readers = {}
        self.dma_sems = [nc.alloc_semaphore(f"s_dma_{i}") for i in range(N_DMA_SEMS)]
        self.dma_cnt = [0] * N_DMA_SEMS
        self.dma_rr = 0
        self.n_ins = 0

    def _tok(self, eng):
        if self.cnt[eng] >= SEM_ROLL:
            self.sems[eng].append(self.nc.alloc_semaphore(f"s_{eng}_{len(self.sems[eng])}"))
            self.cnt[eng] = 0
        self.cnt[eng] += 1
        return (self.sems[eng][-1], self.cnt[eng])

    def _deps(self, eng, reads, writes):
        toks = []
        for k in reads:
            t = self.last_w.get(k)
            if t is not None:
                toks.append(t)
        for k in writes:
            t = self.last_w.get(k)
            if t is not None:
                toks.append(t)
            toks.extend(self.readers.get(k, ()))
        waits = {}
        kn = self.known[eng]
        for (sem, val) in toks:
            key = id(sem)
            if kn.get(key, (None, 0))[1] >= val:
                continue
            if key not in waits or waits[key][1] < val:
                waits[key] = (sem, val)
        for key, sv in waits.items():
            kn[key] = sv
        return list(waits.values())

    def _commit(self, tok, reads, writes):
        for k in reads:
            self.readers.setdefault(k, []).append(tok)
        for k in writes:
            self.last_w[k] = tok
            self.readers[k] = []

    @staticmethod
    def _excl(reads, writes):
        r2 = [k for k in reads if not (isinstance(k, tuple) and k[0] == "ps")]
        w2 = list(writes) + [k for k in reads if isinstance(k, tuple) and k[0] == "ps"]
        return r2, w2

    def op(self, eng, fn, reads=(), writes=()):
        reads, writes = self._excl(reads, writes)
        waits = self._deps(eng, reads, writes)
        tok = self._tok(eng)
        self.ops[eng].append((waits, fn, tok[0], 1))
        self._commit(tok, reads, writes)
        self.n_ins += 1
        return tok

    def dma(self, eng, fn, reads=(), writes=()):
        i = self.dma_rr
        self.dma_rr = (self.dma_rr + 1) % N_DMA_SEMS
        sem = self.dma_sems[i]
        waits = self._deps(eng, reads, writes)
        prev = self.dma_cnt[i]
        if prev > 0:
            key = id(sem)
            if self.known[eng].get(key, (None, 0))[1] < prev:
                waits.append((sem, prev))
                self.known[eng][key] = (sem, prev)
        self.dma_cnt[i] += 16
        tok = (sem, self.dma_cnt[i])
        self.ops[eng].append((waits, fn, sem, 16))
        self._commit(tok, reads, writes)
        self.n_ins += 1
        return tok

    def drain_dmas(self, eng="sp"):
        waits = []
        for i, sem in enumerate(self.dma_sems):
            if self.dma_cnt[i] > 0 and self.known[eng].get(id(sem), (None, 0))[1] < self.dma_cnt[i]:
                waits.append((sem, self.dma_cnt[i]))
                self.known[eng][id(sem)] = (sem, self.dma_cnt[i])
        self.ops[eng].append((waits, None, None, 0))

    def emit(self):
        nc = self.nc
        self.drain_dmas("sp")
        emap = {"pe": "tensor", "act": "scalar", "dve": "vector", "pool": "gpsimd", "sp": "sync"}
        with nc.Block() as block:
            for e in ENGS:
                ops = self.ops[e]

                def body(engine, ops=ops):
                    for (waits, fn, sem, amt) in ops:
                        for (s, v) in waits:
                            engine.wait_ge(s, v)
                        if fn is not None:
                            fn(engine).then_inc(sem, amt)

                getattr(block, emap[e])(body)
        self.ops = {e: [] for e in ENGS}
        self.last_w = {}
        self.readers = {}


def _fm(v):
    return np.ascontiguousarray(np.asarray(v).reshape(8, 128).T)


def pack_params(inp):
    cols = {}
    mats = []

    def add(name, arr):
        cols[name] = sum(m.shape[1] for m in mats)
        mats.append(np.asarray(arr, dtype=np.float32))

    for i in range(4):
        for k in range(3):
            add(("lng", i, k), _fm(inp["ln_g"][i, k]))
            add(("lnb", i, k), _fm(inp["ln_b"][i, k]))
    for j in range(2):
        add(("s5d", j), _fm(inp["s5_d"][j]))
        add(("mu", j), np.concatenate([_fm(inp["rwkv_mu"][j, m]) for m in range(6)], axis=1))
        for nm in ("rwkv_w0", "rwkv_a0", "rwkv_k_k", "rwkv_k_a", "rwkv_lnx_g", "rwkv_lnx_b"):
            add((nm, j), _fm(inp[nm][j]))
        add(("rwkv_r_k", j), _fm(inp["rwkv_r_k"][j].reshape(-1)))
    add(("rwkv_v0", 1), _fm(inp["rwkv_v0"][0]))
    for j in range(2):
        for nm in ("s5_a_re", "s5_a_im"):
            a = inp[nm][j].reshape(32, 2, 64)
            add((nm, j), np.ascontiguousarray(a.transpose(1, 2, 0).reshape(128, 32)))
        ld = np.repeat(inp["s5_log_dt"][j].reshape(32, 2, 1), 64, axis=2)
        add(("s5_log_dt", j), np.ascontiguousarray(ld.transpose(1, 2, 0).reshape(128, 32)))
    return np.ascontiguousarray(np.concatenate(mats, axis=1)), cols


def pack_s5_mats(inp, j):
    out = {}
    for nm, key in (("s5_b_re", "bbr"), ("s5_b_im", "bbi")):
        b = inp[nm][j]
        bb = np.zeros((8, 16, 8, 4, 2, 64), np.float32)
        for k in range(8):
            for q in range(4):
                for hf in range(2):
                    g8 = 2 * q + hf
                    bb[g8, :, k, q, hf, :] = b[8 * k + g8].T
        out[key] = bb.reshape(128, 32 * 128)
    for nm, key in (("s5_c_re", "cpr"), ("s5_c_im", "cpi")):
        c = inp[nm][j]
        cp = np.zeros((2, 64, 32, 128), np.float32)
        for tile in range(32):
            for hf in range(2):
                c0 = (tile % 4) * 32 + hf * 16
                cp[hf, :, tile, c0:c0 + 16] = c[2 * tile + hf].T
        out[key] = cp.reshape(128, 32 * 128)
    return out


def make_consts():
    c = np.zeros((128, 1024), np.float32)
    c[:, 0:128] = np.eye(128)
    c[0:64, 128:192] = 1.0
    c[64:128, 192:256] = 1.0
    s = np.arange(64)
    c[0:64, 256:320] = (s[:, None] < s[None, :])
    c[0:64, 320:384] = (s[:, None] <= s[None, :])
    c[0:64, 384:448] = (s[:, None] > s[None, :])
    c[:, 448:512] = 1.0
    c[:, 448] = 0.0
    c[:, 512:576] = 1.0
    c[:, 512:576:8] = 0.0
    c[:, 576:704] = 1.0 / 1024.0
    return c


class KB:
    def __init__(self, pcols, npar):
        nc = bass.Bass("TRN2", target_bir_lowering=False)
        self.nc = nc
        self.P = Prog(nc)
        self.pc = pcols
        self.npar = npar

    def mm(self, out, lhsT, rhs, start, stop, reads, writes):
        self.P.op("pe", lambda e: e.matmul(out, lhsT=lhsT, rhs=rhs, start=start, stop=stop), reads, writes)

    def tr(self, out, in_, ident, reads, writes):
        self.P.op("pe", lambda e: e.transpose(out, in_, ident), reads, writes)

    def tt(self, out, in0, in1, op, reads, writes, eng="dve"):
        self.P.op(eng, lambda e: e.tensor_tensor(out=out, in0=in0, in1=in1, op=op), reads, writes)

    def ts(self, out, in0, s1, op0, reads, writes, s2=None, op1=None, eng="dve"):
        if op1 is None:
            self.P.op(eng, lambda e: e.tensor_scalar(out=out, in0=in0, scalar1=s1, scalar2=None, op0=op0), reads, writes)
        else:
            self.P.op(eng, lambda e: e.tensor_scalar(out=out, in0=in0, scalar1=s1, scalar2=s2, op0=op0, op1=op1), reads, writes)

    def stt(self, out, in0, scalar, in1, op0, op1, reads, writes):
        self.P.op("dve", lambda e: e.scalar_tensor_tensor(out=out, in0=in0, scalar=scalar, in1=in1, op0=op0, op1=op1), reads, writes)

    def act(self, out, in_, func, reads, writes, bias=None, scale=1.0, accum_out=None):
        kw = {}
        if bias is not None:
            kw["bias"] = bias
        if accum_out is not None:
            kw["accum_out"] = accum_out
        self.P.op("act", lambda e: e.activation(out=out, in_=in_, func=func, scale=scale, **kw), reads, writes)

    def cp(self, eng, out, in_, reads, writes):
        if eng == "act":
            self.P.op("act", lambda e: e.copy(out=out, in_=in_), reads, writes)
        else:
            self.P.op(eng, lambda e: e.tensor_copy(out=out, in_=in_), reads, writes)

    def red(self, out, in_, op, reads, writes):
        self.P.op("dve", lambda e: e.tensor_reduce(out=out, in_=in_, op=op, axis=AX.X), reads, writes)

    def rcp(self, out, in_, reads, writes):
        self.P.op("dve", lambda e: e.reciprocal(out=out, in_=in_), reads, writes)

    def mset(self, out, val, reads, writes, eng="dve"):
        self.P.op(eng, lambda e: e.memset(out, val), reads, writes)

    def scan(self, out, d0, d1, reads, writes):
        self.P.op("dve", lambda e: e.tensor_tensor_scan(out=out, data0=d0, data1=d1, initial=0.0, op0=ALU.mult, op1=ALU.add), reads, writes)

    def dma(self, eng, out, in_, reads, writes):
        self.P.dma(eng, lambda e: e.dma_start(out=out, in_=in_), reads, writes)

    def setup(self):
        nc = self.nc

        def din(name, shape):
            return nc.dram_tensor(name, list(shape), F32, kind="ExternalInput").ap()

        def dout(name, shape):
            return nc.dram_tensor(name, list(shape), F32, kind="ExternalOutput").ap()

        shapes = {"xin": [TT, D], "mem": [256, D], "ck": [4, NSEQ, 256, D], "cv": [4, NSEQ, 256, D], "s5re0": [2, NSEQ, 4096],
                  "s5im0": [2, NSEQ, 4096], "rw0": [2, NSEQ, 16, 64, 64], "sh0": [2, NSEQ, D], "par": [128, self.npar], "consts": [128, 1024]}
        for j in range(2):
            for k in ("bbr", "bbi", "cpr", "cpi"):
                shapes[(k, j)] = [128, 4096]
        for nm, shp in WSHAPES:
            shapes[nm] = shp

        class Lazy(dict):
            def __missing__(d, key):
                name = key if isinstance(key, str) else f"{key[0]}{key[1]}"
                d[key] = din(name, shapes[key])
                return d[key]
        I = Lazy()
        O = {}
        O["y"] = dout("y", [TT, D])
        O["memk"] = dout("memk", [4, 256, D])
        O["memv"] = dout("memv", [4, 256, D])
        O["s5pre"] = dout("s5pre", [2, 4096])
        O["s5pim"] = dout("s5pim", [2, 4096])
        O["rwp"] = dout("rwp", [2, 16, 64, 64])
        O["shp"] = dout("shp", [2, D])
        O["s5sre"] = dout("s5sre", [2, NSEQ, 4096])
        O["s5sim"] = dout("s5sim", [2, NSEQ, 4096])
        O["rws"] = dout("rws", [2, NSEQ, 16, 64, 64])
        O["shs"] = dout("shs", [2, NSEQ, D])
        self.I, self.O = I, O
        self.VF = nc.dram_tensor("vfirst", [128, 8, TT], F32, kind="Internal").ap()

        self.X = nc.alloc_sbuf_tensor("X", [128, 8, TT], F32)
        self.AR = nc.alloc_sbuf_tensor("AR", [128, 51200], BF16)
        self.MEMT = nc.alloc_sbuf_tensor("MEMT", [128, 8, 256], BF16)
        self.PAR = nc.alloc_sbuf_tensor("PAR", [128, self.npar], F32)
        self.CON = nc.alloc_sbuf_tensor("CON", [128, 1024], F32)
        self.CONB = nc.alloc_sbuf_tensor("CONB", [128, 256], BF16)
        self.T = [nc.alloc_sbuf_tensor(f"T{i}", [128, 512], F32) for i in range(3)]
        self.SQ = nc.alloc_sbuf_tensor("SQ", [128, 8, 512], BF16)
        self.RS = nc.alloc_sbuf_tensor("RS", [128, 512], F32)
        self.SM = nc.alloc_sbuf_tensor("SM", [128, 256], F32)
        self.PS = [nc.alloc_psum_tensor(f"ps{i}", [128, 512], F32) for i in range(8)]
        AR = self.AR
        self.XB = AR[:, 0:17408].rearrange("p (c t) -> p c t", c=8)
        self.A1 = AR[:, 17408:34816].rearrange("p (c t) -> p c t", c=8)
        self.WR = [AR[:, 34816 + i * 8192: 34816 + (i + 1) * 8192].rearrange("p (c n) -> p c n", c=8) for i in range(2)]
        CON, CONB = self.CON, self.CONB
        self.IDF = CON[:, 0:128]
        self.BLK = CON[:, 128:256]
        self.MLT = CON[0:64, 256:320]
        self.MLE = CON[0:64, 320:384]
        self.MGT = CON[0:64, 384:448]
        self.RST64 = CON[:, 448:512]
        self.RST8 = CON[:, 512:576]
        self.IDB = CONB[:, 0:128]
        self.ONB = CONB[:, 128:256]
        self.wr_i = 0
        self.bank_i = 0
        self.EPS = self.SM[:, 0:1]
        self.EPS2 = self.SM[:, 1:2]

    def par(self, key, n=8):
        c0 = self.pc[key]
        return self.PAR[:, c0:c0 + n]

    def wload(self, dram2d, ncols=1024):
        i = self.wr_i
        self.wr_i ^= 1
        dst = self.WR[i][:, :, 0:ncols]
        src = dram2d.rearrange("(c p) n -> p c n", p=128)
        self.dma("pool", dst, src, [], [("WR", i)])
        return self.WR[i], ("WR", i)

    def next_bank(self):
        i = self.bank_i
        self.bank_i = (i + 1) % 4
        return self.PS[i], ("ps", i)

    def dense(self, w, wkey, src, srckey, evac, n_oc=8):
        for (t0, n) in TBLK:
            for oc in range(n_oc):
                ps, pk = self.next_bank()
                for c in range(8):
                    self.mm(ps[:, 0:n], w[:, c, oc * 128:(oc + 1) * 128], src[:, c, t0:t0 + n], c == 0, c == 7, [wkey, (srckey, t0)], [pk])
                evac(ps, pk, oc, t0, n)

    def layer_norm(self, i, k):
        X, XB, SQ, RS, PS, ONB = self.X, self.XB, self.SQ, self.RS, self.PS, self.ONB
        g = self.par(("lng", i, k))
        b = self.par(("lnb", i, k))
        for (t0, n) in TBLK:
            xb = X[:, :, t0:t0 + n]
            kx = ("X", t0)
            self.cp("act", SQ[:, :, 0:n], xb, [kx], ["SQ"])
            for c in range(8):
                self.mm(PS[4][:, 0:n], ONB, SQ[:, c, 0:n], c == 0, c == 7, ["SQ", "CONB"], [("ps", 4)])
            self.tt(xb, xb, PS[4][:, 0:n].unsqueeze(1).broadcast_to([128, 8, n]), ALU.subtract, [kx, ("ps", 4)], [kx])
            self.act(SQ[:, :, 0:n], xb, AF.Square, [kx], ["SQ"])
            for c in range(8):
                self.mm(PS[5][:, 0:n], ONB, SQ[:, c, 0:n], c == 0, c == 7, ["SQ", "CONB"], [("ps", 5)])
            self.act(RS[:, 0:n], PS[5][:, 0:n], AF.Sqrt, [("ps", 5), "eps"], ["RS"], bias=self.EPS)
            self.rcp(RS[:, 0:n], RS[:, 0:n], ["RS"], ["RS"])
            self.tt(xb, xb, RS[:, 0:n].unsqueeze(1).broadcast_to([128, 8, n]), ALU.mult, [kx, "RS"], [kx])
            for c in range(8):
                self.ts(X[:, c, t0:t0 + n], X[:, c, t0:t0 + n], g[:, c:c + 1], ALU.mult, [kx, "PAR"], [kx], s2=b[:, c:c + 1], op1=ALU.add)
            self.cp("act", XB[:, :, t0:t0 + n], xb, [kx], [("XB", t0)])

    def phase_input(self):
        I, X, XB, AR, PS, IDF = self.I, self.X, self.XB, self.AR, self.PS, self.IDF
        self.dma("sp", self.PAR[:], I["par"], [], ["PAR"])
        self.dma("sp", self.CON[:], I["consts"], [], ["CON"])
        self.cp("dve", self.CONB[:, 0:128], self.CON[:, 0:128], ["CON"], ["CONB"])
        self.cp("dve", self.CONB[:, 128:256], self.CON[:, 576:704], ["CON"], ["CONB"])
        self.mset(self.SM[:, 0:1], LN_EPS, [], ["eps"])
        self.mset(self.SM[:, 1:2], GN_EPS, [], ["eps"])
        STG = [AR[:, 17408 + i * 2048: 17408 + (i + 1) * 2048].bitcast(F32) for i in range(2)]
        import os
        for tt in [int(x) for x in os.environ.get('PI_TILES', ','.join(str(i) for i in range(19))).split(',')]:
            st = STG[tt % 2]
            sk = ("stg", tt % 2)
            if tt < 17:
                self.dma("sp", st, I["xin"][tt * 128:(tt + 1) * 128, :], [], [sk])
            else:
                self.dma("sp", st, I["mem"][(tt - 17) * 128:(tt - 16) * 128, :], [], [sk])
            for cg in range(2):
                ps, pk = self.next_bank()
                for cc in range(4):
                    c = cg * 4 + cc
                    self.tr(ps[:, cc * 128:(cc + 1) * 128], st[:, c * 128:(c + 1) * 128], IDF, [sk, "CON"], [pk])
                psv = ps[:].rearrange("p (a b) -> p a b", a=4)
                if tt < 17:
                    self.cp("dve", X[:, cg * 4:(cg + 1) * 4, tt * 128:(tt + 1) * 128], psv, [pk], [("Xi", tt, cg)])
                    self.cp("act", XB[:, cg * 4:(cg + 1) * 4, tt * 128:(tt + 1) * 128], psv, [pk], [("XBi", tt, cg)])
                else:
                    m0 = (tt - 17) * 128
                    self.cp("act", self.MEMT[:, cg * 4:(cg + 1) * 4, m0:m0 + 128], psv, [pk], ["MEMT"])
        self.P.emit()

    def phase_output(self):
        AR, PS, IDF, X, O = self.AR, self.PS, self.IDF, self.X, self.O
        STG = [AR[:, 17408 + i * 2048: 17408 + (i + 1) * 2048].bitcast(F32) for i in range(2)]
        for tt in range(17):
            st = STG[tt % 2]
            sk = ("stg", tt % 2)
            for cg in range(2):
                ps, pk = self.next_bank()
                for cc in range(4):
                    c = cg * 4 + cc
                    self.tr(ps[:, cc * 128:(cc + 1) * 128], X[:, c, tt * 128:(tt + 1) * 128], IDF, [], [pk])
                self.cp("dve" if cg == 0 else "act", st[:, cg * 512:(cg + 1) * 512], ps[:], [pk], [sk])
            self.dma("sp", O["y"][tt * 128:(tt + 1) * 128, :], st, [sk], [("yout", tt)])
        self.P.emit()

    def mlp_layer(self, i):
        X, XB, A1, T, I = self.X, self.XB, self.A1, self.T, self.I
        cnt = [0]
        for fg in range(4):
            w1, k1 = self.wload(I["mlp_w1"][i][:, fg * 1024:(fg + 1) * 1024])

            def evac1(ps, pk, oc, t0, n):
                tt_, tk = T[cnt[0] % 2], ("T", cnt[0] % 2)
                cnt[0] += 1
                self.act(tt_[:, 0:n], ps[:, 0:n], AF.Relu, [pk], [tk])
                self.tt(A1[:, oc, t0:t0 + n], tt_[:, 0:n], tt_[:, 0:n], ALU.mult, [tk], [("A1", t0)])
            self.dense(w1, k1, XB, "XB", evac1)
            w2, k2 = self.wload(I["mlp_w2"][i][fg * 1024:(fg + 1) * 1024, :])
            if fg == 0:
                def evac2(ps, pk, oc, t0, n):
                    self.stt(X[:, oc, t0:t0 + n], X[:, oc, t0:t0 + n], ALPHA, ps[:, 0:n], ALU.mult, ALU.add, [pk, ("X", t0)], [("X", t0)])
            else:
                def evac2(ps, pk, oc, t0, n):
                    self.tt(X[:, oc, t0:t0 + n], X[:, oc, t0:t0 + n], ps[:, 0:n], ALU.add, [pk, ("X", t0)], [("X", t0)])
            self.dense(w2, k2, A1, "A1", evac2)

    def xa_layer(self, i):
        X, XB, A1, T, I, O, PS, MEMT, AR, IDB, SM = self.X, self.XB, self.A1, self.T, self.I, self.O, self.PS, self.MEMT, self.AR, self.IDB, self.SM
        wq, kq = self.wload(I["xa_w_q"][i])

        def evq(ps, pk, oc, t0, n):
            self.act(A1[:, oc, t0:t0 + n], ps[:, 0:n], AF.Copy, [pk], [("A1", t0)], scale=0.0625)
        self.dense(wq, kq, XB, "XB", evq)
        self.P.emit()
        KT = AR[:, 0:2048].rearrange("p (c m) -> p c m", c=8)
        VB = AR[:, 2048:4096].rearrange("p (a n) -> p a n", a=2)
        PNB = AR[:, 4096:5120].rearrange("p (h m) -> p h m", h=4)
        PT = AR[:, 5120:6144].rearrange("p (a t) -> p a t", a=8)
        KS = [AR[:, 6144 + s * 2048: 8192 + s * 2048].rearrange("p (a n) -> p a n", a=2) for s in range(2)]
        VS = [AR[:, 10240 + s * 2048: 12288 + s * 2048].rearrange("p (a n) -> p a n", a=2) for s in range(2)]
        KTS = AR[:, 14336:16384].rearrange("p (c m) -> p c m", c=8)
        PTS = AR[:, 16384:16448].rearrange("p (a t) -> p a t", a=8)
        PEXP = [T[0], T[1]]
        psT = PS[4][:].bitcast(BF16)
        MX, NMX, SUM, RSM = SM[:, 8:12], SM[:, 12:16], SM[:, 16:20], SM[:, 20:24]
        SC = [PS[5], PS[6]]

        wk, kk = self.wload(I["xa_w_k"][i])
        for oc in range(8):
            ps, pk = self.next_bank()
            for c in range(8):
                self.mm(ps[:, 0:256], wk[:, c, oc * 128:(oc + 1) * 128], MEMT[:, c, :], c == 0, c == 7, [kk, "MEMT"], [pk])
            self.cp("act", KT[:, oc, :], ps[:, 0:256], [pk], ["KT"])

        def tokmajor(w, wkey, outap, vb):
            for mt in range(2):
                for nb in range(2):
                    ps, pk = self.next_bank()
                    for c in range(8):
                        self.mm(ps[:], MEMT[:, c, mt * 128:(mt + 1) * 128], w[:, c, nb * 512:(nb + 1) * 512], c == 0, c == 7, [wkey, "MEMT"], [pk])
                    self.cp("dve", T[2][:], ps[:], [pk], ["T2"])
                    if vb:
                        self.cp("act", VB[:, mt, nb * 512:(nb + 1) * 512], ps[:], [pk], ["VB"])
                    self.dma("sp", outap[mt * 128:(mt + 1) * 128, nb * 512:(nb + 1) * 512], T[2][:], ["T2"], [("mo", id(outap), mt, nb)])
        tokmajor(wk, kk, O["memk"][i], False)
        wv, kv = self.wload(I["xa_w_v"][i])
        tokmajor(wv, kv, O["memv"][i], True)

        def softmax(np_):
            for b in range(2):
                self.red(MX[0:np_, 2 * b:2 * b + 2], SC[b][0:np_, :].rearrange("p (h m) -> p h m", h=2), ALU.max, [("ps", 5 + b)], ["MX"])
            self.ts(NMX[0:np_, :], MX[0:np_, :], -1.0, ALU.mult, ["MX"], ["NMX"])
            for h in range(4):
                sl = slice((h % 2) * 256, (h % 2) * 256 + 256)
                self.act(PEXP[h // 2][0:np_, sl], SC[h // 2][0:np_, sl], AF.Exp, [("ps", 5 + h // 2), "NMX"], [("PEXP", h), ("SUM", h)],
                         bias=NMX[0:np_, h:h + 1], accum_out=SUM[0:np_, h:h + 1])
            self.rcp(RSM[0:np_, :], SUM[0:np_, :], [("SUM", h) for h in range(4)], ["RSM"])
            for b in range(2):
                self.tt(PNB[0:np_, 2 * b:2 * b + 2, :], PEXP[b][0:np_, :].rearrange("p (h m) -> p h m", h=2),
                        RSM[0:np_, 2 * b:2 * b + 2].unsqueeze(2).broadcast_to([np_, 2, 256]), ALU.mult,
                        [("PEXP", 2 * b), ("PEXP", 2 * b + 1), "RSM"], ["PNB"])

        for tt in range(16):
            t0 = tt * 128
            ak = ("A1", (t0 // 512) * 512)
            for h in range(4):
                for dc in range(2):
                    fc = 2 * h + dc
                    self.mm(SC[h // 2][:, (h % 2) * 256:(h % 2) * 256 + 256], A1[:, fc, t0:t0 + 128], KT[:, fc, :], dc == 0, dc == 1, [ak, "KT"], [("ps", 5 + h // 2)])
            softmax(128)
            for h in range(4):
                for mt in range(2):
                    a = h * 2 + mt
                    self.tr(psT[:, a * 128:(a + 1) * 128], PNB[:, h, mt * 128:(mt + 1) * 128], IDB, ["PNB", "CONB"], [("ps", 4)])
            self.cp("act", PT[:].rearrange("p a t -> p (a t)"), psT[:, 0:1024], [("ps", 4)], ["PT"])
            for fc in range(8):
                h, dc = fc // 2, fc % 2
                bank, bk = (PS[7], ("ps", 7)) if fc >= 4 else (PS[3], ("ps", 3))
                for mt in range(2):
                    self.mm(bank[:, (fc % 4) * 128:(fc % 4) * 128 + 128], VB[:, mt, h * 256 + dc * 128:h * 256 + dc * 128 + 128], PT[:, h * 2 + mt, :],
                            mt == 0, mt == 1, ["VB", "PT"], [bk])
            self.cp("dve", A1[:, 0:4, t0:t0 + 128], PS[3][:].rearrange("p (a t) -> p a t", a=4), [("ps", 3)], [ak])
            self.cp("act", A1[:, 4:8, t0:t0 + 128], PS[7][:].rearrange("p (a t) -> p a t", a=4), [("ps", 7)], [ak])
        for s in range(NSEQ):
            ks, vs = KS[s % 2], VS[s % 2]
            kkey, vkey = ("KS", s % 2), ("VS", s % 2)
            self.dma("pool", ks, I["ck"][i, s].rearrange("(a p) n -> p a n", p=128), [], [kkey])
            self.dma("pool", vs, I["cv"][i, s].rearrange("(a p) n -> p a n", p=128), [], [vkey])
            for mt in range(2):
                for fc in range(8):
                    self.tr(psT[:, fc * 128:(fc + 1) * 128], ks[:, mt, fc * 128:(fc + 1) * 128], IDB, [kkey, "CONB"], [("ps", 4)])
                self.cp("act", KTS[:, :, mt * 128:(mt + 1) * 128], psT[:, 0:1024].rearrange("p (c m) -> p c m", c=8), [("ps", 4)], ["KTS"])
            c0 = TP + 8 * s
            for h in range(4):
                for dc in range(2):
                    fc = 2 * h + dc
                    self.mm(SC[h // 2][0:8, (h % 2) * 256:(h % 2) * 256 + 256], A1[:, fc, c0:c0 + 8], KTS[:, fc, :], dc == 0, dc == 1, [("A1", 2048), "KTS"], [("ps", 5 + h // 2)])
            softmax(8)
            for h in range(4):
                for mt in range(2):
                    a = h * 2 + mt
                    self.tr(psT[:, a * 8:(a + 1) * 8], PNB[0:8, h, mt * 128:(mt + 1) * 128], IDB[0:8, 0:8], ["PNB", "CONB"], [("ps", 4)])
            self.cp("act", PTS[:].rearrange("p a t -> p (a t)"), psT[:, 0:64], [("ps", 4)], ["PTS"])
            for fc in range(8):
                h, dc = fc // 2, fc % 2
                for mt in range(2):
                    self.mm(PS[7][:, fc * 8:fc * 8 + 8], vs[:, mt, h * 256 + dc * 128:h * 256 + dc * 128 + 128], PTS[:, h * 2 + mt, :], mt == 0, mt == 1, [vkey, "PTS"], [("ps", 7)])
            self.cp("dve", A1[:, :, c0:c0 + 8], PS[7][:, 0:64].rearrange("p (a t) -> p a t", a=8), [("ps", 7)], [("A1", 2048)])
        wo, ko = self.wload(I["xa_w_o"][i])

        def evo(ps, pk, oc, t0, n):
            self.stt(X[:, oc, t0:t0 + n], X[:, oc, t0:t0 + n], ALPHA, ps[:, 0:n], ALU.mult, ALU.add, [pk, ("X", t0)], [("X", t0)])
        self.dense(wo, ko, A1, "A1", evo)

    def s5_layer(self, i, j):
        X, XB, T, I, O, PS, AR, SM, IDF, RS = self.X, self.XB, self.T, self.I, self.O, self.PS, self.AR, self.SM, self.IDF, self.RS
        B0 = 17408
        BBR = AR[:, B0:B0 + 4096].rearrange("p (t c) -> p t c", t=32)
        BBI = AR[:, B0 + 4096:B0 + 8192].rearrange("p (t c) -> p t c", t=32)
        CRB = AR[:, B0 + 8192:B0 + 12288].rearrange("p (t c) -> p t c", t=32)
        CIN = AR[:, B0 + 12288:B0 + 16384].rearrange("p (t c) -> p t c", t=32)
        BU = [AR[:, B0 + 16384 + k * 4096:B0 + 20480 + k * 4096].bitcast(F32).rearrange("p (r t s) -> p r t s", r=2, t=32) for k in range(2)]
        HH = AR[:, B0 + 24576:B0 + 28672].bitcast(F32).rearrange("p (r t s) -> p r t s", r=2, t=32)
        HHB = AR[:, B0 + 28672:B0 + 30720].rearrange("p (r t s) -> p r t s", r=2, t=32)
        SP = AR[:, B0 + 30720:B0 + 33792].bitcast(F32)
        H0 = SP[:, 0:1024].rearrange("p (r t q) -> p r t q", r=2, t=32)

        def sm(k):
            return SP[:, 1024 + 32 * k:1024 + 32 * (k + 1)]
        pair = lambda k: SP[:, 1024 + 32 * k:1024 + 32 * (k + 2)].rearrange("p (r t) -> p r t", r=2)
        C1, C2, HL, M1, M2 = pair(0), pair(2), pair(4), pair(6), pair(8)
        abre, abim = sm(0), sm(1)
        FRE, FIM = sm(10), sm(11)
        dt, ang, mag, sn = sm(12), sm(13), sm(14), sm(15)
        cs, den, nre, r_, m_, kf = [SM[:, 64 + 32 * k:96 + 32 * k] for k in range(6)]
        TI = SM[:, 32:64].bitcast(I32)
        pc = self.pc
        LRE = self.PAR[:, pc[("s5_a_re", j)]:pc[("s5_a_re", j)] + 32]
        LIM = self.PAR[:, pc[("s5_a_im", j)]:pc[("s5_a_im", j)] + 32]
        LDT = self.PAR[:, pc[("s5_log_dt", j)]:pc[("s5_log_dt", j)] + 32]
        K = ["s5p", "PAR"]
        TWO_PI = 2.0 * math.pi
        tt = lambda o, a, b, op: self.tt(o, a, b, op, K, K)
        ts = lambda o, a, s1, op0, s2=None, op1=None: self.ts(o, a, s1, op0, K, K, s2=s2, op1=op1)
        ac = lambda o, a, f, scale=1.0: self.act(o, a, f, K, K, scale=scale)
        ac(dt, LDT, AF.Exp)
        tt(mag, LRE, dt, ALU.mult)
        ac(mag, mag, AF.Exp)
        tt(ang, LIM, dt, ALU.mult)
        ts(kf, ang, 1.0 / TWO_PI, ALU.mult)
        self.cp("dve", TI, kf, K, K)
        self.cp("dve", kf, TI, K, K)
        self.stt(r_, kf, -TWO_PI, ang, ALU.mult, ALU.add, K, K)

        def wrap(x):
            ts(m_, x, math.pi, ALU.is_gt)
            self.stt(x, m_, -TWO_PI, x, ALU.mult, ALU.add, K, K)
            ts(m_, x, -math.pi, ALU.is_lt)
            self.stt(x, m_, TWO_PI, x, ALU.mult, ALU.add, K, K)
        wrap(r_)
        ac(sn, r_, AF.Sin)
        ts(r_, r_, math.pi / 2, ALU.add)
        wrap(r_)
        ac(cs, r_, AF.Sin)
        tt(abre, mag, cs, ALU.mult)
        tt(abim, mag, sn, ALU.mult)
        ts(sm(2), abim, -1.0, ALU.mult)
        self.cp("dve", sm(3), abre, K, K)
        tt(den, LRE, LRE, ALU.mult)
        tt(m_, LIM, LIM, ALU.mult)
        tt(den, den, m_, ALU.add)
        self.rcp(den, den, K, K)
        ts(nre, abre, -1.0, ALU.add)
        tt(FRE, nre, LRE, ALU.mult)
        tt(m_, abim, LIM, ALU.mult)
        tt(FRE, FRE, m_, ALU.add)
        tt(FRE, FRE, den, ALU.mult)
        tt(FIM, abim, LRE, ALU.mult)
        tt(m_, nre, LIM, ALU.mult)
        tt(FIM, FIM, m_, ALU.subtract)
        tt(FIM, FIM, den, ALU.mult)
        self.dma("pool", BBR[:].rearrange("p t c -> p (t c)"), I[("bbr", j)], [], ["BB"])
        self.dma("pool", BBI[:].rearrange("p t c -> p (t c)"), I[("bbi", j)], [], ["BB"])
        v3 = lambda a: a[:].rearrange("p (t c) -> p t c", t=4)
        for g in range(8):
            sr, si, tm, r4 = T[0], T[1], T[2], RS
            self.dma("sp", sr[:], I[("cpr", j)][:, g * 512:(g + 1) * 512], [], ["T0"])
            self.dma("sp", si[:], I[("cpi", j)][:, g * 512:(g + 1) * 512], [], ["T1"])
            fre_b = FRE[:, g * 4:(g + 1) * 4].unsqueeze(2).broadcast_to([128, 4, 128])
            fim_b = FIM[:, g * 4:(g + 1) * 4].unsqueeze(2).broadcast_to([128, 4, 128])
            kk = ["T0", "T1", "T2", "RS", "CC"] + K
            self.tt(v3(tm), v3(si), fim_b, ALU.mult, kk, ["T2"])
            self.tt(v3(si), v3(si), fre_b, ALU.mult, kk, ["T1"])
            self.tt(v3(r4), v3(sr), fre_b, ALU.mult, kk, ["RS"])
            self.tt(CRB[:, g * 4:(g + 1) * 4, :], v3(r4), v3(tm), ALU.subtract, kk, ["CC"])
            self.tt(v3(sr), v3(sr), fim_b, ALU.mult, kk, ["T0"])
            self.tt(v3(sr), v3(sr), v3(si), ALU.add, kk, ["T0"])
            self.ts(CIN[:, g * 4:(g + 1) * 4, :], v3(sr), -1.0, ALU.mult, kk, ["CC"])
        for r, nm in enumerate(("s5re0", "s5im0")):
            for half in range(8):
                st, sk = T[half % 2], "T%d" % (half % 2)
                self.dma("sp", st[0:16, :], I[nm][j][:, half * 512:(half + 1) * 512], ["CC"], [sk])
                for q in range(4):
                    tile = half * 4 + q
                    self.tr(PS[0][:, tile * 16:(tile + 1) * 16], st[0:16, q * 128:(q + 1) * 128], IDF[0:16, 0:16], [sk, "CON"], [("ps", 0)])
            self.cp("dve", H0[:, r, :, :], PS[0][:].rearrange("p (t q) -> p t q", t=32), [("ps", 0)], ["H0"])
        fn2, gre, gim = dt, ang, mag
        tt(fn2, FRE, FRE, ALU.mult)
        tt(m_, FIM, FIM, ALU.mult)
        tt(fn2, fn2, m_, ALU.add)
        self.rcp(fn2, fn2, K, K)
        tt(gre, FRE, fn2, ALU.mult)
        tt(gim, FIM, fn2, ALU.mult)
        ts(gim, gim, -1.0, ALU.mult)
        HT = BU[1]
        bq = lambda f: f.unsqueeze(2).broadcast_to([128, 32, 16])
        K2 = K + ["H0", ("BU", 1)]
        ta, tb = HT[:, 0, :, 0:16], HT[:, 1, :, 0:16]
        self.tt(ta, H0[:, 0], bq(gre), ALU.mult, K2, [("BU", 1)])
        self.tt(tb, H0[:, 1], bq(gim), ALU.mult, K2, [("BU", 1)])
        self.tt(ta, ta, tb, ALU.subtract, K2, [("BU", 1)])
        self.tt(tb, H0[:, 0], bq(gim), ALU.mult, K2, [("BU", 1)])
        self.tt(H0[:, 1], H0[:, 1], bq(gre), ALU.mult, K2, ["H0"])
        self.tt(H0[:, 1], H0[:, 1], tb, ALU.add, K2, ["H0"])
        self.cp("dve", H0[:, 0], ta, K2, ["H0"])

        DPAR = self.par(("s5d", j))
        for b in range(64 + 4):
            bu, bk = BU[b % 2], ("BU", b % 2)
            c0 = b * 32
            tb0 = (c0 // 512) * 512 if c0 < 2048 else 2048
            xk, xk2 = ("XB", tb0), ("X", tb0)
            for r, BB in enumerate((BBR, BBI)):
                for half in range(2):
                    ps, pk = self.next_bank()
                    for tl in range(16):
                        tile = half * 16 + tl
                        self.mm(ps[:, tl * 32:(tl + 1) * 32], BB[:, tile, :], XB[:, tile // 4, c0:c0 + 32], True, True, ["BB", xk], [pk])
                    self.cp("act", bu[:, r, half * 16:(half + 1) * 16, :], ps[:].rearrange("p (t s) -> p t s", t=16), [pk], [bk])
            hk = ["HH", "HL", "m", "H0"] + K
            if b < 64:
                for s in range(32):
                    if b == 0 and s == 0:
                        self.cp("dve", HH[:, :, :, 0], bu[:, :, :, 0], [bk], ["HH"])
                        continue
                    prev = HH[:, :, :, s - 1] if s > 0 else HL
                    self.tt(M1, C1, prev[:, 0:1, :].broadcast_to([128, 2, 32]), ALU.mult, hk, ["m"])
                    self.tt(M2, C2, prev[:, 1:2, :].broadcast_to([128, 2, 32]), ALU.mult, hk, ["m"])
                    self.tt(M1, M1, M2, ALU.add, hk, ["m"])
                    self.tt(HH[:, :, :, s], M1, bu[:, :, :, s], ALU.add, hk + [bk], ["HH"])
                self.cp("dve", HL, HH[:, :, :, 31], ["HH"], ["HL"])
            else:
                q0 = (b - 64) * 4
                buv = bu[:].rearrange("p r t (q s) -> p r t q s", q=4)
                hhv = HH[:].rearrange("p r t (q s) -> p r t q s", q=4)
                ob = BU[1 - b % 2]
                ok = ("BU", 1 - b % 2)
                m1, m2 = ob[:, :, :, 0:4], ob[:, :, :, 4:8]
                c1b = C1.unsqueeze(3).broadcast_to([128, 2, 32, 4])
                c2b = C2.unsqueeze(3).broadcast_to([128, 2, 32, 4])
                for s in range(8):
                    prev = hhv[:, :, :, :, s - 1] if s > 0 else H0[:, :, :, q0:q0 + 4]
                    self.tt(m1, c1b, prev[:, 0:1].broadcast_to([128, 2, 32, 4]), ALU.mult, hk + [ok], [ok])
                    self.tt(m2, c2b, prev[:, 1:2].broadcast_to([128, 2, 32, 4]), ALU.mult, hk + [ok], [ok])
                    self.tt(m1, m1, m2, ALU.add, hk + [ok], [ok])
                    self.tt(hhv[:, :, :, :, s], m1, buv[:, :, :, :, s], ALU.add, hk + [ok, bk], ["HH"])
            if b >= 63:
                nq = 1 if b == 63 else 4
                src = HH[:, :, :, 31:32] if b == 63 else HH[:].rearrange("p r t (q s) -> p r t q s", q=4)[:, :, :, :, 7]
                FT, fk = BU[1 - b % 2], ("BU", 1 - b % 2)
                fo, ft = FT[:, :, :, 8:8 + nq], FT[:, :, :, 12:12 + nq]
                fb = lambda f: f.unsqueeze(2).broadcast_to([128, 32, nq])
                rk = ["HH", fk] + K
                self.tt(fo[:, 0], src[:, 0], fb(FRE), ALU.mult, rk, [fk])
                self.tt(ft[:, 0], src[:, 1], fb(FIM), ALU.mult, rk, [fk])
                self.tt(fo[:, 0], fo[:, 0], ft[:, 0], ALU.subtract, rk, [fk])
                self.tt(fo[:, 1], src[:, 0], fb(FIM), ALU.mult, rk, [fk])
                self.tt(ft[:, 1], src[:, 1], fb(FRE), ALU.mult, rk, [fk])
                self.tt(fo[:, 1], fo[:, 1], ft[:, 1], ALU.add, rk, [fk])
                for r in range(2):
                    if b == 63:
                        outn = ("s5pre", "s5pim")[r]
                        self.tr(PS[7][0:32, 0:128], fo[:, r, :, 0], IDF, [fk, "CON"], [("ps", 7)])
                        self.cp("act", T[2][0:32, 0:128], PS[7][0:32, 0:128], [("ps", 7)], ["T2"])
                        self.dma("sp", O[outn][j].rearrange("(t p) -> t p", p=128), T[2][0:32, 0:128], ["T2"], [("so", outn)])
                    else:
                        outn = ("s5sre", "s5sim")[r]
                        q0 = (b - 64) * 4
                        for tg in range(8):
                            for tl in range(4):
                                self.tr(PS[7][0:4, tl * 128:(tl + 1) * 128], fo[:, r, tg * 4 + tl, :], IDF, [fk, "CON"], [("ps", 7)])
                            self.cp("act", T[2][0:4, :], PS[7][0:4, :], [("ps", 7)], ["T2"])
                            self.dma("sp", O[outn][j][q0:q0 + 4, tg * 512:(tg + 1) * 512], T[2][0:4, :], ["T2"], [("so", outn, tg, q0)])
            self.cp("act", HHB[:], HH[:], ["HH"], ["HHB"])
            for k in range(8):
                n = 0
                for q in range(4):
                    tile = k * 4 + q
                    for r, CC in enumerate((CRB, CIN)):
                        self.mm(PS[6][:, k * 32:(k + 1) * 32], CC[:, tile, :], HHB[:, r, tile, :], n == 0, n == 7, ["CC", "HHB"], [("ps", 6)])
                        n += 1
            yv = PS[6][:, 0:256].rearrange("p (k s) -> p k s", k=8)
            v = T[0][:, 0:256].rearrange("p (k s) -> p k s", k=8)
            t1 = T[1][:, 0:256].rearrange("p (k s) -> p k s", k=8)
            gk = ["T0", "T1", ("ps", 6)]
            self.tt(v, X[:, :, c0:c0 + 32], DPAR.unsqueeze(2).broadcast_to([128, 8, 32]), ALU.mult, gk + [xk2, "PAR"], ["T0"])
            self.tt(v, v, yv, ALU.add, gk, ["T0"])
            self.tt(t1, v, v, ALU.mult, gk, ["T1"])
            self.ts(t1, t1, 0.044715, ALU.mult, gk, ["T1"], s2=1.0, op1=ALU.add)
            self.tt(t1, t1, v, ALU.mult, gk, ["T1"])
            self.act(t1, t1, AF.Tanh, gk, ["T1"], scale=0.7978845608028654)
            self.ts(t1, t1, 1.0, ALU.add, gk, ["T1"], s2=0.5, op1=ALU.mult)
            self.tt(XB[:, :, c0:c0 + 32], t1, v, ALU.mult, gk + [xk], [xk])
        self.P.emit()
        wv, kv = self.wload(I["s5_w_glu_v"][j])
        wg, kg = self.wload(I["s5_w_glu_g"][j])
        for (t0, n) in TBLK:
            for oc in range(8):
                pv, pg = PS[(oc % 2) * 2], PS[1 + (oc % 2) * 2]
                kpv, kpg = ("ps", (oc % 2) * 2), ("ps", 1 + (oc % 2) * 2)
                for c in range(8):
                    self.mm(pv[:, 0:n], wv[:, c, oc * 128:(oc + 1) * 128], XB[:, c, t0:t0 + n], c == 0, c == 7, [kv, ("XB", t0)], [kpv])
                for c in range(8):
                    self.mm(pg[:, 0:n], wg[:, c, oc * 128:(oc + 1) * 128], XB[:, c, t0:t0 + n], c == 0, c == 7, [kg, ("XB", t0)], [kpg])
                tt_, tk = T[oc % 2], ("T", oc % 2)
                self.act(tt_[:, 0:n], pg[:, 0:n], AF.Sigmoid, [kpg], [tk])
                self.tt(tt_[:, 0:n], tt_[:, 0:n], pv[:, 0:n], ALU.mult, [kpv, tk], [tk])
                self.stt(X[:, oc, t0:t0 + n], X[:, oc, t0:t0 + n], ALPHA, tt_[:, 0:n], ALU.mult, ALU.add, [tk, ("X", t0)], [("X", t0)])

    def rwkv_layer(self, i, j):
        X, XB, T, I, O, PS, AR, SM, IDF, BLK = self.X, self.XB, self.T, self.I, self.O, self.PS, self.AR, self.SM, self.IDF, self.BLK
        if STUB_RWKV:
            for (t0, n) in TBLK:
                self.ts(X[:, :, t0:t0 + n], X[:, :, t0:t0 + n], ALPHA, ALU.mult, [("X", t0)], [("X", t0)])
            return
        off = [17408]

        def ab(n):
            a = AR[:, off[0]:off[0] + n]
            off[0] += n
            assert off[0] <= 51200
            return a

        def af(n):
            return ab(2 * n).bitcast(F32)
        q3 = lambda a: a.rearrange("p (c t) -> p c t", c=2)
        WRq, WKq, WVq = [ab(2048).rearrange("p (c n) -> p c n", c=8) for _ in range(3)]
        WOq = ab(2048).rearrange("p (c n) -> p c n", c=2)
        W1, A1w = [ab(512).rearrange("p (c n) -> p c n", c=8) for _ in range(2)]
        G1w = ab(1024).rearrange("p (c n) -> p c n", c=8)
        V1w = ab(256).rearrange("p (c n) -> p c n", c=8)
        W2q, A2q, V2q, G2q = [ab(256) for _ in range(4)]
        XM = ab(3072).rearrange("p (m c t) -> p m c t", m=6, c=8)
        XX = af(512).rearrange("p (c t) -> p c t", c=8)
        TMX = af(512).rearrange("p (c t) -> p c t", c=8)
        R_, K_, V_, A_, G_, KAP, LW, LG, E_, T1t, BON, Y_, BT, KT, BH, KH, YBs, VFt = [q3(af(128)) for _ in range(18)]
        ATRT = af(256).rearrange("p (w c t) -> p w c t", w=2, c=2)
        AT, RT = ATRT[:, 0], ATRT[:, 1]
        Vt, BHt, KHt, Ut = [ab(256) for _ in range(4)]
        h4 = lambda a: a.rearrange("p (h t) -> p h t", h=4)
        Xa, XTa, Xb, XTb, Pm, Rs = [h4(af(256)) for _ in range(6)]
        MBR, MKA, MKR = [h4(ab(256)) for _ in range(3)]
        STp, STs, BDin = [af(256).rearrange("p (c n) -> p c n", c=2) for _ in range(3)]
        OB = q3(ab(128))
        T1b, A1b, V1b, G1b = ab(64), ab(64), ab(64), ab(64)
        SH0 = af(128).rearrange("p (c s) -> p c s", c=8)
        XL = af(136).rearrange("p (c s) -> p c s", c=8)
        OMKA = af(8)
        RSTa, RSTb = af(128), af(128)
        GLu = af(16).rearrange("p (c u) -> p c u", c=2)
        LGL = af(16).rearrange("p (c u) -> p c u", c=2)
        SSn = q3(af(128))
        pc = self.pc
        MU = self.par(("mu", j), 48).rearrange("p (m c) -> p m c", m=6)
        W0, A0, KKp, KAp, LXG, LXB, RKp = [self.par((nm, j)) for nm in ("rwkv_w0", "rwkv_a0", "rwkv_k_k", "rwkv_k_a", "rwkv_lnx_g", "rwkv_lnx_b", "rwkv_r_k")]
        V0p = self.par(("rwkv_v0", 1))
        B = [PS[k] for k in range(8)]
        bk = lambda k: ("ps", k)
        K0 = ["rk"]

        self.dma("pool", W1, I["rwkv_w1"][j].rearrange("(c p) n -> p c n", p=128), [], ["Wl"])
        self.dma("pool", A1w, I["rwkv_a1"][j].rearrange("(c p) n -> p c n", p=128), [], ["Wl"])
        self.dma("pool", G1w, I["rwkv_g1"][j].rearrange("(c p) n -> p c n", p=128), [], ["Wl"])
        if j == 1:
            self.dma("pool", V1w, I["rwkv_v1"][0].rearrange("(c p) n -> p c n", p=128), [], ["Wl"])
        self.ts(OMKA, KAp, -1.0, ALU.mult, ["PAR"], K0, s2=1.0, op1=ALU.add)
        for a, src in ((RSTa, self.RST64), (RSTb, self.RST8)):
            self.cp("dve", a[:, 0:64], src, ["CON"], K0)
            self.cp("dve", a[:, 64:128], src, ["CON"], K0)
        self.cp("dve", XL[:, :, 0:1], X[:, :, 2047:2048], [("X", 1536)], ["XL"])
        self.cp("dve", XL[:, :, 1:17], X[:, :, 2048:2176].rearrange("p c (u t) -> p c u t", u=16)[:, :, :, 7], [("X", 2048)], ["XL"])
        for half in range(2):
            for cc in range(4):
                c = half * 4 + cc
                self.tr(B[6][0:17, cc * 128:(cc + 1) * 128], XL[:, c, :], IDF, ["XL", "CON"], [bk(6)])
            self.cp("act", T[half][0:17, :], B[6][0:17, :], [bk(6)], [("T", half)])
            self.dma("sp", O["shp"][j:j + 1, half * 512:(half + 1) * 512], T[half][0:1, :], [("T", half)], [("sho", half)])
            self.dma("sp", O["shs"][j][:, half * 512:(half + 1) * 512], T[half][1:17, :], [("T", half)], [("shso", half)])
        for half in range(2):
            self.dma("sp", T[half][0:16, :], I["sh0"][j][:, half * 512:(half + 1) * 512], [], [("T", half)])
            for cc in range(4):
                c = half * 4 + cc
                self.tr(B[7][:, c * 16:(c + 1) * 16], T[half][0:16, cc * 128:(cc + 1) * 128], IDF[0:16, 0:16], [("T", half), "CON"], [bk(7)])
        self.cp("dve", SH0, B[7][:, 0:128].rearrange("p (c s) -> p c s", c=8), [bk(7)], ["SH0"])
        self.mset(BDin, 0.0, [], ["BDin"])

        for hq in range(4):
            cs = slice(hq * 256, (hq + 1) * 256)
            for w, nm in ((WRq, "rwkv_w_r"), (WKq, "rwkv_w_k"), (WVq, "rwkv_w_v")):
                self.dma("pool", w, I[nm][j][:, cs].rearrange("(c p) n -> p c n", p=128), [], ["Wq"])
            self.dma("pool", WOq, I["rwkv_w_o"][j][cs, :].rearrange("(c p) n -> p c n", p=128), [], ["Wq"])
            self.dma("pool", W2q[0:64, :], I["rwkv_w2"][j][:, cs], [], ["Wq"])
            self.dma("pool", A2q[0:64, :], I["rwkv_a2"][j][:, cs], [], ["Wq"])
            if j == 1:
                self.dma("pool", V2q[0:32, :], I["rwkv_v2"][0][:, cs], [], ["Wq"])
            self.dma("pool", G2q, I["rwkv_g2"][j][:, cs], [], ["Wq"])
            self.mset(STp, 0.0, [], ["STp"])
            pcs = slice(2 * hq, 2 * hq + 2)
            bc = lambda p: p[:, pcs].unsqueeze(2).broadcast_to([128, 2, 64])
            for blk in range(34):
                samp = blk >= 32
                g0 = 64 * blk
                tb0 = (g0 // 512) * 512 if g0 < 2048 else 2048
                xbb = XB[:, :, g0:g0 + 64]
                xk = ("XB", tb0)
                kb_ = ["blk"]
                if samp:
                    sb = blk - 32
                    xbv = xbb.rearrange("p c (u t) -> p c u t", u=8)
                    xxv = XX.rearrange("p c (u t) -> p c u t", u=8)
                    self.tt(xxv[:, :, :, 1:8], xbv[:, :, :, 0:7], xbv[:, :, :, 1:8], ALU.subtract, [xk], kb_)
                    self.tt(xxv[:, :, :, 0], SH0[:, :, 8 * sb:8 * sb + 8], xbv[:, :, :, 0], ALU.subtract, [xk, "SH0"], kb_)
                elif blk == 0:
                    self.tt(XX[:, :, 1:64], XB[:, :, 0:63], XB[:, :, 1:64], ALU.subtract, [xk], kb_)
                    self.ts(XX[:, :, 0:1], XB[:, :, 0:1], -1.0, ALU.mult, [xk], kb_)
                else:
                    pk_ = ("XB", ((g0 - 1) // 512) * 512)
                    self.tt(XX, XB[:, :, g0 - 1:g0 + 63], xbb, ALU.subtract, [xk, pk_], kb_)
                for m in range(6):
                    self.tt(TMX, XX, MU[:, m, :].unsqueeze(2).broadcast_to([128, 8, 64]), ALU.mult, kb_ + ["PAR"], kb_)
                    self.tt(XM[:, m], TMX, xbb, ALU.add, kb_ + [xk], kb_)
                for n_, (w, m) in enumerate(((WRq, 0), (WKq, 2), (WVq, 3))):
                    for c2 in range(2):
                        for c in range(8):
                            self.mm(B[0][:, n_ * 128 + c2 * 64:n_ * 128 + c2 * 64 + 64], w[:, c, c2 * 128:(c2 + 1) * 128], XM[:, m, c, :], c == 0, c == 7, kb_ + ["Wq"], [bk(0)])
                self.cp("act", R_, q3(B[0][:, 0:128]), [bk(0)], kb_)
                self.cp("dve", K_, q3(B[0][:, 128:256]), [bk(0)], kb_)
                self.cp("act", V_, q3(B[0][:, 256:384]), [bk(0)], kb_)
                for c in range(8):
                    self.mm(B[1][0:64, 0:64], W1[:, c, :], XM[:, 1, c, :], c == 0, c == 7, kb_ + ["Wl"], [bk(1)])
                for c in range(8):
                    self.mm(B[1][0:64, 64:128], A1w[:, c, :], XM[:, 4, c, :], c == 0, c == 7, kb_ + ["Wl"], [bk(1)])
                if j == 1:
                    for c in range(8):
                        self.mm(B[1][0:32, 128:192], V1w[:, c, :], XM[:, 3, c, :], c == 0, c == 7, kb_ + ["Wl"], [bk(1)])
                for c in range(8):
                    self.mm(B[1][:, 192:256], G1w[:, c, :], XM[:, 5, c, :], c == 0, c == 7, kb_ + ["Wl"], [bk(1)])
                self.act(T1b[0:64, :], B[1][0:64, 0:64], AF.Tanh, [bk(1)], kb_)
                self.cp("act", A1b[0:64, :], B[1][0:64, 64:128], [bk(1)], kb_)
                if j == 1:
                    self.cp("act", V1b[0:32, :], B[1][0:32, 128:192], [bk(1)], kb_)
                self.act(G1b, B[1][:, 192:256], AF.Sigmoid, [bk(1)], kb_)
                for c2 in range(2):
                    self.mm(B[2][:, c2 * 64:c2 * 64 + 64], W2q[0:64, c2 * 128:(c2 + 1) * 128], T1b[0:64, :], True, True, kb_ + ["Wq"], [bk(2)])
                    self.mm(B[2][:, 128 + c2 * 64:128 + c2 * 64 + 64], A2q[0:64, c2 * 128:(c2 + 1) * 128], A1b[0:64, :], True, True, kb_ + ["Wq"], [bk(2)])
                    if j == 1:
                        self.mm(B[2][:, 256 + c2 * 64:256 + c2 * 64 + 64], V2q[0:32, c2 * 128:(c2 + 1) * 128], V1b[0:32, :], True, True, kb_ + ["Wq"], [bk(2)])
                    self.mm(B[2][:, 384 + c2 * 64:384 + c2 * 64 + 64], G2q[:, c2 * 128:(c2 + 1) * 128], G1b, True, True, kb_ + ["Wq"], [bk(2)])
                kp = kb_ + ["PAR"]
                self.tt(LW, q3(B[2][:, 0:128]), bc(W0), ALU.add, kp + [bk(2)], kb_)
                self.act(LW, LW, AF.Sigmoid, kb_, kb_)
                self.ts(LW, LW, -0.6065306597126334, ALU.mult, kb_, kb_)
                self.tt(A_, q3(B[2][:, 128:256]), bc(A0), ALU.add, kp + [bk(2)], kb_)
                self.act(A_, A_, AF.Sigmoid, kb_, kb_)
                self.cp("dve", G_, q3(B[2][:, 384:512]), [bk(2)], kb_)
                vfd = self.VF[:, 2 * hq:2 * hq + 2, g0:g0 + 64]
                if j == 1:
                    self.tt(T1t, q3(B[2][:, 256:384]), bc(V0p), ALU.add, kp + [bk(2)], kb_)
                    self.act(T1t, T1t, AF.Sigmoid, kb_, kb_)
                    self.dma("sp", VFt, vfd, kb_, kb_)
                    self.tt(VFt, VFt, V_, ALU.subtract, kb_, kb_)
                    self.tt(VFt, VFt, T1t, ALU.mult, kb_, kb_)
                    self.tt(V_, V_, VFt, ALU.add, kb_, kb_)
                else:
                    self.dma("sp", vfd, V_, kb_, [("vf", hq, blk)])
                self.tt(KAP, K_, bc(KKp), ALU.mult, kp, kb_)
                self.tt(T1t, KAP, KAP, ALU.mult, kb_, kb_)
                for c2 in range(2):
                    self.mm(B[3][:, c2 * 64:c2 * 64 + 64], BLK, T1t[:, c2, :], True, True, kb_ + ["CON"], [bk(3)])
                self.act(SSn, q3(B[3][:, 0:128]), AF.Sqrt, [bk(3)], kb_)
                self.ts(SSn, SSn, 1e-12, ALU.max, kb_, kb_)
                self.rcp(SSn, SSn, kb_, kb_)
                self.tt(KAP, KAP, SSn, ALU.mult, kb_, kb_)
                self.tt(T1t, A_, bc(KAp), ALU.mult, kp, kb_)
                self.tt(T1t, T1t, OMKA[:, pcs].unsqueeze(2).broadcast_to([128, 2, 64]), ALU.add, kb_ + K0, kb_)
                self.tt(K_, K_, T1t, ALU.mult, kb_, kb_)
                self.tt(T1t, R_, K_, ALU.mult, kb_, kb_)
                self.tt(T1t, T1t, bc(RKp), ALU.mult, kp, kb_)
                for c2 in range(2):
                    self.mm(B[3][:, 128 + c2 * 64:128 + c2 * 64 + 64], BLK, T1t[:, c2, :], True, True, kb_ + ["CON"], [bk(3)])
                self.tt(BON, q3(B[3][:, 128:256]), V_, ALU.mult, kb_ + [bk(3)], kb_)
                self.scan(LG.rearrange("p c t -> p (c t)"), RSTb if samp else RSTa, LW.rearrange("p c t -> p (c t)"), kb_ + K0, kb_)
                nu = 8 if samp else 1
                L = 8 if samp else 64
                lgv = LG.rearrange("p c (u t) -> p c u t", u=nu)
                self.cp("dve", LGL[:, :, 0:nu], lgv[:, :, :, L - 1], kb_, kb_)
                self.act(GLu[:, :, 0:nu], LGL[:, :, 0:nu], AF.Exp, kb_, kb_)
                self.tt(A_, KAP, A_, ALU.mult, kb_, kb_)
                self.act(E_, LG, AF.Exp, kb_, kb_)
                self.tt(RT, R_, E_, ALU.mult, kb_, kb_)
                self.tt(T1t, LG, LW, ALU.subtract, kb_, kb_)
                self.act(E_, T1t, AF.Exp, kb_, kb_)
                self.stt(AT, KAP, -1.0, E_, ALU.mult, ALU.mult, kb_, kb_)
                self.act(E_, LG, AF.Exp, kb_, kb_, scale=-1.0)
                self.tt(BT, A_, E_, ALU.mult, kb_, kb_)
                self.tt(KT, K_, E_, ALU.mult, kb_, kb_)
                t1v = T1t.rearrange("p c (u t) -> p c u t", u=nu)
                self.tt(t1v, LGL[:, :, 0:nu].unsqueeze(3).broadcast_to([128, 2, nu, L]), lgv, ALU.subtract, kb_, kb_)
                self.act(E_, T1t, AF.Exp, kb_, kb_)
                self.tt(BH, A_, E_, ALU.mult, kb_, kb_)
                self.tt(KH, K_, E_, ALU.mult, kb_, kb_)
                for u in range(nu):
                    c0 = u * L
                    if samp:
                        sq = (blk - 32) * 8 + u
                        ST = STs
                        for c2 in range(2):
                            for h2 in range(2):
                                hd = 4 * hq + 2 * c2 + h2
                                self.dma("sp", BDin[64 * h2:64 * h2 + 64, c2, 64 * h2:64 * h2 + 64], I["rw0"][j, sq, hd], ["BDin"] + kb_, ["BDin"])
                        for c2 in range(2):
                            self.tr(B[7][:, c2 * 128:(c2 + 1) * 128], BDin[:, c2, :], IDF, ["BDin", "CON"], [bk(7)])
                        self.cp("dve", STs, B[7][:, 0:256].rearrange("p (c n) -> p c n", c=2), [bk(7)], ["ST"])
                    else:
                        ST = STp
                    self.rwkv_unit(c0, L, ST, dict(V_=V_, BH=BH, KH=KH, AT=AT, RT=RT, ATRT=ATRT, BT=BT, KT=KT, Vt=Vt, BHt=BHt, KHt=KHt, Ut=Ut, Xa=Xa, XTa=XTa, Xb=Xb, XTb=XTb,
                                                     Pm=Pm, Rs=Rs, MBR=MBR, MKA=MKA, MKR=MKR, YBs=YBs, Y_=Y_, GL=GLu[:, :, u]), kb_)
                    last = (blk == 31) or samp
                    if last:
                        for c2 in range(2):
                            self.tr(B[7][:, c2 * 128:(c2 + 1) * 128], ST[:, c2, :], IDF, ["ST", "CON"], [bk(7)])
                        self.cp("dve", BDin, B[7][:, 0:256].rearrange("p (c n) -> p c n", c=2), [bk(7)], ["BDin"])
                        for c2 in range(2):
                            for h2 in range(2):
                                hd = 4 * hq + 2 * c2 + h2
                                dst = O["rws"][j, sq, hd] if samp else O["rwp"][j, hd]
                                self.dma("sp", dst, BDin[64 * h2:64 * h2 + 64, c2, 64 * h2:64 * h2 + 64], ["BDin"], [("rwo", hq, blk, u, c2, h2)])
                for c2 in range(2):
                    self.mm(B[3][:, 256 + c2 * 64:256 + c2 * 64 + 64], BLK, Y_[:, c2, :], True, True, kb_ + ["CON"], [bk(3)])
                self.stt(Y_, q3(B[3][:, 256:384]), -1.0 / 64.0, Y_, ALU.mult, ALU.add, kb_ + [bk(3)], kb_)
                self.tt(T1t, Y_, Y_, ALU.mult, kb_, kb_)
                for c2 in range(2):
                    self.mm(B[3][:, 384 + c2 * 64:384 + c2 * 64 + 64], BLK, T1t[:, c2, :], True, True, kb_ + ["CON"], [bk(3)])
                self.act(SSn, q3(B[3][:, 384:512]), AF.Sqrt, [bk(3), "eps"], kb_, bias=self.EPS2, scale=1.0 / 64.0)
                self.rcp(SSn, SSn, kb_, kb_)
                self.tt(Y_, Y_, SSn, ALU.mult, kb_, kb_)
                for c2 in range(2):
                    c = 2 * hq + c2
                    self.ts(Y_[:, c2, :], Y_[:, c2, :], LXG[:, c:c + 1], ALU.mult, kp, kb_, s2=LXB[:, c:c + 1], op1=ALU.add)
                self.tt(Y_, Y_, BON, ALU.add, kb_, kb_)
                self.tt(OB, Y_, G_, ALU.mult, kb_, kb_)
                for oc in range(8):
                    for c2 in range(2):
                        self.mm(B[4][:, oc * 64:(oc + 1) * 64], WOq[:, c2, oc * 128:(oc + 1) * 128], OB[:, c2, :], c2 == 0, c2 == 1, kb_ + ["Wq"], [bk(4)])
                xg = ("X", tb0)
                pso = B[4][:].rearrange("p (c t) -> p c t", c=8)
                if hq == 0:
                    self.stt(X[:, :, g0:g0 + 64], X[:, :, g0:g0 + 64], ALPHA, pso, ALU.mult, ALU.add, [bk(4), xg, "XL"], [xg])
                else:
                    self.tt(X[:, :, g0:g0 + 64], X[:, :, g0:g0 + 64], pso, ALU.add, [bk(4), xg], [xg])

    def rwkv_unit(self, c0, L, ST, t, kb_):
        PS, IDF = self.PS, self.IDF
        B = PS
        bk = lambda k: ("ps", k)
        ku = kb_ + ["ST"]
        cs = slice(c0, c0 + L)
        nlev = {64: 5, 8: 2}[L]
        for n_, (src, dst, bank, col) in enumerate(((t["V_"], t["Vt"], 5, 0), (t["BH"], t["BHt"], 5, 256), (t["KH"], t["KHt"], 6, 0))):
            for c2 in range(2):
                self.tr(B[bank][0:L, col + c2 * 128:col + (c2 + 1) * 128], src[:, c2, cs], IDF, kb_ + ["CON"], [bk(bank)])
            self.cp("act" if n_ % 2 == 0 else "dve", dst[0:L, :], B[bank][0:L, col:col + 256], [bk(bank)], ku)
        ATRT, AT, BT, KT = t["ATRT"], t["AT"], t["BT"], t["KT"]
        for h2 in range(2):
            b0 = 64 * h2
            bs, bn = B[h2], B[2 + h2]
            for c2 in range(2):
                rhs = ATRT[b0:b0 + 64, :, c2, cs]
                self.mm(bs[0:L, c2 * 128:c2 * 128 + 2 * L], BT[b0:b0 + 64, c2, cs], rhs, True, True, ku, [bk(h2)])
                self.mm(bs[0:L, 256 + c2 * 128:256 + c2 * 128 + 2 * L], KT[b0:b0 + 64, c2, cs], rhs, True, True, ku, [bk(h2)])
                self.mm(bn[0:L, c2 * 64:c2 * 64 + L], AT[b0:b0 + 64, c2, cs], BT[b0:b0 + 64, c2, cs], True, True, ku, [bk(2 + h2)])
            v4 = bs[0:L, :].rearrange("p (k x) -> p k x", k=4)
            mlt = self.MLT[0:L, 0:L].unsqueeze(1).broadcast_to([L, 2, L])
            mle = self.MLE[0:L, 0:L].unsqueeze(1).broadcast_to([L, 2, L])
            mgt = self.MGT[0:L, 0:L].unsqueeze(1).broadcast_to([L, 2, L])
            hsel = slice(h2, 4, 2)
            self.tt(t["Xa"][0:L, hsel, 0:L], v4[:, 0:2, 0:L], mlt, ALU.mult, [bk(h2), "CON"], ku)
            self.tt(t["MBR"][0:L, hsel, 0:L], v4[:, 0:2, L:2 * L], mle, ALU.mult, [bk(h2), "CON"], ku)
            self.tt(t["MKA"][0:L, hsel, 0:L], v4[:, 2:4, 0:L], mlt, ALU.mult, [bk(h2), "CON"], ku)
            self.tt(t["MKR"][0:L, hsel, 0:L], v4[:, 2:4, L:2 * L], mle, ALU.mult, [bk(h2), "CON"], ku)
            self.tt(t["XTa"][0:L, hsel, 0:L], bn[0:L, 0:128].rearrange("p (k x) -> p k x", k=2)[:, :, 0:L], mgt, ALU.mult, [bk(2 + h2), "CON"], ku)
        Xc, XTc, Xn, XTn, Pm = t["Xa"], t["XTa"], t["Xb"], t["XTb"], t["Pm"]
        self.tt(Pm[0:L, :, 0:L], Xc[0:L, :, 0:L], IDF[0:L, 0:L].unsqueeze(1).broadcast_to([L, 4, L]), ALU.add, ku + ["CON"], ku)
        for lev in range(nlev):
            lastlev = lev == nlev - 1
            for hl in range(4):
                if not lastlev:
                    self.mm(B[0][0:L, hl * 64:hl * 64 + L], XTc[0:L, hl, 0:L], Xc[0:L, hl, 0:L], True, True, ku, [bk(0)])
                self.mm(B[1][0:L, hl * 64:hl * 64 + L], Xc[0:L, hl, 0:L], XTc[0:L, hl, 0:L], True, True, ku, [bk(1)])
            if not lastlev:
                self.cp("act", Xn[0:L, :, 0:L], B[0][0:L, 0:256].rearrange("p (h x) -> p h x", h=4)[:, :, 0:L], [bk(0)], ku)
            self.cp("dve", XTn[0:L, :, 0:L], B[1][0:L, 0:256].rearrange("p (h x) -> p h x", h=4)[:, :, 0:L], [bk(1)], ku)
            for hl in range(4):
                self.mm(B[2][0:L, hl * 64:hl * 64 + L], XTn[0:L, hl, 0:L], Pm[0:L, hl, 0:L], True, True, ku, [bk(2)])
            self.tt(Pm[0:L, :, 0:L], Pm[0:L, :, 0:L], B[2][0:L, 0:256].rearrange("p (h x) -> p h x", h=4)[:, :, 0:L], ALU.add, ku + [bk(2)], ku)
            Xc, XTc, Xn, XTn = Xn, XTn, Xc, XTc
        Vt, BHt, KHt, Ut, Rs, MKA, MBR, MKR = t["Vt"], t["BHt"], t["KHt"], t["Ut"], t["Rs"], t["MKA"], t["MBR"], t["MKR"]
        for c2 in range(2):
            self.mm(B[3][0:L, c2 * 128:(c2 + 1) * 128], AT[:, c2, cs], ST[:, c2, :], True, False, ku, [bk(3)])
            for h2 in range(2):
                hl = 2 * c2 + h2
                self.mm(B[3][0:L, hl * 64:(hl + 1) * 64], MKA[0:L, hl, 0:L], Vt[0:L, hl * 64:(hl + 1) * 64], False, h2 == 1, ku, [bk(3)])
        self.cp("act", Rs[0:L, :, :], B[3][0:L, 0:256].rearrange("p (h x) -> p h x", h=4), [bk(3)], ku)
        for hl in range(4):
            self.mm(B[4][0:L, hl * 64:(hl + 1) * 64], Pm[0:L, hl, 0:L], Rs[0:L, hl, :], True, True, ku, [bk(4)])
        self.cp("dve", Ut[0:L, :], B[4][0:L, 0:256], [bk(4)], ku)
        for c2 in range(2):
            self.mm(B[5][:, c2 * 64:c2 * 64 + L], ST[:, c2, :], t["RT"][:, c2, cs], True, True, ku, [bk(5)])
        for hl in range(4):
            self.mm(B[6][0:64, hl * 64:hl * 64 + L], Ut[0:L, hl * 64:(hl + 1) * 64], MBR[0:L, hl, 0:L], True, False, ku, [bk(6)])
            self.mm(B[6][0:64, hl * 64:hl * 64 + L], Vt[0:L, hl * 64:(hl + 1) * 64], MKR[0:L, hl, 0:L], False, True, ku, [bk(6)])
        ybv = B[6][0:64, 0:256].rearrange("p (c h x) -> p c h x", c=2, h=2)
        YBs = t["YBs"]
        for h2 in range(2):
            self.cp("act", YBs[64 * h2:64 * h2 + 64, :, 0:L], ybv[:, :, h2, 0:L], [bk(6)], ku)
        self.tt(t["Y_"][:, :, cs], B[5][:, 0:128].rearrange("p (c x) -> p c x", c=2)[:, :, 0:L], YBs[:, :, 0:L], ALU.add, ku + [bk(5)], ku)
        for hl in range(4):
            c2 = hl // 2
            self.mm(B[7][:, hl * 64:(hl + 1) * 64], BHt[0:L, c2 * 128:(c2 + 1) * 128], Ut[0:L, hl * 64:(hl + 1) * 64], True, False, ku, [bk(7)])
            self.mm(B[7][:, hl * 64:(hl + 1) * 64], KHt[0:L, c2 * 128:(c2 + 1) * 128], Vt[0:L, hl * 64:(hl + 1) * 64], False, True, ku, [bk(7)])
        suv = B[7][:, 0:256].rearrange("p (c h x) -> p c h x", c=2, h=2)
        GL = t["GL"]
        for h2 in range(2):
            r = slice(64 * h2, 64 * h2 + 64)
            blkv = ST[r, :, 64 * h2:64 * h2 + 64]
            self.tt(blkv, blkv, GL[r, :].unsqueeze(2).broadcast_to([64, 2, 64]), ALU.mult, ku, ku)
            self.tt(blkv, blkv, suv[r, :, h2, :], ALU.add, ku + [bk(7)], ku)

    def build(self):
        self.setup()
        self.phase_input()
        for i in range(DEPTH):
            j = i // 2
            if i % 2 == 0:
                self.s5_layer(i, j)
            else:
                self.rwkv_layer(i, j)
            self.layer_norm(i, 0)
            self.P.emit()
            self.xa_layer(i)
            self.layer_norm(i, 1)
            self.P.emit()
            self.mlp_layer(i)
            self.layer_norm(i, 2)
            self.P.emit()
        self.phase_output()
        return self.nc


WSHAPES = (("s5_w_glu_v", [2, D, D]), ("s5_w_glu_g", [2, D, D]), ("rwkv_w_r", [2, D, D]), ("rwkv_w_k", [2, D, D]),
           ("rwkv_w_v", [2, D, D]), ("rwkv_w_o", [2, D, D]), ("rwkv_w1", [2, D, 64]), ("rwkv_w2", [2, 64, D]),
           ("rwkv_a1", [2, D, 64]), ("rwkv_a2", [2, 64, D]), ("rwkv_v1", [1, D, 32]), ("rwkv_v2", [1, 32, D]),
           ("rwkv_g1", [2, D, 128]), ("rwkv_g2", [2, 128, D]), ("xa_w_q", [4, D, D]), ("xa_w_k", [4, D, D]),
           ("xa_w_v", [4, D, D]), ("xa_w_o", [4, D, D]), ("mlp_w1", [4, D, 4 * D]), ("mlp_w2", [4, 4 * D, D]))

_CACHE = {}


def kernel(**inp):
    inp = {k: np.asarray(v) for k, v in inp.items()}
    par, pcols = pack_params(inp)
    npar = par.shape[1]
    if "nc" not in _CACHE:
        kb = KB(pcols, npar)
        _CACHE["nc"] = kb.build()
        _CACHE["names"] = [k if isinstance(k, str) else f"{k[0]}{k[1]}" for k in kb.I.keys()]
    nc = _CACHE["nc"]
    consts = make_consts()
    s5m = [pack_s5_mats(inp, j) for j in range(2)]
    in_maps = []
    for cid in range(8):
        sl = slice(cid * NSEQ, (cid + 1) * NSEQ)
        m = {}
        m["xin"] = np.ascontiguousarray(np.concatenate([inp["x_prompt"][cid], inp["x_sample"][sl].reshape(TS, D)], axis=0))
        m["mem"] = np.ascontiguousarray(inp["mem_prompt"][cid])
        m["ck"] = np.ascontiguousarray(inp["cache_mem_k"][:, sl].reshape(4, NSEQ, 256, D))
        m["cv"] = np.ascontiguousarray(inp["cache_mem_v"][:, sl].reshape(4, NSEQ, 256, D))
        m["s5re0"] = np.ascontiguousarray(inp["state_s5_re"][:, sl].reshape(2, NSEQ, 4096))
        m["s5im0"] = np.ascontiguousarray(inp["state_s5_im"][:, sl].reshape(2, NSEQ, 4096))
        m["rw0"] = np.ascontiguousarray(inp["state_rwkv"][:, sl])
        m["sh0"] = np.ascontiguousarray(inp["state_shift"][:, sl])
        m["par"] = par
        m["consts"] = consts
        for j in range(2):
            for k in ("bbr", "bbi", "cpr", "cpi"):
                m[f"{k}{j}"] = s5m[j][k]
        for nm, _ in WSHAPES:
            m[nm] = inp[nm]
        in_maps.append(m)
    declared = set(_CACHE["names"])
    in_maps = [{k: v for k, v in m.items() if k in declared} for m in in_maps]
    res = run_bass_kernel_spmd(nc, in_maps, core_ids=list(range(8)))
    R = res.results
    f32 = np.float32
    y_prompt = np.stack([R[c]["y"][:TP] for c in range(8)]).astype(f32)
    y_sample = np.concatenate([R[c]["y"][TP:].reshape(NSEQ, 8, D) for c in range(8)]).astype(f32)
    memk = np.stack([R[c]["memk"] for c in range(8)], axis=1).reshape(4, 8, 256, 4, 256).astype(f32)
    memv = np.stack([R[c]["memv"] for c in range(8)], axis=1).reshape(4, 8, 256, 4, 256).astype(f32)
    s5pre = np.stack([R[c]["s5pre"] for c in range(8)], axis=1).reshape(2, 8, 64, 64).astype(f32)
    s5pim = np.stack([R[c]["s5pim"] for c in range(8)], axis=1).reshape(2, 8, 64, 64).astype(f32)
    rwp = np.stack([R[c]["rwp"] for c in range(8)], axis=1).astype(f32)
    shp = np.stack([R[c]["shp"] for c in range(8)], axis=1).astype(f32)
    s5sre = np.concatenate([R[c]["s5sre"] for c in range(8)], axis=1).reshape(2, 128, 64, 64).astype(f32)
    s5sim = np.concatenate([R[c]["s5sim"] for c in range(8)], axis=1).reshape(2, 128, 64, 64).astype(f32)
    rws = np.concatenate([R[c]["rws"] for c in range(8)], axis=1).astype(f32)
    shs = np.concatenate([R[c]["shs"] for c in range(8)], axis=1).astype(f32)
    return (y_prompt, y_sample, memk, memv, s5pre, s5pim, rwp, shp, s5sre, s5sim, rws, shs)
```

```python
import math
import numpy as np
import concourse.bass as bass
import concourse.mybir as mybir
from concourse.bass_utils import run_bass_kernel_spmd

F32 = mybir.dt.float32
BF16 = mybir.dt.bfloat16
I32 = mybir.dt.int32
AF = mybir.ActivationFunctionType
ALU = mybir.AluOpType
AX = mybir.AxisListType

D = 1024
DEPTH = 4
TP = 2048
TS = 128
TT = TP + TS
NSEQ = 16
ALPHA = (2.0 * DEPTH) ** 0.25
LN_EPS = 1e-5
GN_EPS = 64e-5
TBLK = [(0, 512), (512, 512), (1024, 512), (1536, 512), (2048, 128)]
STUB_RWKV = False

ENGS = ("pe", "act", "dve", "pool", "sp")
SEM_ROLL = 20000
SAME_ENGINE_WAIT = True
N_DMA_SEMS = 24


class Prog:
    def __init__(self, nc):
        self.nc = nc
        self.ops = {e: [] for e in ENGS}
        self.sems = {e: [nc.alloc_semaphore(f"s_{e}_0")] for e in ENGS}
        self.own_sems = {e: {id(self.sems[e][0])} for e in ENGS}
        self.cnt = {e: 0 for e in ENGS}
        self.known = {e: {} for e in ENGS}
        self.last_w = {}
        self.readers = {}
        self.dma_sems = [nc.alloc_semaphore(f"s_dma_{i}") for i in range(N_DMA_SEMS)]
        self.dma_cnt = [0] * N_DMA_SEMS
        self.dma_rr = 0
        self.n_ins = 0

    def _tok(self, eng):
        if self.cnt[eng] >= SEM_ROLL:
            self.sems[eng].append(self.nc.alloc_semaphore(f"s_{eng}_{len(self.sems[eng])}"))
            self.own_sems[eng].add(id(self.sems[eng][-1]))
            self.cnt[eng] = 0
        self.cnt[eng] += 1
        return (self.sems[eng][-1], self.cnt[eng])

    def _deps(self, eng, reads, writes):
        toks = []
        for k in reads:
            t = self.last_w.get(k)
            if t is not None:
                toks.append(t)
        for k in writes:
            t = self.last_w.get(k)
            if t is not None:
                toks.append(t)
            toks.extend(self.readers.get(k, ()))
        waits = {}
        kn = self.known[eng]
        own = self.own_sems[eng]
        for (sem, val) in toks:
            key = id(sem)
            if not SAME_ENGINE_WAIT and key in own:
                continue
            if kn.get(key, (None, 0))[1] >= val:
                continue
            if key not in waits or waits[key][1] < val:
                waits[key] = (sem, val)
        for key, sv in waits.items():
            kn[key] = sv
        return list(waits.values())

    def _commit(self, tok, reads, writes):
        for k in reads:
            self.readers.setdefault(k, []).append(tok)
        for k in writes:
            self.last_w[k] = tok
            self.readers[k] = []

    @staticmethod
    def _excl(reads, writes):
        r2 = [k for k in reads if not (isinstance(k, tuple) and k[0] == "ps")]
        w2 = list(writes) + [k for k in reads if isinstance(k, tuple) and k[0] == "ps"]
        return r2, w2

    def op(self, eng, fn, reads=(), writes=()):
        reads, writes = self._excl(reads, writes)
        waits = self._deps(eng, reads, writes)
        tok = self._tok(eng)
        self.ops[eng].append((waits, fn, tok[0], 1))
        self._commit(tok, reads, writes)
        self.n_ins += 1
        return tok

    def dma(self, eng, fn, reads=(), writes=()):
        i = self.dma_rr
        self.dma_rr = (self.dma_rr + 1) % N_DMA_SEMS
        sem = self.dma_sems[i]
        waits = self._deps(eng, reads, writes)
        prev = self.dma_cnt[i]
        if prev > 0:
            key = id(sem)
            if self.known[eng].get(key, (None, 0))[1] < prev:
                waits.append((sem, prev))
                self.known[eng][key] = (sem, prev)
        self.dma_cnt[i] += 16
        tok = (sem, self.dma_cnt[i])
        self.ops[eng].append((waits, fn, sem, 16))
        self._commit(tok, reads, writes)
        self.n_ins += 1
        return tok

    def drain_dmas(self, eng="sp"):
        waits = []
        for i, sem in enumerate(self.dma_sems):
            if self.dma_cnt[i] > 0 and self.known[eng].get(id(sem), (None, 0))[1] < self.dma_cnt[i]:
                waits.append((sem, self.dma_cnt[i]))
                self.known[eng][id(sem)] = (sem, self.dma_cnt[i])
        self.ops[eng].append((waits, None, None, 0))

    def emit(self):
        nc = self.nc
        self.drain_dmas("sp")
        emap = {"pe": "tensor", "act": "scalar", "dve": "vector", "pool": "gpsimd", "sp": "sync"}
        with nc.Block() as block:
            for e in ENGS:
                ops = self.ops[e]

                def body(engine, ops=ops):
                    for (waits, fn, sem, amt) in ops:
                        for (s, v) in waits:
                            engine.wait_ge(s, v)
                        if fn is not None:
                            fn(engine).then_inc(sem, amt)

                getattr(block, emap[e])(body)
        self.ops = {e: [] for e in ENGS}
        self.last_w = {}
        self.readers = {}


def _fm(v):
    return np.ascontiguousarray(np.asarray(v).reshape(8, 128).T)


def pack_params(inp):
    cols = {}
    mats = []

    def add(name, arr):
        cols[name] = sum(m.shape[1] for m in mats)
        mats.append(np.asarray(arr, dtype=np.float32))

    for i in range(4):
        for k in range(3):
            add(("lng", i, k), _fm(inp["ln_g"][i, k]))
            add(("lnb", i, k), _fm(inp["ln_b"][i, k]))
    for j in range(2):
        add(("s5d", j), _fm(inp["s5_d"][j]))
        add(("mu", j), np.concatenate([_fm(inp["rwkv_mu"][j, m]) for m in range(6)], axis=1))
        for nm in ("rwkv_w0", "rwkv_a0", "rwkv_k_k", "rwkv_k_a", "rwkv_lnx_g", "rwkv_lnx_b"):
            add((nm, j), _fm(inp[nm][j]))
        add(("rwkv_r_k", j), _fm(inp["rwkv_r_k"][j].reshape(-1)))
    add(("rwkv_v0", 1), _fm(inp["rwkv_v0"][0]))
    for j in range(2):
        for nm in ("s5_a_re", "s5_a_im"):
            a = inp[nm][j].reshape(32, 2, 64)
            add((nm, j), np.ascontiguousarray(a.transpose(1, 2, 0).reshape(128, 32)))
        ld = np.repeat(inp["s5_log_dt"][j].reshape(32, 2, 1), 64, axis=2)
        add(("s5_log_dt", j), np.ascontiguousarray(ld.transpose(1, 2, 0).reshape(128, 32)))
    return np.ascontiguousarray(np.concatenate(mats, axis=1)), cols


def pack_s5_mats(inp, j):
    out = {}
    for nm, key in (("s5_b_re", "bbr"), ("s5_b_im", "bbi")):
        b = inp[nm][j]
        bb = np.zeros((8, 16, 8, 4, 2, 64), np.float32)
        for k in range(8):
            for q in range(4):
                for hf in range(2):
                    g8 = 2 * q + hf
                    bb[g8, :, k, q, hf, :] = b[8 * k + g8].T
        out[key] = bb.reshape(128, 32 * 128)
    for nm, key in (("s5_c_re", "cpr"), ("s5_c_im", "cpi")):
        c = inp[nm][j]
        cp = np.zeros((2, 64, 32, 128), np.float32)
        for tile in range(32):
            for hf in range(2):
                c0 = (tile % 4) * 32 + hf * 16
                cp[hf, :, tile, c0:c0 + 16] = c[2 * tile + hf].T
        out[key] = cp.reshape(128, 32 * 128)
    return out


def make_consts():
    c = np.zeros((128, 1024), np.float32)
    c[:, 0:128] = np.eye(128)
    c[0:64, 128:192] = 1.0
    c[64:128, 192:256] = 1.0
    s = np.arange(64)
    c[0:64, 256:320] = (s[:, None] < s[None, :])
    c[0:64, 320:384] = (s[:, None] <= s[None, :])
    c[0:64, 384:448] = (s[:, None] > s[None, :])
    c[:, 448:512] = 1.0
    c[:, 448] = 0.0
    c[:, 512:576] = 1.0
    c[:, 512:576:8] = 0.0
    c[:, 576:704] = 1.0 / 1024.0
    return c


class KB:
    def __init__(self, pcols, npar):
        nc = bass.Bass("TRN2", target_bir_lowering=False)
        self.nc = nc
        self.P = Prog(nc)
        self.pc = pcols
        self.npar = npar
        self.force_auto = False
        self.regions = []
        self.stream = None

    def _tblocks(self, col, ap, width):
        t_lo = col % width
        ext = 1
        for (st, cnt) in list(ap.ap)[1:]:
            if abs(st) < width:
                ext += (cnt - 1) * abs(st)
        t_hi = t_lo + ext
        return [t0 for (t0, n) in TBLK if t0 < t_hi and t0 + n > t_lo]

    def akeys(self, ap):
        if ap is None or not hasattr(ap, "tensor"):
            return []
        t = ap.tensor
        if type(t).__name__.startswith("DRam"):
            return []
        name = t.name
        if name.startswith("ps"):
            return [("ps", int(name[2:]))]
        dims = list(ap.ap)
        pstride = dims[0][0]
        col = ap.offset % pstride if pstride > 0 else ap.offset
        if name == "X":
            return [("X", tb) for tb in self._tblocks(col, ap, TT)]
        if name == "AR":
            es = 4 if ap.dtype == F32 or ap.dtype == I32 else 2
            ext = 1
            for (st, cnt) in dims[1:]:
                ext += (cnt - 1) * abs(st)
            b0, b1 = col * es, (col + ext) * es
            if b1 <= 17408 * 2:
                if es == 2:
                    return [("XB", tb) for tb in self._tblocks(col, ap, TT)]
                return [("XB", tb) for (tb, n) in TBLK]
            ks = [k for (r0, r1, k) in self.regions if r0 < b1 and r1 > b0]
            return ks if ks else [("AR", b0 // 2048)]
        return [name]

    def _k(self, reads, writes, ins, outs):
        if not self.force_auto:
            return reads, writes
        r, w = [], []
        for a in ins:
            r.extend(self.akeys(a))
        for a in outs:
            w.extend(self.akeys(a))
        return r, w

    def mm(self, out, lhsT, rhs, start, stop, reads=None, writes=None):
        reads, writes = self._k(reads, writes, [lhsT, rhs], [out])
        self._emit("op", "pe", lambda e: e.matmul(out, lhsT=lhsT, rhs=rhs, start=start, stop=stop), reads, writes)

    def tr(self, out, in_, ident, reads=None, writes=None):
        reads, writes = self._k(reads, writes, [in_, ident], [out])
        self._emit("op", "pe", lambda e: e.transpose(out, in_, ident), reads, writes)

    def tt(self, out, in0, in1, op, reads=None, writes=None, eng="dve"):
        reads, writes = self._k(reads, writes, [in0, in1], [out])
        self._emit("op", eng, lambda e: e.tensor_tensor(out=out, in0=in0, in1=in1, op=op), reads, writes)

    def ts(self, out, in0, s1, op0, reads=None, writes=None, s2=None, op1=None, eng="dve"):
        reads, writes = self._k(reads, writes, [in0, s1, s2], [out])
        if op1 is None:
            self._emit("op", eng, lambda e: e.tensor_scalar(out=out, in0=in0, scalar1=s1, scalar2=None, op0=op0), reads, writes)
        else:
            self._emit("op", eng, lambda e: e.tensor_scalar(out=out, in0=in0, scalar1=s1, scalar2=s2, op0=op0, op1=op1), reads, writes)

    def stt(self, out, in0, scalar, in1, op0, op1, reads=None, writes=None):
        reads, writes = self._k(reads, writes, [in0, scalar, in1], [out])
        self._emit("op", "dve", lambda e: e.scalar_tensor_tensor(out=out, in0=in0, scalar=scalar, in1=in1, op0=op0, op1=op1), reads, writes)

    def act(self, out, in_, func, reads=None, writes=None, bias=None, scale=1.0, accum_out=None):
        reads, writes = self._k(reads, writes, [in_, bias], [out, accum_out])
        kw = {}
        if bias is not None:
            kw["bias"] = bias
        if accum_out is not None:
            kw["accum_out"] = accum_out
        self._emit("op", "act", lambda e: e.activation(out=out, in_=in_, func=func, scale=scale, **kw), reads, writes)

    def cp(self, eng, out, in_, reads=None, writes=None):
        reads, writes = self._k(reads, writes, [in_], [out])
        if eng == "act":
            self._emit("op", "act", lambda e: e.copy(out=out, in_=in_), reads, writes)
        else:
            self._emit("op", eng, lambda e: e.tensor_copy(out=out, in_=in_), reads, writes)

    def red(self, out, in_, op, reads=None, writes=None):
        reads, writes = self._k(reads, writes, [in_], [out])
        self._emit("op", "dve", lambda e: e.tensor_reduce(out=out, in_=in_, op=op, axis=AX.X), reads, writes)

    def rcp(self, out, in_, reads=None, writes=None):
        reads, writes = self._k(reads, writes, [in_], [out])
        self._emit("op", "dve", lambda e: e.reciprocal(out=out, in_=in_), reads, writes)

    def mset(self, out, val, reads=None, writes=None, eng="dve"):
        reads, writes = self._k(reads, writes, [], [out])
        self._emit("op", eng, lambda e: e.memset(out, val), reads, writes)

    def scan(self, out, d0, d1, reads=None, writes=None):
        reads, writes = self._k(reads, writes, [d0, d1], [out])
        self._emit("op", "dve", lambda e: e.tensor_tensor_scan(out=out, data0=d0, data1=d1, initial=0.0, op0=ALU.mult, op1=ALU.add), reads, writes)

    def dma(self, eng, out, in_, reads=None, writes=None):
        reads, writes = self._k(reads, writes, [in_], [out])
        self._emit("dma", eng, lambda e: e.dma_start(out=out, in_=in_), reads, writes)

    def _emit(self, kind, eng, fn, reads, writes):
        rec = (kind, eng, fn, reads, writes)
        if self.stream is not None:
            self.stream.append(rec)
        else:
            self._flush(rec)

    def _flush(self, rec):
        kind, eng, fn, reads, writes = rec
        if kind == "op":
            self.P.op(eng, fn, reads, writes)
        else:
            self.P.dma(eng, fn, reads, writes)

    def setup(self):
        nc = self.nc

        def din(name, shape):
            return nc.dram_tensor(name, list(shape), F32, kind="ExternalInput").ap()

        def dout(name, shape):
            return nc.dram_tensor(name, list(shape), F32, kind="ExternalOutput").ap()

        shapes = {"xin": [TT, D], "mem": [256, D], "ck": [4, NSEQ, 256, D], "cv": [4, NSEQ, 256, D], "s5re0": [2, NSEQ, 4096],
                  "s5im0": [2, NSEQ, 4096], "rw0": [2, NSEQ, 16, 64, 64], "sh0": [2, NSEQ, D], "par": [128, self.npar], "consts": [128, 1024]}
        for j in range(2):
            for k in ("bbr", "bbi", "cpr", "cpi"):
                shapes[(k, j)] = [128, 4096]
        for nm, shp in WSHAPES:
            shapes[nm] = shp

        class Lazy(dict):
            def __missing__(d, key):
                name = key if isinstance(key, str) else f"{key[0]}{key[1]}"
                d[key] = din(name, shapes[key])
                return d[key]
        I = Lazy()
        O = {}
        O["y"] = dout("y", [TT, D])
        O["memk"] = dout("memk", [4, 256, D])
        O["memv"] = dout("memv", [4, 256, D])
        O["s5pre"] = dout("s5pre", [2, 4096])
        O["s5pim"] = dout("s5pim", [2, 4096])
        O["rwp"] = dout("rwp", [2, 16, 64, 64])
        O["shp"] = dout("shp", [2, D])
        O["s5sre"] = dout("s5sre", [2, NSEQ, 4096])
        O["s5sim"] = dout("s5sim", [2, NSEQ, 4096])
        O["rws"] = dout("rws", [2, NSEQ, 16, 64, 64])
        O["shs"] = dout("shs", [2, NSEQ, D])
        self.I, self.O = I, O
        self.VF = nc.dram_tensor("vfirst", [128, 8, TT], F32, kind="Internal").ap()

        self.X = nc.alloc_sbuf_tensor("X", [128, 8, TT], F32)
        self.AR = nc.alloc_sbuf_tensor("AR", [128, 51200], BF16)
        self.MEMT = nc.alloc_sbuf_tensor("MEMT", [128, 8, 256], BF16)
        self.PAR = nc.alloc_sbuf_tensor("PAR", [128, self.npar], F32)
        self.CON = nc.alloc_sbuf_tensor("CON", [128, 1024], F32)
        self.CONB = nc.alloc_sbuf_tensor("CONB", [128, 256], BF16)
        self.T = [nc.alloc_sbuf_tensor(f"T{i}", [128, 512], F32) for i in range(3)]
        self.SQ = nc.alloc_sbuf_tensor("SQ", [128, 8, 512], BF16)
        self.RS = nc.alloc_sbuf_tensor("RS", [128, 512], F32)
        self.SM = nc.alloc_sbuf_tensor("SM", [128, 256], F32)
        self.PS = [nc.alloc_psum_tensor(f"ps{i}", [128, 512], F32) for i in range(8)]
        AR = self.AR
        self.XB = AR[:, 0:17408].rearrange("p (c t) -> p c t", c=8)
        self.A1 = AR[:, 17408:34816].rearrange("p (c t) -> p c t", c=8)
        self.WR = [AR[:, 34816 + i * 8192: 34816 + (i + 1) * 8192].rearrange("p (c n) -> p c n", c=8) for i in range(2)]
        CON, CONB = self.CON, self.CONB
        self.IDF = CON[:, 0:128]
        self.BLK = CON[:, 128:256]
        self.MLT = CON[0:64, 256:320]
        self.MLE = CON[0:64, 320:384]
        self.MGT = CON[0:64, 384:448]
        self.RST64 = CON[:, 448:512]
        self.RST8 = CON[:, 512:576]
        self.IDB = CONB[:, 0:128]
        self.ONB = CONB[:, 128:256]
        self.wr_i = 0
        self.bank_i = 0
        self.EPS = self.SM[:, 0:1]
        self.EPS2 = self.SM[:, 1:2]

    def par(self, key, n=8):
        c0 = self.pc[key]
        return self.PAR[:, c0:c0 + n]

    def wload(self, dram2d, ncols=1024):
        i = self.wr_i
        self.wr_i ^= 1
        dst = self.WR[i][:, :, 0:ncols]
        src = dram2d.rearrange("(c p) n -> p c n", p=128)
        self.dma("pool", dst, src, [], [("WR", i)])
        return self.WR[i], ("WR", i)

    def next_bank(self):
        i = self.bank_i
        self.bank_i = (i + 1) % 4
        return self.PS[i], ("ps", i)

    def dense(self, w, wkey, src, srckey, evac, n_oc=8):
        for (t0, n) in TBLK:
            for oc in range(n_oc):
                ps, pk = self.next_bank()
                for c in range(8):
                    self.mm(ps[:, 0:n], w[:, c, oc * 128:(oc + 1) * 128], src[:, c, t0:t0 + n], c == 0, c == 7, [wkey, (srckey, t0)], [pk])
                evac(ps, pk, oc, t0, n)

    def layer_norm(self, i, k):
        X, XB, SQ, RS, PS, ONB = self.X, self.XB, self.SQ, self.RS, self.PS, self.ONB
        g = self.par(("lng", i, k))
        b = self.par(("lnb", i, k))
        for (t0, n) in TBLK:
            xb = X[:, :, t0:t0 + n]
            kx = ("X", t0)
            self.cp("act", SQ[:, :, 0:n], xb, [kx], ["SQ"])
            for c in range(8):
                self.mm(PS[4][:, 0:n], ONB, SQ[:, c, 0:n], c == 0, c == 7, ["SQ", "CONB"], [("ps", 4)])
            self.tt(xb, xb, PS[4][:, 0:n].unsqueeze(1).broadcast_to([128, 8, n]), ALU.subtract, [kx, ("ps", 4)], [kx])
            self.act(SQ[:, :, 0:n], xb, AF.Square, [kx], ["SQ"])
            for c in range(8):
                self.mm(PS[5][:, 0:n], ONB, SQ[:, c, 0:n], c == 0, c == 7, ["SQ", "CONB"], [("ps", 5)])
            self.act(RS[:, 0:n], PS[5][:, 0:n], AF.Sqrt, [("ps", 5), "eps"], ["RS"], bias=self.EPS)
            self.rcp(RS[:, 0:n], RS[:, 0:n], ["RS"], ["RS"])
            self.tt(xb, xb, RS[:, 0:n].unsqueeze(1).broadcast_to([128, 8, n]), ALU.mult, [kx, "RS"], [kx])
            for c in range(8):
                self.ts(X[:, c, t0:t0 + n], X[:, c, t0:t0 + n], g[:, c:c + 1], ALU.mult, [kx, "PAR"], [kx], s2=b[:, c:c + 1], op1=ALU.add)
            self.cp("act", XB[:, :, t0:t0 + n], xb, [kx], [("XB", t0)])

    def phase_input(self):
        I, X, XB, AR, PS, IDF = self.I, self.X, self.XB, self.AR, self.PS, self.IDF
        self.dma("sp", self.PAR[:], I["par"], [], ["PAR"])
        self.dma("sp", self.CON[:], I["consts"], [], ["CON"])
        self.cp("dve", self.CONB[:, 0:128], self.CON[:, 0:128], ["CON"], ["CONB"])
        self.cp("dve", self.CONB[:, 128:256], self.CON[:, 576:704], ["CON"], ["CONB"])
        self.mset(self.SM[:, 0:1], LN_EPS, [], ["eps"])
        self.mset(self.SM[:, 1:2], GN_EPS, [], ["eps"])
        STG = [AR[:, 17408 + i * 2048: 17408 + (i + 1) * 2048].bitcast(F32) for i in range(2)]
        import os
        for tt in [int(x) for x in os.environ.get('PI_TILES', ','.join(str(i) for i in range(19))).split(',')]:
            st = STG[tt % 2]
            sk = ("stg", tt % 2)
            if tt < 17:
                self.dma("sp", st, I["xin"][tt * 128:(tt + 1) * 128, :], [], [sk])
            else:
                self.dma("sp", st, I["mem"][(tt - 17) * 128:(tt - 16) * 128, :], [], [sk])
            for cg in range(2):
                ps, pk = self.next_bank()
                for cc in range(4):
                    c = cg * 4 + cc
                    self.tr(ps[:, cc * 128:(cc + 1) * 128], st[:, c * 128:(c + 1) * 128], IDF, [sk, "CON"], [pk])
                psv = ps[:].rearrange("p (a b) -> p a b", a=4)
                if tt < 17:
                    self.cp("dve", X[:, cg * 4:(cg + 1) * 4, tt * 128:(tt + 1) * 128], psv, [pk], [("Xi", tt, cg)])
                    self.cp("act", XB[:, cg * 4:(cg + 1) * 4, tt * 128:(tt + 1) * 128], psv, [pk], [("XBi", tt, cg)])
                else:
                    m0 = (tt - 17) * 128
                    self.cp("act", self.MEMT[:, cg * 4:(cg + 1) * 4, m0:m0 + 128], psv, [pk], ["MEMT"])
        self.P.emit()

    def phase_output(self):
        AR, PS, IDF, X, O = self.AR, self.PS, self.IDF, self.X, self.O
        STG = [AR[:, 17408 + i * 2048: 17408 + (i + 1) * 2048].bitcast(F32) for i in range(2)]
        for tt in range(17):
            st = STG[tt % 2]
            sk = ("stg", tt % 2)
            for cg in range(2):
                ps, pk = self.next_bank()
                for cc in range(4):
                    c = cg * 4 + cc
                    self.tr(ps[:, cc * 128:(cc + 1) * 128], X[:, c, tt * 128:(tt + 1) * 128], IDF, [], [pk])
                self.cp("dve" if cg == 0 else "act", st[:, cg * 512:(cg + 1) * 512], ps[:], [pk], [sk])
            self.dma("sp", O["y"][tt * 128:(tt + 1) * 128, :], st, [sk], [("yout", tt)])
        self.P.emit()

    def mlp_layer(self, i):
        X, XB, A1, T, I = self.X, self.XB, self.A1, self.T, self.I
        cnt = [0]
        for fg in range(4):
            w1, k1 = self.wload(I["mlp_w1"][i][:, fg * 1024:(fg + 1) * 1024])

            def evac1(ps, pk, oc, t0, n):
                tt_, tk = T[cnt[0] % 2], ("T", cnt[0] % 2)
                cnt[0] += 1
                self.act(tt_[:, 0:n], ps[:, 0:n], AF.Relu, [pk], [tk])
                self.tt(A1[:, oc, t0:t0 + n], tt_[:, 0:n], tt_[:, 0:n], ALU.mult, [tk], [("A1", t0)])
            self.dense(w1, k1, XB, "XB", evac1)
            w2, k2 = self.wload(I["mlp_w2"][i][fg * 1024:(fg + 1) * 1024, :])
            if fg == 0:
                def evac2(ps, pk, oc, t0, n):
                    self.stt(X[:, oc, t0:t0 + n], X[:, oc, t0:t0 + n], ALPHA, ps[:, 0:n], ALU.mult, ALU.add, [pk, ("X", t0)], [("X", t0)])
            else:
                def evac2(ps, pk, oc, t0, n):
                    self.tt(X[:, oc, t0:t0 + n], X[:, oc, t0:t0 + n], ps[:, 0:n], ALU.add, [pk, ("X", t0)], [("X", t0)])
            self.dense(w2, k2, A1, "A1", evac2)

    def xa_layer(self, i):
        X, XB, A1, T, I, O, PS, MEMT, AR, IDB, SM = self.X, self.XB, self.A1, self.T, self.I, self.O, self.PS, self.MEMT, self.AR, self.IDB, self.SM
        wq, kq = self.wload(I["xa_w_q"][i])

        def evq(ps, pk, oc, t0, n):
            self.act(A1[:, oc, t0:t0 + n], ps[:, 0:n], AF.Copy, [pk], [("A1", t0)], scale=0.0625)
        self.dense(wq, kq, XB, "XB", evq)
        self.P.emit()
        KT = AR[:, 0:2048].rearrange("p (c m) -> p c m", c=8)
        VB = AR[:, 2048:4096].rearrange("p (a n) -> p a n", a=2)
        PNB = AR[:, 4096:5120].rearrange("p (h m) -> p h m", h=4)
        PT = AR[:, 5120:6144].rearrange("p (a t) -> p a t", a=8)
        KS = [AR[:, 6144 + s * 2048: 8192 + s * 2048].rearrange("p (a n) -> p a n", a=2) for s in range(2)]
        VS = [AR[:, 10240 + s * 2048: 12288 + s * 2048].rearrange("p (a n) -> p a n", a=2) for s in range(2)]
        KTS = AR[:, 14336:16384].rearrange("p (c m) -> p c m", c=8)
        PTS = AR[:, 16384:16448].rearrange("p (a t) -> p a t", a=8)
        PEXP = [T[0], T[1]]
        psT = PS[4][:].bitcast(BF16)
        MX, NMX, SUM, RSM = SM[:, 8:12], SM[:, 12:16], SM[:, 16:20], SM[:, 20:24]
        SC = [PS[5], PS[6]]

        wk, kk = self.wload(I["xa_w_k"][i])
        for oc in range(8):
            ps, pk = self.next_bank()
            for c in range(8):
                self.mm(ps[:, 0:256], wk[:, c, oc * 128:(oc + 1) * 128], MEMT[:, c, :], c == 0, c == 7, [kk, "MEMT"], [pk])
            self.cp("act", KT[:, oc, :], ps[:, 0:256], [pk], ["KT"])

        def tokmajor(w, wkey, outap, vb):
            for mt in range(2):
                for nb in range(2):
                    ps, pk = self.next_bank()
                    for c in range(8):
                        self.mm(ps[:], MEMT[:, c, mt * 128:(mt + 1) * 128], w[:, c, nb * 512:(nb + 1) * 512], c == 0, c == 7, [wkey, "MEMT"], [pk])
                    self.cp("dve", T[2][:], ps[:], [pk], ["T2"])
                    if vb:
                        self.cp("act", VB[:, mt, nb * 512:(nb + 1) * 512], ps[:], [pk], ["VB"])
                    self.dma("sp", outap[mt * 128:(mt + 1) * 128, nb * 512:(nb + 1) * 512], T[2][:], ["T2"], [("mo", id(outap), mt, nb)])
        tokmajor(wk, kk, O["memk"][i], False)
        wv, kv = self.wload(I["xa_w_v"][i])
        tokmajor(wv, kv, O["memv"][i], True)

        def softmax(np_):
            for b in range(2):
                self.red(MX[0:np_, 2 * b:2 * b + 2], SC[b][0:np_, :].rearrange("p (h m) -> p h m", h=2), ALU.max, [("ps", 5 + b)], ["MX"])
            self.ts(NMX[0:np_, :], MX[0:np_, :], -1.0, ALU.mult, ["MX"], ["NMX"])
            for h in range(4):
                sl = slice((h % 2) * 256, (h % 2) * 256 + 256)
                self.act(PEXP[h // 2][0:np_, sl], SC[h // 2][0:np_, sl], AF.Exp, [("ps", 5 + h // 2), "NMX"], [("PEXP", h), ("SUM", h)],
                         bias=NMX[0:np_, h:h + 1], accum_out=SUM[0:np_, h:h + 1])
            self.rcp(RSM[0:np_, :], SUM[0:np_, :], [("SUM", h) for h in range(4)], ["RSM"])
            for b in range(2):
                self.tt(PNB[0:np_, 2 * b:2 * b + 2, :], PEXP[b][0:np_, :].rearrange("p (h m) -> p h m", h=2),
                        RSM[0:np_, 2 * b:2 * b + 2].unsqueeze(2).broadcast_to([np_, 2, 256]), ALU.mult,
                        [("PEXP", 2 * b), ("PEXP", 2 * b + 1), "RSM"], ["PNB"])

        for tt in range(16):
            t0 = tt * 128
            ak = ("A1", (t0 // 512) * 512)
            for h in range(4):
                for dc in range(2):
                    fc = 2 * h + dc
                    self.mm(SC[h // 2][:, (h % 2) * 256:(h % 2) * 256 + 256], A1[:, fc, t0:t0 + 128], KT[:, fc, :], dc == 0, dc == 1, [ak, "KT"], [("ps", 5 + h // 2)])
            softmax(128)
            for h in range(4):
                for mt in range(2):
                    a = h * 2 + mt
                    self.tr(psT[:, a * 128:(a + 1) * 128], PNB[:, h, mt * 128:(mt + 1) * 128], IDB, ["PNB", "CONB"], [("ps", 4)])
            self.cp("act", PT[:].rearrange("p a t -> p (a t)"), psT[:, 0:1024], [("ps", 4)], ["PT"])
            for fc in range(8):
                h, dc = fc // 2, fc % 2
                bank, bk = (PS[7], ("ps", 7)) if fc >= 4 else (PS[3], ("ps", 3))
                for mt in range(2):
                    self.mm(bank[:, (fc % 4) * 128:(fc % 4) * 128 + 128], VB[:, mt, h * 256 + dc * 128:h * 256 + dc * 128 + 128], PT[:, h * 2 + mt, :],
                            mt == 0, mt == 1, ["VB", "PT"], [bk])
            self.cp("dve", A1[:, 0:4, t0:t0 + 128], PS[3][:].rearrange("p (a t) -> p a t", a=4), [("ps", 3)], [ak])
            self.cp("act", A1[:, 4:8, t0:t0 + 128], PS[7][:].rearrange("p (a t) -> p a t", a=4), [("ps", 7)], [ak])
        for s in range(NSEQ):
            ks, vs = KS[s % 2], VS[s % 2]
            kkey, vkey = ("KS", s % 2), ("VS", s % 2)
            self.dma("pool", ks, I["ck"][i, s].rearrange("(a p) n -> p a n", p=128), [], [kkey])
            self.dma("pool", vs, I["cv"][i, s].rearrange("(a p) n -> p a n", p=128), [], [vkey])
            for mt in range(2):
                for fc in range(8):
                    self.tr(psT[:, fc * 128:(fc + 1) * 128], ks[:, mt, fc * 128:(fc + 1) * 128], IDB, [kkey, "CONB"], [("ps", 4)])
                self.cp("act", KTS[:, :, mt * 128:(mt + 1) * 128], psT[:, 0:1024].rearrange("p (c m) -> p c m", c=8), [("ps", 4)], ["KTS"])
            c0 = TP + 8 * s
            for h in range(4):
                for dc in range(2):
                    fc = 2 * h + dc
                    self.mm(SC[h // 2][0:8, (h % 2) * 256:(h % 2) * 256 + 256], A1[:, fc, c0:c0 + 8], KTS[:, fc, :], dc == 0, dc == 1, [("A1", 2048), "KTS"], [("ps", 5 + h // 2)])
            softmax(8)
            for h in range(4):
                for mt in range(2):
                    a = h * 2 + mt
                    self.tr(psT[:, a * 8:(a + 1) * 8], PNB[0:8, h, mt * 128:(mt + 1) * 128], IDB[0:8, 0:8], ["PNB", "CONB"], [("ps", 4)])
            self.cp("act", PTS[:].rearrange("p a t -> p (a t)"), psT[:, 0:64], [("ps", 4)], ["PTS"])
            for fc in range(8):
                h, dc = fc // 2, fc % 2
                for mt in range(2):
                    self.mm(PS[7][:, fc * 8:fc * 8 + 8], vs[:, mt, h * 256 + dc * 128:h * 256 + dc * 128 + 128], PTS[:, h * 2 + mt, :], mt == 0, mt == 1, [vkey, "PTS"], [("ps", 7)])
            self.cp("dve", A1[:, :, c0:c0 + 8], PS[7][:, 0:64].rearrange("p (a t) -> p a t", a=8), [("ps", 7)], [("A1", 2048)])
        wo, ko = self.wload(I["xa_w_o"][i])

        def evo(ps, pk, oc, t0, n):
            self.stt(X[:, oc, t0:t0 + n], X[:, oc, t0:t0 + n], ALPHA, ps[:, 0:n], ALU.mult, ALU.add, [pk, ("X", t0)], [("X", t0)])
        self.dense(wo, ko, A1, "A1", evo)

    def s5_layer(self, i, j):
        X, XB, T, I, O, PS, AR, SM, IDF, RS = self.X, self.XB, self.T, self.I, self.O, self.PS, self.AR, self.SM, self.IDF, self.RS
        B0 = 17408
        BBR = AR[:, B0:B0 + 4096].rearrange("p (t c) -> p t c", t=32)
        BBI = AR[:, B0 + 4096:B0 + 8192].rearrange("p (t c) -> p t c", t=32)
        CRB = AR[:, B0 + 8192:B0 + 12288].rearrange("p (t c) -> p t c", t=32)
        CIN = AR[:, B0 + 12288:B0 + 16384].rearrange("p (t c) -> p t c", t=32)
        BU0 = AR[:, B0 + 16384:B0 + 20480].bitcast(F32).rearrange("p (r t s) -> p r t s", r=2, t=32)
        BU = [BU0, BU0]
        TAB = AR[:, B0 + 20480:B0 + 24576].bitcast(F32).rearrange("p (r t s) -> p r t s", r=2, t=32)
        EC, ES = TAB[:, 0], TAB[:, 1]
        SQF = self.SQ[:].rearrange("p c t -> p (c t)").bitcast(F32)
        TQ4 = SQF[:, 0:1024].rearrange("p (r t s) -> p r t s", r=2, t=32)
        tq = SQF[:, 0:1024].rearrange("p (t s) -> p t s", t=32)
        RH = SQF[:, 1024:2048].rearrange("p (t s) -> p t s", t=32)
        HH = AR[:, B0 + 24576:B0 + 28672].bitcast(F32).rearrange("p (r t s) -> p r t s", r=2, t=32)
        HHB = AR[:, B0 + 28672:B0 + 30720].rearrange("p (r t s) -> p r t s", r=2, t=32)
        SP = AR[:, B0 + 30720:B0 + 33792].bitcast(F32)
        H0 = SP[:, 0:1024].rearrange("p (r t q) -> p r t q", r=2, t=32)

        def sm(k):
            return SP[:, 1024 + 32 * k:1024 + 32 * (k + 1)]
        pair = lambda k: SP[:, 1024 + 32 * k:1024 + 32 * (k + 2)].rearrange("p (r t) -> p r t", r=2)
        C1, C2, HL, M1, M2 = pair(0), pair(2), pair(4), pair(6), pair(8)
        abre, abim = sm(0), sm(1)
        FRE, FIM = sm(10), sm(11)
        dt, ang, mag, sn = sm(12), sm(13), sm(14), sm(15)
        cs, den, nre, r_, m_, kf = [SM[:, 64 + 32 * k:96 + 32 * k] for k in range(6)]
        TI = SM[:, 32:64].bitcast(I32)
        pc = self.pc
        LRE = self.PAR[:, pc[("s5_a_re", j)]:pc[("s5_a_re", j)] + 32]
        LIM = self.PAR[:, pc[("s5_a_im", j)]:pc[("s5_a_im", j)] + 32]
        LDT = self.PAR[:, pc[("s5_log_dt", j)]:pc[("s5_log_dt", j)] + 32]
        K = ["s5p", "PAR"]
        TWO_PI = 2.0 * math.pi
        tt = lambda o, a, b, op: self.tt(o, a, b, op, K, K)
        ts = lambda o, a, s1, op0, s2=None, op1=None: self.ts(o, a, s1, op0, K, K, s2=s2, op1=op1)
        ac = lambda o, a, f, scale=1.0: self.act(o, a, f, K, K, scale=scale)
        ac(dt, LDT, AF.Exp)
        tt(mag, LRE, dt, ALU.mult)
        ac(mag, mag, AF.Exp)
        tt(ang, LIM, dt, ALU.mult)
        ts(kf, ang, 1.0 / TWO_PI, ALU.mult)
        self.cp("dve", TI, kf, K, K)
        self.cp("dve", kf, TI, K, K)
        self.stt(r_, kf, -TWO_PI, ang, ALU.mult, ALU.add, K, K)

        def wrap(x):
            ts(m_, x, math.pi, ALU.is_gt)
            self.stt(x, m_, -TWO_PI, x, ALU.mult, ALU.add, K, K)
            ts(m_, x, -math.pi, ALU.is_lt)
            self.stt(x, m_, TWO_PI, x, ALU.mult, ALU.add, K, K)
        wrap(r_)
        ac(sn, r_, AF.Sin)
        ts(r_, r_, math.pi / 2, ALU.add)
        wrap(r_)
        ac(cs, r_, AF.Sin)
        tt(abre, mag, cs, ALU.mult)
        tt(abim, mag, sn, ALU.mult)
        ts(sm(2), abim, -1.0, ALU.mult)
        self.cp("dve", sm(3), abre, K, K)
        tt(den, LRE, LRE, ALU.mult)
        tt(m_, LIM, LIM, ALU.mult)
        tt(den, den, m_, ALU.add)
        self.rcp(den, den, K, K)
        ts(nre, abre, -1.0, ALU.add)
        tt(FRE, nre, LRE, ALU.mult)
        tt(m_, abim, LIM, ALU.mult)
        tt(FRE, FRE, m_, ALU.add)
        tt(FRE, FRE, den, ALU.mult)
        tt(FIM, abim, LRE, ALU.mult)
        tt(m_, nre, LIM, ALU.mult)
        tt(FIM, FIM, m_, ALU.subtract)
        tt(FIM, FIM, den, ALU.mult)
        KT_ = K + ["TAB", "tq", "RH"]
        self.mset(EC[:, :, 0:1], 1.0, KT_, ["TAB"])
        self.mset(ES[:, :, 0:1], 0.0, KT_, ["TAB"])
        self.cp("dve", EC[:, :, 1], cs, KT_, ["TAB"])
        self.cp("dve", ES[:, :, 1], sn, KT_, ["TAB"])
        for tau in range(2, 32):
            self.tt(EC[:, :, tau], EC[:, :, tau - 1], cs, ALU.mult, KT_, ["TAB"])
            self.tt(dt, ES[:, :, tau - 1], sn, ALU.mult, KT_, K)
            self.tt(EC[:, :, tau], EC[:, :, tau], dt, ALU.subtract, KT_, ["TAB"])
            self.tt(ES[:, :, tau], ES[:, :, tau - 1], cs, ALU.mult, KT_, ["TAB"])
            self.tt(dt, EC[:, :, tau - 1], sn, ALU.mult, KT_, K)
            self.tt(ES[:, :, tau], ES[:, :, tau], dt, ALU.add, KT_, ["TAB"])
        self.cp("dve", RH, mag.unsqueeze(2).broadcast_to([128, 32, 32]), KT_, ["RH"])
        self.mset(RH[:, :, 0:1], 0.0, KT_, ["RH"])
        self.dma("pool", BBR[:].rearrange("p t c -> p (t c)"), I[("bbr", j)], [], ["BB"])
        self.dma("pool", BBI[:].rearrange("p t c -> p (t c)"), I[("bbi", j)], [], ["BB"])
        v3 = lambda a: a[:].rearrange("p (t c) -> p t c", t=4)
        for g in range(8):
            sr, si, tm, r4 = T[0], T[1], T[2], RS
            self.dma("sp", sr[:], I[("cpr", j)][:, g * 512:(g + 1) * 512], [], ["T0"])
            self.dma("sp", si[:], I[("cpi", j)][:, g * 512:(g + 1) * 512], [], ["T1"])
            fre_b = FRE[:, g * 4:(g + 1) * 4].unsqueeze(2).broadcast_to([128, 4, 128])
            fim_b = FIM[:, g * 4:(g + 1) * 4].unsqueeze(2).broadcast_to([128, 4, 128])
            kk = ["T0", "T1", "T2", "RS", "CC"] + K
            self.tt(v3(tm), v3(si), fim_b, ALU.mult, kk, ["T2"])
            self.tt(v3(si), v3(si), fre_b, ALU.mult, kk, ["T1"])
            self.tt(v3(r4), v3(sr), fre_b, ALU.mult, kk, ["RS"])
            self.tt(CRB[:, g * 4:(g + 1) * 4, :], v3(r4), v3(tm), ALU.subtract, kk, ["CC"])
            self.tt(v3(sr), v3(sr), fim_b, ALU.mult, kk, ["T0"])
            self.tt(v3(sr), v3(sr), v3(si), ALU.add, kk, ["T0"])
            self.ts(CIN[:, g * 4:(g + 1) * 4, :], v3(sr), -1.0, ALU.mult, kk, ["CC"])
        for r, nm in enumerate(("s5re0", "s5im0")):
            for half in range(8):
                st, sk = T[half % 2], "T%d" % (half % 2)
                self.dma("sp", st[0:16, :], I[nm][j][:, half * 512:(half + 1) * 512], ["CC"], [sk])
                for q in range(4):
                    tile = half * 4 + q
                    self.tr(PS[0][:, tile * 16:(tile + 1) * 16], st[0:16, q * 128:(q + 1) * 128], IDF[0:16, 0:16], [sk, "CON"], [("ps", 0)])
            self.cp("dve", H0[:, r, :, :], PS[0][:].rearrange("p (t q) -> p t q", t=32), [("ps", 0)], ["H0"])
        fn2, gre, gim = dt, ang, mag
        tt(fn2, FRE, FRE, ALU.mult)
        tt(m_, FIM, FIM, ALU.mult)
        tt(fn2, fn2, m_, ALU.add)
        self.rcp(fn2, fn2, K, K)
        tt(gre, FRE, fn2, ALU.mult)
        tt(gim, FIM, fn2, ALU.mult)
        ts(gim, gim, -1.0, ALU.mult)
        HT = TQ4
        bq = lambda f: f.unsqueeze(2).broadcast_to([128, 32, 16])
        K2 = K + ["H0", "tq"]
        ta, tb = HT[:, 0, :, 0:16], HT[:, 1, :, 0:16]
        self.tt(ta, H0[:, 0], bq(gre), ALU.mult, K2, ["tq"])
        self.tt(tb, H0[:, 1], bq(gim), ALU.mult, K2, ["tq"])
        self.tt(ta, ta, tb, ALU.subtract, K2, ["tq"])
        self.tt(tb, H0[:, 0], bq(gim), ALU.mult, K2, ["tq"])
        self.tt(H0[:, 1], H0[:, 1], bq(gre), ALU.mult, K2, ["H0"])
        self.tt(H0[:, 1], H0[:, 1], tb, ALU.add, K2, ["H0"])
        self.cp("dve", H0[:, 0], ta, K2, ["H0"])

        DPAR = self.par(("s5d", j))
        for b in range(64 + 4):
            bu, bk = BU0, ("BU", 0)
            c0 = b * 32
            tb0 = (c0 // 512) * 512 if c0 < 2048 else 2048
            xk, xk2 = ("XB", tb0), ("X", tb0)
            for r, BB in enumerate((BBR, BBI)):
                for half in range(2):
                    ps, pk = self.next_bank()
                    for tl in range(16):
                        tile = half * 16 + tl
                        self.mm(ps[:, tl * 32:(tl + 1) * 32], BB[:, tile, :], XB[:, tile // 4, c0:c0 + 32], True, True, ["BB", xk], [pk])
                    self.cp("act", bu[:, r, half * 16:(half + 1) * 16, :], ps[:].rearrange("p (t s) -> p t s", t=16), [pk], [bk])
            hk = ["HH", "HL", "m", "H0"] + K
            if b < 64:
                kr = hk + [bk, "TAB", "tq", "RH"]
                br, bi = bu[:, 0], bu[:, 1]
                self.tt(HH[:, 0], EC, br, ALU.mult, kr, ["HH"])
                self.tt(tq, ES, bi, ALU.mult, kr, ["tq"])
                self.tt(HH[:, 0], HH[:, 0], tq, ALU.add, kr, ["HH"])
                self.tt(HH[:, 1], EC, bi, ALU.mult, kr, ["HH"])
                self.tt(tq, ES, br, ALU.mult, kr, ["tq"])
                self.tt(HH[:, 1], HH[:, 1], tq, ALU.subtract, kr, ["HH"])
                if b > 0:
                    self.tt(M1, C1, HL[:, 0:1, :].broadcast_to([128, 2, 32]), ALU.mult, kr, ["m"])
                    self.tt(M2, C2, HL[:, 1:2, :].broadcast_to([128, 2, 32]), ALU.mult, kr, ["m"])
                    self.tt(HH[:, :, :, 0], HH[:, :, :, 0], M1, ALU.add, kr, ["HH"])
                    self.tt(HH[:, :, :, 0], HH[:, :, :, 0], M2, ALU.add, kr, ["HH"])
                for r in range(2):
                    self.scan(bu[:, r].rearrange("p t s -> p (t s)"), RH.rearrange("p t s -> p (t s)"), HH[:, r].rearrange("p t s -> p (t s)"), kr, [bk])
                self.tt(HH[:, 0], EC, br, ALU.mult, kr, ["HH"])
                self.tt(tq, ES, bi, ALU.mult, kr, ["tq"])
                self.tt(HH[:, 0], HH[:, 0], tq, ALU.subtract, kr, ["HH"])
                self.tt(HH[:, 1], ES, br, ALU.mult, kr, ["HH"])
                self.tt(tq, EC, bi, ALU.mult, kr, ["tq"])
                self.tt(HH[:, 1], HH[:, 1], tq, ALU.add, kr, ["HH"])
                self.cp("dve", HL, HH[:, :, :, 31], ["HH"], ["HL"])
            else:
                q0 = (b - 64) * 4
                buv = bu[:].rearrange("p r t (q s) -> p r t q s", q=4)
                hhv = HH[:].rearrange("p r t (q s) -> p r t q s", q=4)
                ob = TQ4
                ok = "tq"
                m1, m2 = ob[:, :, :, 0:4], ob[:, :, :, 4:8]
                c1b = C1.unsqueeze(3).broadcast_to([128, 2, 32, 4])
                c2b = C2.unsqueeze(3).broadcast_to([128, 2, 32, 4])
                for s in range(8):
                    prev = hhv[:, :, :, :, s - 1] if s > 0 else H0[:, :, :, q0:q0 + 4]
                    self.tt(m1, c1b, prev[:, 0:1].broadcast_to([128, 2, 32, 4]), ALU.mult, hk + [ok], [ok])
                    self.tt(m2, c2b, prev[:, 1:2].broadcast_to([128, 2, 32, 4]), ALU.mult, hk + [ok], [ok])
                    self.tt(m1, m1, m2, ALU.add, hk + [ok], [ok])
                    self.tt(hhv[:, :, :, :, s], m1, buv[:, :, :, :, s], ALU.add, hk + [ok, bk], ["HH"])
            if b >= 63:
                nq = 1 if b == 63 else 4
                src = HH[:, :, :, 31:32] if b == 63 else HH[:].rearrange("p r t (q s) -> p r t q s", q=4)[:, :, :, :, 7]
                FT, fk = TQ4, "tq"
                fo, ft = FT[:, :, :, 8:8 + nq], FT[:, :, :, 12:12 + nq]
                fb = lambda f: f.unsqueeze(2).broadcast_to([128, 32, nq])
                rk = ["HH", fk] + K
                self.tt(fo[:, 0], src[:, 0], fb(FRE), ALU.mult, rk, [fk])
                self.tt(ft[:, 0], src[:, 1], fb(FIM), ALU.mult, rk, [fk])
                self.tt(fo[:, 0], fo[:, 0], ft[:, 0], ALU.subtract, rk, [fk])
                self.tt(fo[:, 1], src[:, 0], fb(FIM), ALU.mult, rk, [fk])
                self.tt(ft[:, 1], src[:, 1], fb(FRE), ALU.mult, rk, [fk])
                self.tt(fo[:, 1], fo[:, 1], ft[:, 1], ALU.add, rk, [fk])
                for r in range(2):
                    if b == 63:
                        outn = ("s5pre", "s5pim")[r]
                        self.tr(PS[7][0:32, 0:128], fo[:, r, :, 0], IDF, [fk, "CON"], [("ps", 7)])
                        self.cp("act", T[2][0:32, 0:128], PS[7][0:32, 0:128], [("ps", 7)], ["T2"])
                        self.dma("sp", O[outn][j].rearrange("(t p) -> t p", p=128), T[2][0:32, 0:128], ["T2"], [("so", outn)])
                    else:
                        outn = ("s5sre", "s5sim")[r]
                        q0 = (b - 64) * 4
                        for tg in range(8):
                            for tl in range(4):
                                self.tr(PS[7][0:4, tl * 128:(tl + 1) * 128], fo[:, r, tg * 4 + tl, :], IDF, [fk, "CON"], [("ps", 7)])
                            self.cp("act", T[2][0:4, :], PS[7][0:4, :], [("ps", 7)], ["T2"])
                            self.dma("sp", O[outn][j][q0:q0 + 4, tg * 512:(tg + 1) * 512], T[2][0:4, :], ["T2"], [("so", outn, tg, q0)])
            self.cp("act", HHB[:], HH[:], ["HH"], ["HHB"])
            for k in range(8):
                n = 0
                for q in range(4):
                    tile = k * 4 + q
                    for r, CC in enumerate((CRB, CIN)):
                        self.mm(PS[6][:, k * 32:(k + 1) * 32], CC[:, tile, :], HHB[:, r, tile, :], n == 0, n == 7, ["CC", "HHB"], [("ps", 6)])
                        n += 1
            yv = PS[6][:, 0:256].rearrange("p (k s) -> p k s", k=8)
            v = T[0][:, 0:256].rearrange("p (k s) -> p k s", k=8)
            t1 = T[1][:, 0:256].rearrange("p (k s) -> p k s", k=8)
            gk = ["T0", "T1", ("ps", 6)]
            self.tt(v, X[:, :, c0:c0 + 32], DPAR.unsqueeze(2).broadcast_to([128, 8, 32]), ALU.mult, gk + [xk2, "PAR"], ["T0"])
            self.tt(v, v, yv, ALU.add, gk, ["T0"])
            self.tt(t1, v, v, ALU.mult, gk, ["T1"])
            self.ts(t1, t1, 0.044715, ALU.mult, gk, ["T1"], s2=1.0, op1=ALU.add)
            self.tt(t1, t1, v, ALU.mult, gk, ["T1"])
            self.act(t1, t1, AF.Tanh, gk, ["T1"], scale=0.7978845608028654)
            self.ts(t1, t1, 1.0, ALU.add, gk, ["T1"], s2=0.5, op1=ALU.mult)
            self.tt(XB[:, :, c0:c0 + 32], t1, v, ALU.mult, gk + [xk], [xk])
        self.P.emit()
        wv, kv = self.wload(I["s5_w_glu_v"][j])
        wg, kg = self.wload(I["s5_w_glu_g"][j])
        for (t0, n) in TBLK:
            for oc in range(8):
                pv, pg = PS[(oc % 2) * 2], PS[1 + (oc % 2) * 2]
                kpv, kpg = ("ps", (oc % 2) * 2), ("ps", 1 + (oc % 2) * 2)
                for c in range(8):
                    self.mm(pv[:, 0:n], wv[:, c, oc * 128:(oc + 1) * 128], XB[:, c, t0:t0 + n], c == 0, c == 7, [kv, ("XB", t0)], [kpv])
                for c in range(8):
                    self.mm(pg[:, 0:n], wg[:, c, oc * 128:(oc + 1) * 128], XB[:, c, t0:t0 + n], c == 0, c == 7, [kg, ("XB", t0)], [kpg])
                tt_, tk = T[oc % 2], ("T", oc % 2)
                self.act(tt_[:, 0:n], pg[:, 0:n], AF.Sigmoid, [kpg], [tk])
                self.tt(tt_[:, 0:n], tt_[:, 0:n], pv[:, 0:n], ALU.mult, [kpv, tk], [tk])
                self.stt(X[:, oc, t0:t0 + n], X[:, oc, t0:t0 + n], ALPHA, tt_[:, 0:n], ALU.mult, ALU.add, [tk, ("X", t0)], [("X", t0)])

    def rwkv_layer(self, i, j):
        X, XB, T, I, O, PS, AR, SM, IDF, BLK = self.X, self.XB, self.T, self.I, self.O, self.PS, self.AR, self.SM, self.IDF, self.BLK
        if STUB_RWKV:
            for (t0, n) in TBLK:
                self.ts(X[:, :, t0:t0 + n], X[:, :, t0:t0 + n], ALPHA, ALU.mult, [("X", t0)], [("X", t0)])
            return
        off = [17408]

        self.force_auto = True
        self.regions = []

        def ab(n):
            a = AR[:, off[0]:off[0] + n]
            self.regions.append((off[0] * 2, (off[0] + n) * 2, ("rk", len(self.regions))))
            off[0] += n
            assert off[0] <= 51200, off[0]
            return a

        def af(n):
            return ab(2 * n).bitcast(F32)
        q3 = lambda a: a.rearrange("p (c t) -> p c t", c=2)
        WRq, WKq, WVq = [ab(2048).rearrange("p (c n) -> p c n", c=8) for _ in range(3)]
        WOq = ab(2048).rearrange("p (c n) -> p c n", c=2)
        W1, A1w = [ab(512).rearrange("p (c n) -> p c n", c=8) for _ in range(2)]
        G1w = ab(1024).rearrange("p (c n) -> p c n", c=8)
        V1w = ab(256).rearrange("p (c n) -> p c n", c=8)
        W2q, A2q, V2q, G2q = [ab(256) for _ in range(4)]
        XM = ab(3072).rearrange("p (m c t) -> p m c t", m=6, c=8)
        XX = af(512).rearrange("p (c t) -> p c t", c=8)
        TMX = af(512).rearrange("p (c t) -> p c t", c=8)
        R_, K_, A_, KAP, LW, LG, E_, T1t, Y_, YBs, VFt, T1e, SSe = [q3(af(128)) for _ in range(13)]
        V2, G2, BON2, BT2, KT2, BH2, KH2 = [[q3(af(128)) for _ in range(2)] for _ in range(7)]
        ATRT2 = [af(256).rearrange("p (w c t) -> p w c t", w=2, c=2) for _ in range(2)]
        Vt, BHt, KHt, Ut = [ab(256) for _ in range(4)]
        h4 = lambda a: a.rearrange("p (h t) -> p h t", h=4)
        Xa, XTa, Xb, XTb, Pm, Rs = [h4(af(256)) for _ in range(6)]
        MBR, MKA, MKR = [h4(ab(256)) for _ in range(3)]
        STp, STs, BDin = [af(256).rearrange("p (c n) -> p c n", c=2) for _ in range(3)]
        OB = q3(ab(128))
        T1b, A1b, V1b, G1b = ab(64), ab(64), ab(64), ab(64)
        SH0 = af(128).rearrange("p (c s) -> p c s", c=8)
        XL = af(136).rearrange("p (c s) -> p c s", c=8)
        OMKA = af(8)
        RSTa, RSTb = af(128), af(128)
        GLu2 = [af(16).rearrange("p (c u) -> p c u", c=2) for _ in range(2)]
        LGL = af(16).rearrange("p (c u) -> p c u", c=2)
        SSn = q3(af(128))
        pc = self.pc
        MU = self.par(("mu", j), 48).rearrange("p (m c) -> p m c", m=6)
        W0, A0, KKp, KAp, LXG, LXB, RKp = [self.par((nm, j)) for nm in ("rwkv_w0", "rwkv_a0", "rwkv_k_k", "rwkv_k_a", "rwkv_lnx_g", "rwkv_lnx_b", "rwkv_r_k")]
        V0p = self.par(("rwkv_v0", 1))
        B = [PS[k] for k in range(8)]
        bk = lambda k: ("ps", k)
        K0 = ["rk"]

        self.dma("pool", W1, I["rwkv_w1"][j].rearrange("(c p) n -> p c n", p=128), [], ["Wl"])
        self.dma("pool", A1w, I["rwkv_a1"][j].rearrange("(c p) n -> p c n", p=128), [], ["Wl"])
        self.dma("pool", G1w, I["rwkv_g1"][j].rearrange("(c p) n -> p c n", p=128), [], ["Wl"])
        if j == 1:
            self.dma("pool", V1w, I["rwkv_v1"][0].rearrange("(c p) n -> p c n", p=128), [], ["Wl"])
        self.ts(OMKA, KAp, -1.0, ALU.mult, ["PAR"], K0, s2=1.0, op1=ALU.add)
        for a, src in ((RSTa, self.RST64), (RSTb, self.RST8)):
            self.cp("dve", a[:, 0:64], src, ["CON"], K0)
            self.cp("dve", a[:, 64:128], src, ["CON"], K0)
        self.cp("dve", XL[:, :, 0:1], X[:, :, 2047:2048], [("X", 1536)], ["XL"])
        self.cp("dve", XL[:, :, 1:17], X[:, :, 2048:2176].rearrange("p c (u t) -> p c u t", u=16)[:, :, :, 7], [("X", 2048)], ["XL"])
        for half in range(2):
            for cc in range(4):
                c = half * 4 + cc
                self.tr(B[6][0:17, cc * 128:(cc + 1) * 128], XL[:, c, :], IDF, ["XL", "CON"], [bk(6)])
            self.cp("act", T[half][0:17, :], B[6][0:17, :], [bk(6)], [("T", half)])
            self.dma("sp", O["shp"][j:j + 1, half * 512:(half + 1) * 512], T[half][0:1, :], [("T", half)], [("sho", half)])
            self.dma("sp", O["shs"][j][:, half * 512:(half + 1) * 512], T[half][1:17, :], [("T", half)], [("shso", half)])
        for half in range(2):
            self.dma("sp", T[half][0:16, :], I["sh0"][j][:, half * 512:(half + 1) * 512], [], [("T", half)])
            for cc in range(4):
                c = half * 4 + cc
                self.tr(B[7][:, c * 16:(c + 1) * 16], T[half][0:16, cc * 128:(cc + 1) * 128], IDF[0:16, 0:16], [("T", half), "CON"], [bk(7)])
        self.cp("dve", SH0, B[7][:, 0:128].rearrange("p (c s) -> p c s", c=8), [bk(7)], ["SH0"])
        self.mset(BDin, 0.0, [], ["BDin"])
        for (t0_, n_) in TBLK:
            self.ts(X[:, :, t0_:t0_ + n_], X[:, :, t0_:t0_ + n_], ALPHA, ALU.mult)

        for hq in range(4):
            cs = slice(hq * 256, (hq + 1) * 256)
            for w, nm in ((WRq, "rwkv_w_r"), (WKq, "rwkv_w_k"), (WVq, "rwkv_w_v")):
                self.dma("pool", w, I[nm][j][:, cs].rearrange("(c p) n -> p c n", p=128), [], ["Wq"])
            self.dma("pool", WOq, I["rwkv_w_o"][j][cs, :].rearrange("(c p) n -> p c n", p=128), [], ["Wq"])
            self.dma("pool", W2q[0:64, :], I["rwkv_w2"][j][:, cs], [], ["Wq"])
            self.dma("pool", A2q[0:64, :], I["rwkv_a2"][j][:, cs], [], ["Wq"])
            if j == 1:
                self.dma("pool", V2q[0:32, :], I["rwkv_v2"][0][:, cs], [], ["Wq"])
            self.dma("pool", G2q, I["rwkv_g2"][j][:, cs], [], ["Wq"])
            self.mset(STp, 0.0, [], ["STp"])
            pcs = slice(2 * hq, 2 * hq + 2)
            bc = lambda p: p[:, pcs].unsqueeze(2).broadcast_to([128, 2, 64])
            def prep(blk):
                bs = blk % 2
                V_, G_, BON, BT, KT, BH, KH, ATRT, GLu = V2[bs], G2[bs], BON2[bs], BT2[bs], KT2[bs], BH2[bs], KH2[bs], ATRT2[bs], GLu2[bs]
                AT, RT = ATRT[:, 0], ATRT[:, 1]
                samp = blk >= 32
                g0 = 64 * blk
                tb0 = (g0 // 512) * 512 if g0 < 2048 else 2048
                xbb = XB[:, :, g0:g0 + 64]
                xk = ("XB", tb0)
                kb_ = ["blk"]
                if samp:
                    sb = blk - 32
                    xbv = xbb.rearrange("p c (u t) -> p c u t", u=8)
                    xxv = XX.rearrange("p c (u t) -> p c u t", u=8)
                    self.tt(xxv[:, :, :, 1:8], xbv[:, :, :, 0:7], xbv[:, :, :, 1:8], ALU.subtract, [xk], kb_)
                    self.tt(xxv[:, :, :, 0], SH0[:, :, 8 * sb:8 * sb + 8], xbv[:, :, :, 0], ALU.subtract, [xk, "SH0"], kb_)
                elif blk == 0:
                    self.tt(XX[:, :, 1:64], XB[:, :, 0:63], XB[:, :, 1:64], ALU.subtract, [xk], kb_)
                    self.ts(XX[:, :, 0:1], XB[:, :, 0:1], -1.0, ALU.mult, [xk], kb_)
                else:
                    pk_ = ("XB", ((g0 - 1) // 512) * 512)
                    self.tt(XX, XB[:, :, g0 - 1:g0 + 63], xbb, ALU.subtract, [xk, pk_], kb_)
                for m in range(6):
                    self.tt(TMX, XX, MU[:, m, :].unsqueeze(2).broadcast_to([128, 8, 64]), ALU.mult, kb_ + ["PAR"], kb_)
                    self.tt(XM[:, m], TMX, xbb, ALU.add, kb_ + [xk], kb_)
                for n_, (w, m) in enumerate(((WRq, 0), (WKq, 2), (WVq, 3))):
                    for c2 in range(2):
                        for c in range(8):
                            self.mm(B[0][:, n_ * 128 + c2 * 64:n_ * 128 + c2 * 64 + 64], w[:, c, c2 * 128:(c2 + 1) * 128], XM[:, m, c, :], c == 0, c == 7, kb_ + ["Wq"], [bk(0)])
                self.cp("act", R_, q3(B[0][:, 0:128]), [bk(0)], kb_)
                self.cp("dve", K_, q3(B[0][:, 128:256]), [bk(0)], kb_)
                self.cp("act", V_, q3(B[0][:, 256:384]), [bk(0)], kb_)
                for c in range(8):
                    self.mm(B[1][0:64, 0:64], W1[:, c, :], XM[:, 1, c, :], c == 0, c == 7, kb_ + ["Wl"], [bk(1)])
                for c in range(8):
                    self.mm(B[1][0:64, 64:128], A1w[:, c, :], XM[:, 4, c, :], c == 0, c == 7, kb_ + ["Wl"], [bk(1)])
                if j == 1:
                    for c in range(8):
                        self.mm(B[1][0:32, 128:192], V1w[:, c, :], XM[:, 3, c, :], c == 0, c == 7, kb_ + ["Wl"], [bk(1)])
                for c in range(8):
                    self.mm(B[1][:, 192:256], G1w[:, c, :], XM[:, 5, c, :], c == 0, c == 7, kb_ + ["Wl"], [bk(1)])
                self.act(T1b[0:64, :], B[1][0:64, 0:64], AF.Tanh, [bk(1)], kb_)
                self.cp("act", A1b[0:64, :], B[1][0:64, 64:128], [bk(1)], kb_)
                if j == 1:
                    self.cp("act", V1b[0:32, :], B[1][0:32, 128:192], [bk(1)], kb_)
                self.act(G1b, B[1][:, 192:256], AF.Sigmoid, [bk(1)], kb_)
                for c2 in range(2):
                    self.mm(B[2][:, c2 * 64:c2 * 64 + 64], W2q[0:64, c2 * 128:(c2 + 1) * 128], T1b[0:64, :], True, True, kb_ + ["Wq"], [bk(2)])
                    self.mm(B[2][:, 128 + c2 * 64:128 + c2 * 64 + 64], A2q[0:64, c2 * 128:(c2 + 1) * 128], A1b[0:64, :], True, True, kb_ + ["Wq"], [bk(2)])
                    if j == 1:
                        self.mm(B[2][:, 256 + c2 * 64:256 + c2 * 64 + 64], V2q[0:32, c2 * 128:(c2 + 1) * 128], V1b[0:32, :], True, True, kb_ + ["Wq"], [bk(2)])
                    self.mm(B[2][:, 384 + c2 * 64:384 + c2 * 64 + 64], G2q[:, c2 * 128:(c2 + 1) * 128], G1b, True, True, kb_ + ["Wq"], [bk(2)])
                kp = kb_ + ["PAR"]
                self.tt(LW, q3(B[2][:, 0:128]), bc(W0), ALU.add, kp + [bk(2)], kb_)
                self.act(LW, LW, AF.Sigmoid, kb_, kb_)
                self.ts(LW, LW, -0.6065306597126334, ALU.mult, kb_, kb_)
                self.tt(A_, q3(B[2][:, 128:256]), bc(A0), ALU.add, kp + [bk(2)], kb_)
                self.act(A_, A_, AF.Sigmoid, kb_, kb_)
                self.cp("dve", G_, q3(B[2][:, 384:512]), [bk(2)], kb_)
                vfd = self.VF[:, 2 * hq:2 * hq + 2, g0:g0 + 64]
                if j == 1:
                    self.tt(T1t, q3(B[2][:, 256:384]), bc(V0p), ALU.add, kp + [bk(2)], kb_)
                    self.act(T1t, T1t, AF.Sigmoid, kb_, kb_)
                    self.dma("sp", VFt, vfd, kb_, kb_)
                    self.tt(VFt, VFt, V_, ALU.subtract, kb_, kb_)
                    self.tt(VFt, VFt, T1t, ALU.mult, kb_, kb_)
                    self.tt(V_, V_, VFt, ALU.add, kb_, kb_)
                else:
                    self.dma("sp", vfd, V_, kb_, [("vf", hq, blk)])
                self.tt(KAP, K_, bc(KKp), ALU.mult, kp, kb_)
                self.tt(T1t, KAP, KAP, ALU.mult, kb_, kb_)
                for c2 in range(2):
                    self.mm(B[3][:, c2 * 64:c2 * 64 + 64], BLK, T1t[:, c2, :], True, True, kb_ + ["CON"], [bk(3)])
                self.act(SSn, q3(B[3][:, 0:128]), AF.Sqrt, [bk(3)], kb_)
                self.ts(SSn, SSn, 1e-12, ALU.max, kb_, kb_)
                self.rcp(SSn, SSn, kb_, kb_)
                self.tt(KAP, KAP, SSn, ALU.mult, kb_, kb_)
                self.tt(T1t, A_, bc(KAp), ALU.mult, kp, kb_)
                self.tt(T1t, T1t, OMKA[:, pcs].unsqueeze(2).broadcast_to([128, 2, 64]), ALU.add, kb_ + K0, kb_)
                self.tt(K_, K_, T1t, ALU.mult, kb_, kb_)
                self.tt(T1t, R_, K_, ALU.mult, kb_, kb_)
                self.tt(T1t, T1t, bc(RKp), ALU.mult, kp, kb_)
                for c2 in range(2):
                    self.mm(B[3][:, 128 + c2 * 64:128 + c2 * 64 + 64], BLK, T1t[:, c2, :], True, True, kb_ + ["CON"], [bk(3)])
                self.tt(BON, q3(B[3][:, 128:256]), V_, ALU.mult, kb_ + [bk(3)], kb_)
                self.scan(LG.rearrange("p c t -> p (c t)"), RSTb if samp else RSTa, LW.rearrange("p c t -> p (c t)"), kb_ + K0, kb_)
                nu = 8 if samp else 1
                L = 8 if samp else 64
                lgv = LG.rearrange("p c (u t) -> p c u t", u=nu)
                self.cp("dve", LGL[:, :, 0:nu], lgv[:, :, :, L - 1], kb_, kb_)
                self.act(GLu[:, :, 0:nu], LGL[:, :, 0:nu], AF.Exp, kb_, kb_)
                self.tt(A_, KAP, A_, ALU.mult, kb_, kb_)
                self.act(E_, LG, AF.Exp, kb_, kb_)
                self.tt(RT, R_, E_, ALU.mult, kb_, kb_)
                self.tt(T1t, LG, LW, ALU.subtract, kb_, kb_)
                self.act(E_, T1t, AF.Exp, kb_, kb_)
                self.stt(AT, KAP, -1.0, E_, ALU.mult, ALU.mult, kb_, kb_)
                self.act(E_, LG, AF.Exp, kb_, kb_, scale=-1.0)
                self.tt(BT, A_, E_, ALU.mult, kb_, kb_)
                self.tt(KT, K_, E_, ALU.mult, kb_, kb_)
                t1v = T1t.rearrange("p c (u t) -> p c u t", u=nu)
                self.tt(t1v, LGL[:, :, 0:nu].unsqueeze(3).broadcast_to([128, 2, nu, L]), lgv, ALU.subtract, kb_, kb_)
                self.act(E_, T1t, AF.Exp, kb_, kb_)
                self.tt(BH, A_, E_, ALU.mult, kb_, kb_)
                self.tt(KH, K_, E_, ALU.mult, kb_, kb_)
            def tail(blk):
                bs = blk % 2
                V_, G_, BON, BT, KT, BH, KH, ATRT, GLu = V2[bs], G2[bs], BON2[bs], BT2[bs], KT2[bs], BH2[bs], KH2[bs], ATRT2[bs], GLu2[bs]
                AT, RT = ATRT[:, 0], ATRT[:, 1]
                samp = blk >= 32
                g0 = 64 * blk
                tb0 = (g0 // 512) * 512 if g0 < 2048 else 2048
                kb_ = ["blk"]
                kp = kb_ + ["PAR"]
                nu = 8 if samp else 1
                L = 8 if samp else 64
                for u in range(nu):
                    c0 = u * L
                    if samp:
                        sq = (blk - 32) * 8 + u
                        ST = STs
                        for c2 in range(2):
                            for h2 in range(2):
                                hd = 4 * hq + 2 * c2 + h2
                                self.dma("sp", BDin[64 * h2:64 * h2 + 64, c2, 64 * h2:64 * h2 + 64], I["rw0"][j, sq, hd], ["BDin"] + kb_, ["BDin"])
                        for c2 in range(2):
                            self.tr(B[7][:, c2 * 128:(c2 + 1) * 128], BDin[:, c2, :], IDF, ["BDin", "CON"], [bk(7)])
                        self.cp("dve", STs, B[7][:, 0:256].rearrange("p (c n) -> p c n", c=2), [bk(7)], ["ST"])
                    else:
                        ST = STp
                    self.rwkv_unit(c0, L, ST, dict(V_=V_, BH=BH, KH=KH, AT=AT, RT=RT, ATRT=ATRT, BT=BT, KT=KT, Vt=Vt, BHt=BHt, KHt=KHt, Ut=Ut, Xa=Xa, XTa=XTa, Xb=Xb, XTb=XTb,
                                                     Pm=Pm, Rs=Rs, MBR=MBR, MKA=MKA, MKR=MKR, YBs=YBs, Y_=Y_, GL=GLu[:, :, u]), kb_)
                    last = (blk == 31) or samp
                    if last:
                        for c2 in range(2):
                            self.tr(B[7][:, c2 * 128:(c2 + 1) * 128], ST[:, c2, :], IDF, ["ST", "CON"], [bk(7)])
                        self.cp("dve", BDin, B[7][:, 0:256].rearrange("p (c n) -> p c n", c=2), [bk(7)], ["BDin"])
                        for c2 in range(2):
                            for h2 in range(2):
                                hd = 4 * hq + 2 * c2 + h2
                                dst = O["rws"][j, sq, hd] if samp else O["rwp"][j, hd]
                                self.dma("sp", dst, BDin[64 * h2:64 * h2 + 64, c2, 64 * h2:64 * h2 + 64], ["BDin"], [("rwo", hq, blk, u, c2, h2)])
                for c2 in range(2):
                    self.mm(B[6][:, c2 * 64:c2 * 64 + 64], BLK, Y_[:, c2, :], True, True, kb_ + ["CON"], [bk(6)])
                self.stt(Y_, q3(B[6][:, 0:128]), -1.0 / 64.0, Y_, ALU.mult, ALU.add, kb_ + [bk(6)], kb_)
                self.tt(T1e, Y_, Y_, ALU.mult, kb_, kb_)
                for c2 in range(2):
                    self.mm(B[6][:, 128 + c2 * 64:128 + c2 * 64 + 64], BLK, T1e[:, c2, :], True, True, kb_ + ["CON"], [bk(6)])
                self.act(SSe, q3(B[6][:, 128:256]), AF.Sqrt, [bk(6), "eps"], kb_, bias=self.EPS2, scale=1.0 / 64.0)
                self.rcp(SSe, SSe, kb_, kb_)
                self.tt(Y_, Y_, SSe, ALU.mult, kb_, kb_)
                for c2 in range(2):
                    c = 2 * hq + c2
                    self.ts(Y_[:, c2, :], Y_[:, c2, :], LXG[:, c:c + 1], ALU.mult, kp, kb_, s2=LXB[:, c:c + 1], op1=ALU.add)
                self.tt(Y_, Y_, BON, ALU.add, kb_, kb_)
                self.tt(OB, Y_, G_, ALU.mult, kb_, kb_)
                for oc in range(8):
                    for c2 in range(2):
                        self.mm(B[7][:, oc * 64:(oc + 1) * 64], WOq[:, c2, oc * 128:(oc + 1) * 128], OB[:, c2, :], c2 == 0, c2 == 1, kb_ + ["Wq"], [bk(7)])
                xg = ("X", tb0)
                pso = B[7][:].rearrange("p (c t) -> p c t", c=8)
                self.tt(X[:, :, g0:g0 + 64], X[:, :, g0:g0 + 64], pso, ALU.add, [bk(7), xg], [xg])

            prep(0)
            for blk in range(34):
                self.stream = []
                tail(blk)
                lt = self.stream
                self.stream = []
                if blk + 1 < 34:
                    prep(blk + 1)
                lp = self.stream
                self.stream = None
                ia = ib = 0
                while ia < len(lt) or ib < len(lp):
                    if ia < len(lt):
                        self._flush(lt[ia]); ia += 1
                    if ib < len(lp):
                        self._flush(lp[ib]); ib += 1
        self.force_auto = False

    def rwkv_unit(self, c0, L, ST, t, kb_):
        PS, IDF = self.PS, self.IDF
        B = PS
        bk = lambda k: ("ps", k)
        ku = kb_ + ["ST"]
        cs = slice(c0, c0 + L)
        nlev = {64: 5, 8: 2}[L]
        for n_, (src, dst, bank, col) in enumerate(((t["V_"], t["Vt"], 4, 0), (t["BH"], t["BHt"], 4, 256), (t["KH"], t["KHt"], 5, 0))):
            for c2 in range(2):
                self.tr(B[bank][0:L, col + c2 * 128:col + (c2 + 1) * 128], src[:, c2, cs], IDF, kb_ + ["CON"], [bk(bank)])
            self.cp("act" if n_ % 2 == 0 else "dve", dst[0:L, :], B[bank][0:L, col:col + 256], [bk(bank)], ku)
        ATRT, AT, BT, KT = t["ATRT"], t["AT"], t["BT"], t["KT"]
        for h2 in range(2):
            b0 = 64 * h2
            bs, bn = B[4 + h2], B[6 + h2]
            for c2 in range(2):
                rhs = ATRT[b0:b0 + 64, :, c2, cs]
                self.mm(bs[0:L, c2 * 128:c2 * 128 + 2 * L], BT[b0:b0 + 64, c2, cs], rhs, True, True, ku, [bk(4 + h2)])
                self.mm(bs[0:L, 256 + c2 * 128:256 + c2 * 128 + 2 * L], KT[b0:b0 + 64, c2, cs], rhs, True, True, ku, [bk(4 + h2)])
                self.mm(bn[0:L, c2 * 64:c2 * 64 + L], AT[b0:b0 + 64, c2, cs], BT[b0:b0 + 64, c2, cs], True, True, ku, [bk(6 + h2)])
            v4 = bs[0:L, :].rearrange("p (k x) -> p k x", k=4)
            mlt = self.MLT[0:L, 0:L].unsqueeze(1).broadcast_to([L, 2, L])
            mle = self.MLE[0:L, 0:L].unsqueeze(1).broadcast_to([L, 2, L])
            mgt = self.MGT[0:L, 0:L].unsqueeze(1).broadcast_to([L, 2, L])
            hsel = slice(h2, 4, 2)
            self.tt(t["Xa"][0:L, hsel, 0:L], v4[:, 0:2, 0:L], mlt, ALU.mult, [bk(4 + h2), "CON"], ku)
            self.tt(t["MBR"][0:L, hsel, 0:L], v4[:, 0:2, L:2 * L], mle, ALU.mult, [bk(4 + h2), "CON"], ku)
            self.tt(t["MKA"][0:L, hsel, 0:L], v4[:, 2:4, 0:L], mlt, ALU.mult, [bk(4 + h2), "CON"], ku)
            self.tt(t["MKR"][0:L, hsel, 0:L], v4[:, 2:4, L:2 * L], mle, ALU.mult, [bk(4 + h2), "CON"], ku)
            self.tt(t["XTa"][0:L, hsel, 0:L], bn[0:L, 0:128].rearrange("p (k x) -> p k x", k=2)[:, :, 0:L], mgt, ALU.mult, [bk(6 + h2), "CON"], ku)
        Xc, XTc, Xn, XTn, Pm = t["Xa"], t["XTa"], t["Xb"], t["XTb"], t["Pm"]
        self.tt(Pm[0:L, :, 0:L], Xc[0:L, :, 0:L], IDF[0:L, 0:L].unsqueeze(1).broadcast_to([L, 4, L]), ALU.add, ku + ["CON"], ku)
        for lev in range(nlev):
            lastlev = lev == nlev - 1
            for hl in range(4):
                if not lastlev:
                    self.mm(B[4][0:L, hl * 64:hl * 64 + L], XTc[0:L, hl, 0:L], Xc[0:L, hl, 0:L], True, True, ku, [bk(4)])
                self.mm(B[5][0:L, hl * 64:hl * 64 + L], Xc[0:L, hl, 0:L], XTc[0:L, hl, 0:L], True, True, ku, [bk(5)])
            if not lastlev:
                self.cp("act", Xn[0:L, :, 0:L], B[4][0:L, 0:256].rearrange("p (h x) -> p h x", h=4)[:, :, 0:L], [bk(4)], ku)
            self.cp("dve", XTn[0:L, :, 0:L], B[5][0:L, 0:256].rearrange("p (h x) -> p h x", h=4)[:, :, 0:L], [bk(5)], ku)
            for hl in range(4):
                self.mm(B[6][0:L, hl * 64:hl * 64 + L], XTn[0:L, hl, 0:L], Pm[0:L, hl, 0:L], True, True, ku, [bk(6)])
            self.tt(Pm[0:L, :, 0:L], Pm[0:L, :, 0:L], B[6][0:L, 0:256].rearrange("p (h x) -> p h x", h=4)[:, :, 0:L], ALU.add, ku + [bk(6)], ku)
            Xc, XTc, Xn, XTn = Xn, XTn, Xc, XTc
        Vt, BHt, KHt, Ut, Rs, MKA, MBR, MKR = t["Vt"], t["BHt"], t["KHt"], t["Ut"], t["Rs"], t["MKA"], t["MBR"], t["MKR"]
        for c2 in range(2):
            self.mm(B[7][0:L, c2 * 128:(c2 + 1) * 128], AT[:, c2, cs], ST[:, c2, :], True, False, ku, [bk(7)])
            for h2 in range(2):
                hl = 2 * c2 + h2
                self.mm(B[7][0:L, hl * 64:(hl + 1) * 64], MKA[0:L, hl, 0:L], Vt[0:L, hl * 64:(hl + 1) * 64], False, h2 == 1, ku, [bk(7)])
        self.cp("act", Rs[0:L, :, :], B[7][0:L, 0:256].rearrange("p (h x) -> p h x", h=4), [bk(7)], ku)
        for hl in range(4):
            self.mm(B[4][0:L, hl * 64:(hl + 1) * 64], Pm[0:L, hl, 0:L], Rs[0:L, hl, :], True, True, ku, [bk(4)])
        self.cp("dve", Ut[0:L, :], B[4][0:L, 0:256], [bk(4)], ku)
        for c2 in range(2):
            self.mm(B[5][:, c2 * 64:c2 * 64 + L], ST[:, c2, :], t["RT"][:, c2, cs], True, True, ku, [bk(5)])
        for hl in range(4):
            self.mm(B[6][0:64, hl * 64:hl * 64 + L], Ut[0:L, hl * 64:(hl + 1) * 64], MBR[0:L, hl, 0:L], True, False, ku, [bk(6)])
            self.mm(B[6][0:64, hl * 64:hl * 64 + L], Vt[0:L, hl * 64:(hl + 1) * 64], MKR[0:L, hl, 0:L], False, True, ku, [bk(6)])
        ybv = B[6][0:64, 0:256].rearrange("p (c h x) -> p c h x", c=2, h=2)
        YBs = t["YBs"]
        for h2 in range(2):
            self.cp("act", YBs[64 * h2:64 * h2 + 64, :, 0:L], ybv[:, :, h2, 0:L], [bk(6)], ku)
        self.tt(t["Y_"][:, :, cs], B[5][:, 0:128].rearrange("p (c x) -> p c x", c=2)[:, :, 0:L], YBs[:, :, 0:L], ALU.add, ku + [bk(5)], ku)
        for hl in range(4):
            c2 = hl // 2
            self.mm(B[7][:, hl * 64:(hl + 1) * 64], BHt[0:L, c2 * 128:(c2 + 1) * 128], Ut[0:L, hl * 64:(hl + 1) * 64], True, False, ku, [bk(7)])
            self.mm(B[7][:, hl * 64:(hl + 1) * 64], KHt[0:L, c2 * 128:(c2 + 1) * 128], Vt[0:L, hl * 64:(hl + 1) * 64], False, True, ku, [bk(7)])
        suv = B[7][:, 0:256].rearrange("p (c h x) -> p c h x", c=2, h=2)
        GL = t["GL"]
        for h2 in range(2):
            r = slice(64 * h2, 64 * h2 + 64)
            blkv = ST[r, :, 64 * h2:64 * h2 + 64]
            self.tt(blkv, blkv, GL[r, :].unsqueeze(2).broadcast_to([64, 2, 64]), ALU.mult, ku, ku)
            self.tt(blkv, blkv, suv[r, :, h2, :], ALU.add, ku + [bk(7)], ku)

    def build(self):
        self.setup()
        self.phase_input()
        for i in range(DEPTH):
            j = i // 2
            if i % 2 == 0:
                self.s5_layer(i, j)
            else:
                self.rwkv_layer(i, j)
            self.layer_norm(i, 0)
            self.P.emit()
            self.xa_layer(i)
            self.layer_norm(i, 1)
            self.P.emit()
            self.mlp_layer(i)
            self.layer_norm(i, 2)
            self.P.emit()
        self.phase_output()
        return self.nc


WSHAPES = (("s5_w_glu_v", [2, D, D]), ("s5_w_glu_g", [2, D, D]), ("rwkv_w_r", [2, D, D]), ("rwkv_w_k", [2, D, D]),
           ("rwkv_w_v", [2, D, D]), ("rwkv_w_o", [2, D, D]), ("rwkv_w1", [2, D, 64]), ("rwkv_w2", [2, 64, D]),
           ("rwkv_a1", [2, D, 64]), ("rwkv_a2", [2, 64, D]), ("rwkv_v1", [1, D, 32]), ("rwkv_v2", [1, 32, D]),
           ("rwkv_g1", [2, D, 128]), ("rwkv_g2", [2, 128, D]), ("xa_w_q", [4, D, D]), ("xa_w_k", [4, D, D]),
           ("xa_w_v", [4, D, D]), ("xa_w_o", [4, D, D]), ("mlp_w1", [4, D, 4 * D]), ("mlp_w2", [4, 4 * D, D]))

_CACHE = {}


def kernel(**inp):
    inp = {k: np.asarray(v) for k, v in inp.items()}
    par, pcols = pack_params(inp)
    npar = par.shape[1]
    if "nc" not in _CACHE:
        kb = KB(pcols, npar)
        _CACHE["nc"] = kb.build()
        _CACHE["names"] = [k if isinstance(k, str) else f"{k[0]}{k[1]}" for k in kb.I.keys()]
    nc = _CACHE["nc"]
    consts = make_consts()
    s5m = [pack_s5_mats(inp, j) for j in range(2)]
    in_maps = []
    for cid in range(8):
        sl = slice(cid * NSEQ, (cid + 1) * NSEQ)
        m = {}
        m["xin"] = np.ascontiguousarray(np.concatenate([inp["x_prompt"][cid], inp["x_sample"][sl].reshape(TS, D)], axis=0))
        m["mem"] = np.ascontiguousarray(inp["mem_prompt"][cid])
        m["ck"] = np.ascontiguousarray(inp["cache_mem_k"][:, sl].reshape(4, NSEQ, 256, D))
        m["cv"] = np.ascontiguousarray(inp["cache_mem_v"][:, sl].reshape(4, NSEQ, 256, D))
        m["s5re0"] = np.ascontiguousarray(inp["state_s5_re"][:, sl].reshape(2, NSEQ, 4096))
        m["s5im0"] = np.ascontiguousarray(inp["state_s5_im"][:, sl].reshape(2, NSEQ, 4096))
        m["rw0"] = np.ascontiguousarray(inp["state_rwkv"][:, sl])
        m["sh0"] = np.ascontiguousarray(inp["state_shift"][:, sl])
        m["par"] = par
        m["consts"] = consts
        for j in range(2):
            for k in ("bbr", "bbi", "cpr", "cpi"):
                m[f"{k}{j}"] = s5m[j][k]
        for nm, _ in WSHAPES:
            m[nm] = inp[nm]
        in_maps.append(m)
    declared = set(_CACHE["names"])
    in_maps = [{k: v for k, v in m.items() if k in declared} for m in in_maps]
    res = run_bass_kernel_spmd(nc, in_maps, core_ids=list(range(8)))
    R = res.results
    f32 = np.float32
    y_prompt = np.stack([R[c]["y"][:TP] for c in range(8)]).astype(f32)
    y_sample = np.concatenate([R[c]["y"][TP:].reshape(NSEQ, 8, D) for c in range(8)]).astype(f32)
    memk = np.stack([R[c]["memk"] for c in range(8)], axis=1).reshape(4, 8, 256, 4, 256).astype(f32)
    memv = np.stack([R[c]["memv"] for c in range(8)], axis=1).reshape(4, 8, 256, 4, 256).astype(f32)
    s5pre = np.stack([R[c]["s5pre"] for c in range(8)], axis=1).reshape(2, 8, 64, 64).astype(f32)
    s5pim = np.stack([R[c]["s5pim"] for c in range(8)], axis=1).reshape(2, 8, 64, 64).astype(f32)
    rwp = np.stack([R[c]["rwp"] for c in range(8)], axis=1).astype(f32)
    shp = np.stack([R[c]["shp"] for c in range(8)], axis=1).astype(f32)
    s5sre = np.concatenate([R[c]["s5sre"] for c in range(8)], axis=1).reshape(2, 128, 64, 64).astype(f32)
    s5sim = np.concatenate([R[c]["s5sim"] for c in range(8)], axis=1).reshape(2, 128, 64, 64).astype(f32)
    rws = np.concatenate([R[c]["rws"] for c in range(8)], axis=1).astype(f32)
    shs = np.concatenate([R[c]["shs"] for c in range(8)], axis=1).astype(f32)
    return (y_prompt, y_sample, memk, memv, s5pre, s5pim, rwp, shp, s5sre, s5sim, rws, shs)
```

```python
import math
import numpy as np
import concourse.bass as bass
import concourse.mybir as mybir
from concourse.bass_utils import run_bass_kernel_spmd

F32 = mybir.dt.float32
BF16 = mybir.dt.bfloat16
I32 = mybir.dt.int32
AF = mybir.ActivationFunctionType
ALU = mybir.AluOpType
AX = mybir.AxisListType

D = 1024
DEPTH = 4
TP = 2048
TS = 128
TT = TP + TS
NSEQ = 16
ALPHA = (2.0 * DEPTH) ** 0.25
LN_EPS = 1e-5
GN_EPS = 64e-5
TBLK = [(0, 512), (512, 512), (1024, 512), (1536, 512), (2048, 128)]
STUB_RWKV = False

ENGS = ("pe", "act", "dve", "pool", "sp")
SEM_ROLL = 20000
SAME_ENGINE_WAIT = True
N_DMA_SEMS = 24


class Prog:
    def __init__(self, nc):
        self.nc = nc
        self.ops = {e: [] for e in ENGS}
        self.sems = {e: [nc.alloc_semaphore(f"s_{e}_0")] for e in ENGS}
        self.own_sems = {e: {id(self.sems[e][0])} for e in ENGS}
        self.cnt = {e: 0 for e in ENGS}
        self.known = {e: {} for e in ENGS}
        self.last_w = {}
        self.readers = {}
        self.dma_sems = [nc.alloc_semaphore(f"s_dma_{i}") for i in range(N_DMA_SEMS)]
        self.dma_cnt = [0] * N_DMA_SEMS
        self.dma_rr = 0
        self.n_ins = 0

    def _tok(self, eng):
        if self.cnt[eng] >= SEM_ROLL:
            self.sems[eng].append(self.nc.alloc_semaphore(f"s_{eng}_{len(self.sems[eng])}"))
            self.own_sems[eng].add(id(self.sems[eng][-1]))
            self.cnt[eng] = 0
        self.cnt[eng] += 1
        return (self.sems[eng][-1], self.cnt[eng])

    def _deps(self, eng, reads, writes):
        toks = []
        for k in reads:
            t = self.last_w.get(k)
            if t is not None:
                toks.append(t)
        for k in writes:
            t = self.last_w.get(k)
            if t is not None:
                toks.append(t)
            toks.extend(self.readers.get(k, ()))
        waits = {}
        kn = self.known[eng]
        own = self.own_sems[eng]
        for (sem, val) in toks:
            key = id(sem)
            if not SAME_ENGINE_WAIT and key in own:
                continue
            if kn.get(key, (None, 0))[1] >= val:
                continue
            if key not in waits or waits[key][1] < val:
                waits[key] = (sem, val)
        for key, sv in waits.items():
            kn[key] = sv
        return list(waits.values())

    def _commit(self, tok, reads, writes):
        for k in reads:
            self.readers.setdefault(k, []).append(tok)
        for k in writes:
            self.last_w[k] = tok
            self.readers[k] = []

    @staticmethod
    def _excl(reads, writes):
        r2 = [k for k in reads if not (isinstance(k, tuple) and k[0] == "ps")]
        w2 = list(writes) + [k for k in reads if isinstance(k, tuple) and k[0] == "ps"]
        return r2, w2

    def op(self, eng, fn, reads=(), writes=()):
        reads, writes = self._excl(reads, writes)
        waits = self._deps(eng, reads, writes)
        tok = self._tok(eng)
        self.ops[eng].append((waits, fn, tok[0], 1))
        self._commit(tok, reads, writes)
        self.n_ins += 1
        return tok

    def dma(self, eng, fn, reads=(), writes=()):
        i = self.dma_rr
        self.dma_rr = (self.dma_rr + 1) % N_DMA_SEMS
        sem = self.dma_sems[i]
        waits = self._deps(eng, reads, writes)
        prev = self.dma_cnt[i]
        if prev > 0:
            key = id(sem)
            if self.known[eng].get(key, (None, 0))[1] < prev:
                waits.append((sem, prev))
                self.known[eng][key] = (sem, prev)
        self.dma_cnt[i] += 16
        tok = (sem, self.dma_cnt[i])
        self.ops[eng].append((waits, fn, sem, 16))
        self._commit(tok, reads, writes)
        self.n_ins += 1
        return tok

    def drain_dmas(self, eng="sp"):
        waits = []
        for i, sem in enumerate(self.dma_sems):
            if self.dma_cnt[i] > 0 and self.known[eng].get(id(sem), (None, 0))[1] < self.dma_cnt[i]:
                waits.append((sem, self.dma_cnt[i]))
                self.known[eng][id(sem)] = (sem, self.dma_cnt[i])
        self.ops[eng].append((waits, None, None, 0))

    def emit(self):
        nc = self.nc
        self.drain_dmas("sp")
        emap = {"pe": "tensor", "act": "scalar", "dve": "vector", "pool": "gpsimd", "sp": "sync"}
        with nc.Block() as block:
            for e in ENGS:
                ops = self.ops[e]

                def body(engine, ops=ops):
                    for (waits, fn, sem, amt) in ops:
                        for (s, v) in waits:
                            engine.wait_ge(s, v)
                        if fn is not None:
                            fn(engine).then_inc(sem, amt)

                getattr(block, emap[e])(body)
        self.ops = {e: [] for e in ENGS}
        self.last_w = {}
        self.readers = {}


def _fm(v):
    return np.ascontiguousarray(np.asarray(v).reshape(8, 128).T)


def pack_params(inp):
    cols = {}
    mats = []

    def add(name, arr):
        cols[name] = sum(m.shape[1] for m in mats)
        mats.append(np.asarray(arr, dtype=np.float32))

    for i in range(4):
        for k in range(3):
            add(("lng", i, k), _fm(inp["ln_g"][i, k]))
            add(("lnb", i, k), _fm(inp["ln_b"][i, k]))
    for j in range(2):
        add(("s5d", j), _fm(inp["s5_d"][j]))
        add(("mu", j), np.concatenate([_fm(inp["rwkv_mu"][j, m]) for m in range(6)], axis=1))
        for nm in ("rwkv_w0", "rwkv_a0", "rwkv_k_k", "rwkv_k_a", "rwkv_lnx_g", "rwkv_lnx_b"):
            add((nm, j), _fm(inp[nm][j]))
        add(("rwkv_r_k", j), _fm(inp["rwkv_r_k"][j].reshape(-1)))
    add(("rwkv_v0", 1), _fm(inp["rwkv_v0"][0]))
    for j in range(2):
        for nm in ("s5_a_re", "s5_a_im"):
            a = inp[nm][j].reshape(32, 2, 64)
            add((nm, j), np.ascontiguousarray(a.transpose(1, 2, 0).reshape(128, 32)))
        ld = np.repeat(inp["s5_log_dt"][j].reshape(32, 2, 1), 64, axis=2)
        add(("s5_log_dt", j), np.ascontiguousarray(ld.transpose(1, 2, 0).reshape(128, 32)))
    return np.ascontiguousarray(np.concatenate(mats, axis=1)), cols


def pack_s5_mats(inp, j):
    out = {}
    for nm, key in (("s5_b_re", "bbr"), ("s5_b_im", "bbi")):
        b = inp[nm][j]
        bb = np.zeros((8, 16, 8, 4, 2, 64), np.float32)
        for k in range(8):
            for q in range(4):
                for hf in range(2):
                    g8 = 2 * q + hf
                    bb[g8, :, k, q, hf, :] = b[8 * k + g8].T
        out[key] = bb.reshape(128, 32 * 128)
    for nm, key in (("s5_c_re", "cpr"), ("s5_c_im", "cpi")):
        c = inp[nm][j]
        cp = np.zeros((2, 64, 32, 128), np.float32)
        for tile in range(32):
            for hf in range(2):
                c0 = (tile % 4) * 32 + hf * 16
                cp[hf, :, tile, c0:c0 + 16] = c[2 * tile + hf].T
        out[key] = cp.reshape(128, 32 * 128)
    return out


def make_consts():
    c = np.zeros((128, 1024), np.float32)
    c[:, 0:128] = np.eye(128)
    c[0:64, 128:192] = 1.0
    c[64:128, 192:256] = 1.0
    s = np.arange(64)
    c[0:64, 256:320] = (s[:, None] < s[None, :])
    c[0:64, 320:384] = (s[:, None] <= s[None, :])
    c[0:64, 384:448] = (s[:, None] > s[None, :])
    c[:, 448:512] = 1.0
    c[:, 448] = 0.0
    c[:, 512:576] = 1.0
    c[:, 512:576:8] = 0.0
    c[:, 576:704] = 1.0 / 1024.0
    return c


class KB:
    def __init__(self, pcols, npar):
        nc = bass.Bass("TRN2", target_bir_lowering=False)
        self.nc = nc
        self.P = Prog(nc)
        self.pc = pcols
        self.npar = npar
        self.force_auto = False
        self.regions = []
        self.stream = None

    def _tblocks(self, col, ap, width):
        t_lo = col % width
        ext = 1
        for (st, cnt) in list(ap.ap)[1:]:
            if abs(st) < width:
                ext += (cnt - 1) * abs(st)
        t_hi = t_lo + ext
        return [t0 for (t0, n) in TBLK if t0 < t_hi and t0 + n > t_lo]

    def akeys(self, ap):
        if ap is None or not hasattr(ap, "tensor"):
            return []
        t = ap.tensor
        if type(t).__name__.startswith("DRam"):
            return []
        name = t.name
        if name.startswith("ps"):
            return [("ps", int(name[2:]))]
        dims = list(ap.ap)
        pstride = dims[0][0]
        col = ap.offset % pstride if pstride > 0 else ap.offset
        if name == "X":
            return [("X", tb) for tb in self._tblocks(col, ap, TT)]
        if name == "AR":
            es = 4 if ap.dtype == F32 or ap.dtype == I32 else 2
            ext = 1
            for (st, cnt) in dims[1:]:
                ext += (cnt - 1) * abs(st)
            b0, b1 = col * es, (col + ext) * es
            if b1 <= 17408 * 2:
                if es == 2:
                    return [("XB", tb) for tb in self._tblocks(col, ap, TT)]
                return [("XB", tb) for (tb, n) in TBLK]
            ks = [k for (r0, r1, k) in self.regions if r0 < b1 and r1 > b0]
            return ks if ks else [("AR", b0 // 2048)]
        return [name]

    def _k(self, reads, writes, ins, outs):
        if not self.force_auto:
            return reads, writes
        r, w = [], []
        for a in ins:
            r.extend(self.akeys(a))
        for a in outs:
            w.extend(self.akeys(a))
        return r, w

    def mm(self, out, lhsT, rhs, start, stop, reads=None, writes=None):
        reads, writes = self._k(reads, writes, [lhsT, rhs], [out])
        self._emit("op", "pe", lambda e: e.matmul(out, lhsT=lhsT, rhs=rhs, start=start, stop=stop), reads, writes)

    def tr(self, out, in_, ident, reads=None, writes=None):
        reads, writes = self._k(reads, writes, [in_, ident], [out])
        self._emit("op", "pe", lambda e: e.transpose(out, in_, ident), reads, writes)

    def tt(self, out, in0, in1, op, reads=None, writes=None, eng="dve"):
        reads, writes = self._k(reads, writes, [in0, in1], [out])
        self._emit("op", eng, lambda e: e.tensor_tensor(out=out, in0=in0, in1=in1, op=op), reads, writes)

    def ts(self, out, in0, s1, op0, reads=None, writes=None, s2=None, op1=None, eng="dve"):
        reads, writes = self._k(reads, writes, [in0, s1, s2], [out])
        if op1 is None:
            self._emit("op", eng, lambda e: e.tensor_scalar(out=out, in0=in0, scalar1=s1, scalar2=None, op0=op0), reads, writes)
        else:
            self._emit("op", eng, lambda e: e.tensor_scalar(out=out, in0=in0, scalar1=s1, scalar2=s2, op0=op0, op1=op1), reads, writes)

    def stt(self, out, in0, scalar, in1, op0, op1, reads=None, writes=None):
        reads, writes = self._k(reads, writes, [in0, scalar, in1], [out])
        self._emit("op", "dve", lambda e: e.scalar_tensor_tensor(out=out, in0=in0, scalar=scalar, in1=in1, op0=op0, op1=op1), reads, writes)

    def act(self, out, in_, func, reads=None, writes=None, bias=None, scale=1.0, accum_out=None):
        reads, writes = self._k(reads, writes, [in_, bias], [out, accum_out])
        kw = {}
        if bias is not None:
            kw["bias"] = bias
        if accum_out is not None:
            kw["accum_out"] = accum_out
        self._emit("op", "act", lambda e: e.activation(out=out, in_=in_, func=func, scale=scale, **kw), reads, writes)

    def cp(self, eng, out, in_, reads=None, writes=None):
        reads, writes = self._k(reads, writes, [in_], [out])
        if eng == "act":
            self._emit("op", "act", lambda e: e.copy(out=out, in_=in_), reads, writes)
        else:
            self._emit("op", eng, lambda e: e.tensor_copy(out=out, in_=in_), reads, writes)

    def red(self, out, in_, op, reads=None, writes=None):
        reads, writes = self._k(reads, writes, [in_], [out])
        self._emit("op", "dve", lambda e: e.tensor_reduce(out=out, in_=in_, op=op, axis=AX.X), reads, writes)

    def rcp(self, out, in_, reads=None, writes=None):
        reads, writes = self._k(reads, writes, [in_], [out])
        self._emit("op", "dve", lambda e: e.reciprocal(out=out, in_=in_), reads, writes)

    def mset(self, out, val, reads=None, writes=None, eng="dve"):
        reads, writes = self._k(reads, writes, [], [out])
        self._emit("op", eng, lambda e: e.memset(out, val), reads, writes)

    def scan(self, out, d0, d1, reads=None, writes=None):
        reads, writes = self._k(reads, writes, [d0, d1], [out])
        self._emit("op", "dve", lambda e: e.tensor_tensor_scan(out=out, data0=d0, data1=d1, initial=0.0, op0=ALU.mult, op1=ALU.add), reads, writes)

    def dma(self, eng, out, in_, reads=None, writes=None):
        reads, writes = self._k(reads, writes, [in_], [out])
        self._emit("dma", eng, lambda e: e.dma_start(out=out, in_=in_), reads, writes)

    def _emit(self, kind, eng, fn, reads, writes):
        rec = (kind, eng, fn, reads, writes)
        if self.stream is not None:
            self.stream.append(rec)
        else:
            self._flush(rec)

    def _flush(self, rec):
        kind, eng, fn, reads, writes = rec
        if kind == "op":
            self.P.op(eng, fn, reads, writes)
        else:
            self.P.dma(eng, fn, reads, writes)

    def setup(self):
        nc = self.nc

        def din(name, shape):
            return nc.dram_tensor(name, list(shape), F32, kind="ExternalInput").ap()

        def dout(name, shape):
            return nc.dram_tensor(name, list(shape), F32, kind="ExternalOutput").ap()

        shapes = {"xin": [TT, D], "mem": [256, D], "ck": [4, NSEQ, 256, D], "cv": [4, NSEQ, 256, D], "s5re0": [2, NSEQ, 4096],
                  "s5im0": [2, NSEQ, 4096], "rw0": [2, NSEQ, 16, 64, 64], "sh0": [2, NSEQ, D], "par": [128, self.npar], "consts": [128, 1024]}
        for j in range(2):
            for k in ("bbr", "bbi", "cpr", "cpi"):
                shapes[(k, j)] = [128, 4096]
        for nm, shp in WSHAPES:
            shapes[nm] = shp

        class Lazy(dict):
            def __missing__(d, key):
                name = key if isinstance(key, str) else f"{key[0]}{key[1]}"
                d[key] = din(name, shapes[key])
                return d[key]
        I = Lazy()
        O = {}
        O["y"] = dout("y", [TT, D])
        O["memk"] = dout("memk", [4, 256, D])
        O["memv"] = dout("memv", [4, 256, D])
        O["s5pre"] = dout("s5pre", [2, 4096])
        O["s5pim"] = dout("s5pim", [2, 4096])
        O["rwp"] = dout("rwp", [2, 16, 64, 64])
        O["shp"] = dout("shp", [2, D])
        O["s5sre"] = dout("s5sre", [2, NSEQ, 4096])
        O["s5sim"] = dout("s5sim", [2, NSEQ, 4096])
        O["rws"] = dout("rws", [2, NSEQ, 16, 64, 64])
        O["shs"] = dout("shs", [2, NSEQ, D])
        self.I, self.O = I, O
        self.VF = nc.dram_tensor("vfirst", [128, 8, TT], F32, kind="Internal").ap()

        self.X = nc.alloc_sbuf_tensor("X", [128, 8, TT], F32)
        self.AR = nc.alloc_sbuf_tensor("AR", [128, 51200], BF16)
        self.MEMT = nc.alloc_sbuf_tensor("MEMT", [128, 8, 256], BF16)
        self.PAR = nc.alloc_sbuf_tensor("PAR", [128, self.npar], F32)
        self.CON = nc.alloc_sbuf_tensor("CON", [128, 1024], F32)
        self.CONB = nc.alloc_sbuf_tensor("CONB", [128, 256], BF16)
        self.T = [nc.alloc_sbuf_tensor(f"T{i}", [128, 512], F32) for i in range(3)]
        self.SQ = nc.alloc_sbuf_tensor("SQ", [128, 8, 512], BF16)
        self.RS = nc.alloc_sbuf_tensor("RS", [128, 512], F32)
        self.SM = nc.alloc_sbuf_tensor("SM", [128, 256], F32)
        self.PS = [nc.alloc_psum_tensor(f"ps{i}", [128, 512], F32) for i in range(8)]
        AR = self.AR
        self.XB = AR[:, 0:17408].rearrange("p (c t) -> p c t", c=8)
        self.A1 = AR[:, 17408:34816].rearrange("p (c t) -> p c t", c=8)
        self.WR = [AR[:, 34816 + i * 8192: 34816 + (i + 1) * 8192].rearrange("p (c n) -> p c n", c=8) for i in range(2)]
        CON, CONB = self.CON, self.CONB
        self.IDF = CON[:, 0:128]
        self.BLK = CON[:, 128:256]
        self.MLT = CON[0:64, 256:320]
        self.MLE = CON[0:64, 320:384]
        self.MGT = CON[0:64, 384:448]
        self.RST64 = CON[:, 448:512]
        self.RST8 = CON[:, 512:576]
        self.IDB = CONB[:, 0:128]
        self.ONB = CONB[:, 128:256]
        self.wr_i = 0
        self.bank_i = 0
        self.EPS = self.SM[:, 0:1]
        self.EPS2 = self.SM[:, 1:2]

    def par(self, key, n=8):
        c0 = self.pc[key]
        return self.PAR[:, c0:c0 + n]

    def wload(self, dram2d, ncols=1024):
        i = self.wr_i
        self.wr_i ^= 1
        dst = self.WR[i][:, :, 0:ncols]
        src = dram2d.rearrange("(c p) n -> p c n", p=128)
        self.dma("pool", dst, src, [], [("WR", i)])
        return self.WR[i], ("WR", i)

    def next_bank(self):
        i = self.bank_i
        self.bank_i = (i + 1) % 4
        return self.PS[i], ("ps", i)

    def dense(self, w, wkey, src, srckey, evac, n_oc=8):
        for (t0, n) in TBLK:
            for oc in range(n_oc):
                ps, pk = self.next_bank()
                for c in range(8):
                    self.mm(ps[:, 0:n], w[:, c, oc * 128:(oc + 1) * 128], src[:, c, t0:t0 + n], c == 0, c == 7, [wkey, (srckey, t0)], [pk])
                evac(ps, pk, oc, t0, n)

    def layer_norm(self, i, k):
        X, XB, SQ, RS, PS, ONB = self.X, self.XB, self.SQ, self.RS, self.PS, self.ONB
        g = self.par(("lng", i, k))
        b = self.par(("lnb", i, k))
        for (t0, n) in TBLK:
            xb = X[:, :, t0:t0 + n]
            kx = ("X", t0)
            self.cp("act", SQ[:, :, 0:n], xb, [kx], ["SQ"])
            for c in range(8):
                self.mm(PS[4][:, 0:n], ONB, SQ[:, c, 0:n], c == 0, c == 7, ["SQ", "CONB"], [("ps", 4)])
            self.tt(xb, xb, PS[4][:, 0:n].unsqueeze(1).broadcast_to([128, 8, n]), ALU.subtract, [kx, ("ps", 4)], [kx])
            self.act(SQ[:, :, 0:n], xb, AF.Square, [kx], ["SQ"])
            for c in range(8):
                self.mm(PS[5][:, 0:n], ONB, SQ[:, c, 0:n], c == 0, c == 7, ["SQ", "CONB"], [("ps", 5)])
            self.act(RS[:, 0:n], PS[5][:, 0:n], AF.Sqrt, [("ps", 5), "eps"], ["RS"], bias=self.EPS)
            self.rcp(RS[:, 0:n], RS[:, 0:n], ["RS"], ["RS"])
            self.tt(xb, xb, RS[:, 0:n].unsqueeze(1).broadcast_to([128, 8, n]), ALU.mult, [kx, "RS"], [kx])
            for c in range(8):
                self.ts(X[:, c, t0:t0 + n], X[:, c, t0:t0 + n], g[:, c:c + 1], ALU.mult, [kx, "PAR"], [kx], s2=b[:, c:c + 1], op1=ALU.add)
            self.cp("act", XB[:, :, t0:t0 + n], xb, [kx], [("XB", t0)])

    def phase_input(self):
        I, X, XB, AR, PS, IDF = self.I, self.X, self.XB, self.AR, self.PS, self.IDF
        self.dma("sp", self.PAR[:], I["par"], [], ["PAR"])
        self.dma("sp", self.CON[:], I["consts"], [], ["CON"])
        self.cp("dve", self.CONB[:, 0:128], self.CON[:, 0:128], ["CON"], ["CONB"])
        self.cp("dve", self.CONB[:, 128:256], self.CON[:, 576:704], ["CON"], ["CONB"])
        self.mset(self.SM[:, 0:1], LN_EPS, [], ["eps"])
        self.mset(self.SM[:, 1:2], GN_EPS, [], ["eps"])
        STG = [AR[:, 17408 + i * 2048: 17408 + (i + 1) * 2048].bitcast(F32) for i in range(2)]
        import os
        for tt in [int(x) for x in os.environ.get('PI_TILES', ','.join(str(i) for i in range(19))).split(',')]:
            st = STG[tt % 2]
            sk = ("stg", tt % 2)
            if tt < 17:
                self.dma("sp", st, I["xin"][tt * 128:(tt + 1) * 128, :], [], [sk])
            else:
                self.dma("sp", st, I["mem"][(tt - 17) * 128:(tt - 16) * 128, :], [], [sk])
            for cg in range(2):
                ps, pk = self.next_bank()
                for cc in range(4):
                    c = cg * 4 + cc
                    self.tr(ps[:, cc * 128:(cc + 1) * 128], st[:, c * 128:(c + 1) * 128], IDF, [sk, "CON"], [pk])
                psv = ps[:].rearrange("p (a b) -> p a b", a=4)
                if tt < 17:
                    self.cp("dve", X[:, cg * 4:(cg + 1) * 4, tt * 128:(tt + 1) * 128], psv, [pk], [("Xi", tt, cg)])
                    self.cp("act", XB[:, cg * 4:(cg + 1) * 4, tt * 128:(tt + 1) * 128], psv, [pk], [("XBi", tt, cg)])
                else:
                    m0 = (tt - 17) * 128
                    self.cp("act", self.MEMT[:, cg * 4:(cg + 1) * 4, m0:m0 + 128], psv, [pk], ["MEMT"])
        self.P.emit()

    def phase_output(self):
        AR, PS, IDF, X, O = self.AR, self.PS, self.IDF, self.X, self.O
        STG = [AR[:, 17408 + i * 2048: 17408 + (i + 1) * 2048].bitcast(F32) for i in range(2)]
        for tt in range(17):
            st = STG[tt % 2]
            sk = ("stg", tt % 2)
            for cg in range(2):
                ps, pk = self.next_bank()
                for cc in range(4):
                    c = cg * 4 + cc
                    self.tr(ps[:, cc * 128:(cc + 1) * 128], X[:, c, tt * 128:(tt + 1) * 128], IDF, [], [pk])
                self.cp("dve" if cg == 0 else "act", st[:, cg * 512:(cg + 1) * 512], ps[:], [pk], [sk])
            self.dma("sp", O["y"][tt * 128:(tt + 1) * 128, :], st, [sk], [("yout", tt)])
        self.P.emit()

    def mlp_layer(self, i):
        X, XB, A1, T, I = self.X, self.XB, self.A1, self.T, self.I
        cnt = [0]
        for fg in range(4):
            w1, k1 = self.wload(I["mlp_w1"][i][:, fg * 1024:(fg + 1) * 1024])

            def evac1(ps, pk, oc, t0, n):
                tt_, tk = T[cnt[0] % 2], ("T", cnt[0] % 2)
                cnt[0] += 1
                self.act(tt_[:, 0:n], ps[:, 0:n], AF.Relu, [pk], [tk])
                self.tt(A1[:, oc, t0:t0 + n], tt_[:, 0:n], tt_[:, 0:n], ALU.mult, [tk], [("A1", t0)])
            self.dense(w1, k1, XB, "XB", evac1)
            w2, k2 = self.wload(I["mlp_w2"][i][fg * 1024:(fg + 1) * 1024, :])
            if fg == 0:
                def evac2(ps, pk, oc, t0, n):
                    self.stt(X[:, oc, t0:t0 + n], X[:, oc, t0:t0 + n], ALPHA, ps[:, 0:n], ALU.mult, ALU.add, [pk, ("X", t0)], [("X", t0)])
            else:
                def evac2(ps, pk, oc, t0, n):
                    self.tt(X[:, oc, t0:t0 + n], X[:, oc, t0:t0 + n], ps[:, 0:n], ALU.add, [pk, ("X", t0)], [("X", t0)])
            self.dense(w2, k2, A1, "A1", evac2)

    def xa_layer(self, i):
        X, XB, A1, T, I, O, PS, MEMT, AR, IDB, SM = self.X, self.XB, self.A1, self.T, self.I, self.O, self.PS, self.MEMT, self.AR, self.IDB, self.SM
        wq, kq = self.wload(I["xa_w_q"][i])

        def evq(ps, pk, oc, t0, n):
            self.act(A1[:, oc, t0:t0 + n], ps[:, 0:n], AF.Copy, [pk], [("A1", t0)], scale=0.0625)
        self.dense(wq, kq, XB, "XB", evq)
        self.P.emit()
        KT = AR[:, 0:2048].rearrange("p (c m) -> p c m", c=8)
        VB = AR[:, 2048:4096].rearrange("p (a n) -> p a n", a=2)
        PNB = AR[:, 4096:5120].rearrange("p (h m) -> p h m", h=4)
        PT = AR[:, 5120:6144].rearrange("p (a t) -> p a t", a=8)
        KS = [AR[:, 6144 + s * 2048: 8192 + s * 2048].rearrange("p (a n) -> p a n", a=2) for s in range(2)]
        VS = [AR[:, 10240 + s * 2048: 12288 + s * 2048].rearrange("p (a n) -> p a n", a=2) for s in range(2)]
        KTS = AR[:, 14336:16384].rearrange("p (c m) -> p c m", c=8)
        PTS = AR[:, 16384:16448].rearrange("p (a t) -> p a t", a=8)
        PEXP = [T[0], T[1]]
        psT = PS[4][:].bitcast(BF16)
        MX, NMX, SUM, RSM = SM[:, 8:12], SM[:, 12:16], SM[:, 16:20], SM[:, 20:24]
        SC = [PS[5], PS[6]]

        wk, kk = self.wload(I["xa_w_k"][i])
        for oc in range(8):
            ps, pk = self.next_bank()
            for c in range(8):
                self.mm(ps[:, 0:256], wk[:, c, oc * 128:(oc + 1) * 128], MEMT[:, c, :], c == 0, c == 7, [kk, "MEMT"], [pk])
            self.cp("act", KT[:, oc, :], ps[:, 0:256], [pk], ["KT"])

        def tokmajor(w, wkey, outap, vb):
            for mt in range(2):
                for nb in range(2):
                    ps, pk = self.next_bank()
                    for c in range(8):
                        self.mm(ps[:], MEMT[:, c, mt * 128:(mt + 1) * 128], w[:, c, nb * 512:(nb + 1) * 512], c == 0, c == 7, [wkey, "MEMT"], [pk])
                    self.cp("dve", T[2][:], ps[:], [pk], ["T2"])
                    if vb:
                        self.cp("act", VB[:, mt, nb * 512:(nb + 1) * 512], ps[:], [pk], ["VB"])
                    self.dma("sp", outap[mt * 128:(mt + 1) * 128, nb * 512:(nb + 1) * 512], T[2][:], ["T2"], [("mo", id(outap), mt, nb)])
        tokmajor(wk, kk, O["memk"][i], False)
        wv, kv = self.wload(I["xa_w_v"][i])
        tokmajor(wv, kv, O["memv"][i], True)

        def softmax(np_):
            for b in range(2):
                self.red(MX[0:np_, 2 * b:2 * b + 2], SC[b][0:np_, :].rearrange("p (h m) -> p h m", h=2), ALU.max, [("ps", 5 + b)], ["MX"])
            self.ts(NMX[0:np_, :], MX[0:np_, :], -1.0, ALU.mult, ["MX"], ["NMX"])
            for h in range(4):
                sl = slice((h % 2) * 256, (h % 2) * 256 + 256)
                self.act(PEXP[h // 2][0:np_, sl], SC[h // 2][0:np_, sl], AF.Exp, [("ps", 5 + h // 2), "NMX"], [("PEXP", h), ("SUM", h)],
                         bias=NMX[0:np_, h:h + 1], accum_out=SUM[0:np_, h:h + 1])
            self.rcp(RSM[0:np_, :], SUM[0:np_, :], [("SUM", h) for h in range(4)], ["RSM"])
            for b in range(2):
                self.tt(PNB[0:np_, 2 * b:2 * b + 2, :], PEXP[b][0:np_, :].rearrange("p (h m) -> p h m", h=2),
                        RSM[0:np_, 2 * b:2 * b + 2].unsqueeze(2).broadcast_to([np_, 2, 256]), ALU.mult,
                        [("PEXP", 2 * b), ("PEXP", 2 * b + 1), "RSM"], ["PNB"])

        for tt in range(16):
            t0 = tt * 128
            ak = ("A1", (t0 // 512) * 512)
            for h in range(4):
                for dc in range(2):
                    fc = 2 * h + dc
                    self.mm(SC[h // 2][:, (h % 2) * 256:(h % 2) * 256 + 256], A1[:, fc, t0:t0 + 128], KT[:, fc, :], dc == 0, dc == 1, [ak, "KT"], [("ps", 5 + h // 2)])
            softmax(128)
            for h in range(4):
                for mt in range(2):
                    a = h * 2 + mt
                    self.tr(psT[:, a * 128:(a + 1) * 128], PNB[:, h, mt * 128:(mt + 1) * 128], IDB, ["PNB", "CONB"], [("ps", 4)])
            self.cp("act", PT[:].rearrange("p a t -> p (a t)"), psT[:, 0:1024], [("ps", 4)], ["PT"])
            for fc in range(8):
                h, dc = fc // 2, fc % 2
                bank, bk = (PS[7], ("ps", 7)) if fc >= 4 else (PS[3], ("ps", 3))
                for mt in range(2):
                    self.mm(bank[:, (fc % 4) * 128:(fc % 4) * 128 + 128], VB[:, mt, h * 256 + dc * 128:h * 256 + dc * 128 + 128], PT[:, h * 2 + mt, :],
                            mt == 0, mt == 1, ["VB", "PT"], [bk])
            self.cp("dve", A1[:, 0:4, t0:t0 + 128], PS[3][:].rearrange("p (a t) -> p a t", a=4), [("ps", 3)], [ak])
            self.cp("act", A1[:, 4:8, t0:t0 + 128], PS[7][:].rearrange("p (a t) -> p a t", a=4), [("ps", 7)], [ak])
        for s in range(NSEQ):
            ks, vs = KS[s % 2], VS[s % 2]
            kkey, vkey = ("KS", s % 2), ("VS", s % 2)
            self.dma("pool", ks, I["ck"][i, s].rearrange("(a p) n -> p a n", p=128), [], [kkey])
            self.dma("pool", vs, I["cv"][i, s].rearrange("(a p) n -> p a n", p=128), [], [vkey])
            for mt in range(2):
                for fc in range(8):
                    self.tr(psT[:, fc * 128:(fc + 1) * 128], ks[:, mt, fc * 128:(fc + 1) * 128], IDB, [kkey, "CONB"], [("ps", 4)])
                self.cp("act", KTS[:, :, mt * 128:(mt + 1) * 128], psT[:, 0:1024].rearrange("p (c m) -> p c m", c=8), [("ps", 4)], ["KTS"])
            c0 = TP + 8 * s
            for h in range(4):
                for dc in range(2):
                    fc = 2 * h + dc
                    self.mm(SC[h // 2][0:8, (h % 2) * 256:(h % 2) * 256 + 256], A1[:, fc, c0:c0 + 8], KTS[:, fc, :], dc == 0, dc == 1, [("A1", 2048), "KTS"], [("ps", 5 + h // 2)])
            softmax(8)
            for h in range(4):
                for mt in range(2):
                    a = h * 2 + mt
                    self.tr(psT[:, a * 8:(a + 1) * 8], PNB[0:8, h, mt * 128:(mt + 1) * 128], IDB[0:8, 0:8], ["PNB", "CONB"], [("ps", 4)])
            self.cp("act", PTS[:].rearrange("p a t -> p (a t)"), psT[:, 0:64], [("ps", 4)], ["PTS"])
            for fc in range(8):
                h, dc = fc // 2, fc % 2
                for mt in range(2):
                    self.mm(PS[7][:, fc * 8:fc * 8 + 8], vs[:, mt, h * 256 + dc * 128:h * 256 + dc * 128 + 128], PTS[:, h * 2 + mt, :], mt == 0, mt == 1, [vkey, "PTS"], [("ps", 7)])
            self.cp("dve", A1[:, :, c0:c0 + 8], PS[7][:, 0:64].rearrange("p (a t) -> p a t", a=8), [("ps", 7)], [("A1", 2048)])
        wo, ko = self.wload(I["xa_w_o"][i])

        def evo(ps, pk, oc, t0, n):
            self.stt(X[:, oc, t0:t0 + n], X[:, oc, t0:t0 + n], ALPHA, ps[:, 0:n], ALU.mult, ALU.add, [pk, ("X", t0)], [("X", t0)])
        self.dense(wo, ko, A1, "A1", evo)

    def s5_layer(self, i, j):
        X, XB, T, I, O, PS, AR, SM, IDF, RS = self.X, self.XB, self.T, self.I, self.O, self.PS, self.AR, self.SM, self.IDF, self.RS
        B0 = 17408
        BBR = AR[:, B0:B0 + 4096].rearrange("p (t c) -> p t c", t=32)
        BBI = AR[:, B0 + 4096:B0 + 8192].rearrange("p (t c) -> p t c", t=32)
        CRB = AR[:, B0 + 8192:B0 + 12288].rearrange("p (t c) -> p t c", t=32)
        CIN = AR[:, B0 + 12288:B0 + 16384].rearrange("p (t c) -> p t c", t=32)
        BU0 = AR[:, B0 + 16384:B0 + 20480].bitcast(F32).rearrange("p (r t s) -> p r t s", r=2, t=32)
        BU = [BU0, BU0]
        TAB = AR[:, B0 + 20480:B0 + 24576].bitcast(F32).rearrange("p (r t s) -> p r t s", r=2, t=32)
        EC, ES = TAB[:, 0], TAB[:, 1]
        SQF = self.SQ[:].rearrange("p c t -> p (c t)").bitcast(F32)
        TQ4 = SQF[:, 0:1024].rearrange("p (r t s) -> p r t s", r=2, t=32)
        tq = SQF[:, 0:1024].rearrange("p (t s) -> p t s", t=32)
        RH = SQF[:, 1024:2048].rearrange("p (t s) -> p t s", t=32)
        HH = AR[:, B0 + 24576:B0 + 28672].bitcast(F32).rearrange("p (r t s) -> p r t s", r=2, t=32)
        HHB = AR[:, B0 + 28672:B0 + 30720].rearrange("p (r t s) -> p r t s", r=2, t=32)
        SP = AR[:, B0 + 30720:B0 + 33792].bitcast(F32)
        H0 = SP[:, 0:1024].rearrange("p (r t q) -> p r t q", r=2, t=32)

        def sm(k):
            return SP[:, 1024 + 32 * k:1024 + 32 * (k + 1)]
        pair = lambda k: SP[:, 1024 + 32 * k:1024 + 32 * (k + 2)].rearrange("p (r t) -> p r t", r=2)
        C1, C2, HL, M1, M2 = pair(0), pair(2), pair(4), pair(6), pair(8)
        abre, abim = sm(0), sm(1)
        FRE, FIM = sm(10), sm(11)
        dt, ang, mag, sn = sm(12), sm(13), sm(14), sm(15)
        cs, den, nre, r_, m_, kf = [SM[:, 64 + 32 * k:96 + 32 * k] for k in range(6)]
        TI = SM[:, 32:64].bitcast(I32)
        pc = self.pc
        LRE = self.PAR[:, pc[("s5_a_re", j)]:pc[("s5_a_re", j)] + 32]
        LIM = self.PAR[:, pc[("s5_a_im", j)]:pc[("s5_a_im", j)] + 32]
        LDT = self.PAR[:, pc[("s5_log_dt", j)]:pc[("s5_log_dt", j)] + 32]
        K = ["s5p", "PAR"]
        TWO_PI = 2.0 * math.pi
        tt = lambda o, a, b, op: self.tt(o, a, b, op, K, K)
        ts = lambda o, a, s1, op0, s2=None, op1=None: self.ts(o, a, s1, op0, K, K, s2=s2, op1=op1)
        ac = lambda o, a, f, scale=1.0: self.act(o, a, f, K, K, scale=scale)
        ac(dt, LDT, AF.Exp)
        tt(mag, LRE, dt, ALU.mult)
        ac(mag, mag, AF.Exp)
        tt(ang, LIM, dt, ALU.mult)
        ts(kf, ang, 1.0 / TWO_PI, ALU.mult)
        self.cp("dve", TI, kf, K, K)
        self.cp("dve", kf, TI, K, K)
        self.stt(r_, kf, -TWO_PI, ang, ALU.mult, ALU.add, K, K)

        def wrap(x):
            ts(m_, x, math.pi, ALU.is_gt)
            self.stt(x, m_, -TWO_PI, x, ALU.mult, ALU.add, K, K)
            ts(m_, x, -math.pi, ALU.is_lt)
            self.stt(x, m_, TWO_PI, x, ALU.mult, ALU.add, K, K)
        wrap(r_)
        ac(sn, r_, AF.Sin)
        ts(r_, r_, math.pi / 2, ALU.add)
        wrap(r_)
        ac(cs, r_, AF.Sin)
        tt(abre, mag, cs, ALU.mult)
        tt(abim, mag, sn, ALU.mult)
        ts(sm(2), abim, -1.0, ALU.mult)
        self.cp("dve", sm(3), abre, K, K)
        tt(den, LRE, LRE, ALU.mult)
        tt(m_, LIM, LIM, ALU.mult)
        tt(den, den, m_, ALU.add)
        self.rcp(den, den, K, K)
        ts(nre, abre, -1.0, ALU.add)
        tt(FRE, nre, LRE, ALU.mult)
        tt(m_, abim, LIM, ALU.mult)
        tt(FRE, FRE, m_, ALU.add)
        tt(FRE, FRE, den, ALU.mult)
        tt(FIM, abim, LRE, ALU.mult)
        tt(m_, nre, LIM, ALU.mult)
        tt(FIM, FIM, m_, ALU.subtract)
        tt(FIM, FIM, den, ALU.mult)
        KT_ = K + ["TAB", "tq", "RH"]
        self.mset(EC[:, :, 0:1], 1.0, KT_, ["TAB"])
        self.mset(ES[:, :, 0:1], 0.0, KT_, ["TAB"])
        self.cp("dve", EC[:, :, 1], cs, KT_, ["TAB"])
        self.cp("dve", ES[:, :, 1], sn, KT_, ["TAB"])
        for tau in range(2, 32):
            self.tt(EC[:, :, tau], EC[:, :, tau - 1], cs, ALU.mult, KT_, ["TAB"])
            self.tt(dt, ES[:, :, tau - 1], sn, ALU.mult, KT_, K)
            self.tt(EC[:, :, tau], EC[:, :, tau], dt, ALU.subtract, KT_, ["TAB"])
            self.tt(ES[:, :, tau], ES[:, :, tau - 1], cs, ALU.mult, KT_, ["TAB"])
            self.tt(dt, EC[:, :, tau - 1], sn, ALU.mult, KT_, K)
            self.tt(ES[:, :, tau], ES[:, :, tau], dt, ALU.add, KT_, ["TAB"])
        self.cp("dve", RH, mag.unsqueeze(2).broadcast_to([128, 32, 32]), KT_, ["RH"])
        self.mset(RH[:, :, 0:1], 0.0, KT_, ["RH"])
        self.dma("pool", BBR[:].rearrange("p t c -> p (t c)"), I[("bbr", j)], [], ["BB"])
        self.dma("pool", BBI[:].rearrange("p t c -> p (t c)"), I[("bbi", j)], [], ["BB"])
        v3 = lambda a: a[:].rearrange("p (t c) -> p t c", t=4)
        for g in range(8):
            sr, si, tm, r4 = T[0], T[1], T[2], RS
            self.dma("sp", sr[:], I[("cpr", j)][:, g * 512:(g + 1) * 512], [], ["T0"])
            self.dma("sp", si[:], I[("cpi", j)][:, g * 512:(g + 1) * 512], [], ["T1"])
            fre_b = FRE[:, g * 4:(g + 1) * 4].unsqueeze(2).broadcast_to([128, 4, 128])
            fim_b = FIM[:, g * 4:(g + 1) * 4].unsqueeze(2).broadcast_to([128, 4, 128])
            kk = ["T0", "T1", "T2", "RS", "CC"] + K
            self.tt(v3(tm), v3(si), fim_b, ALU.mult, kk, ["T2"])
            self.tt(v3(si), v3(si), fre_b, ALU.mult, kk, ["T1"])
            self.tt(v3(r4), v3(sr), fre_b, ALU.mult, kk, ["RS"])
            self.tt(CRB[:, g * 4:(g + 1) * 4, :], v3(r4), v3(tm), ALU.subtract, kk, ["CC"])
            self.tt(v3(sr), v3(sr), fim_b, ALU.mult, kk, ["T0"])
            self.tt(v3(sr), v3(sr), v3(si), ALU.add, kk, ["T0"])
            self.ts(CIN[:, g * 4:(g + 1) * 4, :], v3(sr), -1.0, ALU.mult, kk, ["CC"])
        for r, nm in enumerate(("s5re0", "s5im0")):
            for half in range(8):
                st, sk = T[half % 2], "T%d" % (half % 2)
                self.dma("sp", st[0:16, :], I[nm][j][:, half * 512:(half + 1) * 512], ["CC"], [sk])
                for q in range(4):
                    tile = half * 4 + q
                    self.tr(PS[0][:, tile * 16:(tile + 1) * 16], st[0:16, q * 128:(q + 1) * 128], IDF[0:16, 0:16], [sk, "CON"], [("ps", 0)])
            self.cp("dve", H0[:, r, :, :], PS[0][:].rearrange("p (t q) -> p t q", t=32), [("ps", 0)], ["H0"])
        fn2, gre, gim = dt, ang, mag
        tt(fn2, FRE, FRE, ALU.mult)
        tt(m_, FIM, FIM, ALU.mult)
        tt(fn2, fn2, m_, ALU.add)
        self.rcp(fn2, fn2, K, K)
        tt(gre, FRE, fn2, ALU.mult)
        tt(gim, FIM, fn2, ALU.mult)
        ts(gim, gim, -1.0, ALU.mult)
        HT = TQ4
        bq = lambda f: f.unsqueeze(2).broadcast_to([128, 32, 16])
        K2 = K + ["H0", "tq"]
        ta, tb = HT[:, 0, :, 0:16], HT[:, 1, :, 0:16]
        self.tt(ta, H0[:, 0], bq(gre), ALU.mult, K2, ["tq"])
        self.tt(tb, H0[:, 1], bq(gim), ALU.mult, K2, ["tq"])
        self.tt(ta, ta, tb, ALU.subtract, K2, ["tq"])
        self.tt(tb, H0[:, 0], bq(gim), ALU.mult, K2, ["tq"])
        self.tt(H0[:, 1], H0[:, 1], bq(gre), ALU.mult, K2, ["H0"])
        self.tt(H0[:, 1], H0[:, 1], tb, ALU.add, K2, ["H0"])
        self.cp("dve", H0[:, 0], ta, K2, ["H0"])

        DPAR = self.par(("s5d", j))
        def front(b):
            bu, bk = BU0, ("BU", 0)
            c0 = b * 32
            tb0 = (c0 // 512) * 512 if c0 < 2048 else 2048
            xk, xk2 = ("XB", tb0), ("X", tb0)
            for r, BB in enumerate((BBR, BBI)):
                for half in range(2):
                    ps, pk = self.next_bank()
                    for tl in range(16):
                        tile = half * 16 + tl
                        self.mm(ps[:, tl * 32:(tl + 1) * 32], BB[:, tile, :], XB[:, tile // 4, c0:c0 + 32], True, True, ["BB", xk], [pk])
                    self.cp("act", bu[:, r, half * 16:(half + 1) * 16, :], ps[:].rearrange("p (t s) -> p t s", t=16), [pk], [bk])
            hk = ["HH", "HL", "m", "H0"] + K
            if b < 64:
                kr = hk + [bk, "TAB", "tq", "RH"]
                br, bi = bu[:, 0], bu[:, 1]
                self.tt(HH[:, 0], EC, br, ALU.mult, kr, ["HH"])
                self.tt(tq, ES, bi, ALU.mult, kr, ["tq"])
                self.tt(HH[:, 0], HH[:, 0], tq, ALU.add, kr, ["HH"])
                self.tt(HH[:, 1], EC, bi, ALU.mult, kr, ["HH"])
                self.tt(tq, ES, br, ALU.mult, kr, ["tq"])
                self.tt(HH[:, 1], HH[:, 1], tq, ALU.subtract, kr, ["HH"])
                if b > 0:
                    self.tt(M1, C1, HL[:, 0:1, :].broadcast_to([128, 2, 32]), ALU.mult, kr, ["m"])
                    self.tt(M2, C2, HL[:, 1:2, :].broadcast_to([128, 2, 32]), ALU.mult, kr, ["m"])
                    self.tt(HH[:, :, :, 0], HH[:, :, :, 0], M1, ALU.add, kr, ["HH"])
                    self.tt(HH[:, :, :, 0], HH[:, :, :, 0], M2, ALU.add, kr, ["HH"])
                for r in range(2):
                    self.scan(bu[:, r].rearrange("p t s -> p (t s)"), RH.rearrange("p t s -> p (t s)"), HH[:, r].rearrange("p t s -> p (t s)"), kr, [bk])
                self.tt(HH[:, 0], EC, br, ALU.mult, kr, ["HH"])
                self.tt(tq, ES, bi, ALU.mult, kr, ["tq"])
                self.tt(HH[:, 0], HH[:, 0], tq, ALU.subtract, kr, ["HH"])
                self.tt(HH[:, 1], ES, br, ALU.mult, kr, ["HH"])
                self.tt(tq, EC, bi, ALU.mult, kr, ["tq"])
                self.tt(HH[:, 1], HH[:, 1], tq, ALU.add, kr, ["HH"])
                self.cp("dve", HL, HH[:, :, :, 31], ["HH"], ["HL"])
            else:
                q0 = (b - 64) * 4
                buv = bu[:].rearrange("p r t (q s) -> p r t q s", q=4)
                hhv = HH[:].rearrange("p r t (q s) -> p r t q s", q=4)
                ob = TQ4
                ok = "tq"
                m1, m2 = ob[:, :, :, 0:4], ob[:, :, :, 4:8]
                c1b = C1.unsqueeze(3).broadcast_to([128, 2, 32, 4])
                c2b = C2.unsqueeze(3).broadcast_to([128, 2, 32, 4])
                for s in range(8):
                    prev = hhv[:, :, :, :, s - 1] if s > 0 else H0[:, :, :, q0:q0 + 4]
                    self.tt(m1, c1b, prev[:, 0:1].broadcast_to([128, 2, 32, 4]), ALU.mult, hk + [ok], [ok])
                    self.tt(m2, c2b, prev[:, 1:2].broadcast_to([128, 2, 32, 4]), ALU.mult, hk + [ok], [ok])
                    self.tt(m1, m1, m2, ALU.add, hk + [ok], [ok])
                    self.tt(hhv[:, :, :, :, s], m1, buv[:, :, :, :, s], ALU.add, hk + [ok, bk], ["HH"])
            if b >= 63:
                nq = 1 if b == 63 else 4
                src = HH[:, :, :, 31:32] if b == 63 else HH[:].rearrange("p r t (q s) -> p r t q s", q=4)[:, :, :, :, 7]
                FT, fk = TQ4, "tq"
                fo, ft = FT[:, :, :, 8:8 + nq], FT[:, :, :, 12:12 + nq]
                fb = lambda f: f.unsqueeze(2).broadcast_to([128, 32, nq])
                rk = ["HH", fk] + K
                self.tt(fo[:, 0], src[:, 0], fb(FRE), ALU.mult, rk, [fk])
                self.tt(ft[:, 0], src[:, 1], fb(FIM), ALU.mult, rk, [fk])
                self.tt(fo[:, 0], fo[:, 0], ft[:, 0], ALU.subtract, rk, [fk])
                self.tt(fo[:, 1], src[:, 0], fb(FIM), ALU.mult, rk, [fk])
                self.tt(ft[:, 1], src[:, 1], fb(FRE), ALU.mult, rk, [fk])
                self.tt(fo[:, 1], fo[:, 1], ft[:, 1], ALU.add, rk, [fk])
                for r in range(2):
                    if b == 63:
                        outn = ("s5pre", "s5pim")[r]
                        self.tr(PS[7][0:32, 0:128], fo[:, r, :, 0], IDF, [fk, "CON"], [("ps", 7)])
                        self.cp("act", T[2][0:32, 0:128], PS[7][0:32, 0:128], [("ps", 7)], ["T2"])
                        self.dma("sp", O[outn][j].rearrange("(t p) -> t p", p=128), T[2][0:32, 0:128], ["T2"], [("so", outn)])
                    else:
                        outn = ("s5sre", "s5sim")[r]
                        q0 = (b - 64) * 4
                        for tg in range(8):
                            for tl in range(4):
                                self.tr(PS[7][0:4, tl * 128:(tl + 1) * 128], fo[:, r, tg * 4 + tl, :], IDF, [fk, "CON"], [("ps", 7)])
                            self.cp("act", T[2][0:4, :], PS[7][0:4, :], [("ps", 7)], ["T2"])
                            self.dma("sp", O[outn][j][q0:q0 + 4, tg * 512:(tg + 1) * 512], T[2][0:4, :], ["T2"], [("so", outn, tg, q0)])

        def back(b):
            c0 = b * 32
            tb0 = (c0 // 512) * 512 if c0 < 2048 else 2048
            xk, xk2 = ("XB", tb0), ("X", tb0)
            self.cp("act", HHB[:], HH[:], ["HH"], ["HHB"])
            for k in range(8):
                n = 0
                for q in range(4):
                    tile = k * 4 + q
                    for r, CC in enumerate((CRB, CIN)):
                        self.mm(PS[6][:, k * 32:(k + 1) * 32], CC[:, tile, :], HHB[:, r, tile, :], n == 0, n == 7, ["CC", "HHB"], [("ps", 6)])
                        n += 1
            yv = PS[6][:, 0:256].rearrange("p (k s) -> p k s", k=8)
            v = T[0][:, 0:256].rearrange("p (k s) -> p k s", k=8)
            t1 = T[1][:, 0:256].rearrange("p (k s) -> p k s", k=8)
            gk = ["T0", "T1", ("ps", 6)]
            self.tt(v, X[:, :, c0:c0 + 32], DPAR.unsqueeze(2).broadcast_to([128, 8, 32]), ALU.mult, gk + [xk2, "PAR"], ["T0"])
            self.tt(v, v, yv, ALU.add, gk, ["T0"])
            self.tt(t1, v, v, ALU.mult, gk, ["T1"])
            self.ts(t1, t1, 0.044715, ALU.mult, gk, ["T1"], s2=1.0, op1=ALU.add)
            self.tt(t1, t1, v, ALU.mult, gk, ["T1"])
            self.act(t1, t1, AF.Tanh, gk, ["T1"], scale=0.7978845608028654)
            self.ts(t1, t1, 1.0, ALU.add, gk, ["T1"], s2=0.5, op1=ALU.mult)
            self.tt(XB[:, :, c0:c0 + 32], t1, v, ALU.mult, gk + [xk], [xk])

        front(0)
        for b in range(64 + 4):
            self.stream = []
            back(b)
            lt = self.stream
            self.stream = []
            if b + 1 < 68:
                front(b + 1)
            lp = self.stream
            self.stream = None
            ia = ib = 0
            while ia < len(lt) or ib < len(lp):
                if ia < len(lt):
                    self._flush(lt[ia]); ia += 1
                if ib < len(lp):
                    self._flush(lp[ib]); ib += 1
        self.P.emit()
        wv, kv = self.wload(I["s5_w_glu_v"][j])
        wg, kg = self.wload(I["s5_w_glu_g"][j])
        for (t0, n) in TBLK:
            for oc in range(8):
                pv, pg = PS[(oc % 2) * 2], PS[1 + (oc % 2) * 2]
                kpv, kpg = ("ps", (oc % 2) * 2), ("ps", 1 + (oc % 2) * 2)
                for c in range(8):
                    self.mm(pv[:, 0:n], wv[:, c, oc * 128:(oc + 1) * 128], XB[:, c, t0:t0 + n], c == 0, c == 7, [kv, ("XB", t0)], [kpv])
                for c in range(8):
                    self.mm(pg[:, 0:n], wg[:, c, oc * 128:(oc + 1) * 128], XB[:, c, t0:t0 + n], c == 0, c == 7, [kg, ("XB", t0)], [kpg])
                tt_, tk = T[oc % 2], ("T", oc % 2)
                self.act(tt_[:, 0:n], pg[:, 0:n], AF.Sigmoid, [kpg], [tk])
                self.tt(tt_[:, 0:n], tt_[:, 0:n], pv[:, 0:n], ALU.mult, [kpv, tk], [tk])
                self.stt(X[:, oc, t0:t0 + n], X[:, oc, t0:t0 + n], ALPHA, tt_[:, 0:n], ALU.mult, ALU.add, [tk, ("X", t0)], [("X", t0)])

    def rwkv_layer(self, i, j):
        X, XB, T, I, O, PS, AR, SM, IDF, BLK = self.X, self.XB, self.T, self.I, self.O, self.PS, self.AR, self.SM, self.IDF, self.BLK
        if STUB_RWKV:
            for (t0, n) in TBLK:
                self.ts(X[:, :, t0:t0 + n], X[:, :, t0:t0 + n], ALPHA, ALU.mult, [("X", t0)], [("X", t0)])
            return
        off = [17408]

        self.force_auto = True
        self.regions = []

        def ab(n):
            a = AR[:, off[0]:off[0] + n]
            self.regions.append((off[0] * 2, (off[0] + n) * 2, ("rk", len(self.regions))))
            off[0] += n
            assert off[0] <= 51200, off[0]
            return a

        def af(n):
            return ab(2 * n).bitcast(F32)
        q3 = lambda a: a.rearrange("p (c t) -> p c t", c=2)
        WRq, WKq, WVq = [ab(2048).rearrange("p (c n) -> p c n", c=8) for _ in range(3)]
        WOq = ab(2048).rearrange("p (c n) -> p c n", c=2)
        W1, A1w = [ab(512).rearrange("p (c n) -> p c n", c=8) for _ in range(2)]
        G1w = ab(1024).rearrange("p (c n) -> p c n", c=8)
        V1w = ab(256).rearrange("p (c n) -> p c n", c=8)
        W2q, A2q, V2q, G2q = [ab(256) for _ in range(4)]
        XM = ab(3072).rearrange("p (m c t) -> p m c t", m=6, c=8)
        XX = af(512).rearrange("p (c t) -> p c t", c=8)
        TMX = af(512).rearrange("p (c t) -> p c t", c=8)
        R_, K_, A_, KAP, LW, LG, E_, T1t, Y_, YBs, VFt, T1e, SSe = [q3(af(128)) for _ in range(13)]
        V2, G2, BON2 = [[q3(af(128)) for _ in range(2)] for _ in range(3)]
        BT2, KT2, BH2, KH2, Vb2 = [[q3(ab(128)) for _ in range(2)] for _ in range(5)]
        ATRT2 = [ab(256).rearrange("p (w c t) -> p w c t", w=2, c=2) for _ in range(2)]
        Vt, BHt, KHt, Ut = [ab(256) for _ in range(4)]
        h4 = lambda a: a.rearrange("p (h t) -> p h t", h=4)
        Xa, XTa, Xb, XTb, Rs, Pb = [h4(ab(256)) for _ in range(6)]
        Pm = h4(af(256))
        STb = ab(256).rearrange("p (c n) -> p c n", c=2)
        MBR, MKA, MKR = [h4(ab(256)) for _ in range(3)]
        STp, STs, BDin = [af(256).rearrange("p (c n) -> p c n", c=2) for _ in range(3)]
        OB = q3(ab(128))
        T1b, A1b, V1b, G1b = ab(64), ab(64), ab(64), ab(64)
        SH0 = af(128).rearrange("p (c s) -> p c s", c=8)
        XL = af(136).rearrange("p (c s) -> p c s", c=8)
        OMKA = af(8)
        RSTa, RSTb = af(128), af(128)
        GLu2 = [af(16).rearrange("p (c u) -> p c u", c=2) for _ in range(2)]
        LGL = af(16).rearrange("p (c u) -> p c u", c=2)
        SSn = q3(af(128))
        pc = self.pc
        MU = self.par(("mu", j), 48).rearrange("p (m c) -> p m c", m=6)
        W0, A0, KKp, KAp, LXG, LXB, RKp = [self.par((nm, j)) for nm in ("rwkv_w0", "rwkv_a0", "rwkv_k_k", "rwkv_k_a", "rwkv_lnx_g", "rwkv_lnx_b", "rwkv_r_k")]
        V0p = self.par(("rwkv_v0", 1))
        B = [PS[k] for k in range(8)]
        bk = lambda k: ("ps", k)
        K0 = ["rk"]

        self.dma("pool", W1, I["rwkv_w1"][j].rearrange("(c p) n -> p c n", p=128), [], ["Wl"])
        self.dma("pool", A1w, I["rwkv_a1"][j].rearrange("(c p) n -> p c n", p=128), [], ["Wl"])
        self.dma("pool", G1w, I["rwkv_g1"][j].rearrange("(c p) n -> p c n", p=128), [], ["Wl"])
        if j == 1:
            self.dma("pool", V1w, I["rwkv_v1"][0].rearrange("(c p) n -> p c n", p=128), [], ["Wl"])
        self.ts(OMKA, KAp, -1.0, ALU.mult, ["PAR"], K0, s2=1.0, op1=ALU.add)
        for a, src in ((RSTa, self.RST64), (RSTb, self.RST8)):
            self.cp("dve", a[:, 0:64], src, ["CON"], K0)
            self.cp("dve", a[:, 64:128], src, ["CON"], K0)
        self.cp("dve", XL[:, :, 0:1], X[:, :, 2047:2048], [("X", 1536)], ["XL"])
        self.cp("dve", XL[:, :, 1:17], X[:, :, 2048:2176].rearrange("p c (u t) -> p c u t", u=16)[:, :, :, 7], [("X", 2048)], ["XL"])
        for half in range(2):
            for cc in range(4):
                c = half * 4 + cc
                self.tr(B[6][0:17, cc * 128:(cc + 1) * 128], XL[:, c, :], IDF, ["XL", "CON"], [bk(6)])
            self.cp("act", T[half][0:17, :], B[6][0:17, :], [bk(6)], [("T", half)])
            self.dma("sp", O["shp"][j:j + 1, half * 512:(half + 1) * 512], T[half][0:1, :], [("T", half)], [("sho", half)])
            self.dma("sp", O["shs"][j][:, half * 512:(half + 1) * 512], T[half][1:17, :], [("T", half)], [("shso", half)])
        for half in range(2):
            self.dma("sp", T[half][0:16, :], I["sh0"][j][:, half * 512:(half + 1) * 512], [], [("T", half)])
            for cc in range(4):
                c = half * 4 + cc
                self.tr(B[7][:, c * 16:(c + 1) * 16], T[half][0:16, cc * 128:(cc + 1) * 128], IDF[0:16, 0:16], [("T", half), "CON"], [bk(7)])
        self.cp("dve", SH0, B[7][:, 0:128].rearrange("p (c s) -> p c s", c=8), [bk(7)], ["SH0"])
        self.mset(BDin, 0.0, [], ["BDin"])
        for (t0_, n_) in TBLK:
            self.ts(X[:, :, t0_:t0_ + n_], X[:, :, t0_:t0_ + n_], ALPHA, ALU.mult)

        for hq in range(4):
            cs = slice(hq * 256, (hq + 1) * 256)
            for w, nm in ((WRq, "rwkv_w_r"), (WKq, "rwkv_w_k"), (WVq, "rwkv_w_v")):
                self.dma("pool", w, I[nm][j][:, cs].rearrange("(c p) n -> p c n", p=128), [], ["Wq"])
            self.dma("pool", WOq, I["rwkv_w_o"][j][cs, :].rearrange("(c p) n -> p c n", p=128), [], ["Wq"])
            self.dma("pool", W2q[0:64, :], I["rwkv_w2"][j][:, cs], [], ["Wq"])
            self.dma("pool", A2q[0:64, :], I["rwkv_a2"][j][:, cs], [], ["Wq"])
            if j == 1:
                self.dma("pool", V2q[0:32, :], I["rwkv_v2"][0][:, cs], [], ["Wq"])
            self.dma("pool", G2q, I["rwkv_g2"][j][:, cs], [], ["Wq"])
            self.mset(STp, 0.0, [], ["STp"])
            pcs = slice(2 * hq, 2 * hq + 2)
            bc = lambda p: p[:, pcs].unsqueeze(2).broadcast_to([128, 2, 64])
            def prep(blk):
                bs = blk % 2
                V_, G_, BON, BT, KT, BH, KH, ATRT, GLu = V2[bs], G2[bs], BON2[bs], BT2[bs], KT2[bs], BH2[bs], KH2[bs], ATRT2[bs], GLu2[bs]
                AT, RT = ATRT[:, 0], ATRT[:, 1]
                samp = blk >= 32
                g0 = 64 * blk
                tb0 = (g0 // 512) * 512 if g0 < 2048 else 2048
                xbb = XB[:, :, g0:g0 + 64]
                xk = ("XB", tb0)
                kb_ = ["blk"]
                if samp:
                    sb = blk - 32
                    xbv = xbb.rearrange("p c (u t) -> p c u t", u=8)
                    xxv = XX.rearrange("p c (u t) -> p c u t", u=8)
                    self.tt(xxv[:, :, :, 1:8], xbv[:, :, :, 0:7], xbv[:, :, :, 1:8], ALU.subtract, [xk], kb_)
                    self.tt(xxv[:, :, :, 0], SH0[:, :, 8 * sb:8 * sb + 8], xbv[:, :, :, 0], ALU.subtract, [xk, "SH0"], kb_)
                elif blk == 0:
                    self.tt(XX[:, :, 1:64], XB[:, :, 0:63], XB[:, :, 1:64], ALU.subtract, [xk], kb_)
                    self.ts(XX[:, :, 0:1], XB[:, :, 0:1], -1.0, ALU.mult, [xk], kb_)
                else:
                    pk_ = ("XB", ((g0 - 1) // 512) * 512)
                    self.tt(XX, XB[:, :, g0 - 1:g0 + 63], xbb, ALU.subtract, [xk, pk_], kb_)
                for m in range(6):
                    self.tt(TMX, XX, MU[:, m, :].unsqueeze(2).broadcast_to([128, 8, 64]), ALU.mult, kb_ + ["PAR"], kb_)
                    self.tt(XM[:, m], TMX, xbb, ALU.add, kb_ + [xk], kb_)
                for n_, (w, m) in enumerate(((WRq, 0), (WKq, 2), (WVq, 3))):
                    for c2 in range(2):
                        for c in range(8):
                            self.mm(B[0][:, n_ * 128 + c2 * 64:n_ * 128 + c2 * 64 + 64], w[:, c, c2 * 128:(c2 + 1) * 128], XM[:, m, c, :], c == 0, c == 7, kb_ + ["Wq"], [bk(0)])
                self.cp("act", R_, q3(B[0][:, 0:128]), [bk(0)], kb_)
                self.cp("dve", K_, q3(B[0][:, 128:256]), [bk(0)], kb_)
                self.cp("act", V_, q3(B[0][:, 256:384]), [bk(0)], kb_)
                for c in range(8):
                    self.mm(B[1][0:64, 0:64], W1[:, c, :], XM[:, 1, c, :], c == 0, c == 7, kb_ + ["Wl"], [bk(1)])
                for c in range(8):
                    self.mm(B[1][0:64, 64:128], A1w[:, c, :], XM[:, 4, c, :], c == 0, c == 7, kb_ + ["Wl"], [bk(1)])
                if j == 1:
                    for c in range(8):
                        self.mm(B[1][0:32, 128:192], V1w[:, c, :], XM[:, 3, c, :], c == 0, c == 7, kb_ + ["Wl"], [bk(1)])
                for c in range(8):
                    self.mm(B[1][:, 192:256], G1w[:, c, :], XM[:, 5, c, :], c == 0, c == 7, kb_ + ["Wl"], [bk(1)])
                self.act(T1b[0:64, :], B[1][0:64, 0:64], AF.Tanh, [bk(1)], kb_)
                self.cp("act", A1b[0:64, :], B[1][0:64, 64:128], [bk(1)], kb_)
                if j == 1:
                    self.cp("act", V1b[0:32, :], B[1][0:32, 128:192], [bk(1)], kb_)
                self.act(G1b, B[1][:, 192:256], AF.Sigmoid, [bk(1)], kb_)
                for c2 in range(2):
                    self.mm(B[2][:, c2 * 64:c2 * 64 + 64], W2q[0:64, c2 * 128:(c2 + 1) * 128], T1b[0:64, :], True, True, kb_ + ["Wq"], [bk(2)])
                    self.mm(B[2][:, 128 + c2 * 64:128 + c2 * 64 + 64], A2q[0:64, c2 * 128:(c2 + 1) * 128], A1b[0:64, :], True, True, kb_ + ["Wq"], [bk(2)])
                    if j == 1:
                        self.mm(B[2][:, 256 + c2 * 64:256 + c2 * 64 + 64], V2q[0:32, c2 * 128:(c2 + 1) * 128], V1b[0:32, :], True, True, kb_ + ["Wq"], [bk(2)])
                    self.mm(B[2][:, 384 + c2 * 64:384 + c2 * 64 + 64], G2q[:, c2 * 128:(c2 + 1) * 128], G1b, True, True, kb_ + ["Wq"], [bk(2)])
                kp = kb_ + ["PAR"]
                self.tt(LW, q3(B[2][:, 0:128]), bc(W0), ALU.add, kp + [bk(2)], kb_)
                self.act(LW, LW, AF.Sigmoid, kb_, kb_)
                self.ts(LW, LW, -0.6065306597126334, ALU.mult, kb_, kb_)
                self.tt(A_, q3(B[2][:, 128:256]), bc(A0), ALU.add, kp + [bk(2)], kb_)
                self.act(A_, A_, AF.Sigmoid, kb_, kb_)
                self.cp("dve", G_, q3(B[2][:, 384:512]), [bk(2)], kb_)
                vfd = self.VF[:, 2 * hq:2 * hq + 2, g0:g0 + 64]
                if j == 1:
                    self.tt(T1t, q3(B[2][:, 256:384]), bc(V0p), ALU.add, kp + [bk(2)], kb_)
                    self.act(T1t, T1t, AF.Sigmoid, kb_, kb_)
                    self.dma("sp", VFt, vfd, kb_, kb_)
                    self.tt(VFt, VFt, V_, ALU.subtract, kb_, kb_)
                    self.tt(VFt, VFt, T1t, ALU.mult, kb_, kb_)
                    self.tt(V_, V_, VFt, ALU.add, kb_, kb_)
                else:
                    self.dma("sp", vfd, V_, kb_, [("vf", hq, blk)])
                self.tt(KAP, K_, bc(KKp), ALU.mult, kp, kb_)
                self.tt(T1t, KAP, KAP, ALU.mult, kb_, kb_)
                for c2 in range(2):
                    self.mm(B[3][:, c2 * 64:c2 * 64 + 64], BLK, T1t[:, c2, :], True, True, kb_ + ["CON"], [bk(3)])
                self.act(SSn, q3(B[3][:, 0:128]), AF.Sqrt, [bk(3)], kb_)
                self.ts(SSn, SSn, 1e-12, ALU.max, kb_, kb_)
                self.rcp(SSn, SSn, kb_, kb_)
                self.tt(KAP, KAP, SSn, ALU.mult, kb_, kb_)
                self.tt(T1t, A_, bc(KAp), ALU.mult, kp, kb_)
                self.tt(T1t, T1t, OMKA[:, pcs].unsqueeze(2).broadcast_to([128, 2, 64]), ALU.add, kb_ + K0, kb_)
                self.tt(K_, K_, T1t, ALU.mult, kb_, kb_)
                self.tt(T1t, R_, K_, ALU.mult, kb_, kb_)
                self.tt(T1t, T1t, bc(RKp), ALU.mult, kp, kb_)
                for c2 in range(2):
                    self.mm(B[3][:, 128 + c2 * 64:128 + c2 * 64 + 64], BLK, T1t[:, c2, :], True, True, kb_ + ["CON"], [bk(3)])
                self.tt(BON, q3(B[3][:, 128:256]), V_, ALU.mult, kb_ + [bk(3)], kb_)
                self.scan(LG.rearrange("p c t -> p (c t)"), RSTb if samp else RSTa, LW.rearrange("p c t -> p (c t)"), kb_ + K0, kb_)
                nu = 8 if samp else 1
                L = 8 if samp else 64
                lgv = LG.rearrange("p c (u t) -> p c u t", u=nu)
                self.cp("dve", LGL[:, :, 0:nu], lgv[:, :, :, L - 1], kb_, kb_)
                self.act(GLu[:, :, 0:nu], LGL[:, :, 0:nu], AF.Exp, kb_, kb_)
                self.tt(A_, KAP, A_, ALU.mult, kb_, kb_)
                self.act(E_, LG, AF.Exp, kb_, kb_)
                self.tt(RT, R_, E_, ALU.mult, kb_, kb_)
                self.tt(T1t, LG, LW, ALU.subtract, kb_, kb_)
                self.act(E_, T1t, AF.Exp, kb_, kb_)
                self.stt(AT, KAP, -1.0, E_, ALU.mult, ALU.mult, kb_, kb_)
                self.act(E_, LG, AF.Exp, kb_, kb_, scale=-1.0)
                self.tt(BT, A_, E_, ALU.mult, kb_, kb_)
                self.tt(KT, K_, E_, ALU.mult, kb_, kb_)
                t1v = T1t.rearrange("p c (u t) -> p c u t", u=nu)
                self.tt(t1v, LGL[:, :, 0:nu].unsqueeze(3).broadcast_to([128, 2, nu, L]), lgv, ALU.subtract, kb_, kb_)
                self.act(E_, T1t, AF.Exp, kb_, kb_)
                self.tt(BH, A_, E_, ALU.mult, kb_, kb_)
                self.tt(KH, K_, E_, ALU.mult, kb_, kb_)
                self.cp("act", Vb2[bs], V_, kb_, kb_)

            def tail(blk):
                bs = blk % 2
                V_, G_, BON, BT, KT, BH, KH, ATRT, GLu = V2[bs], G2[bs], BON2[bs], BT2[bs], KT2[bs], BH2[bs], KH2[bs], ATRT2[bs], GLu2[bs]
                AT, RT = ATRT[:, 0], ATRT[:, 1]
                samp = blk >= 32
                g0 = 64 * blk
                tb0 = (g0 // 512) * 512 if g0 < 2048 else 2048
                kb_ = ["blk"]
                kp = kb_ + ["PAR"]
                nu = 8 if samp else 1
                L = 8 if samp else 64
                for u in range(nu):
                    c0 = u * L
                    if samp:
                        sq = (blk - 32) * 8 + u
                        ST = STs
                        for c2 in range(2):
                            for h2 in range(2):
                                hd = 4 * hq + 2 * c2 + h2
                                self.dma("sp", BDin[64 * h2:64 * h2 + 64, c2, 64 * h2:64 * h2 + 64], I["rw0"][j, sq, hd], ["BDin"] + kb_, ["BDin"])
                        for c2 in range(2):
                            self.tr(B[7][:, c2 * 128:(c2 + 1) * 128], BDin[:, c2, :], IDF, ["BDin", "CON"], [bk(7)])
                        self.cp("dve", STs, B[7][:, 0:256].rearrange("p (c n) -> p c n", c=2), [bk(7)], ["ST"])
                    else:
                        ST = STp
                    self.rwkv_unit(c0, L, ST, dict(V_=Vb2[bs], Pb=Pb, STb=STb, BH=BH, KH=KH, AT=AT, RT=RT, ATRT=ATRT, BT=BT, KT=KT, Vt=Vt, BHt=BHt, KHt=KHt, Ut=Ut, Xa=Xa, XTa=XTa, Xb=Xb, XTb=XTb,
                                                     Pm=Pm, Rs=Rs, MBR=MBR, MKA=MKA, MKR=MKR, YBs=YBs, Y_=Y_, GL=GLu[:, :, u]), kb_)
                    last = (blk == 31) or samp
                    if last:
                        for c2 in range(2):
                            self.tr(B[7][:, c2 * 128:(c2 + 1) * 128], ST[:, c2, :], IDF, ["ST", "CON"], [bk(7)])
                        self.cp("dve", BDin, B[7][:, 0:256].rearrange("p (c n) -> p c n", c=2), [bk(7)], ["BDin"])
                        for c2 in range(2):
                            for h2 in range(2):
                                hd = 4 * hq + 2 * c2 + h2
                                dst = O["rws"][j, sq, hd] if samp else O["rwp"][j, hd]
                                self.dma("sp", dst, BDin[64 * h2:64 * h2 + 64, c2, 64 * h2:64 * h2 + 64], ["BDin"], [("rwo", hq, blk, u, c2, h2)])
                for c2 in range(2):
                    self.mm(B[6][:, c2 * 64:c2 * 64 + 64], BLK, Y_[:, c2, :], True, True, kb_ + ["CON"], [bk(6)])
                self.stt(Y_, q3(B[6][:, 0:128]), -1.0 / 64.0, Y_, ALU.mult, ALU.add, kb_ + [bk(6)], kb_)
                self.tt(T1e, Y_, Y_, ALU.mult, kb_, kb_)
                for c2 in range(2):
                    self.mm(B[6][:, 128 + c2 * 64:128 + c2 * 64 + 64], BLK, T1e[:, c2, :], True, True, kb_ + ["CON"], [bk(6)])
                self.act(SSe, q3(B[6][:, 128:256]), AF.Sqrt, [bk(6), "eps"], kb_, bias=self.EPS2, scale=1.0 / 64.0)
                self.rcp(SSe, SSe, kb_, kb_)
                self.tt(Y_, Y_, SSe, ALU.mult, kb_, kb_)
                for c2 in range(2):
                    c = 2 * hq + c2
                    self.ts(Y_[:, c2, :], Y_[:, c2, :], LXG[:, c:c + 1], ALU.mult, kp, kb_, s2=LXB[:, c:c + 1], op1=ALU.add)
                self.tt(Y_, Y_, BON, ALU.add, kb_, kb_)
                self.tt(OB, Y_, G_, ALU.mult, kb_, kb_)
                for oc in range(8):
                    for c2 in range(2):
                        self.mm(B[7][:, oc * 64:(oc + 1) * 64], WOq[:, c2, oc * 128:(oc + 1) * 128], OB[:, c2, :], c2 == 0, c2 == 1, kb_ + ["Wq"], [bk(7)])
                xg = ("X", tb0)
                pso = B[7][:].rearrange("p (c t) -> p c t", c=8)
                self.tt(X[:, :, g0:g0 + 64], X[:, :, g0:g0 + 64], pso, ALU.add, [bk(7), xg], [xg])

            import os
            NB = int(os.environ.get('RW_BLKS', '34'))
            prep(0)
            for blk in range(NB):
                self.stream = []
                tail(blk)
                lt = self.stream
                self.stream = []
                if blk + 1 < NB:
                    prep(blk + 1)
                lp = self.stream
                self.stream = None
                ia = ib = 0
                while ia < len(lt) or ib < len(lp):
                    if ia < len(lt):
                        self._flush(lt[ia]); ia += 1
                    if ib < len(lp):
                        self._flush(lp[ib]); ib += 1
        self.force_auto = False

    def rwkv_unit(self, c0, L, ST, t, kb_):
        PS, IDF = self.PS, self.IDF
        B = PS
        bk = lambda k: ("ps", k)
        ku = kb_ + ["ST"]
        cs = slice(c0, c0 + L)
        nlev = {64: 5, 8: 2}[L]
        for n_, (src, dst, bank, col) in enumerate(((t["V_"], t["Vt"], 4, 0), (t["BH"], t["BHt"], 4, 256), (t["KH"], t["KHt"], 5, 0))):
            Bb = B[bank][:].bitcast(BF16)
            for c2 in range(2):
                self.tr(Bb[0:L, col + c2 * 128:col + (c2 + 1) * 128], src[:, c2, cs], self.IDB, kb_ + ["CON"], [bk(bank)])
            self.cp("act" if n_ % 2 == 0 else "dve", dst[0:L, :], Bb[0:L, col:col + 256], [bk(bank)], ku)
        STb, Pb = t["STb"], t["Pb"]
        self.cp("act", STb, ST, ku, ku)
        ATRT, AT, BT, KT = t["ATRT"], t["AT"], t["BT"], t["KT"]
        for h2 in range(2):
            b0 = 64 * h2
            bs, bn = B[4 + h2], B[6 + h2]
            for c2 in range(2):
                rhs = ATRT[b0:b0 + 64, :, c2, cs]
                self.mm(bs[0:L, c2 * 128:c2 * 128 + 2 * L], BT[b0:b0 + 64, c2, cs], rhs, True, True, ku, [bk(4 + h2)])
                self.mm(bs[0:L, 256 + c2 * 128:256 + c2 * 128 + 2 * L], KT[b0:b0 + 64, c2, cs], rhs, True, True, ku, [bk(4 + h2)])
                self.mm(bn[0:L, c2 * 64:c2 * 64 + L], AT[b0:b0 + 64, c2, cs], BT[b0:b0 + 64, c2, cs], True, True, ku, [bk(6 + h2)])
            v4 = bs[0:L, :].rearrange("p (k x) -> p k x", k=4)
            mlt = self.MLT[0:L, 0:L].unsqueeze(1).broadcast_to([L, 2, L])
            mle = self.MLE[0:L, 0:L].unsqueeze(1).broadcast_to([L, 2, L])
            mgt = self.MGT[0:L, 0:L].unsqueeze(1).broadcast_to([L, 2, L])
            hsel = slice(h2, 4, 2)
            self.tt(t["Xa"][0:L, hsel, 0:L], v4[:, 0:2, 0:L], mlt, ALU.mult, [bk(4 + h2), "CON"], ku)
            self.tt(t["MBR"][0:L, hsel, 0:L], v4[:, 0:2, L:2 * L], mle, ALU.mult, [bk(4 + h2), "CON"], ku)
            self.tt(t["MKA"][0:L, hsel, 0:L], v4[:, 2:4, 0:L], mlt, ALU.mult, [bk(4 + h2), "CON"], ku)
            self.tt(t["MKR"][0:L, hsel, 0:L], v4[:, 2:4, L:2 * L], mle, ALU.mult, [bk(4 + h2), "CON"], ku)
            self.tt(t["XTa"][0:L, hsel, 0:L], bn[0:L, 0:128].rearrange("p (k x) -> p k x", k=2)[:, :, 0:L], mgt, ALU.mult, [bk(6 + h2), "CON"], ku)
        Xc, XTc, Xn, XTn, Pm = t["Xa"], t["XTa"], t["Xb"], t["XTb"], t["Pm"]
        self.tt(Pm[0:L, :, 0:L], Xc[0:L, :, 0:L], IDF[0:L, 0:L].unsqueeze(1).broadcast_to([L, 4, L]), ALU.add, ku + ["CON"], ku)
        self.cp("act", Pb[0:L, :, 0:L], Pm[0:L, :, 0:L], ku, ku)
        for lev in range(nlev):
            lastlev = lev == nlev - 1
            for hl in range(4):
                if not lastlev:
                    self.mm(B[4][0:L, hl * 64:hl * 64 + L], XTc[0:L, hl, 0:L], Xc[0:L, hl, 0:L], True, True, ku, [bk(4)])
                self.mm(B[5][0:L, hl * 64:hl * 64 + L], Xc[0:L, hl, 0:L], XTc[0:L, hl, 0:L], True, True, ku, [bk(5)])
            if not lastlev:
                self.cp("act", Xn[0:L, :, 0:L], B[4][0:L, 0:256].rearrange("p (h x) -> p h x", h=4)[:, :, 0:L], [bk(4)], ku)
            self.cp("dve", XTn[0:L, :, 0:L], B[5][0:L, 0:256].rearrange("p (h x) -> p h x", h=4)[:, :, 0:L], [bk(5)], ku)
            for hl in range(4):
                self.mm(B[6][0:L, hl * 64:hl * 64 + L], XTn[0:L, hl, 0:L], Pb[0:L, hl, 0:L], True, True, ku, [bk(6)])
            self.tt(Pm[0:L, :, 0:L], Pm[0:L, :, 0:L], B[6][0:L, 0:256].rearrange("p (h x) -> p h x", h=4)[:, :, 0:L], ALU.add, ku + [bk(6)], ku)
            self.cp("act", Pb[0:L, :, 0:L], Pm[0:L, :, 0:L], ku, ku)
            Xc, XTc, Xn, XTn = Xn, XTn, Xc, XTc
        Vt, BHt, KHt, Ut, Rs, MKA, MBR, MKR = t["Vt"], t["BHt"], t["KHt"], t["Ut"], t["Rs"], t["MKA"], t["MBR"], t["MKR"]
        for c2 in range(2):
            self.mm(B[7][0:L, c2 * 128:(c2 + 1) * 128], AT[:, c2, cs], STb[:, c2, :], True, False, ku, [bk(7)])
            for h2 in range(2):
                hl = 2 * c2 + h2
                self.mm(B[7][0:L, hl * 64:(hl + 1) * 64], MKA[0:L, hl, 0:L], Vt[0:L, hl * 64:(hl + 1) * 64], False, h2 == 1, ku, [bk(7)])
        self.cp("act", Rs[0:L, :, :], B[7][0:L, 0:256].rearrange("p (h x) -> p h x", h=4), [bk(7)], ku)
        for hl in range(4):
            self.mm(B[4][0:L, hl * 64:(hl + 1) * 64], Pb[0:L, hl, 0:L], Rs[0:L, hl, :], True, True, ku, [bk(4)])
        self.cp("dve", Ut[0:L, :], B[4][0:L, 0:256], [bk(4)], ku)
        for c2 in range(2):
            self.mm(B[5][:, c2 * 64:c2 * 64 + L], STb[:, c2, :], t["RT"][:, c2, cs], True, True, ku, [bk(5)])
        for hl in range(4):
            self.mm(B[6][0:64, hl * 64:hl * 64 + L], Ut[0:L, hl * 64:(hl + 1) * 64], MBR[0:L, hl, 0:L], True, False, ku, [bk(6)])
            self.mm(B[6][0:64, hl * 64:hl * 64 + L], Vt[0:L, hl * 64:(hl + 1) * 64], MKR[0:L, hl, 0:L], False, True, ku, [bk(6)])
        ybv = B[6][0:64, 0:256].rearrange("p (c h x) -> p c h x", c=2, h=2)
        YBs = t["YBs"]
        for h2 in range(2):
            self.cp("act", YBs[64 * h2:64 * h2 + 64, :, 0:L], ybv[:, :, h2, 0:L], [bk(6)], ku)
        self.tt(t["Y_"][:, :, cs], B[5][:, 0:128].rearrange("p (c x) -> p c x", c=2)[:, :, 0:L], YBs[:, :, 0:L], ALU.add, ku + [bk(5)], ku)
        for hl in range(4):
            c2 = hl // 2
            self.mm(B[7][:, hl * 64:(hl + 1) * 64], BHt[0:L, c2 * 128:(c2 + 1) * 128], Ut[0:L, hl * 64:(hl + 1) * 64], True, False, ku, [bk(7)])
            self.mm(B[7][:, hl * 64:(hl + 1) * 64], KHt[0:L, c2 * 128:(c2 + 1) * 128], Vt[0:L, hl * 64:(hl + 1) * 64], False, True, ku, [bk(7)])
        suv = B[7][:, 0:256].rearrange("p (c h x) -> p c h x", c=2, h=2)
        GL = t["GL"]
        for h2 in range(2):
            r = slice(64 * h2, 64 * h2 + 64)
            blkv = ST[r, :, 64 * h2:64 * h2 + 64]
            self.tt(blkv, blkv, GL[r, :].unsqueeze(2).broadcast_to([64, 2, 64]), ALU.mult, ku, ku)
            self.tt(blkv, blkv, suv[r, :, h2, :], ALU.add, ku + [bk(7)], ku)

    def build(self):
        self.setup()
        self.phase_input()
        for i in range(DEPTH):
            j = i // 2
            if i % 2 == 0:
                self.s5_layer(i, j)
            else:
                self.rwkv_layer(i, j)
            self.layer_norm(i, 0)
            self.P.emit()
            self.xa_layer(i)
            self.layer_norm(i, 1)
            self.P.emit()
            self.mlp_layer(i)
            self.layer_norm(i, 2)
            self.P.emit()
        self.phase_output()
        return self.nc


WSHAPES = (("s5_w_glu_v", [2, D, D]), ("s5_w_glu_g", [2, D, D]), ("rwkv_w_r", [2, D, D]), ("rwkv_w_k", [2, D, D]),
           ("rwkv_w_v", [2, D, D]), ("rwkv_w_o", [2, D, D]), ("rwkv_w1", [2, D, 64]), ("rwkv_w2", [2, 64, D]),
           ("rwkv_a1", [2, D, 64]), ("rwkv_a2", [2, 64, D]), ("rwkv_v1", [1, D, 32]), ("rwkv_v2", [1, 32, D]),
           ("rwkv_g1", [2, D, 128]), ("rwkv_g2", [2, 128, D]), ("xa_w_q", [4, D, D]), ("xa_w_k", [4, D, D]),
           ("xa_w_v", [4, D, D]), ("xa_w_o", [4, D, D]), ("mlp_w1", [4, D, 4 * D]), ("mlp_w2", [4, 4 * D, D]))

_CACHE = {}


def kernel(**inp):
    inp = {k: np.asarray(v) for k, v in inp.items()}
    par, pcols = pack_params(inp)
    npar = par.shape[1]
    if "nc" not in _CACHE:
        kb = KB(pcols, npar)
        _CACHE["nc"] = kb.build()
        _CACHE["names"] = [k if isinstance(k, str) else f"{k[0]}{k[1]}" for k in kb.I.keys()]
    nc = _CACHE["nc"]
    consts = make_consts()
    s5m = [pack_s5_mats(inp, j) for j in range(2)]
    in_maps = []
    for cid in range(8):
        sl = slice(cid * NSEQ, (cid + 1) * NSEQ)
        m = {}
        m["xin"] = np.ascontiguousarray(np.concatenate([inp["x_prompt"][cid], inp["x_sample"][sl].reshape(TS, D)], axis=0))
        m["mem"] = np.ascontiguousarray(inp["mem_prompt"][cid])
        m["ck"] = np.ascontiguousarray(inp["cache_mem_k"][:, sl].reshape(4, NSEQ, 256, D))
        m["cv"] = np.ascontiguousarray(inp["cache_mem_v"][:, sl].reshape(4, NSEQ, 256, D))
        m["s5re0"] = np.ascontiguousarray(inp["state_s5_re"][:, sl].reshape(2, NSEQ, 4096))
        m["s5im0"] = np.ascontiguousarray(inp["state_s5_im"][:, sl].reshape(2, NSEQ, 4096))
        m["rw0"] = np.ascontiguousarray(inp["state_rwkv"][:, sl])
        m["sh0"] = np.ascontiguousarray(inp["state_shift"][:, sl])
        m["par"] = par
        m["consts"] = consts
        for j in range(2):
            for k in ("bbr", "bbi", "cpr", "cpi"):
                m[f"{k}{j}"] = s5m[j][k]
        for nm, _ in WSHAPES:
            m[nm] = inp[nm]
        in_maps.append(m)
    declared = set(_CACHE["names"])
    in_maps = [{k: v for k, v in m.items() if k in declared} for m in in_maps]
    res = run_bass_kernel_spmd(nc, in_maps, core_ids=list(range(8)))
    R = res.results
    f32 = np.float32
    y_prompt = np.stack([R[c]["y"][:TP] for c in range(8)]).astype(f32)
    y_sample = np.concatenate([R[c]["y"][TP:].reshape(NSEQ, 8, D) for c in range(8)]).astype(f32)
    memk = np.stack([R[c]["memk"] for c in range(8)], axis=1).reshape(4, 8, 256, 4, 256).astype(f32)
    memv = np.stack([R[c]["memv"] for c in range(8)], axis=1).reshape(4, 8, 256, 4, 256).astype(f32)
    s5pre = np.stack([R[c]["s5pre"] for c in range(8)], axis=1).reshape(2, 8, 64, 64).astype(f32)
    s5pim = np.stack([R[c]["s5pim"] for c in range(8)], axis=1).reshape(2, 8, 64, 64).astype(f32)
    rwp = np.stack([R[c]["rwp"] for c in range(8)], axis=1).astype(f32)
    shp = np.stack([R[c]["shp"] for c in range(8)], axis=1).astype(f32)
    s5sre = np.concatenate([R[c]["s5sre"] for c in range(8)], axis=1).reshape(2, 128, 64, 64).astype(f32)
    s5sim = np.concatenate([R[c]["s5sim"] for c in range(8)], axis=1).reshape(2, 128, 64, 64).astype(f32)
    rws = np.concatenate([R[c]["rws"] for c in range(8)], axis=1).astype(f32)
    shs = np.concatenate([R[c]["shs"] for c in range(8)], axis=1).astype(f32)
    return (y_prompt, y_sample, memk, memv, s5pre, s5pim, rwp, shp, s5sre, s5sim, rws, shs)
```

```python
import math
import numpy as np
import concourse.bass as bass
import concourse.mybir as mybir
from concourse.bass_utils import run_bass_kernel_spmd

F32 = mybir.dt.float32
BF16 = mybir.dt.bfloat16
I32 = mybir.dt.int32
AF = mybir.ActivationFunctionType
ALU = mybir.AluOpType
AX = mybir.AxisListType

D = 1024
DEPTH = 4
TP = 2048
TS = 128
TT = TP + TS
NSEQ = 16
ALPHA = (2.0 * DEPTH) ** 0.25
LN_EPS = 1e-5
GN_EPS = 64e-5
TBLK = [(0, 512), (512, 512), (1024, 512), (1536, 512), (2048, 128)]
STUB_RWKV = False

ENGS = ("pe", "act", "dve", "pool", "sp")
SEM_ROLL = 20000
SAME_ENGINE_WAIT = True
N_DMA_SEMS = 24


class Prog:
    def __init__(self, nc):
        self.nc = nc
        self.ops = {e: [] for e in ENGS}
        self.sems = {e: [nc.alloc_semaphore(f"s_{e}_0")] for e in ENGS}
        self.own_sems = {e: {id(self.sems[e][0])} for e in ENGS}
        self.cnt = {e: 0 for e in ENGS}
        self.known = {e: {} for e in ENGS}
        self.last_w = {}
        self.readers = {}
        self.dma_sems = [nc.alloc_semaphore(f"s_dma_{i}") for i in range(N_DMA_SEMS)]
        self.dma_cnt = [0] * N_DMA_SEMS
        self.dma_rr = 0
        self.n_ins = 0

    def _tok(self, eng):
        if self.cnt[eng] >= SEM_ROLL:
            self.sems[eng].append(self.nc.alloc_semaphore(f"s_{eng}_{len(self.sems[eng])}"))
            self.own_sems[eng].add(id(self.sems[eng][-1]))
            self.cnt[eng] = 0
        self.cnt[eng] += 1
        return (self.sems[eng][-1], self.cnt[eng])

    def _deps(self, eng, reads, writes):
        toks = []
        for k in reads:
            t = self.last_w.get(k)
            if t is not None:
                toks.append(t)
        for k in writes:
            t = self.last_w.get(k)
            if t is not None:
                toks.append(t)
            toks.extend(self.readers.get(k, ()))
        waits = {}
        kn = self.known[eng]
        own = self.own_sems[eng]
        for (sem, val) in toks:
            key = id(sem)
            if not SAME_ENGINE_WAIT and key in own:
                continue
            if kn.get(key, (None, 0))[1] >= val:
                continue
            if key not in waits or waits[key][1] < val:
                waits[key] = (sem, val)
        for key, sv in waits.items():
            kn[key] = sv
        return list(waits.values())

    def _commit(self, tok, reads, writes):
        for k in reads:
            self.readers.setdefault(k, []).append(tok)
        for k in writes:
            self.last_w[k] = tok
            self.readers[k] = []

    @staticmethod
    def _excl(reads, writes):
        r2 = [k for k in reads if not (isinstance(k, tuple) and k[0] == "ps")]
        w2 = list(writes) + [k for k in reads if isinstance(k, tuple) and k[0] == "ps"]
        return r2, w2

    def op(self, eng, fn, reads=(), writes=()):
        reads, writes = self._excl(reads, writes)
        waits = self._deps(eng, reads, writes)
        tok = self._tok(eng)
        self.ops[eng].append((waits, fn, tok[0], 1))
        self._commit(tok, reads, writes)
        self.n_ins += 1
        return tok

    def dma(self, eng, fn, reads=(), writes=()):
        i = self.dma_rr
        self.dma_rr = (self.dma_rr + 1) % N_DMA_SEMS
        sem = self.dma_sems[i]
        waits = self._deps(eng, reads, writes)
        prev = self.dma_cnt[i]
        if prev > 0:
            key = id(sem)
            if self.known[eng].get(key, (None, 0))[1] < prev:
                waits.append((sem, prev))
                self.known[eng][key] = (sem, prev)
        self.dma_cnt[i] += 16
        tok = (sem, self.dma_cnt[i])
        self.ops[eng].append((waits, fn, sem, 16))
        self._commit(tok, reads, writes)
        self.n_ins += 1
        return tok

    def drain_dmas(self, eng="sp"):
        waits = []
        for i, sem in enumerate(self.dma_sems):
            if self.dma_cnt[i] > 0 and self.known[eng].get(id(sem), (None, 0))[1] < self.dma_cnt[i]:
                waits.append((sem, self.dma_cnt[i]))
                self.known[eng][id(sem)] = (sem, self.dma_cnt[i])
        self.ops[eng].append((waits, None, None, 0))

    def emit(self):
        nc = self.nc
        self.drain_dmas("sp")
        emap = {"pe": "tensor", "act": "scalar", "dve": "vector", "pool": "gpsimd", "sp": "sync"}
        with nc.Block() as block:
            for e in ENGS:
                ops = self.ops[e]

                def body(engine, ops=ops):
                    for (waits, fn, sem, amt) in ops:
                        for (s, v) in waits:
                            engine.wait_ge(s, v)
                        if fn is not None:
                            fn(engine).then_inc(sem, amt)

                getattr(block, emap[e])(body)
        self.ops = {e: [] for e in ENGS}
        self.last_w = {}
        self.readers = {}


def _fm(v):
    return np.ascontiguousarray(np.asarray(v).reshape(8, 128).T)


def pack_params(inp):
    cols = {}
    mats = []

    def add(name, arr):
        cols[name] = sum(m.shape[1] for m in mats)
        mats.append(np.asarray(arr, dtype=np.float32))

    for i in range(4):
        for k in range(3):
            add(("lng", i, k), _fm(inp["ln_g"][i, k]))
            add(("lnb", i, k), _fm(inp["ln_b"][i, k]))
    for j in range(2):
        add(("s5d", j), _fm(inp["s5_d"][j]))
        add(("mu", j), np.concatenate([_fm(inp["rwkv_mu"][j, m]) for m in range(6)], axis=1))
        for nm in ("rwkv_w0", "rwkv_a0", "rwkv_k_k", "rwkv_k_a", "rwkv_lnx_g", "rwkv_lnx_b"):
            add((nm, j), _fm(inp[nm][j]))
        add(("rwkv_r_k", j), _fm(inp["rwkv_r_k"][j].reshape(-1)))
    add(("rwkv_v0", 1), _fm(inp["rwkv_v0"][0]))
    for j in range(2):
        for nm in ("s5_a_re", "s5_a_im"):
            a = inp[nm][j].reshape(32, 2, 64)
            add((nm, j), np.ascontiguousarray(a.transpose(1, 2, 0).reshape(128, 32)))
        ld = np.repeat(inp["s5_log_dt"][j].reshape(32, 2, 1), 64, axis=2)
        add(("s5_log_dt", j), np.ascontiguousarray(ld.transpose(1, 2, 0).reshape(128, 32)))
    return np.ascontiguousarray(np.concatenate(mats, axis=1)), cols


def pack_s5_mats(inp, j):
    out = {}
    for nm, key in (("s5_b_re", "bbr"), ("s5_b_im", "bbi")):
        b = inp[nm][j]
        bb = np.zeros((8, 16, 8, 4, 2, 64), np.float32)
        for k in range(8):
            for q in range(4):
                for hf in range(2):
                    g8 = 2 * q + hf
                    bb[g8, :, k, q, hf, :] = b[8 * k + g8].T
        out[key] = bb.reshape(128, 32 * 128)
    for nm, key in (("s5_c_re", "cpr"), ("s5_c_im", "cpi")):
        c = inp[nm][j]
        cp = np.zeros((2, 64, 32, 128), np.float32)
        for tile in range(32):
            for hf in range(2):
                c0 = (tile % 4) * 32 + hf * 16
                cp[hf, :, tile, c0:c0 + 16] = c[2 * tile + hf].T
        out[key] = cp.reshape(128, 32 * 128)
    return out


def make_consts():
    c = np.zeros((128, 1024), np.float32)
    c[:, 0:128] = np.eye(128)
    c[0:64, 128:192] = 1.0
    c[64:128, 192:256] = 1.0
    s = np.arange(64)
    c[0:64, 256:320] = (s[:, None] < s[None, :])
    c[0:64, 320:384] = (s[:, None] <= s[None, :])
    c[0:64, 384:448] = (s[:, None] > s[None, :])
    c[:, 448:512] = 1.0
    c[:, 448] = 0.0
    c[:, 512:576] = 1.0
    c[:, 512:576:8] = 0.0
    c[:, 576:704] = 1.0 / 1024.0
    return c


class KB:
    def __init__(self, pcols, npar):
        nc = bass.Bass("TRN2", target_bir_lowering=False)
        self.nc = nc
        self.P = Prog(nc)
        self.pc = pcols
        self.npar = npar
        self.force_auto = False
        self.regions = []
        self.stream = None

    def _tblocks(self, col, ap, width):
        t_lo = col % width
        ext = 1
        for (st, cnt) in list(ap.ap)[1:]:
            if abs(st) < width:
                ext += (cnt - 1) * abs(st)
        t_hi = t_lo + ext
        return [t0 for (t0, n) in TBLK if t0 < t_hi and t0 + n > t_lo]

    def akeys(self, ap):
        if ap is None or not hasattr(ap, "tensor"):
            return []
        t = ap.tensor
        if type(t).__name__.startswith("DRam"):
            return []
        name = t.name
        if name.startswith("ps"):
            return [("ps", int(name[2:]))]
        dims = list(ap.ap)
        pstride = dims[0][0]
        col = ap.offset % pstride if pstride > 0 else ap.offset
        if name == "X":
            return [("X", tb) for tb in self._tblocks(col, ap, TT)]
        if name == "AR":
            es = 4 if ap.dtype == F32 or ap.dtype == I32 else 2
            ext = 1
            for (st, cnt) in dims[1:]:
                ext += (cnt - 1) * abs(st)
            b0, b1 = col * es, (col + ext) * es
            if b1 <= 17408 * 2:
                if es == 2:
                    return [("XB", tb) for tb in self._tblocks(col, ap, TT)]
                return [("XB", tb) for (tb, n) in TBLK]
            ks = [k for (r0, r1, k) in self.regions if r0 < b1 and r1 > b0]
            return ks if ks else [("AR", b0 // 2048)]
        return [name]

    def _k(self, reads, writes, ins, outs):
        if not self.force_auto:
            return reads, writes
        r, w = [], []
        for a in ins:
            r.extend(self.akeys(a))
        for a in outs:
            w.extend(self.akeys(a))
        return r, w

    def mm(self, out, lhsT, rhs, start, stop, reads=None, writes=None):
        reads, writes = self._k(reads, writes, [lhsT, rhs], [out])
        self._emit("op", "pe", lambda e: e.matmul(out, lhsT=lhsT, rhs=rhs, start=start, stop=stop), reads, writes)

    def tr(self, out, in_, ident, reads=None, writes=None):
        reads, writes = self._k(reads, writes, [in_, ident], [out])
        self._emit("op", "pe", lambda e: e.transpose(out, in_, ident), reads, writes)

    def tt(self, out, in0, in1, op, reads=None, writes=None, eng="dve"):
        reads, writes = self._k(reads, writes, [in0, in1], [out])
        self._emit("op", eng, lambda e: e.tensor_tensor(out=out, in0=in0, in1=in1, op=op), reads, writes)

    def ts(self, out, in0, s1, op0, reads=None, writes=None, s2=None, op1=None, eng="dve"):
        reads, writes = self._k(reads, writes, [in0, s1, s2], [out])
        if op1 is None:
            self._emit("op", eng, lambda e: e.tensor_scalar(out=out, in0=in0, scalar1=s1, scalar2=None, op0=op0), reads, writes)
        else:
            self._emit("op", eng, lambda e: e.tensor_scalar(out=out, in0=in0, scalar1=s1, scalar2=s2, op0=op0, op1=op1), reads, writes)

    def stt(self, out, in0, scalar, in1, op0, op1, reads=None, writes=None):
        reads, writes = self._k(reads, writes, [in0, scalar, in1], [out])
        self._emit("op", "dve", lambda e: e.scalar_tensor_tensor(out=out, in0=in0, scalar=scalar, in1=in1, op0=op0, op1=op1), reads, writes)

    def act(self, out, in_, func, reads=None, writes=None, bias=None, scale=1.0, accum_out=None):
        reads, writes = self._k(reads, writes, [in_, bias], [out, accum_out])
        kw = {}
        if bias is not None:
            kw["bias"] = bias
        if accum_out is not None:
            kw["accum_out"] = accum_out
        self._emit("op", "act", lambda e: e.activation(out=out, in_=in_, func=func, scale=scale, **kw), reads, writes)

    def cp(self, eng, out, in_, reads=None, writes=None):
        reads, writes = self._k(reads, writes, [in_], [out])
        if eng == "act":
            self._emit("op", "act", lambda e: e.copy(out=out, in_=in_), reads, writes)
        else:
            self._emit("op", eng, lambda e: e.tensor_copy(out=out, in_=in_), reads, writes)

    def red(self, out, in_, op, reads=None, writes=None):
        reads, writes = self._k(reads, writes, [in_], [out])
        self._emit("op", "dve", lambda e: e.tensor_reduce(out=out, in_=in_, op=op, axis=AX.X), reads, writes)

    def rcp(self, out, in_, reads=None, writes=None):
        reads, writes = self._k(reads, writes, [in_], [out])
        self._emit("op", "dve", lambda e: e.reciprocal(out=out, in_=in_), reads, writes)

    def mset(self, out, val, reads=None, writes=None, eng="dve"):
        reads, writes = self._k(reads, writes, [], [out])
        self._emit("op", eng, lambda e: e.memset(out, val), reads, writes)

    def scan(self, out, d0, d1, reads=None, writes=None):
        reads, writes = self._k(reads, writes, [d0, d1], [out])
        self._emit("op", "dve", lambda e: e.tensor_tensor_scan(out=out, data0=d0, data1=d1, initial=0.0, op0=ALU.mult, op1=ALU.add), reads, writes)

    def dma(self, eng, out, in_, reads=None, writes=None):
        reads, writes = self._k(reads, writes, [in_], [out])
        self._emit("dma", eng, lambda e: e.dma_start(out=out, in_=in_), reads, writes)

    def _emit(self, kind, eng, fn, reads, writes):
        rec = (kind, eng, fn, reads, writes)
        if self.stream is not None:
            self.stream.append(rec)
        else:
            self._flush(rec)

    def _flush(self, rec):
        kind, eng, fn, reads, writes = rec
        if kind == "op":
            self.P.op(eng, fn, reads, writes)
        else:
            self.P.dma(eng, fn, reads, writes)

    def setup(self):
        nc = self.nc

        def din(name, shape):
            return nc.dram_tensor(name, list(shape), F32, kind="ExternalInput").ap()

        def dout(name, shape):
            return nc.dram_tensor(name, list(shape), F32, kind="ExternalOutput").ap()

        shapes = {"xin": [TT, D], "mem": [256, D], "ck": [4, NSEQ, 256, D], "cv": [4, NSEQ, 256, D], "s5re0": [2, NSEQ, 4096],
                  "s5im0": [2, NSEQ, 4096], "rw0": [2, NSEQ, 16, 64, 64], "sh0": [2, NSEQ, D], "par": [128, self.npar], "consts": [128, 1024]}
        for j in range(2):
            for k in ("bbr", "bbi", "cpr", "cpi"):
                shapes[(k, j)] = [128, 4096]
        for nm, shp in WSHAPES:
            shapes[nm] = shp

        class Lazy(dict):
            def __missing__(d, key):
                name = key if isinstance(key, str) else f"{key[0]}{key[1]}"
                d[key] = din(name, shapes[key])
                return d[key]
        I = Lazy()
        O = {}
        O["y"] = dout("y", [TT, D])
        O["memk"] = dout("memk", [4, 256, D])
        O["memv"] = dout("memv", [4, 256, D])
        O["s5pre"] = dout("s5pre", [2, 4096])
        O["s5pim"] = dout("s5pim", [2, 4096])
        O["rwp"] = dout("rwp", [2, 16, 64, 64])
        O["shp"] = dout("shp", [2, D])
        O["s5sre"] = dout("s5sre", [2, NSEQ, 4096])
        O["s5sim"] = dout("s5sim", [2, NSEQ, 4096])
        O["rws"] = dout("rws", [2, NSEQ, 16, 64, 64])
        O["shs"] = dout("shs", [2, NSEQ, D])
        self.I, self.O = I, O
        self.VF = nc.dram_tensor("vfirst", [128, 8, TT], F32, kind="Internal").ap()

        self.X = nc.alloc_sbuf_tensor("X", [128, 8, TT], F32)
        self.AR = nc.alloc_sbuf_tensor("AR", [128, 51200], BF16)
        self.MEMT = nc.alloc_sbuf_tensor("MEMT", [128, 8, 256], BF16)
        self.PAR = nc.alloc_sbuf_tensor("PAR", [128, self.npar], F32)
        self.CON = nc.alloc_sbuf_tensor("CON", [128, 1024], F32)
        self.CONB = nc.alloc_sbuf_tensor("CONB", [128, 256], BF16)
        self.T = [nc.alloc_sbuf_tensor(f"T{i}", [128, 512], F32) for i in range(3)]
        self.SQ = nc.alloc_sbuf_tensor("SQ", [128, 8, 512], BF16)
        self.RS = nc.alloc_sbuf_tensor("RS", [128, 512], F32)
        self.SM = nc.alloc_sbuf_tensor("SM", [128, 256], F32)
        self.PS = [nc.alloc_psum_tensor(f"ps{i}", [128, 512], F32) for i in range(8)]
        AR = self.AR
        self.XB = AR[:, 0:17408].rearrange("p (c t) -> p c t", c=8)
        self.A1 = AR[:, 17408:34816].rearrange("p (c t) -> p c t", c=8)
        self.WR = [AR[:, 34816 + i * 8192: 34816 + (i + 1) * 8192].rearrange("p (c n) -> p c n", c=8) for i in range(2)]
        CON, CONB = self.CON, self.CONB
        self.IDF = CON[:, 0:128]
        self.BLK = CON[:, 128:256]
        self.MLT = CON[0:64, 256:320]
        self.MLE = CON[0:64, 320:384]
        self.MGT = CON[0:64, 384:448]
        self.RST64 = CON[:, 448:512]
        self.RST8 = CON[:, 512:576]
        self.IDB = CONB[:, 0:128]
        self.ONB = CONB[:, 128:256]
        self.wr_i = 0
        self.bank_i = 0
        self.EPS = self.SM[:, 0:1]
        self.EPS2 = self.SM[:, 1:2]

    def par(self, key, n=8):
        c0 = self.pc[key]
        return self.PAR[:, c0:c0 + n]

    def wload(self, dram2d, ncols=1024):
        i = self.wr_i
        self.wr_i ^= 1
        dst = self.WR[i][:, :, 0:ncols]
        src = dram2d.rearrange("(c p) n -> p c n", p=128)
        self.dma("pool", dst, src, [], [("WR", i)])
        return self.WR[i], ("WR", i)

    def next_bank(self):
        i = self.bank_i
        self.bank_i = (i + 1) % 4
        return self.PS[i], ("ps", i)

    def dense(self, w, wkey, src, srckey, evac, n_oc=8):
        big = TBLK[:4]
        for oc in range(n_oc):
            base = (oc % 2) * 4
            for c in range(8):
                for bi, (t0, n) in enumerate(big):
                    self.mm(self.PS[base + bi][:, 0:n], w[:, c, oc * 128:(oc + 1) * 128], src[:, c, t0:t0 + n], c == 0, c == 7, [wkey, (srckey, t0)], [("ps", base + bi)])
            for bi, (t0, n) in enumerate(big):
                evac(self.PS[base + bi], ("ps", base + bi), oc, t0, n)
        (t0, n) = TBLK[4]
        for oc in range(n_oc):
            ps, pk = self.next_bank()
            for c in range(8):
                self.mm(ps[:, 0:n], w[:, c, oc * 128:(oc + 1) * 128], src[:, c, t0:t0 + n], c == 0, c == 7, [wkey, (srckey, t0)], [pk])
            evac(ps, pk, oc, t0, n)

    def layer_norm(self, i, k):
        X, XB, SQ, RS, PS, ONB = self.X, self.XB, self.SQ, self.RS, self.PS, self.ONB
        g = self.par(("lng", i, k))
        b = self.par(("lnb", i, k))
        for (t0, n) in TBLK:
            xb = X[:, :, t0:t0 + n]
            kx = ("X", t0)
            self.cp("act", SQ[:, :, 0:n], xb, [kx], ["SQ"])
            for c in range(8):
                self.mm(PS[4][:, 0:n], ONB, SQ[:, c, 0:n], c == 0, c == 7, ["SQ", "CONB"], [("ps", 4)])
            self.tt(xb, xb, PS[4][:, 0:n].unsqueeze(1).broadcast_to([128, 8, n]), ALU.subtract, [kx, ("ps", 4)], [kx])
            self.act(SQ[:, :, 0:n], xb, AF.Square, [kx], ["SQ"])
            for c in range(8):
                self.mm(PS[5][:, 0:n], ONB, SQ[:, c, 0:n], c == 0, c == 7, ["SQ", "CONB"], [("ps", 5)])
            self.act(RS[:, 0:n], PS[5][:, 0:n], AF.Sqrt, [("ps", 5), "eps"], ["RS"], bias=self.EPS)
            self.rcp(RS[:, 0:n], RS[:, 0:n], ["RS"], ["RS"])
            self.tt(xb, xb, RS[:, 0:n].unsqueeze(1).broadcast_to([128, 8, n]), ALU.mult, [kx, "RS"], [kx])
            for c in range(8):
                self.ts(X[:, c, t0:t0 + n], X[:, c, t0:t0 + n], g[:, c:c + 1], ALU.mult, [kx, "PAR"], [kx], s2=b[:, c:c + 1], op1=ALU.add)
            self.cp("act", XB[:, :, t0:t0 + n], xb, [kx], [("XB", t0)])

    def phase_input(self):
        I, X, XB, AR, PS, IDF = self.I, self.X, self.XB, self.AR, self.PS, self.IDF
        self.dma("sp", self.PAR[:], I["par"], [], ["PAR"])
        self.dma("sp", self.CON[:], I["consts"], [], ["CON"])
        self.cp("dve", self.CONB[:, 0:128], self.CON[:, 0:128], ["CON"], ["CONB"])
        self.cp("dve", self.CONB[:, 128:256], self.CON[:, 576:704], ["CON"], ["CONB"])
        self.mset(self.SM[:, 0:1], LN_EPS, [], ["eps"])
        self.mset(self.SM[:, 1:2], GN_EPS, [], ["eps"])
        STG = [AR[:, 17408 + i * 2048: 17408 + (i + 1) * 2048].bitcast(F32) for i in range(2)]
        import os
        for tt in [int(x) for x in os.environ.get('PI_TILES', ','.join(str(i) for i in range(19))).split(',')]:
            st = STG[tt % 2]
            sk = ("stg", tt % 2)
            if tt < 17:
                self.dma("sp", st, I["xin"][tt * 128:(tt + 1) * 128, :], [], [sk])
            else:
                self.dma("sp", st, I["mem"][(tt - 17) * 128:(tt - 16) * 128, :], [], [sk])
            for cg in range(2):
                ps, pk = self.next_bank()
                for cc in range(4):
                    c = cg * 4 + cc
                    self.tr(ps[:, cc * 128:(cc + 1) * 128], st[:, c * 128:(c + 1) * 128], IDF, [sk, "CON"], [pk])
                psv = ps[:].rearrange("p (a b) -> p a b", a=4)
                if tt < 17:
                    self.cp("dve", X[:, cg * 4:(cg + 1) * 4, tt * 128:(tt + 1) * 128], psv, [pk], [("Xi", tt, cg)])
                    self.cp("act", XB[:, cg * 4:(cg + 1) * 4, tt * 128:(tt + 1) * 128], psv, [pk], [("XBi", tt, cg)])
                else:
                    m0 = (tt - 17) * 128
                    self.cp("act", self.MEMT[:, cg * 4:(cg + 1) * 4, m0:m0 + 128], psv, [pk], ["MEMT"])
        self.P.emit()

    def phase_output(self):
        AR, PS, IDF, X, O = self.AR, self.PS, self.IDF, self.X, self.O
        STG = [AR[:, 17408 + i * 2048: 17408 + (i + 1) * 2048].bitcast(F32) for i in range(2)]
        for tt in range(17):
            st = STG[tt % 2]
            sk = ("stg", tt % 2)
            for cg in range(2):
                ps, pk = self.next_bank()
                for cc in range(4):
                    c = cg * 4 + cc
                    self.tr(ps[:, cc * 128:(cc + 1) * 128], X[:, c, tt * 128:(tt + 1) * 128], IDF, [], [pk])
                self.cp("dve" if cg == 0 else "act", st[:, cg * 512:(cg + 1) * 512], ps[:], [pk], [sk])
            self.dma("sp", O["y"][tt * 128:(tt + 1) * 128, :], st, [sk], [("yout", tt)])
        self.P.emit()

    def mlp_layer(self, i):
        X, XB, A1, T, I = self.X, self.XB, self.A1, self.T, self.I
        cnt = [0]
        for fg in range(4):
            w1, k1 = self.wload(I["mlp_w1"][i][:, fg * 1024:(fg + 1) * 1024])

            def evac1(ps, pk, oc, t0, n):
                tt_, tk = T[cnt[0] % 2], ("T", cnt[0] % 2)
                cnt[0] += 1
                self.act(tt_[:, 0:n], ps[:, 0:n], AF.Relu, [pk], [tk])
                self.tt(A1[:, oc, t0:t0 + n], tt_[:, 0:n], tt_[:, 0:n], ALU.mult, [tk], [("A1", t0)])
            self.dense(w1, k1, XB, "XB", evac1)
            w2, k2 = self.wload(I["mlp_w2"][i][fg * 1024:(fg + 1) * 1024, :])
            if fg == 0:
                def evac2(ps, pk, oc, t0, n):
                    self.stt(X[:, oc, t0:t0 + n], X[:, oc, t0:t0 + n], ALPHA, ps[:, 0:n], ALU.mult, ALU.add, [pk, ("X", t0)], [("X", t0)])
            else:
                def evac2(ps, pk, oc, t0, n):
                    self.tt(X[:, oc, t0:t0 + n], X[:, oc, t0:t0 + n], ps[:, 0:n], ALU.add, [pk, ("X", t0)], [("X", t0)])
            self.dense(w2, k2, A1, "A1", evac2)

    def xa_layer(self, i):
        X, XB, A1, T, I, O, PS, MEMT, AR, IDB, SM = self.X, self.XB, self.A1, self.T, self.I, self.O, self.PS, self.MEMT, self.AR, self.IDB, self.SM
        wq, kq = self.wload(I["xa_w_q"][i])

        def evq(ps, pk, oc, t0, n):
            self.act(A1[:, oc, t0:t0 + n], ps[:, 0:n], AF.Copy, [pk], [("A1", t0)], scale=0.0625)
        self.dense(wq, kq, XB, "XB", evq)
        self.P.emit()
        KT = AR[:, 0:2048].rearrange("p (c m) -> p c m", c=8)
        VB = AR[:, 2048:4096].rearrange("p (a n) -> p a n", a=2)
        PNB = AR[:, 4096:5120].rearrange("p (h m) -> p h m", h=4)
        PT = AR[:, 5120:6144].rearrange("p (a t) -> p a t", a=8)
        KS = [AR[:, 6144 + s * 2048: 8192 + s * 2048].rearrange("p (a n) -> p a n", a=2) for s in range(2)]
        VS = [AR[:, 10240 + s * 2048: 12288 + s * 2048].rearrange("p (a n) -> p a n", a=2) for s in range(2)]
        KTS = AR[:, 14336:16384].rearrange("p (c m) -> p c m", c=8)
        PTS = AR[:, 16384:16448].rearrange("p (a t) -> p a t", a=8)
        PEXP = [T[0], T[1]]
        psT = PS[4][:].bitcast(BF16)
        MX, NMX, SUM, RSM = SM[:, 8:12], SM[:, 12:16], SM[:, 16:20], SM[:, 20:24]
        SC = [PS[5], PS[6]]

        wk, kk = self.wload(I["xa_w_k"][i])
        for oc in range(8):
            ps, pk = self.next_bank()
            for c in range(8):
                self.mm(ps[:, 0:256], wk[:, c, oc * 128:(oc + 1) * 128], MEMT[:, c, :], c == 0, c == 7, [kk, "MEMT"], [pk])
            self.cp("act", KT[:, oc, :], ps[:, 0:256], [pk], ["KT"])

        def tokmajor(w, wkey, outap, vb):
            for mt in range(2):
                for nb in range(2):
                    ps, pk = self.next_bank()
                    for c in range(8):
                        self.mm(ps[:], MEMT[:, c, mt * 128:(mt + 1) * 128], w[:, c, nb * 512:(nb + 1) * 512], c == 0, c == 7, [wkey, "MEMT"], [pk])
                    self.cp("dve", T[2][:], ps[:], [pk], ["T2"])
                    if vb:
                        self.cp("act", VB[:, mt, nb * 512:(nb + 1) * 512], ps[:], [pk], ["VB"])
                    self.dma("sp", outap[mt * 128:(mt + 1) * 128, nb * 512:(nb + 1) * 512], T[2][:], ["T2"], [("mo", id(outap), mt, nb)])
        tokmajor(wk, kk, O["memk"][i], False)
        wv, kv = self.wload(I["xa_w_v"][i])
        tokmajor(wv, kv, O["memv"][i], True)

        def softmax(np_):
            for b in range(2):
                self.red(MX[0:np_, 2 * b:2 * b + 2], SC[b][0:np_, :].rearrange("p (h m) -> p h m", h=2), ALU.max, [("ps", 5 + b)], ["MX"])
            self.ts(NMX[0:np_, :], MX[0:np_, :], -1.0, ALU.mult, ["MX"], ["NMX"])
            for h in range(4):
                sl = slice((h % 2) * 256, (h % 2) * 256 + 256)
                self.act(PEXP[h // 2][0:np_, sl], SC[h // 2][0:np_, sl], AF.Exp, [("ps", 5 + h // 2), "NMX"], [("PEXP", h), ("SUM", h)],
                         bias=NMX[0:np_, h:h + 1], accum_out=SUM[0:np_, h:h + 1])
            self.rcp(RSM[0:np_, :], SUM[0:np_, :], [("SUM", h) for h in range(4)], ["RSM"])
            for b in range(2):
                self.tt(PNB[0:np_, 2 * b:2 * b + 2, :], PEXP[b][0:np_, :].rearrange("p (h m) -> p h m", h=2),
                        RSM[0:np_, 2 * b:2 * b + 2].unsqueeze(2).broadcast_to([np_, 2, 256]), ALU.mult,
                        [("PEXP", 2 * b), ("PEXP", 2 * b + 1), "RSM"], ["PNB"])

        for tt in range(16):
            t0 = tt * 128
            ak = ("A1", (t0 // 512) * 512)
            for h in range(4):
                for dc in range(2):
                    fc = 2 * h + dc
                    self.mm(SC[h // 2][:, (h % 2) * 256:(h % 2) * 256 + 256], A1[:, fc, t0:t0 + 128], KT[:, fc, :], dc == 0, dc == 1, [ak, "KT"], [("ps", 5 + h // 2)])
            softmax(128)
            for h in range(4):
                for mt in range(2):
                    a = h * 2 + mt
                    self.tr(psT[:, a * 128:(a + 1) * 128], PNB[:, h, mt * 128:(mt + 1) * 128], IDB, ["PNB", "CONB"], [("ps", 4)])
            self.cp("act", PT[:].rearrange("p a t -> p (a t)"), psT[:, 0:1024], [("ps", 4)], ["PT"])
            for fc in range(8):
                h, dc = fc // 2, fc % 2
                bank, bk = (PS[7], ("ps", 7)) if fc >= 4 else (PS[3], ("ps", 3))
                for mt in range(2):
                    self.mm(bank[:, (fc % 4) * 128:(fc % 4) * 128 + 128], VB[:, mt, h * 256 + dc * 128:h * 256 + dc * 128 + 128], PT[:, h * 2 + mt, :],
                            mt == 0, mt == 1, ["VB", "PT"], [bk])
            self.cp("dve", A1[:, 0:4, t0:t0 + 128], PS[3][:].rearrange("p (a t) -> p a t", a=4), [("ps", 3)], [ak])
            self.cp("act", A1[:, 4:8, t0:t0 + 128], PS[7][:].rearrange("p (a t) -> p a t", a=4), [("ps", 7)], [ak])
        for s in range(NSEQ):
            ks, vs = KS[s % 2], VS[s % 2]
            kkey, vkey = ("KS", s % 2), ("VS", s % 2)
            self.dma("pool", ks, I["ck"][i, s].rearrange("(a p) n -> p a n", p=128), [], [kkey])
            self.dma("pool", vs, I["cv"][i, s].rearrange("(a p) n -> p a n", p=128), [], [vkey])
            for mt in range(2):
                for fc in range(8):
                    self.tr(psT[:, fc * 128:(fc + 1) * 128], ks[:, mt, fc * 128:(fc + 1) * 128], IDB, [kkey, "CONB"], [("ps", 4)])
                self.cp("act", KTS[:, :, mt * 128:(mt + 1) * 128], psT[:, 0:1024].rearrange("p (c m) -> p c m", c=8), [("ps", 4)], ["KTS"])
            c0 = TP + 8 * s
            for h in range(4):
                for dc in range(2):
                    fc = 2 * h + dc
                    self.mm(SC[h // 2][0:8, (h % 2) * 256:(h % 2) * 256 + 256], A1[:, fc, c0:c0 + 8], KTS[:, fc, :], dc == 0, dc == 1, [("A1", 2048), "KTS"], [("ps", 5 + h // 2)])
            softmax(8)
            for h in range(4):
                for mt in range(2):
                    a = h * 2 + mt
                    self.tr(psT[:, a * 8:(a + 1) * 8], PNB[0:8, h, mt * 128:(mt + 1) * 128], IDB[0:8, 0:8], ["PNB", "CONB"], [("ps", 4)])
            self.cp("act", PTS[:].rearrange("p a t -> p (a t)"), psT[:, 0:64], [("ps", 4)], ["PTS"])
            for fc in range(8):
                h, dc = fc // 2, fc % 2
                for mt in range(2):
                    self.mm(PS[7][:, fc * 8:fc * 8 + 8], vs[:, mt, h * 256 + dc * 128:h * 256 + dc * 128 + 128], PTS[:, h * 2 + mt, :], mt == 0, mt == 1, [vkey, "PTS"], [("ps", 7)])
            self.cp("dve", A1[:, :, c0:c0 + 8], PS[7][:, 0:64].rearrange("p (a t) -> p a t", a=8), [("ps", 7)], [("A1", 2048)])
        wo, ko = self.wload(I["xa_w_o"][i])

        def evo(ps, pk, oc, t0, n):
            self.stt(X[:, oc, t0:t0 + n], X[:, oc, t0:t0 + n], ALPHA, ps[:, 0:n], ALU.mult, ALU.add, [pk, ("X", t0)], [("X", t0)])
        self.dense(wo, ko, A1, "A1", evo)

    def s5_layer(self, i, j):
        X, XB, T, I, O, PS, AR, SM, IDF, RS = self.X, self.XB, self.T, self.I, self.O, self.PS, self.AR, self.SM, self.IDF, self.RS
        B0 = 17408
        BBR = AR[:, B0:B0 + 4096].rearrange("p (t c) -> p t c", t=32)
        BBI = AR[:, B0 + 4096:B0 + 8192].rearrange("p (t c) -> p t c", t=32)
        CRB = AR[:, B0 + 8192:B0 + 12288].rearrange("p (t c) -> p t c", t=32)
        CIN = AR[:, B0 + 12288:B0 + 16384].rearrange("p (t c) -> p t c", t=32)
        BU0 = AR[:, B0 + 16384:B0 + 20480].bitcast(F32).rearrange("p (r t s) -> p r t s", r=2, t=32)
        BU = [BU0, BU0]
        TAB = AR[:, B0 + 20480:B0 + 24576].bitcast(F32).rearrange("p (r t s) -> p r t s", r=2, t=32)
        EC, ES = TAB[:, 0], TAB[:, 1]
        SQF = self.SQ[:].rearrange("p c t -> p (c t)").bitcast(F32)
        TQ4 = SQF[:, 0:1024].rearrange("p (r t s) -> p r t s", r=2, t=32)
        tq = SQF[:, 0:1024].rearrange("p (t s) -> p t s", t=32)
        RH = SQF[:, 1024:2048].rearrange("p (t s) -> p t s", t=32)
        HH = AR[:, B0 + 24576:B0 + 28672].bitcast(F32).rearrange("p (r t s) -> p r t s", r=2, t=32)
        HHB = AR[:, B0 + 28672:B0 + 30720].rearrange("p (r t s) -> p r t s", r=2, t=32)
        SP = AR[:, B0 + 30720:B0 + 33792].bitcast(F32)
        H0 = SP[:, 0:1024].rearrange("p (r t q) -> p r t q", r=2, t=32)

        def sm(k):
            return SP[:, 1024 + 32 * k:1024 + 32 * (k + 1)]
        pair = lambda k: SP[:, 1024 + 32 * k:1024 + 32 * (k + 2)].rearrange("p (r t) -> p r t", r=2)
        C1, C2, HL, M1, M2 = pair(0), pair(2), pair(4), pair(6), pair(8)
        abre, abim = sm(0), sm(1)
        FRE, FIM = sm(10), sm(11)
        dt, ang, mag, sn = sm(12), sm(13), sm(14), sm(15)
        cs, den, nre, r_, m_, kf = [SM[:, 64 + 32 * k:96 + 32 * k] for k in range(6)]
        TI = SM[:, 32:64].bitcast(I32)
        pc = self.pc
        LRE = self.PAR[:, pc[("s5_a_re", j)]:pc[("s5_a_re", j)] + 32]
        LIM = self.PAR[:, pc[("s5_a_im", j)]:pc[("s5_a_im", j)] + 32]
        LDT = self.PAR[:, pc[("s5_log_dt", j)]:pc[("s5_log_dt", j)] + 32]
        K = ["s5p", "PAR"]
        TWO_PI = 2.0 * math.pi
        tt = lambda o, a, b, op: self.tt(o, a, b, op, K, K)
        ts = lambda o, a, s1, op0, s2=None, op1=None: self.ts(o, a, s1, op0, K, K, s2=s2, op1=op1)
        ac = lambda o, a, f, scale=1.0: self.act(o, a, f, K, K, scale=scale)
        ac(dt, LDT, AF.Exp)
        tt(mag, LRE, dt, ALU.mult)
        ac(mag, mag, AF.Exp)
        tt(ang, LIM, dt, ALU.mult)
        ts(kf, ang, 1.0 / TWO_PI, ALU.mult)
        self.cp("dve", TI, kf, K, K)
        self.cp("dve", kf, TI, K, K)
        self.stt(r_, kf, -TWO_PI, ang, ALU.mult, ALU.add, K, K)

        def wrap(x):
            ts(m_, x, math.pi, ALU.is_gt)
            self.stt(x, m_, -TWO_PI, x, ALU.mult, ALU.add, K, K)
            ts(m_, x, -math.pi, ALU.is_lt)
            self.stt(x, m_, TWO_PI, x, ALU.mult, ALU.add, K, K)
        wrap(r_)
        ac(sn, r_, AF.Sin)
        ts(r_, r_, math.pi / 2, ALU.add)
        wrap(r_)
        ac(cs, r_, AF.Sin)
        tt(abre, mag, cs, ALU.mult)
        tt(abim, mag, sn, ALU.mult)
        ts(sm(2), abim, -1.0, ALU.mult)
        self.cp("dve", sm(3), abre, K, K)
        tt(den, LRE, LRE, ALU.mult)
        tt(m_, LIM, LIM, ALU.mult)
        tt(den, den, m_, ALU.add)
        self.rcp(den, den, K, K)
        ts(nre, abre, -1.0, ALU.add)
        tt(FRE, nre, LRE, ALU.mult)
        tt(m_, abim, LIM, ALU.mult)
        tt(FRE, FRE, m_, ALU.add)
        tt(FRE, FRE, den, ALU.mult)
        tt(FIM, abim, LRE, ALU.mult)
        tt(m_, nre, LIM, ALU.mult)
        tt(FIM, FIM, m_, ALU.subtract)
        tt(FIM, FIM, den, ALU.mult)
        KT_ = K + ["TAB", "tq", "RH"]
        self.mset(EC[:, :, 0:1], 1.0, KT_, ["TAB"])
        self.mset(ES[:, :, 0:1], 0.0, KT_, ["TAB"])
        self.cp("dve", EC[:, :, 1], cs, KT_, ["TAB"])
        self.cp("dve", ES[:, :, 1], sn, KT_, ["TAB"])
        for tau in range(2, 32):
            self.tt(EC[:, :, tau], EC[:, :, tau - 1], cs, ALU.mult, KT_, ["TAB"])
            self.tt(dt, ES[:, :, tau - 1], sn, ALU.mult, KT_, K)
            self.tt(EC[:, :, tau], EC[:, :, tau], dt, ALU.subtract, KT_, ["TAB"])
            self.tt(ES[:, :, tau], ES[:, :, tau - 1], cs, ALU.mult, KT_, ["TAB"])
            self.tt(dt, EC[:, :, tau - 1], sn, ALU.mult, KT_, K)
            self.tt(ES[:, :, tau], ES[:, :, tau], dt, ALU.add, KT_, ["TAB"])
        self.cp("dve", RH, mag.unsqueeze(2).broadcast_to([128, 32, 32]), KT_, ["RH"])
        self.mset(RH[:, :, 0:1], 0.0, KT_, ["RH"])
        self.dma("pool", BBR[:].rearrange("p t c -> p (t c)"), I[("bbr", j)], [], ["BB"])
        self.dma("pool", BBI[:].rearrange("p t c -> p (t c)"), I[("bbi", j)], [], ["BB"])
        v3 = lambda a: a[:].rearrange("p (t c) -> p t c", t=4)
        for g in range(8):
            sr, si, tm, r4 = T[0], T[1], T[2], RS
            self.dma("sp", sr[:], I[("cpr", j)][:, g * 512:(g + 1) * 512], [], ["T0"])
            self.dma("sp", si[:], I[("cpi", j)][:, g * 512:(g + 1) * 512], [], ["T1"])
            fre_b = FRE[:, g * 4:(g + 1) * 4].unsqueeze(2).broadcast_to([128, 4, 128])
            fim_b = FIM[:, g * 4:(g + 1) * 4].unsqueeze(2).broadcast_to([128, 4, 128])
            kk = ["T0", "T1", "T2", "RS", "CC"] + K
            self.tt(v3(tm), v3(si), fim_b, ALU.mult, kk, ["T2"])
            self.tt(v3(si), v3(si), fre_b, ALU.mult, kk, ["T1"])
            self.tt(v3(r4), v3(sr), fre_b, ALU.mult, kk, ["RS"])
            self.tt(CRB[:, g * 4:(g + 1) * 4, :], v3(r4), v3(tm), ALU.subtract, kk, ["CC"])
            self.tt(v3(sr), v3(sr), fim_b, ALU.mult, kk, ["T0"])
            self.tt(v3(sr), v3(sr), v3(si), ALU.add, kk, ["T0"])
            self.ts(CIN[:, g * 4:(g + 1) * 4, :], v3(sr), -1.0, ALU.mult, kk, ["CC"])
        for r, nm in enumerate(("s5re0", "s5im0")):
            for half in range(8):
                st, sk = T[half % 2], "T%d" % (half % 2)
                self.dma("sp", st[0:16, :], I[nm][j][:, half * 512:(half + 1) * 512], ["CC"], [sk])
                for q in range(4):
                    tile = half * 4 + q
                    self.tr(PS[0][:, tile * 16:(tile + 1) * 16], st[0:16, q * 128:(q + 1) * 128], IDF[0:16, 0:16], [sk, "CON"], [("ps", 0)])
            self.cp("dve", H0[:, r, :, :], PS[0][:].rearrange("p (t q) -> p t q", t=32), [("ps", 0)], ["H0"])
        fn2, gre, gim = dt, ang, mag
        tt(fn2, FRE, FRE, ALU.mult)
        tt(m_, FIM, FIM, ALU.mult)
        tt(fn2, fn2, m_, ALU.add)
        self.rcp(fn2, fn2, K, K)
        tt(gre, FRE, fn2, ALU.mult)
        tt(gim, FIM, fn2, ALU.mult)
        ts(gim, gim, -1.0, ALU.mult)
        HT = TQ4
        bq = lambda f: f.unsqueeze(2).broadcast_to([128, 32, 16])
        K2 = K + ["H0", "tq"]
        ta, tb = HT[:, 0, :, 0:16], HT[:, 1, :, 0:16]
        self.tt(ta, H0[:, 0], bq(gre), ALU.mult, K2, ["tq"])
        self.tt(tb, H0[:, 1], bq(gim), ALU.mult, K2, ["tq"])
        self.tt(ta, ta, tb, ALU.subtract, K2, ["tq"])
        self.tt(tb, H0[:, 0], bq(gim), ALU.mult, K2, ["tq"])
        self.tt(H0[:, 1], H0[:, 1], bq(gre), ALU.mult, K2, ["H0"])
        self.tt(H0[:, 1], H0[:, 1], tb, ALU.add, K2, ["H0"])
        self.cp("dve", H0[:, 0], ta, K2, ["H0"])

        DPAR = self.par(("s5d", j))
        def front(b):
            bu, bk = BU0, ("BU", 0)
            c0 = b * 32
            tb0 = (c0 // 512) * 512 if c0 < 2048 else 2048
            xk, xk2 = ("XB", tb0), ("X", tb0)
            for r, BB in enumerate((BBR, BBI)):
                for half in range(2):
                    ps, pk = self.next_bank()
                    for tl in range(16):
                        tile = half * 16 + tl
                        self.mm(ps[:, tl * 32:(tl + 1) * 32], BB[:, tile, :], XB[:, tile // 4, c0:c0 + 32], True, True, ["BB", xk], [pk])
                    self.cp("act", bu[:, r, half * 16:(half + 1) * 16, :], ps[:].rearrange("p (t s) -> p t s", t=16), [pk], [bk])
            hk = ["HH", "HL", "m", "H0"] + K
            if b < 64:
                kr = hk + [bk, "TAB", "tq", "RH"]
                br, bi = bu[:, 0], bu[:, 1]
                self.tt(HH[:, 0], EC, br, ALU.mult, kr, ["HH"])
                self.tt(tq, ES, bi, ALU.mult, kr, ["tq"])
                self.tt(HH[:, 0], HH[:, 0], tq, ALU.add, kr, ["HH"])
                self.tt(HH[:, 1], EC, bi, ALU.mult, kr, ["HH"])
                self.tt(tq, ES, br, ALU.mult, kr, ["tq"])
                self.tt(HH[:, 1], HH[:, 1], tq, ALU.subtract, kr, ["HH"])
                if b > 0:
                    self.tt(M1, C1, HL[:, 0:1, :].broadcast_to([128, 2, 32]), ALU.mult, kr, ["m"])
                    self.tt(M2, C2, HL[:, 1:2, :].broadcast_to([128, 2, 32]), ALU.mult, kr, ["m"])
                    self.tt(HH[:, :, :, 0], HH[:, :, :, 0], M1, ALU.add, kr, ["HH"])
                    self.tt(HH[:, :, :, 0], HH[:, :, :, 0], M2, ALU.add, kr, ["HH"])
                for r in range(2):
                    self.scan(bu[:, r].rearrange("p t s -> p (t s)"), RH.rearrange("p t s -> p (t s)"), HH[:, r].rearrange("p t s -> p (t s)"), kr, [bk])
                self.tt(HH[:, 0], EC, br, ALU.mult, kr, ["HH"])
                self.tt(tq, ES, bi, ALU.mult, kr, ["tq"])
                self.tt(HH[:, 0], HH[:, 0], tq, ALU.subtract, kr, ["HH"])
                self.tt(HH[:, 1], ES, br, ALU.mult, kr, ["HH"])
                self.tt(tq, EC, bi, ALU.mult, kr, ["tq"])
                self.tt(HH[:, 1], HH[:, 1], tq, ALU.add, kr, ["HH"])
                self.cp("dve", HL, HH[:, :, :, 31], ["HH"], ["HL"])
            else:
                q0 = (b - 64) * 4
                buv = bu[:].rearrange("p r t (q s) -> p r t q s", q=4)
                hhv = HH[:].rearrange("p r t (q s) -> p r t q s", q=4)
                ob = TQ4
                ok = "tq"
                m1, m2 = ob[:, :, :, 0:4], ob[:, :, :, 4:8]
                c1b = C1.unsqueeze(3).broadcast_to([128, 2, 32, 4])
                c2b = C2.unsqueeze(3).broadcast_to([128, 2, 32, 4])
                for s in range(8):
                    prev = hhv[:, :, :, :, s - 1] if s > 0 else H0[:, :, :, q0:q0 + 4]
                    self.tt(m1, c1b, prev[:, 0:1].broadcast_to([128, 2, 32, 4]), ALU.mult, hk + [ok], [ok])
                    self.tt(m2, c2b, prev[:, 1:2].broadcast_to([128, 2, 32, 4]), ALU.mult, hk + [ok], [ok])
                    self.tt(m1, m1, m2, ALU.add, hk + [ok], [ok])
                    self.tt(hhv[:, :, :, :, s], m1, buv[:, :, :, :, s], ALU.add, hk + [ok, bk], ["HH"])
            if b >= 63:
                nq = 1 if b == 63 else 4
                src = HH[:, :, :, 31:32] if b == 63 else HH[:].rearrange("p r t (q s) -> p r t q s", q=4)[:, :, :, :, 7]
                FT, fk = TQ4, "tq"
                fo, ft = FT[:, :, :, 8:8 + nq], FT[:, :, :, 12:12 + nq]
                fb = lambda f: f.unsqueeze(2).broadcast_to([128, 32, nq])
                rk = ["HH", fk] + K
                self.tt(fo[:, 0], src[:, 0], fb(FRE), ALU.mult, rk, [fk])
                self.tt(ft[:, 0], src[:, 1], fb(FIM), ALU.mult, rk, [fk])
                self.tt(fo[:, 0], fo[:, 0], ft[:, 0], ALU.subtract, rk, [fk])
                self.tt(fo[:, 1], src[:, 0], fb(FIM), ALU.mult, rk, [fk])
                self.tt(ft[:, 1], src[:, 1], fb(FRE), ALU.mult, rk, [fk])
                self.tt(fo[:, 1], fo[:, 1], ft[:, 1], ALU.add, rk, [fk])
                for r in range(2):
                    if b == 63:
                        outn = ("s5pre", "s5pim")[r]
                        self.tr(PS[7][0:32, 0:128], fo[:, r, :, 0], IDF, [fk, "CON"], [("ps", 7)])
                        self.cp("act", T[2][0:32, 0:128], PS[7][0:32, 0:128], [("ps", 7)], ["T2"])
                        self.dma("sp", O[outn][j].rearrange("(t p) -> t p", p=128), T[2][0:32, 0:128], ["T2"], [("so", outn)])
                    else:
                        outn = ("s5sre", "s5sim")[r]
                        q0 = (b - 64) * 4
                        for tg in range(8):
                            for tl in range(4):
                                self.tr(PS[7][0:4, tl * 128:(tl + 1) * 128], fo[:, r, tg * 4 + tl, :], IDF, [fk, "CON"], [("ps", 7)])
                            self.cp("act", T[2][0:4, :], PS[7][0:4, :], [("ps", 7)], ["T2"])
                            self.dma("sp", O[outn][j][q0:q0 + 4, tg * 512:(tg + 1) * 512], T[2][0:4, :], ["T2"], [("so", outn, tg, q0)])

        def back(b):
            c0 = b * 32
            tb0 = (c0 // 512) * 512 if c0 < 2048 else 2048
            xk, xk2 = ("XB", tb0), ("X", tb0)
            self.cp("act", HHB[:], HH[:], ["HH"], ["HHB"])
            for k in range(8):
                n = 0
                for q in range(4):
                    tile = k * 4 + q
                    for r, CC in enumerate((CRB, CIN)):
                        self.mm(PS[6][:, k * 32:(k + 1) * 32], CC[:, tile, :], HHB[:, r, tile, :], n == 0, n == 7, ["CC", "HHB"], [("ps", 6)])
                        n += 1
            yv = PS[6][:, 0:256].rearrange("p (k s) -> p k s", k=8)
            v = T[0][:, 0:256].rearrange("p (k s) -> p k s", k=8)
            t1 = T[1][:, 0:256].rearrange("p (k s) -> p k s", k=8)
            gk = ["T0", "T1", ("ps", 6)]
            self.tt(v, X[:, :, c0:c0 + 32], DPAR.unsqueeze(2).broadcast_to([128, 8, 32]), ALU.mult, gk + [xk2, "PAR"], ["T0"])
            self.tt(v, v, yv, ALU.add, gk, ["T0"])
            self.tt(t1, v, v, ALU.mult, gk, ["T1"])
            self.ts(t1, t1, 0.044715, ALU.mult, gk, ["T1"], s2=1.0, op1=ALU.add)
            self.tt(t1, t1, v, ALU.mult, gk, ["T1"])
            self.act(t1, t1, AF.Tanh, gk, ["T1"], scale=0.7978845608028654)
            self.ts(t1, t1, 1.0, ALU.add, gk, ["T1"], s2=0.5, op1=ALU.mult)
            self.tt(XB[:, :, c0:c0 + 32], t1, v, ALU.mult, gk + [xk], [xk])

        front(0)
        for b in range(64 + 4):
            self.stream = []
            back(b)
            lt = self.stream
            self.stream = []
            if b + 1 < 68:
                front(b + 1)
            lp = self.stream
            self.stream = None
            ia = ib = 0
            while ia < len(lt) or ib < len(lp):
                if ia < len(lt):
                    self._flush(lt[ia]); ia += 1
                if ib < len(lp):
                    self._flush(lp[ib]); ib += 1
        self.P.emit()
        wv, kv = self.wload(I["s5_w_glu_v"][j])
        wg, kg = self.wload(I["s5_w_glu_g"][j])
        for (t0, n) in TBLK:
            for oc in range(8):
                pv, pg = PS[(oc % 2) * 2], PS[1 + (oc % 2) * 2]
                kpv, kpg = ("ps", (oc % 2) * 2), ("ps", 1 + (oc % 2) * 2)
                for c in range(8):
                    self.mm(pv[:, 0:n], wv[:, c, oc * 128:(oc + 1) * 128], XB[:, c, t0:t0 + n], c == 0, c == 7, [kv, ("XB", t0)], [kpv])
                for c in range(8):
                    self.mm(pg[:, 0:n], wg[:, c, oc * 128:(oc + 1) * 128], XB[:, c, t0:t0 + n], c == 0, c == 7, [kg, ("XB", t0)], [kpg])
                tt_, tk = T[oc % 2], ("T", oc % 2)
                self.act(tt_[:, 0:n], pg[:, 0:n], AF.Sigmoid, [kpg], [tk])
                self.tt(tt_[:, 0:n], tt_[:, 0:n], pv[:, 0:n], ALU.mult, [kpv, tk], [tk])
                self.stt(X[:, oc, t0:t0 + n], X[:, oc, t0:t0 + n], ALPHA, tt_[:, 0:n], ALU.mult, ALU.add, [tk, ("X", t0)], [("X", t0)])

    def rwkv_layer(self, i, j):
        X, XB, T, I, O, PS, AR, SM, IDF, BLK = self.X, self.XB, self.T, self.I, self.O, self.PS, self.AR, self.SM, self.IDF, self.BLK
        if STUB_RWKV:
            for (t0, n) in TBLK:
                self.ts(X[:, :, t0:t0 + n], X[:, :, t0:t0 + n], ALPHA, ALU.mult, [("X", t0)], [("X", t0)])
            return
        off = [17408]

        self.force_auto = True
        self.regions = []

        def ab(n):
            a = AR[:, off[0]:off[0] + n]
            self.regions.append((off[0] * 2, (off[0] + n) * 2, ("rk", len(self.regions))))
            off[0] += n
            assert off[0] <= 51200, off[0]
            return a

        def af(n):
            return ab(2 * n).bitcast(F32)
        q3 = lambda a: a.rearrange("p (c t) -> p c t", c=2)
        WRq, WKq, WVq = [ab(2048).rearrange("p (c n) -> p c n", c=8) for _ in range(3)]
        WOq = ab(2048).rearrange("p (c n) -> p c n", c=2)
        W1, A1w = [ab(512).rearrange("p (c n) -> p c n", c=8) for _ in range(2)]
        G1w = ab(1024).rearrange("p (c n) -> p c n", c=8)
        V1w = ab(256).rearrange("p (c n) -> p c n", c=8)
        W2q, A2q, V2q, G2q = [ab(256) for _ in range(4)]
        XM = ab(3072).rearrange("p (m c t) -> p m c t", m=6, c=8)
        XX = af(512).rearrange("p (c t) -> p c t", c=8)
        TMX = af(512).rearrange("p (c t) -> p c t", c=8)
        R_, K_, A_, KAP, LW, LG, E_, T1t, Y_, YBs, VFt, T1e, SSe = [q3(af(128)) for _ in range(13)]
        V2, G2, BON2 = [[q3(af(128)) for _ in range(2)] for _ in range(3)]
        BT2, KT2, BH2, KH2, Vb2 = [[q3(ab(128)) for _ in range(2)] for _ in range(5)]
        ATRT2 = [ab(256).rearrange("p (w c t) -> p w c t", w=2, c=2) for _ in range(2)]
        Vt, BHt, KHt, Ut = [ab(256) for _ in range(4)]
        h4 = lambda a: a.rearrange("p (h t) -> p h t", h=4)
        Xa, XTa, Xb, XTb, Rs, Pb = [h4(ab(256)) for _ in range(6)]
        Pm = h4(af(256))
        STb = ab(256).rearrange("p (c n) -> p c n", c=2)
        MBR, MKA, MKR = [h4(ab(256)) for _ in range(3)]
        STp, STs, BDin = [af(256).rearrange("p (c n) -> p c n", c=2) for _ in range(3)]
        OB = q3(ab(128))
        T1b, A1b, V1b, G1b = ab(64), ab(64), ab(64), ab(64)
        SH0 = af(128).rearrange("p (c s) -> p c s", c=8)
        XL = af(136).rearrange("p (c s) -> p c s", c=8)
        OMKA = af(8)
        RSTa, RSTb = af(128), af(128)
        GLu2 = [af(16).rearrange("p (c u) -> p c u", c=2) for _ in range(2)]
        LGL = af(16).rearrange("p (c u) -> p c u", c=2)
        SSn = q3(af(128))
        pc = self.pc
        MU = self.par(("mu", j), 48).rearrange("p (m c) -> p m c", m=6)
        W0, A0, KKp, KAp, LXG, LXB, RKp = [self.par((nm, j)) for nm in ("rwkv_w0", "rwkv_a0", "rwkv_k_k", "rwkv_k_a", "rwkv_lnx_g", "rwkv_lnx_b", "rwkv_r_k")]
        V0p = self.par(("rwkv_v0", 1))
        B = [PS[k] for k in range(8)]
        bk = lambda k: ("ps", k)
        K0 = ["rk"]

        self.dma("pool", W1, I["rwkv_w1"][j].rearrange("(c p) n -> p c n", p=128), [], ["Wl"])
        self.dma("pool", A1w, I["rwkv_a1"][j].rearrange("(c p) n -> p c n", p=128), [], ["Wl"])
        self.dma("pool", G1w, I["rwkv_g1"][j].rearrange("(c p) n -> p c n", p=128), [], ["Wl"])
        if j == 1:
            self.dma("pool", V1w, I["rwkv_v1"][0].rearrange("(c p) n -> p c n", p=128), [], ["Wl"])
        self.ts(OMKA, KAp, -1.0, ALU.mult, ["PAR"], K0, s2=1.0, op1=ALU.add)
        for a, src in ((RSTa, self.RST64), (RSTb, self.RST8)):
            self.cp("dve", a[:, 0:64], src, ["CON"], K0)
            self.cp("dve", a[:, 64:128], src, ["CON"], K0)
        self.cp("dve", XL[:, :, 0:1], X[:, :, 2047:2048], [("X", 1536)], ["XL"])
        self.cp("dve", XL[:, :, 1:17], X[:, :, 2048:2176].rearrange("p c (u t) -> p c u t", u=16)[:, :, :, 7], [("X", 2048)], ["XL"])
        for half in range(2):
            for cc in range(4):
                c = half * 4 + cc
                self.tr(B[6][0:17, cc * 128:(cc + 1) * 128], XL[:, c, :], IDF, ["XL", "CON"], [bk(6)])
            self.cp("act", T[half][0:17, :], B[6][0:17, :], [bk(6)], [("T", half)])
            self.dma("sp", O["shp"][j:j + 1, half * 512:(half + 1) * 512], T[half][0:1, :], [("T", half)], [("sho", half)])
            self.dma("sp", O["shs"][j][:, half * 512:(half + 1) * 512], T[half][1:17, :], [("T", half)], [("shso", half)])
        for half in range(2):
            self.dma("sp", T[half][0:16, :], I["sh0"][j][:, half * 512:(half + 1) * 512], [], [("T", half)])
            for cc in range(4):
                c = half * 4 + cc
                self.tr(B[7][:, c * 16:(c + 1) * 16], T[half][0:16, cc * 128:(cc + 1) * 128], IDF[0:16, 0:16], [("T", half), "CON"], [bk(7)])
        self.cp("dve", SH0, B[7][:, 0:128].rearrange("p (c s) -> p c s", c=8), [bk(7)], ["SH0"])
        self.mset(BDin, 0.0, [], ["BDin"])
        for (t0_, n_) in TBLK:
            self.ts(X[:, :, t0_:t0_ + n_], X[:, :, t0_:t0_ + n_], ALPHA, ALU.mult)

        for hq in range(4):
            cs = slice(hq * 256, (hq + 1) * 256)
            for w, nm in ((WRq, "rwkv_w_r"), (WKq, "rwkv_w_k"), (WVq, "rwkv_w_v")):
                self.dma("pool", w, I[nm][j][:, cs].rearrange("(c p) n -> p c n", p=128), [], ["Wq"])
            self.dma("pool", WOq, I["rwkv_w_o"][j][cs, :].rearrange("(c p) n -> p c n", p=128), [], ["Wq"])
            self.dma("pool", W2q[0:64, :], I["rwkv_w2"][j][:, cs], [], ["Wq"])
            self.dma("pool", A2q[0:64, :], I["rwkv_a2"][j][:, cs], [], ["Wq"])
            if j == 1:
                self.dma("pool", V2q[0:32, :], I["rwkv_v2"][0][:, cs], [], ["Wq"])
            self.dma("pool", G2q, I["rwkv_g2"][j][:, cs], [], ["Wq"])
            self.mset(STp, 0.0, [], ["STp"])
            pcs = slice(2 * hq, 2 * hq + 2)
            bc = lambda p: p[:, pcs].unsqueeze(2).broadcast_to([128, 2, 64])
            def prep(blk):
                bs = blk % 2
                V_, G_, BON, BT, KT, BH, KH, ATRT, GLu = V2[bs], G2[bs], BON2[bs], BT2[bs], KT2[bs], BH2[bs], KH2[bs], ATRT2[bs], GLu2[bs]
                AT, RT = ATRT[:, 0], ATRT[:, 1]
                samp = blk >= 32
                g0 = 64 * blk
                tb0 = (g0 // 512) * 512 if g0 < 2048 else 2048
                xbb = XB[:, :, g0:g0 + 64]
                xk = ("XB", tb0)
                kb_ = ["blk"]
                if samp:
                    sb = blk - 32
                    xbv = xbb.rearrange("p c (u t) -> p c u t", u=8)
                    xxv = XX.rearrange("p c (u t) -> p c u t", u=8)
                    self.tt(xxv[:, :, :, 1:8], xbv[:, :, :, 0:7], xbv[:, :, :, 1:8], ALU.subtract, [xk], kb_)
                    self.tt(xxv[:, :, :, 0], SH0[:, :, 8 * sb:8 * sb + 8], xbv[:, :, :, 0], ALU.subtract, [xk, "SH0"], kb_)
                elif blk == 0:
                    self.tt(XX[:, :, 1:64], XB[:, :, 0:63], XB[:, :, 1:64], ALU.subtract, [xk], kb_)
                    self.ts(XX[:, :, 0:1], XB[:, :, 0:1], -1.0, ALU.mult, [xk], kb_)
                else:
                    pk_ = ("XB", ((g0 - 1) // 512) * 512)
                    self.tt(XX, XB[:, :, g0 - 1:g0 + 63], xbb, ALU.subtract, [xk, pk_], kb_)
                for m in range(6):
                    self.tt(TMX, XX, MU[:, m, :].unsqueeze(2).broadcast_to([128, 8, 64]), ALU.mult, kb_ + ["PAR"], kb_)
                    self.tt(XM[:, m], TMX, xbb, ALU.add, kb_ + [xk], kb_)
                for n_, (w, m) in enumerate(((WRq, 0), (WKq, 2), (WVq, 3))):
                    for c2 in range(2):
                        for c in range(8):
                            self.mm(B[0][:, n_ * 128 + c2 * 64:n_ * 128 + c2 * 64 + 64], w[:, c, c2 * 128:(c2 + 1) * 128], XM[:, m, c, :], c == 0, c == 7, kb_ + ["Wq"], [bk(0)])
                self.cp("act", R_, q3(B[0][:, 0:128]), [bk(0)], kb_)
                self.cp("dve", K_, q3(B[0][:, 128:256]), [bk(0)], kb_)
                self.cp("act", V_, q3(B[0][:, 256:384]), [bk(0)], kb_)
                for c in range(8):
                    self.mm(B[1][0:64, 0:64], W1[:, c, :], XM[:, 1, c, :], c == 0, c == 7, kb_ + ["Wl"], [bk(1)])
                for c in range(8):
                    self.mm(B[1][0:64, 64:128], A1w[:, c, :], XM[:, 4, c, :], c == 0, c == 7, kb_ + ["Wl"], [bk(1)])
                if j == 1:
                    for c in range(8):
                        self.mm(B[1][0:32, 128:192], V1w[:, c, :], XM[:, 3, c, :], c == 0, c == 7, kb_ + ["Wl"], [bk(1)])
                for c in range(8):
                    self.mm(B[1][:, 192:256], G1w[:, c, :], XM[:, 5, c, :], c == 0, c == 7, kb_ + ["Wl"], [bk(1)])
                self.act(T1b[0:64, :], B[1][0:64, 0:64], AF.Tanh, [bk(1)], kb_)
                self.cp("act", A1b[0:64, :], B[1][0:64, 64:128], [bk(1)], kb_)
                if j == 1:
                    self.cp("act", V1b[0:32, :], B[1][0:32, 128:192], [bk(1)], kb_)
                self.act(G1b, B[1][:, 192:256], AF.Sigmoid, [bk(1)], kb_)
                for c2 in range(2):
                    self.mm(B[2][:, c2 * 64:c2 * 64 + 64], W2q[0:64, c2 * 128:(c2 + 1) * 128], T1b[0:64, :], True, True, kb_ + ["Wq"], [bk(2)])
                    self.mm(B[2][:, 128 + c2 * 64:128 + c2 * 64 + 64], A2q[0:64, c2 * 128:(c2 + 1) * 128], A1b[0:64, :], True, True, kb_ + ["Wq"], [bk(2)])
                    if j == 1:
                        self.mm(B[2][:, 256 + c2 * 64:256 + c2 * 64 + 64], V2q[0:32, c2 * 128:(c2 + 1) * 128], V1b[0:32, :], True, True, kb_ + ["Wq"], [bk(2)])
                    self.mm(B[2][:, 384 + c2 * 64:384 + c2 * 64 + 64], G2q[:, c2 * 128:(c2 + 1) * 128], G1b, True, True, kb_ + ["Wq"], [bk(2)])
                kp = kb_ + ["PAR"]
                self.tt(LW, q3(B[2][:, 0:128]), bc(W0), ALU.add, kp + [bk(2)], kb_)
                self.act(LW, LW, AF.Sigmoid, kb_, kb_)
                self.ts(LW, LW, -0.6065306597126334, ALU.mult, kb_, kb_)
                self.tt(A_, q3(B[2][:, 128:256]), bc(A0), ALU.add, kp + [bk(2)], kb_)
                self.act(A_, A_, AF.Sigmoid, kb_, kb_)
                self.cp("dve", G_, q3(B[2][:, 384:512]), [bk(2)], kb_)
                vfd = self.VF[:, 2 * hq:2 * hq + 2, g0:g0 + 64]
                if j == 1:
                    self.tt(T1t, q3(B[2][:, 256:384]), bc(V0p), ALU.add, kp + [bk(2)], kb_)
                    self.act(T1t, T1t, AF.Sigmoid, kb_, kb_)
                    self.dma("sp", VFt, vfd, kb_, kb_)
                    self.tt(VFt, VFt, V_, ALU.subtract, kb_, kb_)
                    self.tt(VFt, VFt, T1t, ALU.mult, kb_, kb_)
                    self.tt(V_, V_, VFt, ALU.add, kb_, kb_)
                else:
                    self.dma("sp", vfd, V_, kb_, [("vf", hq, blk)])
                self.tt(KAP, K_, bc(KKp), ALU.mult, kp, kb_)
                self.tt(T1t, KAP, KAP, ALU.mult, kb_, kb_)
                for c2 in range(2):
                    self.mm(B[3][:, c2 * 64:c2 * 64 + 64], BLK, T1t[:, c2, :], True, True, kb_ + ["CON"], [bk(3)])
                self.act(SSn, q3(B[3][:, 0:128]), AF.Sqrt, [bk(3)], kb_)
                self.ts(SSn, SSn, 1e-12, ALU.max, kb_, kb_)
                self.rcp(SSn, SSn, kb_, kb_)
                self.tt(KAP, KAP, SSn, ALU.mult, kb_, kb_)
                self.tt(T1t, A_, bc(KAp), ALU.mult, kp, kb_)
                self.tt(T1t, T1t, OMKA[:, pcs].unsqueeze(2).broadcast_to([128, 2, 64]), ALU.add, kb_ + K0, kb_)
                self.tt(K_, K_, T1t, ALU.mult, kb_, kb_)
                self.tt(T1t, R_, K_, ALU.mult, kb_, kb_)
                self.tt(T1t, T1t, bc(RKp), ALU.mult, kp, kb_)
                for c2 in range(2):
                    self.mm(B[3][:, 128 + c2 * 64:128 + c2 * 64 + 64], BLK, T1t[:, c2, :], True, True, kb_ + ["CON"], [bk(3)])
                self.tt(BON, q3(B[3][:, 128:256]), V_, ALU.mult, kb_ + [bk(3)], kb_)
                self.scan(LG.rearrange("p c t -> p (c t)"), RSTb if samp else RSTa, LW.rearrange("p c t -> p (c t)"), kb_ + K0, kb_)
                nu = 8 if samp else 1
                L = 8 if samp else 64
                lgv = LG.rearrange("p c (u t) -> p c u t", u=nu)
                self.cp("dve", LGL[:, :, 0:nu], lgv[:, :, :, L - 1], kb_, kb_)
                self.act(GLu[:, :, 0:nu], LGL[:, :, 0:nu], AF.Exp, kb_, kb_)
                self.tt(A_, KAP, A_, ALU.mult, kb_, kb_)
                self.act(E_, LG, AF.Exp, kb_, kb_)
                self.tt(RT, R_, E_, ALU.mult, kb_, kb_)
                self.tt(T1t, LG, LW, ALU.subtract, kb_, kb_)
                self.act(E_, T1t, AF.Exp, kb_, kb_)
                self.stt(AT, KAP, -1.0, E_, ALU.mult, ALU.mult, kb_, kb_)
                self.act(E_, LG, AF.Exp, kb_, kb_, scale=-1.0)
                self.tt(BT, A_, E_, ALU.mult, kb_, kb_)
                self.tt(KT, K_, E_, ALU.mult, kb_, kb_)
                t1v = T1t.rearrange("p c (u t) -> p c u t", u=nu)
                self.tt(t1v, LGL[:, :, 0:nu].unsqueeze(3).broadcast_to([128, 2, nu, L]), lgv, ALU.subtract, kb_, kb_)
                self.act(E_, T1t, AF.Exp, kb_, kb_)
                self.tt(BH, A_, E_, ALU.mult, kb_, kb_)
                self.tt(KH, K_, E_, ALU.mult, kb_, kb_)
                self.cp("act", Vb2[bs], V_, kb_, kb_)

            def tail(blk):
                bs = blk % 2
                V_, G_, BON, BT, KT, BH, KH, ATRT, GLu = V2[bs], G2[bs], BON2[bs], BT2[bs], KT2[bs], BH2[bs], KH2[bs], ATRT2[bs], GLu2[bs]
                AT, RT = ATRT[:, 0], ATRT[:, 1]
                samp = blk >= 32
                g0 = 64 * blk
                tb0 = (g0 // 512) * 512 if g0 < 2048 else 2048
                kb_ = ["blk"]
                kp = kb_ + ["PAR"]
                nu = 8 if samp else 1
                L = 8 if samp else 64
                for u in range(nu):
                    c0 = u * L
                    if samp:
                        sq = (blk - 32) * 8 + u
                        ST = STs
                        for c2 in range(2):
                            for h2 in range(2):
                                hd = 4 * hq + 2 * c2 + h2
                                self.dma("sp", BDin[64 * h2:64 * h2 + 64, c2, 64 * h2:64 * h2 + 64], I["rw0"][j, sq, hd], ["BDin"] + kb_, ["BDin"])
                        for c2 in range(2):
                            self.tr(B[7][:, c2 * 128:(c2 + 1) * 128], BDin[:, c2, :], IDF, ["BDin", "CON"], [bk(7)])
                        self.cp("dve", STs, B[7][:, 0:256].rearrange("p (c n) -> p c n", c=2), [bk(7)], ["ST"])
                    else:
                        ST = STp
                    self.rwkv_unit(c0, L, ST, dict(V_=Vb2[bs], Pb=Pb, STb=STb, BH=BH, KH=KH, AT=AT, RT=RT, ATRT=ATRT, BT=BT, KT=KT, Vt=Vt, BHt=BHt, KHt=KHt, Ut=Ut, Xa=Xa, XTa=XTa, Xb=Xb, XTb=XTb,
                                                     Pm=Pm, Rs=Rs, MBR=MBR, MKA=MKA, MKR=MKR, YBs=YBs, Y_=Y_, GL=GLu[:, :, u]), kb_)
                    last = (blk == 31) or samp
                    if last:
                        for c2 in range(2):
                            self.tr(B[7][:, c2 * 128:(c2 + 1) * 128], ST[:, c2, :], IDF, ["ST", "CON"], [bk(7)])
                        self.cp("dve", BDin, B[7][:, 0:256].rearrange("p (c n) -> p c n", c=2), [bk(7)], ["BDin"])
                        for c2 in range(2):
                            for h2 in range(2):
                                hd = 4 * hq + 2 * c2 + h2
                                dst = O["rws"][j, sq, hd] if samp else O["rwp"][j, hd]
                                self.dma("sp", dst, BDin[64 * h2:64 * h2 + 64, c2, 64 * h2:64 * h2 + 64], ["BDin"], [("rwo", hq, blk, u, c2, h2)])
                for c2 in range(2):
                    self.mm(B[6][:, c2 * 64:c2 * 64 + 64], BLK, Y_[:, c2, :], True, True, kb_ + ["CON"], [bk(6)])
                self.stt(Y_, q3(B[6][:, 0:128]), -1.0 / 64.0, Y_, ALU.mult, ALU.add, kb_ + [bk(6)], kb_)
                self.tt(T1e, Y_, Y_, ALU.mult, kb_, kb_)
                for c2 in range(2):
                    self.mm(B[6][:, 128 + c2 * 64:128 + c2 * 64 + 64], BLK, T1e[:, c2, :], True, True, kb_ + ["CON"], [bk(6)])
                self.act(SSe, q3(B[6][:, 128:256]), AF.Sqrt, [bk(6), "eps"], kb_, bias=self.EPS2, scale=1.0 / 64.0)
                self.rcp(SSe, SSe, kb_, kb_)
                self.tt(Y_, Y_, SSe, ALU.mult, kb_, kb_)
                for c2 in range(2):
                    c = 2 * hq + c2
                    self.ts(Y_[:, c2, :], Y_[:, c2, :], LXG[:, c:c + 1], ALU.mult, kp, kb_, s2=LXB[:, c:c + 1], op1=ALU.add)
                self.tt(Y_, Y_, BON, ALU.add, kb_, kb_)
                self.tt(OB, Y_, G_, ALU.mult, kb_, kb_)
                for oc in range(8):
                    for c2 in range(2):
                        self.mm(B[7][:, oc * 64:(oc + 1) * 64], WOq[:, c2, oc * 128:(oc + 1) * 128], OB[:, c2, :], c2 == 0, c2 == 1, kb_ + ["Wq"], [bk(7)])
                xg = ("X", tb0)
                pso = B[7][:].rearrange("p (c t) -> p c t", c=8)
                self.tt(X[:, :, g0:g0 + 64], X[:, :, g0:g0 + 64], pso, ALU.add, [bk(7), xg], [xg])

            import os
            NB = int(os.environ.get('RW_BLKS', '34'))
            prep(0)
            for blk in range(NB):
                self.stream = []
                tail(blk)
                lt = self.stream
                self.stream = []
                if blk + 1 < NB:
                    prep(blk + 1)
                lp = self.stream
                self.stream = None
                ia = ib = 0
                while ia < len(lt) or ib < len(lp):
                    if ia < len(lt):
                        self._flush(lt[ia]); ia += 1
                    if ib < len(lp):
                        self._flush(lp[ib]); ib += 1
        self.force_auto = False

    def rwkv_unit(self, c0, L, ST, t, kb_):
        PS, IDF = self.PS, self.IDF
        B = PS
        bk = lambda k: ("ps", k)
        ku = kb_ + ["ST"]
        cs = slice(c0, c0 + L)
        nlev = {64: 5, 8: 2}[L]
        for n_, (src, dst, bank, col) in enumerate(((t["V_"], t["Vt"], 4, 0), (t["BH"], t["BHt"], 4, 256), (t["KH"], t["KHt"], 5, 0))):
            Bb = B[bank][:].bitcast(BF16)
            for c2 in range(2):
                self.tr(Bb[0:L, col + c2 * 128:col + (c2 + 1) * 128], src[:, c2, cs], self.IDB, kb_ + ["CON"], [bk(bank)])
            self.cp("act" if n_ % 2 == 0 else "dve", dst[0:L, :], Bb[0:L, col:col + 256], [bk(bank)], ku)
        STb, Pb = t["STb"], t["Pb"]
        self.cp("act", STb, ST, ku, ku)
        ATRT, AT, BT, KT = t["ATRT"], t["AT"], t["BT"], t["KT"]
        for h2 in range(2):
            b0 = 64 * h2
            bs, bn = B[4 + h2], B[6 + h2]
            for c2 in range(2):
                rhs = ATRT[b0:b0 + 64, :, c2, cs]
                self.mm(bs[0:L, c2 * 128:c2 * 128 + 2 * L], BT[b0:b0 + 64, c2, cs], rhs, True, True, ku, [bk(4 + h2)])
                self.mm(bs[0:L, 256 + c2 * 128:256 + c2 * 128 + 2 * L], KT[b0:b0 + 64, c2, cs], rhs, True, True, ku, [bk(4 + h2)])
                self.mm(bn[0:L, c2 * 64:c2 * 64 + L], AT[b0:b0 + 64, c2, cs], BT[b0:b0 + 64, c2, cs], True, True, ku, [bk(6 + h2)])
            v4 = bs[0:L, :].rearrange("p (k x) -> p k x", k=4)
            mlt = self.MLT[0:L, 0:L].unsqueeze(1).broadcast_to([L, 2, L])
            mle = self.MLE[0:L, 0:L].unsqueeze(1).broadcast_to([L, 2, L])
            mgt = self.MGT[0:L, 0:L].unsqueeze(1).broadcast_to([L, 2, L])
            hsel = slice(h2, 4, 2)
            self.tt(t["Xa"][0:L, hsel, 0:L], v4[:, 0:2, 0:L], mlt, ALU.mult, [bk(4 + h2), "CON"], ku)
            self.tt(t["MBR"][0:L, hsel, 0:L], v4[:, 0:2, L:2 * L], mle, ALU.mult, [bk(4 + h2), "CON"], ku)
            self.tt(t["MKA"][0:L, hsel, 0:L], v4[:, 2:4, 0:L], mlt, ALU.mult, [bk(4 + h2), "CON"], ku)
            self.tt(t["MKR"][0:L, hsel, 0:L], v4[:, 2:4, L:2 * L], mle, ALU.mult, [bk(4 + h2), "CON"], ku)
            self.tt(t["XTa"][0:L, hsel, 0:L], bn[0:L, 0:128].rearrange("p (k x) -> p k x", k=2)[:, :, 0:L], mgt, ALU.mult, [bk(6 + h2), "CON"], ku)
        Xc, XTc, Xn, XTn, Pm = t["Xa"], t["XTa"], t["Xb"], t["XTb"], t["Pm"]
        self.tt(Pm[0:L, :, 0:L], Xc[0:L, :, 0:L], IDF[0:L, 0:L].unsqueeze(1).broadcast_to([L, 4, L]), ALU.add, ku + ["CON"], ku)
        self.cp("act", Pb[0:L, :, 0:L], Pm[0:L, :, 0:L], ku, ku)
        for lev in range(nlev):
            lastlev = lev == nlev - 1
            for hl in range(4):
                if not lastlev:
                    self.mm(B[4][0:L, hl * 64:hl * 64 + L], XTc[0:L, hl, 0:L], Xc[0:L, hl, 0:L], True, True, ku, [bk(4)])
                self.mm(B[5][0:L, hl * 64:hl * 64 + L], Xc[0:L, hl, 0:L], XTc[0:L, hl, 0:L], True, True, ku, [bk(5)])
            if not lastlev:
                self.cp("act", Xn[0:L, :, 0:L], B[4][0:L, 0:256].rearrange("p (h x) -> p h x", h=4)[:, :, 0:L], [bk(4)], ku)
            self.cp("dve", XTn[0:L, :, 0:L], B[5][0:L, 0:256].rearrange("p (h x) -> p h x", h=4)[:, :, 0:L], [bk(5)], ku)
            for hl in range(4):
                self.mm(B[6][0:L, hl * 64:hl * 64 + L], XTn[0:L, hl, 0:L], Pb[0:L, hl, 0:L], True, True, ku, [bk(6)])
            self.tt(Pm[0:L, :, 0:L], Pm[0:L, :, 0:L], B[6][0:L, 0:256].rearrange("p (h x) -> p h x", h=4)[:, :, 0:L], ALU.add, ku + [bk(6)], ku)
            self.cp("act", Pb[0:L, :, 0:L], Pm[0:L, :, 0:L], ku, ku)
            Xc, XTc, Xn, XTn = Xn, XTn, Xc, XTc
        Vt, BHt, KHt, Ut, Rs, MKA, MBR, MKR = t["Vt"], t["BHt"], t["KHt"], t["Ut"], t["Rs"], t["MKA"], t["MBR"], t["MKR"]
        for c2 in range(2):
            self.mm(B[7][0:L, c2 * 128:(c2 + 1) * 128], AT[:, c2, cs], STb[:, c2, :], True, False, ku, [bk(7)])
            for h2 in range(2):
                hl = 2 * c2 + h2
                self.mm(B[7][0:L, hl * 64:(hl + 1) * 64], MKA[0:L, hl, 0:L], Vt[0:L, hl * 64:(hl + 1) * 64], False, h2 == 1, ku, [bk(7)])
        self.cp("act", Rs[0:L, :, :], B[7][0:L, 0:256].rearrange("p (h x) -> p h x", h=4), [bk(7)], ku)
        for hl in range(4):
            self.mm(B[4][0:L, hl * 64:(hl + 1) * 64], Pb[0:L, hl, 0:L], Rs[0:L, hl, :], True, True, ku, [bk(4)])
        self.cp("dve", Ut[0:L, :], B[4][0:L, 0:256], [bk(4)], ku)
        for c2 in range(2):
            self.mm(B[5][:, c2 * 64:c2 * 64 + L], STb[:, c2, :], t["RT"][:, c2, cs], True, True, ku, [bk(5)])
        for hl in range(4):
            self.mm(B[6][0:64, hl * 64:hl * 64 + L], Ut[0:L, hl * 64:(hl + 1) * 64], MBR[0:L, hl, 0:L], True, False, ku, [bk(6)])
            self.mm(B[6][0:64, hl * 64:hl * 64 + L], Vt[0:L, hl * 64:(hl + 1) * 64], MKR[0:L, hl, 0:L], False, True, ku, [bk(6)])
        ybv = B[6][0:64, 0:256].rearrange("p (c h x) -> p c h x", c=2, h=2)
        YBs = t["YBs"]
        for h2 in range(2):
            self.cp("act", YBs[64 * h2:64 * h2 + 64, :, 0:L], ybv[:, :, h2, 0:L], [bk(6)], ku)
        self.tt(t["Y_"][:, :, cs], B[5][:, 0:128].rearrange("p (c x) -> p c x", c=2)[:, :, 0:L], YBs[:, :, 0:L], ALU.add, ku + [bk(5)], ku)
        for hl in range(4):
            c2 = hl // 2
            self.mm(B[7][:, hl * 64:(hl + 1) * 64], BHt[0:L, c2 * 128:(c2 + 1) * 128], Ut[0:L, hl * 64:(hl + 1) * 64], True, False, ku, [bk(7)])
            self.mm(B[7][:, hl * 64:(hl + 1) * 64], KHt[0:L, c2 * 128:(c2 + 1) * 128], Vt[0:L, hl * 64:(hl + 1) * 64], False, True, ku, [bk(7)])
        suv = B[7][:, 0:256].rearrange("p (c h x) -> p c h x", c=2, h=2)
        GL = t["GL"]
        for h2 in range(2):
            r = slice(64 * h2, 64 * h2 + 64)
            blkv = ST[r, :, 64 * h2:64 * h2 + 64]
            self.tt(blkv, blkv, GL[r, :].unsqueeze(2).broadcast_to([64, 2, 64]), ALU.mult, ku, ku)
            self.tt(blkv, blkv, suv[r, :, h2, :], ALU.add, ku + [bk(7)], ku)

    def build(self):
        self.setup()
        self.phase_input()
        for i in range(DEPTH):
            j = i // 2
            if i % 2 == 0:
                self.s5_layer(i, j)
            else:
                self.rwkv_layer(i, j)
            self.layer_norm(i, 0)
            self.P.emit()
            self.xa_layer(i)
            self.layer_norm(i, 1)
            self.P.emit()
            self.mlp_layer(i)
            self.layer_norm(i, 2)
            self.P.emit()
        self.phase_output()
        return self.nc


WSHAPES = (("s5_w_glu_v", [2, D, D]), ("s5_w_glu_g", [2, D, D]), ("rwkv_w_r", [2, D, D]), ("rwkv_w_k", [2, D, D]),
           ("rwkv_w_v", [2, D, D]), ("rwkv_w_o", [2, D, D]), ("rwkv_w1", [2, D, 64]), ("rwkv_w2", [2, 64, D]),
           ("rwkv_a1", [2, D, 64]), ("rwkv_a2", [2, 64, D]), ("rwkv_v1", [1, D, 32]), ("rwkv_v2", [1, 32, D]),
           ("rwkv_g1", [2, D, 128]), ("rwkv_g2", [2, 128, D]), ("xa_w_q", [4, D, D]), ("xa_w_k", [4, D, D]),
           ("xa_w_v", [4, D, D]), ("xa_w_o", [4, D, D]), ("mlp_w1", [4, D, 4 * D]), ("mlp_w2", [4, 4 * D, D]))

_CACHE = {}


def kernel(**inp):
    inp = {k: np.asarray(v) for k, v in inp.items()}
    par, pcols = pack_params(inp)
    npar = par.shape[1]
    if "nc" not in _CACHE:
        kb = KB(pcols, npar)
        _CACHE["nc"] = kb.build()
        _CACHE["names"] = [k if isinstance(k, str) else f"{k[0]}{k[1]}" for k in kb.I.keys()]
    nc = _CACHE["nc"]
    consts = make_consts()
    s5m = [pack_s5_mats(inp, j) for j in range(2)]
    in_maps = []
    for cid in range(8):
        sl = slice(cid * NSEQ, (cid + 1) * NSEQ)
        m = {}
        m["xin"] = np.ascontiguousarray(np.concatenate([inp["x_prompt"][cid], inp["x_sample"][sl].reshape(TS, D)], axis=0))
        m["mem"] = np.ascontiguousarray(inp["mem_prompt"][cid])
        m["ck"] = np.ascontiguousarray(inp["cache_mem_k"][:, sl].reshape(4, NSEQ, 256, D))
        m["cv"] = np.ascontiguousarray(inp["cache_mem_v"][:, sl].reshape(4, NSEQ, 256, D))
        m["s5re0"] = np.ascontiguousarray(inp["state_s5_re"][:, sl].reshape(2, NSEQ, 4096))
        m["s5im0"] = np.ascontiguousarray(inp["state_s5_im"][:, sl].reshape(2, NSEQ, 4096))
        m["rw0"] = np.ascontiguousarray(inp["state_rwkv"][:, sl])
        m["sh0"] = np.ascontiguousarray(inp["state_shift"][:, sl])
        m["par"] = par
        m["consts"] = consts
        for j in range(2):
            for k in ("bbr", "bbi", "cpr", "cpi"):
                m[f"{k}{j}"] = s5m[j][k]
        for nm, _ in WSHAPES:
            m[nm] = inp[nm]
        in_maps.append(m)
    declared = set(_CACHE["names"])
    in_maps = [{k: v for k, v in m.items() if k in declared} for m in in_maps]
    res = run_bass_kernel_spmd(nc, in_maps, core_ids=list(range(8)))
    R = res.results
    f32 = np.float32
    y_prompt = np.stack([R[c]["y"][:TP] for c in range(8)]).astype(f32)
    y_sample = np.concatenate([R[c]["y"][TP:].reshape(NSEQ, 8, D) for c in range(8)]).astype(f32)
    memk = np.stack([R[c]["memk"] for c in range(8)], axis=1).reshape(4, 8, 256, 4, 256).astype(f32)
    memv = np.stack([R[c]["memv"] for c in range(8)], axis=1).reshape(4, 8, 256, 4, 256).astype(f32)
    s5pre = np.stack([R[c]["s5pre"] for c in range(8)], axis=1).reshape(2, 8, 64, 64).astype(f32)
    s5pim = np.stack([R[c]["s5pim"] for c in range(8)], axis=1).reshape(2, 8, 64, 64).astype(f32)
    rwp = np.stack([R[c]["rwp"] for c in range(8)], axis=1).astype(f32)
    shp = np.stack([R[c]["shp"] for c in range(8)], axis=1).astype(f32)
    s5sre = np.concatenate([R[c]["s5sre"] for c in range(8)], axis=1).reshape(2, 128, 64, 64).astype(f32)
    s5sim = np.concatenate([R[c]["s5sim"] for c in range(8)], axis=1).reshape(2, 128, 64, 64).astype(f32)
    rws = np.concatenate([R[c]["rws"] for c in range(8)], axis=1).astype(f32)
    shs = np.concatenate([R[c]["shs"] for c in range(8)], axis=1).astype(f32)
    return (y_prompt, y_sample, memk, memv, s5pre, s5pim, rwp, shp, s5sre, s5sim, rws, shs)
```

```python
import math
import numpy as np
import concourse.bass as bass
import concourse.mybir as mybir
from concourse.bass_utils import run_bass_kernel_spmd

F32 = mybir.dt.float32
BF16 = mybir.dt.bfloat16
I32 = mybir.dt.int32
AF = mybir.ActivationFunctionType
ALU = mybir.AluOpType
AX = mybir.AxisListType

D = 1024
DEPTH = 4
TP = 2048
TS = 128
TT = TP + TS
NSEQ = 16
ALPHA = (2.0 * DEPTH) ** 0.25
LN_EPS = 1e-5
GN_EPS = 64e-5
TBLK = [(0, 512), (512, 512), (1024, 512), (1536, 512), (2048, 128)]
STUB_RWKV = False

ENGS = ("pe", "act", "dve", "pool", "sp")
SEM_ROLL = 20000
SAME_ENGINE_WAIT = True
N_DMA_SEMS = 24


class Prog:
    def __init__(self, nc):
        self.nc = nc
        self.ops = {e: [] for e in ENGS}
        self.sems = {e: [nc.alloc_semaphore(f"s_{e}_0")] for e in ENGS}
        self.own_sems = {e: {id(self.sems[e][0])} for e in ENGS}
        self.cnt = {e: 0 for e in ENGS}
        self.known = {e: {} for e in ENGS}
        self.last_w = {}
        self.readers = {}
        self.dma_sems = [nc.alloc_semaphore(f"s_dma_{i}") for i in range(N_DMA_SEMS)]
        self.dma_cnt = [0] * N_DMA_SEMS
        self.dma_rr = 0
        self.n_ins = 0

    def _tok(self, eng):
        if self.cnt[eng] >= SEM_ROLL:
            self.sems[eng].append(self.nc.alloc_semaphore(f"s_{eng}_{len(self.sems[eng])}"))
            self.own_sems[eng].add(id(self.sems[eng][-1]))
            self.cnt[eng] = 0
        self.cnt[eng] += 1
        return (self.sems[eng][-1], self.cnt[eng])

    def _deps(self, eng, reads, writes):
        toks = []
        for k in reads:
            t = self.last_w.get(k)
            if t is not None:
                toks.append(t)
        for k in writes:
            t = self.last_w.get(k)
            if t is not None:
                toks.append(t)
            toks.extend(self.readers.get(k, ()))
        waits = {}
        kn = self.known[eng]
        own = self.own_sems[eng]
        for (sem, val) in toks:
            key = id(sem)
            if not SAME_ENGINE_WAIT and key in own:
                continue
            if kn.get(key, (None, 0))[1] >= val:
                continue
            if key not in waits or waits[key][1] < val:
                waits[key] = (sem, val)
        for key, sv in waits.items():
            kn[key] = sv
        return list(waits.values())

    def _commit(self, tok, reads, writes):
        for k in reads:
            self.readers.setdefault(k, []).append(tok)
        for k in writes:
            self.last_w[k] = tok
            self.readers[k] = []

    @staticmethod
    def _excl(reads, writes):
        r2 = [k for k in reads if not (isinstance(k, tuple) and k[0] == "ps")]
        w2 = list(writes) + [k for k in reads if isinstance(k, tuple) and k[0] == "ps"]
        return r2, w2

    def op(self, eng, fn, reads=(), writes=()):
        reads, writes = self._excl(reads, writes)
        waits = self._deps(eng, reads, writes)
        tok = self._tok(eng)
        self.ops[eng].append((waits, fn, tok[0], 1))
        self._commit(tok, reads, writes)
        self.n_ins += 1
        return tok

    def dma(self, eng, fn, reads=(), writes=()):
        i = self.dma_rr
        self.dma_rr = (self.dma_rr + 1) % N_DMA_SEMS
        sem = self.dma_sems[i]
        waits = self._deps(eng, reads, writes)
        prev = self.dma_cnt[i]
        if prev > 0:
            key = id(sem)
            if self.known[eng].get(key, (None, 0))[1] < prev:
                waits.append((sem, prev))
                self.known[eng][key] = (sem, prev)
        self.dma_cnt[i] += 16
        tok = (sem, self.dma_cnt[i])
        self.ops[eng].append((waits, fn, sem, 16))
        self._commit(tok, reads, writes)
        self.n_ins += 1
        return tok

    def drain_dmas(self, eng="sp"):
        waits = []
        for i, sem in enumerate(self.dma_sems):
            if self.dma_cnt[i] > 0 and self.known[eng].get(id(sem), (None, 0))[1] < self.dma_cnt[i]:
                waits.append((sem, self.dma_cnt[i]))
                self.known[eng][id(sem)] = (sem, self.dma_cnt[i])
        self.ops[eng].append((waits, None, None, 0))

    def emit(self):
        nc = self.nc
        self.drain_dmas("sp")
        emap = {"pe": "tensor", "act": "scalar", "dve": "vector", "pool": "gpsimd", "sp": "sync"}
        with nc.Block() as block:
            for e in ENGS:
                ops = self.ops[e]

                def body(engine, ops=ops):
                    for (waits, fn, sem, amt) in ops:
                        for (s, v) in waits:
                            engine.wait_ge(s, v)
                        if fn is not None:
                            fn(engine).then_inc(sem, amt)

                getattr(block, emap[e])(body)
        self.ops = {e: [] for e in ENGS}
        self.last_w = {}
        self.readers = {}


def _fm(v):
    return np.ascontiguousarray(np.asarray(v).reshape(8, 128).T)


def pack_params(inp):
    cols = {}
    mats = []

    def add(name, arr):
        cols[name] = sum(m.shape[1] for m in mats)
        mats.append(np.asarray(arr, dtype=np.float32))

    for i in range(4):
        for k in range(3):
            add(("lng", i, k), _fm(inp["ln_g"][i, k]))
            add(("lnb", i, k), _fm(inp["ln_b"][i, k]))
    for j in range(2):
        add(("s5d", j), _fm(inp["s5_d"][j]))
        add(("mu", j), np.concatenate([_fm(inp["rwkv_mu"][j, m]) for m in range(6)], axis=1))
        for nm in ("rwkv_w0", "rwkv_a0", "rwkv_k_k", "rwkv_k_a", "rwkv_lnx_g", "rwkv_lnx_b"):
            add((nm, j), _fm(inp[nm][j]))
        add(("rwkv_r_k", j), _fm(inp["rwkv_r_k"][j].reshape(-1)))
    add(("rwkv_v0", 1), _fm(inp["rwkv_v0"][0]))
    for j in range(2):
        for nm in ("s5_a_re", "s5_a_im"):
            a = inp[nm][j].reshape(32, 2, 64)
            add((nm, j), np.ascontiguousarray(a.transpose(1, 2, 0).reshape(128, 32)))
        ld = np.repeat(inp["s5_log_dt"][j].reshape(32, 2, 1), 64, axis=2)
        add(("s5_log_dt", j), np.ascontiguousarray(ld.transpose(1, 2, 0).reshape(128, 32)))
    return np.ascontiguousarray(np.concatenate(mats, axis=1)), cols


def pack_s5_mats(inp, j):
    out = {}
    for nm, key in (("s5_b_re", "bbr"), ("s5_b_im", "bbi")):
        b = inp[nm][j]
        bb = np.zeros((8, 16, 8, 4, 2, 64), np.float32)
        for k in range(8):
            for q in range(4):
                for hf in range(2):
                    g8 = 2 * q + hf
                    bb[g8, :, k, q, hf, :] = b[8 * k + g8].T
        out[key] = bb.reshape(128, 32 * 128)
    for nm, key in (("s5_c_re", "cpr"), ("s5_c_im", "cpi")):
        c = inp[nm][j]
        cp = np.zeros((2, 64, 32, 128), np.float32)
        for tile in range(32):
            for hf in range(2):
                c0 = (tile % 4) * 32 + hf * 16
                cp[hf, :, tile, c0:c0 + 16] = c[2 * tile + hf].T
        out[key] = cp.reshape(128, 32 * 128)
    return out


def make_consts():
    c = np.zeros((128, 1024), np.float32)
    c[:, 0:128] = np.eye(128)
    c[0:64, 128:192] = 1.0
    c[64:128, 192:256] = 1.0
    s = np.arange(64)
    c[0:64, 256:320] = (s[:, None] < s[None, :])
    c[0:64, 320:384] = (s[:, None] <= s[None, :])
    c[0:64, 384:448] = (s[:, None] > s[None, :])
    c[:, 448:512] = 1.0
    c[:, 448] = 0.0
    c[:, 512:576] = 1.0
    c[:, 512:576:8] = 0.0
    c[:, 576:704] = 1.0 / 1024.0
    return c


class KB:
    def __init__(self, pcols, npar):
        nc = bass.Bass("TRN2", target_bir_lowering=False)
        self.nc = nc
        self.P = Prog(nc)
        self.pc = pcols
        self.npar = npar
        self.force_auto = False
        self.regions = []
        self.stream = None

    def _tblocks(self, col, ap, width):
        t_lo = col % width
        ext = 1
        for (st, cnt) in list(ap.ap)[1:]:
            if abs(st) < width:
                ext += (cnt - 1) * abs(st)
        t_hi = t_lo + ext
        return [t0 for (t0, n) in TBLK if t0 < t_hi and t0 + n > t_lo]

    def akeys(self, ap):
        if ap is None or not hasattr(ap, "tensor"):
            return []
        t = ap.tensor
        if type(t).__name__.startswith("DRam"):
            return []
        name = t.name
        if name.startswith("ps"):
            return [("ps", int(name[2:]))]
        dims = list(ap.ap)
        pstride = dims[0][0]
        col = ap.offset % pstride if pstride > 0 else ap.offset
        if name == "X":
            return [("X", tb) for tb in self._tblocks(col, ap, TT)]
        if name == "AR":
            es = 4 if ap.dtype == F32 or ap.dtype == I32 else 2
            ext = 1
            for (st, cnt) in dims[1:]:
                ext += (cnt - 1) * abs(st)
            b0, b1 = col * es, (col + ext) * es
            if b1 <= 17408 * 2:
                if es == 2:
                    return [("XB", tb) for tb in self._tblocks(col, ap, TT)]
                return [("XB", tb) for (tb, n) in TBLK]
            ks = [k for (r0, r1, k) in self.regions if r0 < b1 and r1 > b0]
            return ks if ks else [("AR", b0 // 2048)]
        return [name]

    def _k(self, reads, writes, ins, outs):
        if not self.force_auto:
            return reads, writes
        r, w = [], []
        for a in ins:
            r.extend(self.akeys(a))
        for a in outs:
            w.extend(self.akeys(a))
        return r, w

    def mm(self, out, lhsT, rhs, start, stop, reads=None, writes=None):
        reads, writes = self._k(reads, writes, [lhsT, rhs], [out])
        self._emit("op", "pe", lambda e: e.matmul(out, lhsT=lhsT, rhs=rhs, start=start, stop=stop), reads, writes)

    def tr(self, out, in_, ident, reads=None, writes=None):
        reads, writes = self._k(reads, writes, [in_, ident], [out])
        self._emit("op", "pe", lambda e: e.transpose(out, in_, ident), reads, writes)

    def tt(self, out, in0, in1, op, reads=None, writes=None, eng="dve"):
        reads, writes = self._k(reads, writes, [in0, in1], [out])
        self._emit("op", eng, lambda e: e.tensor_tensor(out=out, in0=in0, in1=in1, op=op), reads, writes)

    def ts(self, out, in0, s1, op0, reads=None, writes=None, s2=None, op1=None, eng="dve"):
        reads, writes = self._k(reads, writes, [in0, s1, s2], [out])
        if op1 is None:
            self._emit("op", eng, lambda e: e.tensor_scalar(out=out, in0=in0, scalar1=s1, scalar2=None, op0=op0), reads, writes)
        else:
            self._emit("op", eng, lambda e: e.tensor_scalar(out=out, in0=in0, scalar1=s1, scalar2=s2, op0=op0, op1=op1), reads, writes)

    def stt(self, out, in0, scalar, in1, op0, op1, reads=None, writes=None):
        reads, writes = self._k(reads, writes, [in0, scalar, in1], [out])
        self._emit("op", "dve", lambda e: e.scalar_tensor_tensor(out=out, in0=in0, scalar=scalar, in1=in1, op0=op0, op1=op1), reads, writes)

    def act(self, out, in_, func, reads=None, writes=None, bias=None, scale=1.0, accum_out=None):
        reads, writes = self._k(reads, writes, [in_, bias], [out, accum_out])
        kw = {}
        if bias is not None:
            kw["bias"] = bias
        if accum_out is not None:
            kw["accum_out"] = accum_out
        self._emit("op", "act", lambda e: e.activation(out=out, in_=in_, func=func, scale=scale, **kw), reads, writes)

    def cp(self, eng, out, in_, reads=None, writes=None):
        reads, writes = self._k(reads, writes, [in_], [out])
        if eng == "act":
            self._emit("op", "act", lambda e: e.copy(out=out, in_=in_), reads, writes)
        else:
            self._emit("op", eng, lambda e: e.tensor_copy(out=out, in_=in_), reads, writes)

    def red(self, out, in_, op, reads=None, writes=None):
        reads, writes = self._k(reads, writes, [in_], [out])
        self._emit("op", "dve", lambda e: e.tensor_reduce(out=out, in_=in_, op=op, axis=AX.X), reads, writes)

    def rcp(self, out, in_, reads=None, writes=None):
        reads, writes = self._k(reads, writes, [in_], [out])
        self._emit("op", "dve", lambda e: e.reciprocal(out=out, in_=in_), reads, writes)

    def mset(self, out, val, reads=None, writes=None, eng="dve"):
        reads, writes = self._k(reads, writes, [], [out])
        self._emit("op", eng, lambda e: e.memset(out, val), reads, writes)

    def scan(self, out, d0, d1, reads=None, writes=None):
        reads, writes = self._k(reads, writes, [d0, d1], [out])
        self._emit("op", "dve", lambda e: e.tensor_tensor_scan(out=out, data0=d0, data1=d1, initial=0.0, op0=ALU.mult, op1=ALU.add), reads, writes)

    def dma(self, eng, out, in_, reads=None, writes=None):
        reads, writes = self._k(reads, writes, [in_], [out])
        self._emit("dma", eng, lambda e: e.dma_start(out=out, in_=in_), reads, writes)

    def _emit(self, kind, eng, fn, reads, writes):
        rec = (kind, eng, fn, reads, writes)
        if self.stream is not None:
            self.stream.append(rec)
        else:
            self._flush(rec)

    def _flush(self, rec):
        kind, eng, fn, reads, writes = rec
        if kind == "op":
            self.P.op(eng, fn, reads, writes)
        else:
            self.P.dma(eng, fn, reads, writes)

    def setup(self):
        nc = self.nc

        def din(name, shape):
            return nc.dram_tensor(name, list(shape), F32, kind="ExternalInput").ap()

        def dout(name, shape):
            return nc.dram_tensor(name, list(shape), F32, kind="ExternalOutput").ap()

        shapes = {"xin": [TT, D], "mem": [256, D], "ck": [4, NSEQ, 256, D], "cv": [4, NSEQ, 256, D], "s5re0": [2, NSEQ, 4096],
                  "s5im0": [2, NSEQ, 4096], "rw0": [2, NSEQ, 16, 64, 64], "sh0": [2, NSEQ, D], "par": [128, self.npar], "consts": [128, 1024]}
        for j in range(2):
            for k in ("bbr", "bbi", "cpr", "cpi"):
                shapes[(k, j)] = [128, 4096]
        for nm, shp in WSHAPES:
            shapes[nm] = shp

        class Lazy(dict):
            def __missing__(d, key):
                name = key if isinstance(key, str) else f"{key[0]}{key[1]}"
                d[key] = din(name, shapes[key])
                return d[key]
        I = Lazy()
        O = {}
        O["y"] = dout("y", [TT, D])
        O["memk"] = dout("memk", [4, 256, D])
        O["memv"] = dout("memv", [4, 256, D])
        O["s5pre"] = dout("s5pre", [2, 4096])
        O["s5pim"] = dout("s5pim", [2, 4096])
        O["rwp"] = dout("rwp", [2, 16, 64, 64])
        O["shp"] = dout("shp", [2, D])
        O["s5sre"] = dout("s5sre", [2, NSEQ, 4096])
        O["s5sim"] = dout("s5sim", [2, NSEQ, 4096])
        O["rws"] = dout("rws", [2, NSEQ, 16, 64, 64])
        O["shs"] = dout("shs", [2, NSEQ, D])
        self.I, self.O = I, O
        self.VF = nc.dram_tensor("vfirst", [128, 8, TT], F32, kind="Internal").ap()

        self.X = nc.alloc_sbuf_tensor("X", [128, 8, TT], F32)
        self.AR = nc.alloc_sbuf_tensor("AR", [128, 51200], BF16)
        self.MEMT = nc.alloc_sbuf_tensor("MEMT", [128, 8, 256], BF16)
        self.PAR = nc.alloc_sbuf_tensor("PAR", [128, self.npar], F32)
        self.CON = nc.alloc_sbuf_tensor("CON", [128, 1024], F32)
        self.CONB = nc.alloc_sbuf_tensor("CONB", [128, 256], BF16)
        self.T = [nc.alloc_sbuf_tensor(f"T{i}", [128, 512], F32) for i in range(3)]
        self.SQ = nc.alloc_sbuf_tensor("SQ", [128, 8, 512], BF16)
        self.RS = nc.alloc_sbuf_tensor("RS", [128, 512], F32)
        self.SM = nc.alloc_sbuf_tensor("SM", [128, 256], F32)
        self.PS = [nc.alloc_psum_tensor(f"ps{i}", [128, 512], F32) for i in range(8)]
        AR = self.AR
        self.XB = AR[:, 0:17408].rearrange("p (c t) -> p c t", c=8)
        self.A1 = AR[:, 17408:34816].rearrange("p (c t) -> p c t", c=8)
        self.WR = [AR[:, 34816 + i * 8192: 34816 + (i + 1) * 8192].rearrange("p (c n) -> p c n", c=8) for i in range(2)]
        CON, CONB = self.CON, self.CONB
        self.IDF = CON[:, 0:128]
        self.BLK = CON[:, 128:256]
        self.MLT = CON[0:64, 256:320]
        self.MLE = CON[0:64, 320:384]
        self.MGT = CON[0:64, 384:448]
        self.RST64 = CON[:, 448:512]
        self.RST8 = CON[:, 512:576]
        self.IDB = CONB[:, 0:128]
        self.ONB = CONB[:, 128:256]
        self.wr_i = 0
        self.bank_i = 0
        self.EPS = self.SM[:, 0:1]
        self.EPS2 = self.SM[:, 1:2]

    def par(self, key, n=8):
        c0 = self.pc[key]
        return self.PAR[:, c0:c0 + n]

    def prefetch(self, tag, dram2d):
        self.prefetched = (tag, self.wload(dram2d))

    def wload(self, dram2d, ncols=1024, tag=None):
        pf = getattr(self, "prefetched", None)
        if tag is not None and pf is not None and pf[0] == tag:
            self.prefetched = None
            return pf[1]
        i = self.wr_i
        self.wr_i ^= 1
        dst = self.WR[i][:, :, 0:ncols]
        src = dram2d.rearrange("(c p) n -> p c n", p=128)
        self.dma("pool", dst, src, [], [("WR", i)])
        return self.WR[i], ("WR", i)

    def next_bank(self):
        i = self.bank_i
        self.bank_i = (i + 1) % 4
        return self.PS[i], ("ps", i)

    def dense(self, w, wkey, src, srckey, evac, n_oc=8):
        big = TBLK[:4]
        for oc in range(n_oc):
            base = (oc % 2) * 4
            for c in range(8):
                for bi, (t0, n) in enumerate(big):
                    self.mm(self.PS[base + bi][:, 0:n], w[:, c, oc * 128:(oc + 1) * 128], src[:, c, t0:t0 + n], c == 0, c == 7, [wkey, (srckey, t0)], [("ps", base + bi)])
            for bi, (t0, n) in enumerate(big):
                evac(self.PS[base + bi], ("ps", base + bi), oc, t0, n)
        (t0, n) = TBLK[4]
        for oc in range(n_oc):
            ps, pk = self.next_bank()
            for c in range(8):
                self.mm(ps[:, 0:n], w[:, c, oc * 128:(oc + 1) * 128], src[:, c, t0:t0 + n], c == 0, c == 7, [wkey, (srckey, t0)], [pk])
            evac(ps, pk, oc, t0, n)

    def layer_norm(self, i, k):
        X, XB, SQ, RS, PS, ONB = self.X, self.XB, self.SQ, self.RS, self.PS, self.ONB
        g = self.par(("lng", i, k))
        b = self.par(("lnb", i, k))
        for (t0, n) in TBLK:
            xb = X[:, :, t0:t0 + n]
            kx = ("X", t0)
            self.cp("act", SQ[:, :, 0:n], xb, [kx], ["SQ"])
            for c in range(8):
                self.mm(PS[4][:, 0:n], ONB, SQ[:, c, 0:n], c == 0, c == 7, ["SQ", "CONB"], [("ps", 4)])
            self.tt(xb, xb, PS[4][:, 0:n].unsqueeze(1).broadcast_to([128, 8, n]), ALU.subtract, [kx, ("ps", 4)], [kx])
            self.act(SQ[:, :, 0:n], xb, AF.Square, [kx], ["SQ"])
            for c in range(8):
                self.mm(PS[5][:, 0:n], ONB, SQ[:, c, 0:n], c == 0, c == 7, ["SQ", "CONB"], [("ps", 5)])
            self.act(RS[:, 0:n], PS[5][:, 0:n], AF.Sqrt, [("ps", 5), "eps"], ["RS"], bias=self.EPS)
            self.rcp(RS[:, 0:n], RS[:, 0:n], ["RS"], ["RS"])
            self.tt(xb, xb, RS[:, 0:n].unsqueeze(1).broadcast_to([128, 8, n]), ALU.mult, [kx, "RS"], [kx])
            for c in range(8):
                self.ts(X[:, c, t0:t0 + n], X[:, c, t0:t0 + n], g[:, c:c + 1], ALU.mult, [kx, "PAR"], [kx], s2=b[:, c:c + 1], op1=ALU.add)
            self.cp("act", XB[:, :, t0:t0 + n], xb, [kx], [("XB", t0)])

    def phase_input(self):
        I, X, XB, AR, PS, IDF = self.I, self.X, self.XB, self.AR, self.PS, self.IDF
        self.dma("sp", self.PAR[:], I["par"], [], ["PAR"])
        self.dma("sp", self.CON[:], I["consts"], [], ["CON"])
        self.cp("dve", self.CONB[:, 0:128], self.CON[:, 0:128], ["CON"], ["CONB"])
        self.cp("dve", self.CONB[:, 128:256], self.CON[:, 576:704], ["CON"], ["CONB"])
        self.mset(self.SM[:, 0:1], LN_EPS, [], ["eps"])
        self.mset(self.SM[:, 1:2], GN_EPS, [], ["eps"])
        STG = [AR[:, 17408 + i * 2048: 17408 + (i + 1) * 2048].bitcast(F32) for i in range(2)]
        import os
        for tt in [int(x) for x in os.environ.get('PI_TILES', ','.join(str(i) for i in range(19))).split(',')]:
            st = STG[tt % 2]
            sk = ("stg", tt % 2)
            if tt < 17:
                self.dma("sp", st, I["xin"][tt * 128:(tt + 1) * 128, :], [], [sk])
            else:
                self.dma("sp", st, I["mem"][(tt - 17) * 128:(tt - 16) * 128, :], [], [sk])
            for cg in range(2):
                ps, pk = self.next_bank()
                for cc in range(4):
                    c = cg * 4 + cc
                    self.tr(ps[:, cc * 128:(cc + 1) * 128], st[:, c * 128:(c + 1) * 128], IDF, [sk, "CON"], [pk])
                psv = ps[:].rearrange("p (a b) -> p a b", a=4)
                if tt < 17:
                    self.cp("dve", X[:, cg * 4:(cg + 1) * 4, tt * 128:(tt + 1) * 128], psv, [pk], [("Xi", tt, cg)])
                    self.cp("act", XB[:, cg * 4:(cg + 1) * 4, tt * 128:(tt + 1) * 128], psv, [pk], [("XBi", tt, cg)])
                else:
                    m0 = (tt - 17) * 128
                    self.cp("act", self.MEMT[:, cg * 4:(cg + 1) * 4, m0:m0 + 128], psv, [pk], ["MEMT"])
        self.P.emit()

    def phase_output(self):
        AR, PS, IDF, X, O = self.AR, self.PS, self.IDF, self.X, self.O
        STG = [AR[:, 17408 + i * 2048: 17408 + (i + 1) * 2048].bitcast(F32) for i in range(2)]
        for tt in range(17):
            st = STG[tt % 2]
            sk = ("stg", tt % 2)
            for cg in range(2):
                ps, pk = self.next_bank()
                for cc in range(4):
                    c = cg * 4 + cc
                    self.tr(ps[:, cc * 128:(cc + 1) * 128], X[:, c, tt * 128:(tt + 1) * 128], IDF, [], [pk])
                self.cp("dve" if cg == 0 else "act", st[:, cg * 512:(cg + 1) * 512], ps[:], [pk], [sk])
            self.dma("sp", O["y"][tt * 128:(tt + 1) * 128, :], st, [sk], [("yout", tt)])
        self.P.emit()

    def mlp_layer(self, i):
        X, XB, A1, T, I = self.X, self.XB, self.A1, self.T, self.I
        cnt = [0]
        for fg in range(4):
            w1, k1 = self.wload(I["mlp_w1"][i][:, fg * 1024:(fg + 1) * 1024], tag=("w1", i) if fg == 0 else None)

            def evac1(ps, pk, oc, t0, n):
                tt_, tk = T[cnt[0] % 2], ("T", cnt[0] % 2)
                cnt[0] += 1
                self.act(tt_[:, 0:n], ps[:, 0:n], AF.Relu, [pk], [tk])
                self.tt(A1[:, oc, t0:t0 + n], tt_[:, 0:n], tt_[:, 0:n], ALU.mult, [tk], [("A1", t0)])
            self.dense(w1, k1, XB, "XB", evac1)
            w2, k2 = self.wload(I["mlp_w2"][i][fg * 1024:(fg + 1) * 1024, :])
            if fg == 0:
                def evac2(ps, pk, oc, t0, n):
                    self.stt(X[:, oc, t0:t0 + n], X[:, oc, t0:t0 + n], ALPHA, ps[:, 0:n], ALU.mult, ALU.add, [pk, ("X", t0)], [("X", t0)])
            else:
                def evac2(ps, pk, oc, t0, n):
                    self.tt(X[:, oc, t0:t0 + n], X[:, oc, t0:t0 + n], ps[:, 0:n], ALU.add, [pk, ("X", t0)], [("X", t0)])
            self.dense(w2, k2, A1, "A1", evac2)

    def xa_layer(self, i):
        X, XB, A1, T, I, O, PS, MEMT, AR, IDB, SM = self.X, self.XB, self.A1, self.T, self.I, self.O, self.PS, self.MEMT, self.AR, self.IDB, self.SM
        wq, kq = self.wload(I["xa_w_q"][i], tag=("q", i))

        def evq(ps, pk, oc, t0, n):
            self.act(A1[:, oc, t0:t0 + n], ps[:, 0:n], AF.Copy, [pk], [("A1", t0)], scale=0.0625)
        self.dense(wq, kq, XB, "XB", evq)
        self.prefetch(("k", i), I["xa_w_k"][i])
        self.P.emit()
        KT = AR[:, 0:2048].rearrange("p (c m) -> p c m", c=8)
        VB = AR[:, 2048:4096].rearrange("p (a n) -> p a n", a=2)
        PNB = AR[:, 4096:5120].rearrange("p (h m) -> p h m", h=4)
        PT = AR[:, 5120:6144].rearrange("p (a t) -> p a t", a=8)
        KS = [AR[:, 6144 + s * 2048: 8192 + s * 2048].rearrange("p (a n) -> p a n", a=2) for s in range(2)]
        VS = [AR[:, 10240 + s * 2048: 12288 + s * 2048].rearrange("p (a n) -> p a n", a=2) for s in range(2)]
        KTS = AR[:, 14336:16384].rearrange("p (c m) -> p c m", c=8)
        PTS = AR[:, 16384:16448].rearrange("p (a t) -> p a t", a=8)
        PEXP = [T[0], T[1]]
        psT = PS[4][:].bitcast(BF16)
        MX, NMX, SUM, RSM = SM[:, 8:12], SM[:, 12:16], SM[:, 16:20], SM[:, 20:24]
        SC = [PS[5], PS[6]]

        wk, kk = self.wload(I["xa_w_k"][i], tag=("k", i))
        for oc in range(8):
            ps, pk = self.next_bank()
            for c in range(8):
                self.mm(ps[:, 0:256], wk[:, c, oc * 128:(oc + 1) * 128], MEMT[:, c, :], c == 0, c == 7, [kk, "MEMT"], [pk])
            self.cp("act", KT[:, oc, :], ps[:, 0:256], [pk], ["KT"])

        def tokmajor(w, wkey, outap, vb):
            for mt in range(2):
                for nb in range(2):
                    ps, pk = self.next_bank()
                    for c in range(8):
                        self.mm(ps[:], MEMT[:, c, mt * 128:(mt + 1) * 128], w[:, c, nb * 512:(nb + 1) * 512], c == 0, c == 7, [wkey, "MEMT"], [pk])
                    self.cp("dve", T[2][:], ps[:], [pk], ["T2"])
                    if vb:
                        self.cp("act", VB[:, mt, nb * 512:(nb + 1) * 512], ps[:], [pk], ["VB"])
                    self.dma("sp", outap[mt * 128:(mt + 1) * 128, nb * 512:(nb + 1) * 512], T[2][:], ["T2"], [("mo", id(outap), mt, nb)])
        tokmajor(wk, kk, O["memk"][i], False)
        wv, kv = self.wload(I["xa_w_v"][i])
        tokmajor(wv, kv, O["memv"][i], True)

        def softmax(np_):
            for b in range(2):
                self.red(MX[0:np_, 2 * b:2 * b + 2], SC[b][0:np_, :].rearrange("p (h m) -> p h m", h=2), ALU.max, [("ps", 5 + b)], ["MX"])
            self.ts(NMX[0:np_, :], MX[0:np_, :], -1.0, ALU.mult, ["MX"], ["NMX"])
            for h in range(4):
                sl = slice((h % 2) * 256, (h % 2) * 256 + 256)
                self.act(PEXP[h // 2][0:np_, sl], SC[h // 2][0:np_, sl], AF.Exp, [("ps", 5 + h // 2), "NMX"], [("PEXP", h), ("SUM", h)],
                         bias=NMX[0:np_, h:h + 1], accum_out=SUM[0:np_, h:h + 1])
            self.rcp(RSM[0:np_, :], SUM[0:np_, :], [("SUM", h) for h in range(4)], ["RSM"])
            for b in range(2):
                self.tt(PNB[0:np_, 2 * b:2 * b + 2, :], PEXP[b][0:np_, :].rearrange("p (h m) -> p h m", h=2),
                        RSM[0:np_, 2 * b:2 * b + 2].unsqueeze(2).broadcast_to([np_, 2, 256]), ALU.mult,
                        [("PEXP", 2 * b), ("PEXP", 2 * b + 1), "RSM"], ["PNB"])

        for tt in range(16):
            t0 = tt * 128
            ak = ("A1", (t0 // 512) * 512)
            for h in range(4):
                for dc in range(2):
                    fc = 2 * h + dc
                    self.mm(SC[h // 2][:, (h % 2) * 256:(h % 2) * 256 + 256], A1[:, fc, t0:t0 + 128], KT[:, fc, :], dc == 0, dc == 1, [ak, "KT"], [("ps", 5 + h // 2)])
            softmax(128)
            for h in range(4):
                for mt in range(2):
                    a = h * 2 + mt
                    self.tr(psT[:, a * 128:(a + 1) * 128], PNB[:, h, mt * 128:(mt + 1) * 128], IDB, ["PNB", "CONB"], [("ps", 4)])
            self.cp("act", PT[:].rearrange("p a t -> p (a t)"), psT[:, 0:1024], [("ps", 4)], ["PT"])
            for fc in range(8):
                h, dc = fc // 2, fc % 2
                bank, bk = (PS[7], ("ps", 7)) if fc >= 4 else (PS[3], ("ps", 3))
                for mt in range(2):
                    self.mm(bank[:, (fc % 4) * 128:(fc % 4) * 128 + 128], VB[:, mt, h * 256 + dc * 128:h * 256 + dc * 128 + 128], PT[:, h * 2 + mt, :],
                            mt == 0, mt == 1, ["VB", "PT"], [bk])
            self.cp("dve", A1[:, 0:4, t0:t0 + 128], PS[3][:].rearrange("p (a t) -> p a t", a=4), [("ps", 3)], [ak])
            self.cp("act", A1[:, 4:8, t0:t0 + 128], PS[7][:].rearrange("p (a t) -> p a t", a=4), [("ps", 7)], [ak])
        for s in range(NSEQ):
            ks, vs = KS[s % 2], VS[s % 2]
            kkey, vkey = ("KS", s % 2), ("VS", s % 2)
            self.dma("pool", ks, I["ck"][i, s].rearrange("(a p) n -> p a n", p=128), [], [kkey])
            self.dma("pool", vs, I["cv"][i, s].rearrange("(a p) n -> p a n", p=128), [], [vkey])
            for mt in range(2):
                for fc in range(8):
                    self.tr(psT[:, fc * 128:(fc + 1) * 128], ks[:, mt, fc * 128:(fc + 1) * 128], IDB, [kkey, "CONB"], [("ps", 4)])
                self.cp("act", KTS[:, :, mt * 128:(mt + 1) * 128], psT[:, 0:1024].rearrange("p (c m) -> p c m", c=8), [("ps", 4)], ["KTS"])
            c0 = TP + 8 * s
            for h in range(4):
                for dc in range(2):
                    fc = 2 * h + dc
                    self.mm(SC[h // 2][0:8, (h % 2) * 256:(h % 2) * 256 + 256], A1[:, fc, c0:c0 + 8], KTS[:, fc, :], dc == 0, dc == 1, [("A1", 2048), "KTS"], [("ps", 5 + h // 2)])
            softmax(8)
            for h in range(4):
                for mt in range(2):
                    a = h * 2 + mt
                    self.tr(psT[:, a * 8:(a + 1) * 8], PNB[0:8, h, mt * 128:(mt + 1) * 128], IDB[0:8, 0:8], ["PNB", "CONB"], [("ps", 4)])
            self.cp("act", PTS[:].rearrange("p a t -> p (a t)"), psT[:, 0:64], [("ps", 4)], ["PTS"])
            for fc in range(8):
                h, dc = fc // 2, fc % 2
                for mt in range(2):
                    self.mm(PS[7][:, fc * 8:fc * 8 + 8], vs[:, mt, h * 256 + dc * 128:h * 256 + dc * 128 + 128], PTS[:, h * 2 + mt, :], mt == 0, mt == 1, [vkey, "PTS"], [("ps", 7)])
            self.cp("dve", A1[:, :, c0:c0 + 8], PS[7][:, 0:64].rearrange("p (a t) -> p a t", a=8), [("ps", 7)], [("A1", 2048)])
        wo, ko = self.wload(I["xa_w_o"][i])

        def evo(ps, pk, oc, t0, n):
            self.stt(X[:, oc, t0:t0 + n], X[:, oc, t0:t0 + n], ALPHA, ps[:, 0:n], ALU.mult, ALU.add, [pk, ("X", t0)], [("X", t0)])
        self.dense(wo, ko, A1, "A1", evo)
        self.prefetch(("w1", i), I["mlp_w1"][i][:, 0:1024])

    def s5_layer(self, i, j):
        X, XB, T, I, O, PS, AR, SM, IDF, RS = self.X, self.XB, self.T, self.I, self.O, self.PS, self.AR, self.SM, self.IDF, self.RS
        B0 = 17408
        BBR = AR[:, B0:B0 + 4096].rearrange("p (t c) -> p t c", t=32)
        BBI = AR[:, B0 + 4096:B0 + 8192].rearrange("p (t c) -> p t c", t=32)
        CRB = AR[:, B0 + 8192:B0 + 12288].rearrange("p (t c) -> p t c", t=32)
        CIN = AR[:, B0 + 12288:B0 + 16384].rearrange("p (t c) -> p t c", t=32)
        BU0 = AR[:, B0 + 16384:B0 + 20480].bitcast(F32).rearrange("p (r t s) -> p r t s", r=2, t=32)
        BU = [BU0, BU0]
        TAB = AR[:, B0 + 20480:B0 + 24576].bitcast(F32).rearrange("p (r t s) -> p r t s", r=2, t=32)
        EC, ES = TAB[:, 0], TAB[:, 1]
        SQF = self.SQ[:].rearrange("p c t -> p (c t)").bitcast(F32)
        TQ4 = SQF[:, 0:1024].rearrange("p (r t s) -> p r t s", r=2, t=32)
        tq = SQF[:, 0:1024].rearrange("p (t s) -> p t s", t=32)
        RH = SQF[:, 1024:2048].rearrange("p (t s) -> p t s", t=32)
        HH = AR[:, B0 + 24576:B0 + 28672].bitcast(F32).rearrange("p (r t s) -> p r t s", r=2, t=32)
        HHB = AR[:, B0 + 28672:B0 + 30720].rearrange("p (r t s) -> p r t s", r=2, t=32)
        SP = AR[:, B0 + 30720:B0 + 33792].bitcast(F32)
        H0 = SP[:, 0:1024].rearrange("p (r t q) -> p r t q", r=2, t=32)

        def sm(k):
            return SP[:, 1024 + 32 * k:1024 + 32 * (k + 1)]
        pair = lambda k: SP[:, 1024 + 32 * k:1024 + 32 * (k + 2)].rearrange("p (r t) -> p r t", r=2)
        C1, C2, HL, M1, M2 = pair(0), pair(2), pair(4), pair(6), pair(8)
        abre, abim = sm(0), sm(1)
        FRE, FIM = sm(10), sm(11)
        dt, ang, mag, sn = sm(12), sm(13), sm(14), sm(15)
        cs, den, nre, r_, m_, kf = [SM[:, 64 + 32 * k:96 + 32 * k] for k in range(6)]
        TI = SM[:, 32:64].bitcast(I32)
        pc = self.pc
        LRE = self.PAR[:, pc[("s5_a_re", j)]:pc[("s5_a_re", j)] + 32]
        LIM = self.PAR[:, pc[("s5_a_im", j)]:pc[("s5_a_im", j)] + 32]
        LDT = self.PAR[:, pc[("s5_log_dt", j)]:pc[("s5_log_dt", j)] + 32]
        K = ["s5p", "PAR"]
        TWO_PI = 2.0 * math.pi
        tt = lambda o, a, b, op: self.tt(o, a, b, op, K, K)
        ts = lambda o, a, s1, op0, s2=None, op1=None: self.ts(o, a, s1, op0, K, K, s2=s2, op1=op1)
        ac = lambda o, a, f, scale=1.0: self.act(o, a, f, K, K, scale=scale)
        ac(dt, LDT, AF.Exp)
        tt(mag, LRE, dt, ALU.mult)
        ac(mag, mag, AF.Exp)
        tt(ang, LIM, dt, ALU.mult)
        ts(kf, ang, 1.0 / TWO_PI, ALU.mult)
        self.cp("dve", TI, kf, K, K)
        self.cp("dve", kf, TI, K, K)
        self.stt(r_, kf, -TWO_PI, ang, ALU.mult, ALU.add, K, K)

        def wrap(x):
            ts(m_, x, math.pi, ALU.is_gt)
            self.stt(x, m_, -TWO_PI, x, ALU.mult, ALU.add, K, K)
            ts(m_, x, -math.pi, ALU.is_lt)
            self.stt(x, m_, TWO_PI, x, ALU.mult, ALU.add, K, K)
        wrap(r_)
        ac(sn, r_, AF.Sin)
        ts(r_, r_, math.pi / 2, ALU.add)
        wrap(r_)
        ac(cs, r_, AF.Sin)
        tt(abre, mag, cs, ALU.mult)
        tt(abim, mag, sn, ALU.mult)
        ts(sm(2), abim, -1.0, ALU.mult)
        self.cp("dve", sm(3), abre, K, K)
        tt(den, LRE, LRE, ALU.mult)
        tt(m_, LIM, LIM, ALU.mult)
        tt(den, den, m_, ALU.add)
        self.rcp(den, den, K, K)
        ts(nre, abre, -1.0, ALU.add)
        tt(FRE, nre, LRE, ALU.mult)
        tt(m_, abim, LIM, ALU.mult)
        tt(FRE, FRE, m_, ALU.add)
        tt(FRE, FRE, den, ALU.mult)
        tt(FIM, abim, LRE, ALU.mult)
        tt(m_, nre, LIM, ALU.mult)
        tt(FIM, FIM, m_, ALU.subtract)
        tt(FIM, FIM, den, ALU.mult)
        KT_ = K + ["TAB", "tq", "RH"]
        self.mset(EC[:, :, 0:1], 1.0, KT_, ["TAB"])
        self.mset(ES[:, :, 0:1], 0.0, KT_, ["TAB"])
        self.cp("dve", EC[:, :, 1], cs, KT_, ["TAB"])
        self.cp("dve", ES[:, :, 1], sn, KT_, ["TAB"])
        for tau in range(2, 32):
            self.tt(EC[:, :, tau], EC[:, :, tau - 1], cs, ALU.mult, KT_, ["TAB"])
            self.tt(dt, ES[:, :, tau - 1], sn, ALU.mult, KT_, K)
            self.tt(EC[:, :, tau], EC[:, :, tau], dt, ALU.subtract, KT_, ["TAB"])
            self.tt(ES[:, :, tau], ES[:, :, tau - 1], cs, ALU.mult, KT_, ["TAB"])
            self.tt(dt, EC[:, :, tau - 1], sn, ALU.mult, KT_, K)
            self.tt(ES[:, :, tau], ES[:, :, tau], dt, ALU.add, KT_, ["TAB"])
        self.cp("dve", RH, mag.unsqueeze(2).broadcast_to([128, 32, 32]), KT_, ["RH"])
        self.mset(RH[:, :, 0:1], 0.0, KT_, ["RH"])
        self.dma("pool", BBR[:].rearrange("p t c -> p (t c)"), I[("bbr", j)], [], ["BB"])
        self.dma("pool", BBI[:].rearrange("p t c -> p (t c)"), I[("bbi", j)], [], ["BB"])
        v3 = lambda a: a[:].rearrange("p (t c) -> p t c", t=4)
        for g in range(8):
            sr, si, tm, r4 = T[0], T[1], T[2], RS
            self.dma("sp", sr[:], I[("cpr", j)][:, g * 512:(g + 1) * 512], [], ["T0"])
            self.dma("sp", si[:], I[("cpi", j)][:, g * 512:(g + 1) * 512], [], ["T1"])
            fre_b = FRE[:, g * 4:(g + 1) * 4].unsqueeze(2).broadcast_to([128, 4, 128])
            fim_b = FIM[:, g * 4:(g + 1) * 4].unsqueeze(2).broadcast_to([128, 4, 128])
            kk = ["T0", "T1", "T2", "RS", "CC"] + K
            self.tt(v3(tm), v3(si), fim_b, ALU.mult, kk, ["T2"])
            self.tt(v3(si), v3(si), fre_b, ALU.mult, kk, ["T1"])
            self.tt(v3(r4), v3(sr), fre_b, ALU.mult, kk, ["RS"])
            self.tt(CRB[:, g * 4:(g + 1) * 4, :], v3(r4), v3(tm), ALU.subtract, kk, ["CC"])
            self.tt(v3(sr), v3(sr), fim_b, ALU.mult, kk, ["T0"])
            self.tt(v3(sr), v3(sr), v3(si), ALU.add, kk, ["T0"])
            self.ts(CIN[:, g * 4:(g + 1) * 4, :], v3(sr), -1.0, ALU.mult, kk, ["CC"])
        for r, nm in enumerate(("s5re0", "s5im0")):
            for half in range(8):
                st, sk = T[half % 2], "T%d" % (half % 2)
                self.dma("sp", st[0:16, :], I[nm][j][:, half * 512:(half + 1) * 512], ["CC"], [sk])
                for q in range(4):
                    tile = half * 4 + q
                    self.tr(PS[0][:, tile * 16:(tile + 1) * 16], st[0:16, q * 128:(q + 1) * 128], IDF[0:16, 0:16], [sk, "CON"], [("ps", 0)])
            self.cp("dve", H0[:, r, :, :], PS[0][:].rearrange("p (t q) -> p t q", t=32), [("ps", 0)], ["H0"])
        fn2, gre, gim = dt, ang, mag
        tt(fn2, FRE, FRE, ALU.mult)
        tt(m_, FIM, FIM, ALU.mult)
        tt(fn2, fn2, m_, ALU.add)
        self.rcp(fn2, fn2, K, K)
        tt(gre, FRE, fn2, ALU.mult)
        tt(gim, FIM, fn2, ALU.mult)
        ts(gim, gim, -1.0, ALU.mult)
        HT = TQ4
        bq = lambda f: f.unsqueeze(2).broadcast_to([128, 32, 16])
        K2 = K + ["H0", "tq"]
        ta, tb = HT[:, 0, :, 0:16], HT[:, 1, :, 0:16]
        self.tt(ta, H0[:, 0], bq(gre), ALU.mult, K2, ["tq"])
        self.tt(tb, H0[:, 1], bq(gim), ALU.mult, K2, ["tq"])
        self.tt(ta, ta, tb, ALU.subtract, K2, ["tq"])
        self.tt(tb, H0[:, 0], bq(gim), ALU.mult, K2, ["tq"])
        self.tt(H0[:, 1], H0[:, 1], bq(gre), ALU.mult, K2, ["H0"])
        self.tt(H0[:, 1], H0[:, 1], tb, ALU.add, K2, ["H0"])
        self.cp("dve", H0[:, 0], ta, K2, ["H0"])

        DPAR = self.par(("s5d", j))
        def front(b):
            bu, bk = BU0, ("BU", 0)
            c0 = b * 32
            tb0 = (c0 // 512) * 512 if c0 < 2048 else 2048
            xk, xk2 = ("XB", tb0), ("X", tb0)
            for r, BB in enumerate((BBR, BBI)):
                for half in range(2):
                    ps, pk = self.next_bank()
                    for tl in range(16):
                        tile = half * 16 + tl
                        self.mm(ps[:, tl * 32:(tl + 1) * 32], BB[:, tile, :], XB[:, tile // 4, c0:c0 + 32], True, True, ["BB", xk], [pk])
                    self.cp("act", bu[:, r, half * 16:(half + 1) * 16, :], ps[:].rearrange("p (t s) -> p t s", t=16), [pk], [bk])
            hk = ["HH", "HL", "m", "H0"] + K
            if b < 64:
                kr = hk + [bk, "TAB", "tq", "RH"]
                br, bi = bu[:, 0], bu[:, 1]
                self.tt(HH[:, 0], EC, br, ALU.mult, kr, ["HH"])
                self.tt(tq, ES, bi, ALU.mult, kr, ["tq"])
                self.tt(HH[:, 0], HH[:, 0], tq, ALU.add, kr, ["HH"])
                self.tt(HH[:, 1], EC, bi, ALU.mult, kr, ["HH"])
                self.tt(tq, ES, br, ALU.mult, kr, ["tq"])
                self.tt(HH[:, 1], HH[:, 1], tq, ALU.subtract, kr, ["HH"])
                if b > 0:
                    self.tt(M1, C1, HL[:, 0:1, :].broadcast_to([128, 2, 32]), ALU.mult, kr, ["m"])
                    self.tt(M2, C2, HL[:, 1:2, :].broadcast_to([128, 2, 32]), ALU.mult, kr, ["m"])
                    self.tt(HH[:, :, :, 0], HH[:, :, :, 0], M1, ALU.add, kr, ["HH"])
                    self.tt(HH[:, :, :, 0], HH[:, :, :, 0], M2, ALU.add, kr, ["HH"])
                for r in range(2):
                    self.scan(bu[:, r].rearrange("p t s -> p (t s)"), RH.rearrange("p t s -> p (t s)"), HH[:, r].rearrange("p t s -> p (t s)"), kr, [bk])
                self.tt(HH[:, 0], EC, br, ALU.mult, kr, ["HH"])
                self.tt(tq, ES, bi, ALU.mult, kr, ["tq"])
                self.tt(HH[:, 0], HH[:, 0], tq, ALU.subtract, kr, ["HH"])
                self.tt(HH[:, 1], ES, br, ALU.mult, kr, ["HH"])
                self.tt(tq, EC, bi, ALU.mult, kr, ["tq"])
                self.tt(HH[:, 1], HH[:, 1], tq, ALU.add, kr, ["HH"])
                self.cp("dve", HL, HH[:, :, :, 31], ["HH"], ["HL"])
            else:
                q0 = (b - 64) * 4
                buv = bu[:].rearrange("p r t (q s) -> p r t q s", q=4)
                hhv = HH[:].rearrange("p r t (q s) -> p r t q s", q=4)
                ob = TQ4
                ok = "tq"
                m1, m2 = ob[:, :, :, 0:4], ob[:, :, :, 4:8]
                c1b = C1.unsqueeze(3).broadcast_to([128, 2, 32, 4])
                c2b = C2.unsqueeze(3).broadcast_to([128, 2, 32, 4])
                for s in range(8):
                    prev = hhv[:, :, :, :, s - 1] if s > 0 else H0[:, :, :, q0:q0 + 4]
                    self.tt(m1, c1b, prev[:, 0:1].broadcast_to([128, 2, 32, 4]), ALU.mult, hk + [ok], [ok])
                    self.tt(m2, c2b, prev[:, 1:2].broadcast_to([128, 2, 32, 4]), ALU.mult, hk + [ok], [ok])
                    self.tt(m1, m1, m2, ALU.add, hk + [ok], [ok])
                    self.tt(hhv[:, :, :, :, s], m1, buv[:, :, :, :, s], ALU.add, hk + [ok, bk], ["HH"])
            if b >= 63:
                nq = 1 if b == 63 else 4
                src = HH[:, :, :, 31:32] if b == 63 else HH[:].rearrange("p r t (q s) -> p r t q s", q=4)[:, :, :, :, 7]
                FT, fk = TQ4, "tq"
                fo, ft = FT[:, :, :, 8:8 + nq], FT[:, :, :, 12:12 + nq]
                fb = lambda f: f.unsqueeze(2).broadcast_to([128, 32, nq])
                rk = ["HH", fk] + K
                self.tt(fo[:, 0], src[:, 0], fb(FRE), ALU.mult, rk, [fk])
                self.tt(ft[:, 0], src[:, 1], fb(FIM), ALU.mult, rk, [fk])
                self.tt(fo[:, 0], fo[:, 0], ft[:, 0], ALU.subtract, rk, [fk])
                self.tt(fo[:, 1], src[:, 0], fb(FIM), ALU.mult, rk, [fk])
                self.tt(ft[:, 1], src[:, 1], fb(FRE), ALU.mult, rk, [fk])
                self.tt(fo[:, 1], fo[:, 1], ft[:, 1], ALU.add, rk, [fk])
                for r in range(2):
                    if b == 63:
                        outn = ("s5pre", "s5pim")[r]
                        self.tr(PS[7][0:32, 0:128], fo[:, r, :, 0], IDF, [fk, "CON"], [("ps", 7)])
                        self.cp("act", T[2][0:32, 0:128], PS[7][0:32, 0:128], [("ps", 7)], ["T2"])
                        self.dma("sp", O[outn][j].rearrange("(t p) -> t p", p=128), T[2][0:32, 0:128], ["T2"], [("so", outn)])
                    else:
                        outn = ("s5sre", "s5sim")[r]
                        q0 = (b - 64) * 4
                        for tg in range(8):
                            for tl in range(4):
                                self.tr(PS[7][0:4, tl * 128:(tl + 1) * 128], fo[:, r, tg * 4 + tl, :], IDF, [fk, "CON"], [("ps", 7)])
                            self.cp("act", T[2][0:4, :], PS[7][0:4, :], [("ps", 7)], ["T2"])
                            self.dma("sp", O[outn][j][q0:q0 + 4, tg * 512:(tg + 1) * 512], T[2][0:4, :], ["T2"], [("so", outn, tg, q0)])

        def back(b):
            c0 = b * 32
            tb0 = (c0 // 512) * 512 if c0 < 2048 else 2048
            xk, xk2 = ("XB", tb0), ("X", tb0)
            self.cp("act", HHB[:], HH[:], ["HH"], ["HHB"])
            for k in range(8):
                n = 0
                for q in range(4):
                    tile = k * 4 + q
                    for r, CC in enumerate((CRB, CIN)):
                        self.mm(PS[6][:, k * 32:(k + 1) * 32], CC[:, tile, :], HHB[:, r, tile, :], n == 0, n == 7, ["CC", "HHB"], [("ps", 6)])
                        n += 1
            yv = PS[6][:, 0:256].rearrange("p (k s) -> p k s", k=8)
            v = T[0][:, 0:256].rearrange("p (k s) -> p k s", k=8)
            t1 = T[1][:, 0:256].rearrange("p (k s) -> p k s", k=8)
            gk = ["T0", "T1", ("ps", 6)]
            self.tt(v, X[:, :, c0:c0 + 32], DPAR.unsqueeze(2).broadcast_to([128, 8, 32]), ALU.mult, gk + [xk2, "PAR"], ["T0"])
            self.tt(v, v, yv, ALU.add, gk, ["T0"])
            self.tt(t1, v, v, ALU.mult, gk, ["T1"])
            self.ts(t1, t1, 0.044715, ALU.mult, gk, ["T1"], s2=1.0, op1=ALU.add)
            self.tt(t1, t1, v, ALU.mult, gk, ["T1"])
            self.act(t1, t1, AF.Tanh, gk, ["T1"], scale=0.7978845608028654)
            self.ts(t1, t1, 1.0, ALU.add, gk, ["T1"], s2=0.5, op1=ALU.mult)
            self.tt(XB[:, :, c0:c0 + 32], t1, v, ALU.mult, gk + [xk], [xk])

        front(0)
        for b in range(64 + 4):
            self.stream = []
            back(b)
            lt = self.stream
            self.stream = []
            if b + 1 < 68:
                front(b + 1)
            lp = self.stream
            self.stream = None
            ia = ib = 0
            while ia < len(lt) or ib < len(lp):
                if ia < len(lt):
                    self._flush(lt[ia]); ia += 1
                if ib < len(lp):
                    self._flush(lp[ib]); ib += 1
        self.P.emit()
        wv, kv = self.wload(I["s5_w_glu_v"][j])
        wg, kg = self.wload(I["s5_w_glu_g"][j])
        big = TBLK[:4]
        cnt = [0]

        def glu_evac(pv, kpv, pg, kpg, oc, t0, n):
            tt_, tk = T[cnt[0] % 2], ("T", cnt[0] % 2)
            cnt[0] += 1
            self.act(tt_[:, 0:n], pg[:, 0:n], AF.Sigmoid, [kpg], [tk])
            self.tt(tt_[:, 0:n], tt_[:, 0:n], pv[:, 0:n], ALU.mult, [kpv, tk], [tk])
            self.stt(X[:, oc, t0:t0 + n], X[:, oc, t0:t0 + n], ALPHA, tt_[:, 0:n], ALU.mult, ALU.add, [tk, ("X", t0)], [("X", t0)])
        for oc in range(8):
            for c in range(8):
                for bi, (t0, n) in enumerate(big):
                    self.mm(PS[bi][:, 0:n], wv[:, c, oc * 128:(oc + 1) * 128], XB[:, c, t0:t0 + n], c == 0, c == 7, [kv, ("XB", t0)], [("ps", bi)])
            for c in range(8):
                for bi, (t0, n) in enumerate(big):
                    self.mm(PS[4 + bi][:, 0:n], wg[:, c, oc * 128:(oc + 1) * 128], XB[:, c, t0:t0 + n], c == 0, c == 7, [kg, ("XB", t0)], [("ps", 4 + bi)])
            for bi, (t0, n) in enumerate(big):
                glu_evac(PS[bi], ("ps", bi), PS[4 + bi], ("ps", 4 + bi), oc, t0, n)
        (t0, n) = TBLK[4]
        for oc in range(8):
            pv, pg = PS[(oc % 2) * 2], PS[1 + (oc % 2) * 2]
            kpv, kpg = ("ps", (oc % 2) * 2), ("ps", 1 + (oc % 2) * 2)
            for c in range(8):
                self.mm(pv[:, 0:n], wv[:, c, oc * 128:(oc + 1) * 128], XB[:, c, t0:t0 + n], c == 0, c == 7, [kv, ("XB", t0)], [kpv])
            for c in range(8):
                self.mm(pg[:, 0:n], wg[:, c, oc * 128:(oc + 1) * 128], XB[:, c, t0:t0 + n], c == 0, c == 7, [kg, ("XB", t0)], [kpg])
            glu_evac(pv, kpv, pg, kpg, oc, t0, n)
        self.prefetch(("q", i), I["xa_w_q"][i])

    def rwkv_layer(self, i, j):
        X, XB, T, I, O, PS, AR, SM, IDF, BLK = self.X, self.XB, self.T, self.I, self.O, self.PS, self.AR, self.SM, self.IDF, self.BLK
        if STUB_RWKV:
            for (t0, n) in TBLK:
                self.ts(X[:, :, t0:t0 + n], X[:, :, t0:t0 + n], ALPHA, ALU.mult, [("X", t0)], [("X", t0)])
            return
        off = [17408]

        self.force_auto = True
        self.regions = []

        def ab(n):
            a = AR[:, off[0]:off[0] + n]
            self.regions.append((off[0] * 2, (off[0] + n) * 2, ("rk", len(self.regions))))
            off[0] += n
            assert off[0] <= 51200, off[0]
            return a

        def af(n):
            return ab(2 * n).bitcast(F32)
        q3 = lambda a: a.rearrange("p (c t) -> p c t", c=2)
        WRq, WKq, WVq = [ab(2048).rearrange("p (c n) -> p c n", c=8) for _ in range(3)]
        WOq = ab(2048).rearrange("p (c n) -> p c n", c=2)
        W1, A1w = [ab(512).rearrange("p (c n) -> p c n", c=8) for _ in range(2)]
        G1w = ab(1024).rearrange("p (c n) -> p c n", c=8)
        V1w = ab(256).rearrange("p (c n) -> p c n", c=8)
        W2q, A2q, V2q, G2q = [ab(256) for _ in range(4)]
        XM = ab(3072).rearrange("p (m c t) -> p m c t", m=6, c=8)
        XX = af(512).rearrange("p (c t) -> p c t", c=8)
        TMX = af(512).rearrange("p (c t) -> p c t", c=8)
        R_, K_, A_, KAP, LW, LG, E_, T1t, Y_, YBs, VFt, T1e, SSe = [q3(af(128)) for _ in range(13)]
        V2, G2, BON2 = [[q3(af(128)) for _ in range(2)] for _ in range(3)]
        BT2, KT2, BH2, KH2, Vb2 = [[q3(ab(128)) for _ in range(2)] for _ in range(5)]
        ATRT2 = [ab(256).rearrange("p (w c t) -> p w c t", w=2, c=2) for _ in range(2)]
        Vt, BHt, KHt, Ut = [ab(256) for _ in range(4)]
        h4 = lambda a: a.rearrange("p (h t) -> p h t", h=4)
        Xa, XTa, Xb, XTb, Rs, Pb = [h4(ab(256)) for _ in range(6)]
        Pm = h4(af(256))
        STb = ab(256).rearrange("p (c n) -> p c n", c=2)
        MBR, MKA, MKR = [h4(ab(256)) for _ in range(3)]
        STp, STs, BDin = [af(256).rearrange("p (c n) -> p c n", c=2) for _ in range(3)]
        OB = q3(ab(128))
        T1b, A1b, V1b, G1b = ab(64), ab(64), ab(64), ab(64)
        SH0 = af(128).rearrange("p (c s) -> p c s", c=8)
        XL = af(136).rearrange("p (c s) -> p c s", c=8)
        OMKA = af(8)
        RSTa, RSTb = af(128), af(128)
        GLu2 = [af(16).rearrange("p (c u) -> p c u", c=2) for _ in range(2)]
        LGL = af(16).rearrange("p (c u) -> p c u", c=2)
        SSn = q3(af(128))
        pc = self.pc
        MU = self.par(("mu", j), 48).rearrange("p (m c) -> p m c", m=6)
        W0, A0, KKp, KAp, LXG, LXB, RKp = [self.par((nm, j)) for nm in ("rwkv_w0", "rwkv_a0", "rwkv_k_k", "rwkv_k_a", "rwkv_lnx_g", "rwkv_lnx_b", "rwkv_r_k")]
        V0p = self.par(("rwkv_v0", 1))
        B = [PS[k] for k in range(8)]
        bk = lambda k: ("ps", k)
        K0 = ["rk"]

        self.dma("pool", W1, I["rwkv_w1"][j].rearrange("(c p) n -> p c n", p=128), [], ["Wl"])
        self.dma("pool", A1w, I["rwkv_a1"][j].rearrange("(c p) n -> p c n", p=128), [], ["Wl"])
        self.dma("pool", G1w, I["rwkv_g1"][j].rearrange("(c p) n -> p c n", p=128), [], ["Wl"])
        if j == 1:
            self.dma("pool", V1w, I["rwkv_v1"][0].rearrange("(c p) n -> p c n", p=128), [], ["Wl"])
        self.ts(OMKA, KAp, -1.0, ALU.mult, ["PAR"], K0, s2=1.0, op1=ALU.add)
        for a, src in ((RSTa, self.RST64), (RSTb, self.RST8)):
            self.cp("dve", a[:, 0:64], src, ["CON"], K0)
            self.cp("dve", a[:, 64:128], src, ["CON"], K0)
        self.cp("dve", XL[:, :, 0:1], X[:, :, 2047:2048], [("X", 1536)], ["XL"])
        self.cp("dve", XL[:, :, 1:17], X[:, :, 2048:2176].rearrange("p c (u t) -> p c u t", u=16)[:, :, :, 7], [("X", 2048)], ["XL"])
        for half in range(2):
            for cc in range(4):
                c = half * 4 + cc
                self.tr(B[6][0:17, cc * 128:(cc + 1) * 128], XL[:, c, :], IDF, ["XL", "CON"], [bk(6)])
            self.cp("act", T[half][0:17, :], B[6][0:17, :], [bk(6)], [("T", half)])
            self.dma("sp", O["shp"][j:j + 1, half * 512:(half + 1) * 512], T[half][0:1, :], [("T", half)], [("sho", half)])
            self.dma("sp", O["shs"][j][:, half * 512:(half + 1) * 512], T[half][1:17, :], [("T", half)], [("shso", half)])
        for half in range(2):
            self.dma("sp", T[half][0:16, :], I["sh0"][j][:, half * 512:(half + 1) * 512], [], [("T", half)])
            for cc in range(4):
                c = half * 4 + cc
                self.tr(B[7][:, c * 16:(c + 1) * 16], T[half][0:16, cc * 128:(cc + 1) * 128], IDF[0:16, 0:16], [("T", half), "CON"], [bk(7)])
        self.cp("dve", SH0, B[7][:, 0:128].rearrange("p (c s) -> p c s", c=8), [bk(7)], ["SH0"])
        self.mset(BDin, 0.0, [], ["BDin"])
        for (t0_, n_) in TBLK:
            self.ts(X[:, :, t0_:t0_ + n_], X[:, :, t0_:t0_ + n_], ALPHA, ALU.mult)

        for hq in range(4):
            cs = slice(hq * 256, (hq + 1) * 256)
            for w, nm in ((WRq, "rwkv_w_r"), (WKq, "rwkv_w_k"), (WVq, "rwkv_w_v")):
                self.dma("pool", w, I[nm][j][:, cs].rearrange("(c p) n -> p c n", p=128), [], ["Wq"])
            self.dma("pool", WOq, I["rwkv_w_o"][j][cs, :].rearrange("(c p) n -> p c n", p=128), [], ["Wq"])
            self.dma("pool", W2q[0:64, :], I["rwkv_w2"][j][:, cs], [], ["Wq"])
            self.dma("pool", A2q[0:64, :], I["rwkv_a2"][j][:, cs], [], ["Wq"])
            if j == 1:
                self.dma("pool", V2q[0:32, :], I["rwkv_v2"][0][:, cs], [], ["Wq"])
            self.dma("pool", G2q, I["rwkv_g2"][j][:, cs], [], ["Wq"])
            self.mset(STp, 0.0, [], ["STp"])
            pcs = slice(2 * hq, 2 * hq + 2)
            bc = lambda p: p[:, pcs].unsqueeze(2).broadcast_to([128, 2, 64])
            def prep(blk):
                bs = blk % 2
                V_, G_, BON, BT, KT, BH, KH, ATRT, GLu = V2[bs], G2[bs], BON2[bs], BT2[bs], KT2[bs], BH2[bs], KH2[bs], ATRT2[bs], GLu2[bs]
                AT, RT = ATRT[:, 0], ATRT[:, 1]
                samp = blk >= 32
                g0 = 64 * blk
                tb0 = (g0 // 512) * 512 if g0 < 2048 else 2048
                xbb = XB[:, :, g0:g0 + 64]
                xk = ("XB", tb0)
                kb_ = ["blk"]
                if samp:
                    sb = blk - 32
                    xbv = xbb.rearrange("p c (u t) -> p c u t", u=8)
                    xxv = XX.rearrange("p c (u t) -> p c u t", u=8)
                    self.tt(xxv[:, :, :, 1:8], xbv[:, :, :, 0:7], xbv[:, :, :, 1:8], ALU.subtract, [xk], kb_)
                    self.tt(xxv[:, :, :, 0], SH0[:, :, 8 * sb:8 * sb + 8], xbv[:, :, :, 0], ALU.subtract, [xk, "SH0"], kb_)
                elif blk == 0:
                    self.tt(XX[:, :, 1:64], XB[:, :, 0:63], XB[:, :, 1:64], ALU.subtract, [xk], kb_)
                    self.ts(XX[:, :, 0:1], XB[:, :, 0:1], -1.0, ALU.mult, [xk], kb_)
                else:
                    pk_ = ("XB", ((g0 - 1) // 512) * 512)
                    self.tt(XX, XB[:, :, g0 - 1:g0 + 63], xbb, ALU.subtract, [xk, pk_], kb_)
                for m in range(6):
                    self.tt(TMX, XX, MU[:, m, :].unsqueeze(2).broadcast_to([128, 8, 64]), ALU.mult, kb_ + ["PAR"], kb_)
                    self.tt(XM[:, m], TMX, xbb, ALU.add, kb_ + [xk], kb_)
                for n_, (w, m) in enumerate(((WRq, 0), (WKq, 2), (WVq, 3))):
                    for c2 in range(2):
                        for c in range(8):
                            self.mm(B[0][:, n_ * 128 + c2 * 64:n_ * 128 + c2 * 64 + 64], w[:, c, c2 * 128:(c2 + 1) * 128], XM[:, m, c, :], c == 0, c == 7, kb_ + ["Wq"], [bk(0)])
                self.cp("act", R_, q3(B[0][:, 0:128]), [bk(0)], kb_)
                self.cp("dve", K_, q3(B[0][:, 128:256]), [bk(0)], kb_)
                self.cp("act", V_, q3(B[0][:, 256:384]), [bk(0)], kb_)
                for c in range(8):
                    self.mm(B[1][0:64, 0:64], W1[:, c, :], XM[:, 1, c, :], c == 0, c == 7, kb_ + ["Wl"], [bk(1)])
                for c in range(8):
                    self.mm(B[1][0:64, 64:128], A1w[:, c, :], XM[:, 4, c, :], c == 0, c == 7, kb_ + ["Wl"], [bk(1)])
                if j == 1:
                    for c in range(8):
                        self.mm(B[1][0:32, 128:192], V1w[:, c, :], XM[:, 3, c, :], c == 0, c == 7, kb_ + ["Wl"], [bk(1)])
                for c in range(8):
                    self.mm(B[1][:, 192:256], G1w[:, c, :], XM[:, 5, c, :], c == 0, c == 7, kb_ + ["Wl"], [bk(1)])
                self.act(T1b[0:64, :], B[1][0:64, 0:64], AF.Tanh, [bk(1)], kb_)
                self.cp("act", A1b[0:64, :], B[1][0:64, 64:128], [bk(1)], kb_)
                if j == 1:
                    self.cp("act", V1b[0:32, :], B[1][0:32, 128:192], [bk(1)], kb_)
                self.act(G1b, B[1][:, 192:256], AF.Sigmoid, [bk(1)], kb_)
                for c2 in range(2):
                    self.mm(B[2][:, c2 * 64:c2 * 64 + 64], W2q[0:64, c2 * 128:(c2 + 1) * 128], T1b[0:64, :], True, True, kb_ + ["Wq"], [bk(2)])
                    self.mm(B[2][:, 128 + c2 * 64:128 + c2 * 64 + 64], A2q[0:64, c2 * 128:(c2 + 1) * 128], A1b[0:64, :], True, True, kb_ + ["Wq"], [bk(2)])
                    if j == 1:
                        self.mm(B[2][:, 256 + c2 * 64:256 + c2 * 64 + 64], V2q[0:32, c2 * 128:(c2 + 1) * 128], V1b[0:32, :], True, True, kb_ + ["Wq"], [bk(2)])
                    self.mm(B[2][:, 384 + c2 * 64:384 + c2 * 64 + 64], G2q[:, c2 * 128:(c2 + 1) * 128], G1b, True, True, kb_ + ["Wq"], [bk(2)])
                kp = kb_ + ["PAR"]
                self.tt(LW, q3(B[2][:, 0:128]), bc(W0), ALU.add, kp + [bk(2)], kb_)
                self.act(LW, LW, AF.Sigmoid, kb_, kb_)
                self.ts(LW, LW, -0.6065306597126334, ALU.mult, kb_, kb_)
                self.tt(A_, q3(B[2][:, 128:256]), bc(A0), ALU.add, kp + [bk(2)], kb_)
                self.act(A_, A_, AF.Sigmoid, kb_, kb_)
                self.cp("dve", G_, q3(B[2][:, 384:512]), [bk(2)], kb_)
                vfd = self.VF[:, 2 * hq:2 * hq + 2, g0:g0 + 64]
                if j == 1:
                    self.tt(T1t, q3(B[2][:, 256:384]), bc(V0p), ALU.add, kp + [bk(2)], kb_)
                    self.act(T1t, T1t, AF.Sigmoid, kb_, kb_)
                    self.dma("sp", VFt, vfd, kb_, kb_)
                    self.tt(VFt, VFt, V_, ALU.subtract, kb_, kb_)
                    self.tt(VFt, VFt, T1t, ALU.mult, kb_, kb_)
                    self.tt(V_, V_, VFt, ALU.add, kb_, kb_)
                else:
                    self.dma("sp", vfd, V_, kb_, [("vf", hq, blk)])
                self.tt(KAP, K_, bc(KKp), ALU.mult, kp, kb_)
                self.tt(T1t, KAP, KAP, ALU.mult, kb_, kb_)
                for c2 in range(2):
                    self.mm(B[3][:, c2 * 64:c2 * 64 + 64], BLK, T1t[:, c2, :], True, True, kb_ + ["CON"], [bk(3)])
                self.act(SSn, q3(B[3][:, 0:128]), AF.Sqrt, [bk(3)], kb_)
                self.ts(SSn, SSn, 1e-12, ALU.max, kb_, kb_)
                self.rcp(SSn, SSn, kb_, kb_)
                self.tt(KAP, KAP, SSn, ALU.mult, kb_, kb_)
                self.tt(T1t, A_, bc(KAp), ALU.mult, kp, kb_)
                self.tt(T1t, T1t, OMKA[:, pcs].unsqueeze(2).broadcast_to([128, 2, 64]), ALU.add, kb_ + K0, kb_)
                self.tt(K_, K_, T1t, ALU.mult, kb_, kb_)
                self.tt(T1t, R_, K_, ALU.mult, kb_, kb_)
                self.tt(T1t, T1t, bc(RKp), ALU.mult, kp, kb_)
                for c2 in range(2):
                    self.mm(B[3][:, 128 + c2 * 64:128 + c2 * 64 + 64], BLK, T1t[:, c2, :], True, True, kb_ + ["CON"], [bk(3)])
                self.tt(BON, q3(B[3][:, 128:256]), V_, ALU.mult, kb_ + [bk(3)], kb_)
                self.scan(LG.rearrange("p c t -> p (c t)"), RSTb if samp else RSTa, LW.rearrange("p c t -> p (c t)"), kb_ + K0, kb_)
                nu = 8 if samp else 1
                L = 8 if samp else 64
                lgv = LG.rearrange("p c (u t) -> p c u t", u=nu)
                self.cp("dve", LGL[:, :, 0:nu], lgv[:, :, :, L - 1], kb_, kb_)
                self.act(GLu[:, :, 0:nu], LGL[:, :, 0:nu], AF.Exp, kb_, kb_)
                self.tt(A_, KAP, A_, ALU.mult, kb_, kb_)
                self.act(E_, LG, AF.Exp, kb_, kb_)
                self.tt(RT, R_, E_, ALU.mult, kb_, kb_)
                self.tt(T1t, LG, LW, ALU.subtract, kb_, kb_)
                self.act(E_, T1t, AF.Exp, kb_, kb_)
                self.stt(AT, KAP, -1.0, E_, ALU.mult, ALU.mult, kb_, kb_)
                self.act(E_, LG, AF.Exp, kb_, kb_, scale=-1.0)
                self.tt(BT, A_, E_, ALU.mult, kb_, kb_)
                self.tt(KT, K_, E_, ALU.mult, kb_, kb_)
                t1v = T1t.rearrange("p c (u t) -> p c u t", u=nu)
                self.tt(t1v, LGL[:, :, 0:nu].unsqueeze(3).broadcast_to([128, 2, nu, L]), lgv, ALU.subtract, kb_, kb_)
                self.act(E_, T1t, AF.Exp, kb_, kb_)
                self.tt(BH, A_, E_, ALU.mult, kb_, kb_)
                self.tt(KH, K_, E_, ALU.mult, kb_, kb_)
                self.cp("act", Vb2[bs], V_, kb_, kb_)

            def tail(blk):
                bs = blk % 2
                V_, G_, BON, BT, KT, BH, KH, ATRT, GLu = V2[bs], G2[bs], BON2[bs], BT2[bs], KT2[bs], BH2[bs], KH2[bs], ATRT2[bs], GLu2[bs]
                AT, RT = ATRT[:, 0], ATRT[:, 1]
                samp = blk >= 32
                g0 = 64 * blk
                tb0 = (g0 // 512) * 512 if g0 < 2048 else 2048
                kb_ = ["blk"]
                kp = kb_ + ["PAR"]
                nu = 8 if samp else 1
                L = 8 if samp else 64
                for u in range(nu):
                    c0 = u * L
                    if samp:
                        sq = (blk - 32) * 8 + u
                        ST = STs
                        for c2 in range(2):
                            for h2 in range(2):
                                hd = 4 * hq + 2 * c2 + h2
                                self.dma("sp", BDin[64 * h2:64 * h2 + 64, c2, 64 * h2:64 * h2 + 64], I["rw0"][j, sq, hd], ["BDin"] + kb_, ["BDin"])
                        for c2 in range(2):
                            self.tr(B[7][:, c2 * 128:(c2 + 1) * 128], BDin[:, c2, :], IDF, ["BDin", "CON"], [bk(7)])
                        self.cp("dve", STs, B[7][:, 0:256].rearrange("p (c n) -> p c n", c=2), [bk(7)], ["ST"])
                    else:
                        ST = STp
                    self.rwkv_unit(c0, L, ST, dict(V_=Vb2[bs], Pb=Pb, STb=STb, BH=BH, KH=KH, AT=AT, RT=RT, ATRT=ATRT, BT=BT, KT=KT, Vt=Vt, BHt=BHt, KHt=KHt, Ut=Ut, Xa=Xa, XTa=XTa, Xb=Xb, XTb=XTb,
                                                     Pm=Pm, Rs=Rs, MBR=MBR, MKA=MKA, MKR=MKR, YBs=YBs, Y_=Y_, GL=GLu[:, :, u]), kb_)
                    last = (blk == 31) or samp
                    if last:
                        for c2 in range(2):
                            self.tr(B[7][:, c2 * 128:(c2 + 1) * 128], ST[:, c2, :], IDF, ["ST", "CON"], [bk(7)])
                        self.cp("dve", BDin, B[7][:, 0:256].rearrange("p (c n) -> p c n", c=2), [bk(7)], ["BDin"])
                        for c2 in range(2):
                            for h2 in range(2):
                                hd = 4 * hq + 2 * c2 + h2
                                dst = O["rws"][j, sq, hd] if samp else O["rwp"][j, hd]
                                self.dma("sp", dst, BDin[64 * h2:64 * h2 + 64, c2, 64 * h2:64 * h2 + 64], ["BDin"], [("rwo", hq, blk, u, c2, h2)])
                for c2 in range(2):
                    self.mm(B[6][:, c2 * 64:c2 * 64 + 64], BLK, Y_[:, c2, :], True, True, kb_ + ["CON"], [bk(6)])
                self.stt(Y_, q3(B[6][:, 0:128]), -1.0 / 64.0, Y_, ALU.mult, ALU.add, kb_ + [bk(6)], kb_)
                self.tt(T1e, Y_, Y_, ALU.mult, kb_, kb_)
                for c2 in range(2):
                    self.mm(B[6][:, 128 + c2 * 64:128 + c2 * 64 + 64], BLK, T1e[:, c2, :], True, True, kb_ + ["CON"], [bk(6)])
                self.act(SSe, q3(B[6][:, 128:256]), AF.Sqrt, [bk(6), "eps"], kb_, bias=self.EPS2, scale=1.0 / 64.0)
                self.rcp(SSe, SSe, kb_, kb_)
                self.tt(Y_, Y_, SSe, ALU.mult, kb_, kb_)
                for c2 in range(2):
                    c = 2 * hq + c2
                    self.ts(Y_[:, c2, :], Y_[:, c2, :], LXG[:, c:c + 1], ALU.mult, kp, kb_, s2=LXB[:, c:c + 1], op1=ALU.add)
                self.tt(Y_, Y_, BON, ALU.add, kb_, kb_)
                self.tt(OB, Y_, G_, ALU.mult, kb_, kb_)
                for oc in range(8):
                    for c2 in range(2):
                        self.mm(B[7][:, oc * 64:(oc + 1) * 64], WOq[:, c2, oc * 128:(oc + 1) * 128], OB[:, c2, :], c2 == 0, c2 == 1, kb_ + ["Wq"], [bk(7)])
                xg = ("X", tb0)
                pso = B[7][:].rearrange("p (c t) -> p c t", c=8)
                self.tt(X[:, :, g0:g0 + 64], X[:, :, g0:g0 + 64], pso, ALU.add, [bk(7), xg], [xg])

            import os
            NB = int(os.environ.get('RW_BLKS', '34'))
            prep(0)
            for blk in range(NB):
                self.stream = []
                tail(blk)
                lt = self.stream
                self.stream = []
                if blk + 1 < NB:
                    prep(blk + 1)
                lp = self.stream
                self.stream = None
                ia = ib = 0
                while ia < len(lt) or ib < len(lp):
                    if ia < len(lt):
                        self._flush(lt[ia]); ia += 1
                    if ib < len(lp):
                        self._flush(lp[ib]); ib += 1
        self.force_auto = False

    def rwkv_unit(self, c0, L, ST, t, kb_):
        PS, IDF = self.PS, self.IDF
        B = PS
        bk = lambda k: ("ps", k)
        ku = kb_ + ["ST"]
        cs = slice(c0, c0 + L)
        nlev = {64: 5, 8: 2}[L]
        for n_, (src, dst, bank, col) in enumerate(((t["V_"], t["Vt"], 4, 0), (t["BH"], t["BHt"], 4, 256), (t["KH"], t["KHt"], 5, 0))):
            Bb = B[bank][:].bitcast(BF16)
            for c2 in range(2):
                self.tr(Bb[0:L, col + c2 * 128:col + (c2 + 1) * 128], src[:, c2, cs], self.IDB, kb_ + ["CON"], [bk(bank)])
            self.cp("act" if n_ % 2 == 0 else "dve", dst[0:L, :], Bb[0:L, col:col + 256], [bk(bank)], ku)
        STb, Pb = t["STb"], t["Pb"]
        self.cp("act", STb, ST, ku, ku)
        ATRT, AT, BT, KT = t["ATRT"], t["AT"], t["BT"], t["KT"]
        for h2 in range(2):
            b0 = 64 * h2
            bs, bn = B[4 + h2], B[6 + h2]
            for c2 in range(2):
                rhs = ATRT[b0:b0 + 64, :, c2, cs]
                self.mm(bs[0:L, c2 * 128:c2 * 128 + 2 * L], BT[b0:b0 + 64, c2, cs], rhs, True, True, ku, [bk(4 + h2)])
                self.mm(bs[0:L, 256 + c2 * 128:256 + c2 * 128 + 2 * L], KT[b0:b0 + 64, c2, cs], rhs, True, True, ku, [bk(4 + h2)])
                self.mm(bn[0:L, c2 * 64:c2 * 64 + L], AT[b0:b0 + 64, c2, cs], BT[b0:b0 + 64, c2, cs], True, True, ku, [bk(6 + h2)])
            v4 = bs[0:L, :].rearrange("p (k x) -> p k x", k=4)
            mlt = self.MLT[0:L, 0:L].unsqueeze(1).broadcast_to([L, 2, L])
            mle = self.MLE[0:L, 0:L].unsqueeze(1).broadcast_to([L, 2, L])
            mgt = self.MGT[0:L, 0:L].unsqueeze(1).broadcast_to([L, 2, L])
            hsel = slice(h2, 4, 2)
            self.tt(t["Xa"][0:L, hsel, 0:L], v4[:, 0:2, 0:L], mlt, ALU.mult, [bk(4 + h2), "CON"], ku)
            self.tt(t["MBR"][0:L, hsel, 0:L], v4[:, 0:2, L:2 * L], mle, ALU.mult, [bk(4 + h2), "CON"], ku)
            self.tt(t["MKA"][0:L, hsel, 0:L], v4[:, 2:4, 0:L], mlt, ALU.mult, [bk(4 + h2), "CON"], ku)
            self.tt(t["MKR"][0:L, hsel, 0:L], v4[:, 2:4, L:2 * L], mle, ALU.mult, [bk(4 + h2), "CON"], ku)
            self.tt(t["XTa"][0:L, hsel, 0:L], bn[0:L, 0:128].rearrange("p (k x) -> p k x", k=2)[:, :, 0:L], mgt, ALU.mult, [bk(6 + h2), "CON"], ku)
        Xc, XTc, Xn, XTn, Pm = t["Xa"], t["XTa"], t["Xb"], t["XTb"], t["Pm"]
        self.tt(Pm[0:L, :, 0:L], Xc[0:L, :, 0:L], IDF[0:L, 0:L].unsqueeze(1).broadcast_to([L, 4, L]), ALU.add, ku + ["CON"], ku)
        self.cp("act", Pb[0:L, :, 0:L], Pm[0:L, :, 0:L], ku, ku)
        for lev in range(nlev):
            lastlev = lev == nlev - 1
            for hl in range(4):
                if not lastlev:
                    self.mm(B[4][0:L, hl * 64:hl * 64 + L], XTc[0:L, hl, 0:L], Xc[0:L, hl, 0:L], True, True, ku, [bk(4)])
                self.mm(B[5][0:L, hl * 64:hl * 64 + L], Xc[0:L, hl, 0:L], XTc[0:L, hl, 0:L], True, True, ku, [bk(5)])
            if not lastlev:
                self.cp("act", Xn[0:L, :, 0:L], B[4][0:L, 0:256].rearrange("p (h x) -> p h x", h=4)[:, :, 0:L], [bk(4)], ku)
            self.cp("dve", XTn[0:L, :, 0:L], B[5][0:L, 0:256].rearrange("p (h x) -> p h x", h=4)[:, :, 0:L], [bk(5)], ku)
            for hl in range(4):
                self.mm(B[6][0:L, hl * 64:hl * 64 + L], XTn[0:L, hl, 0:L], Pb[0:L, hl, 0:L], True, True, ku, [bk(6)])
            self.tt(Pm[0:L, :, 0:L], Pm[0:L, :, 0:L], B[6][0:L, 0:256].rearrange("p (h x) -> p h x", h=4)[:, :, 0:L], ALU.add, ku + [bk(6)], ku)
            self.cp("act", Pb[0:L, :, 0:L], Pm[0:L, :, 0:L], ku, ku)
            Xc, XTc, Xn, XTn = Xn, XTn, Xc, XTc
        Vt, BHt, KHt, Ut, Rs, MKA, MBR, MKR = t["Vt"], t["BHt"], t["KHt"], t["Ut"], t["Rs"], t["MKA"], t["MBR"], t["MKR"]
        for c2 in range(2):
            self.mm(B[7][0:L, c2 * 128:(c2 + 1) * 128], AT[:, c2, cs], STb[:, c2, :], True, False, ku, [bk(7)])
            for h2 in range(2):
                hl = 2 * c2 + h2
                self.mm(B[7][0:L, hl * 64:(hl + 1) * 64], MKA[0:L, hl, 0:L], Vt[0:L, hl * 64:(hl + 1) * 64], False, h2 == 1, ku, [bk(7)])
        self.cp("act", Rs[0:L, :, :], B[7][0:L, 0:256].rearrange("p (h x) -> p h x", h=4), [bk(7)], ku)
        for hl in range(4):
            self.mm(B[4][0:L, hl * 64:(hl + 1) * 64], Pb[0:L, hl, 0:L], Rs[0:L, hl, :], True, True, ku, [bk(4)])
        self.cp("dve", Ut[0:L, :], B[4][0:L, 0:256], [bk(4)], ku)
        for c2 in range(2):
            self.mm(B[5][:, c2 * 64:c2 * 64 + L], STb[:, c2, :], t["RT"][:, c2, cs], True, True, ku, [bk(5)])
        for hl in range(4):
            self.mm(B[6][0:64, hl * 64:hl * 64 + L], Ut[0:L, hl * 64:(hl + 1) * 64], MBR[0:L, hl, 0:L], True, False, ku, [bk(6)])
            self.mm(B[6][0:64, hl * 64:hl * 64 + L], Vt[0:L, hl * 64:(hl + 1) * 64], MKR[0:L, hl, 0:L], False, True, ku, [bk(6)])
        ybv = B[6][0:64, 0:256].rearrange("p (c h x) -> p c h x", c=2, h=2)
        YBs = t["YBs"]
        for h2 in range(2):
            self.cp("act", YBs[64 * h2:64 * h2 + 64, :, 0:L], ybv[:, :, h2, 0:L], [bk(6)], ku)
        self.tt(t["Y_"][:, :, cs], B[5][:, 0:128].rearrange("p (c x) -> p c x", c=2)[:, :, 0:L], YBs[:, :, 0:L], ALU.add, ku + [bk(5)], ku)
        for hl in range(4):
            c2 = hl // 2
            self.mm(B[7][:, hl * 64:(hl + 1) * 64], BHt[0:L, c2 * 128:(c2 + 1) * 128], Ut[0:L, hl * 64:(hl + 1) * 64], True, False, ku, [bk(7)])
            self.mm(B[7][:, hl * 64:(hl + 1) * 64], KHt[0:L, c2 * 128:(c2 + 1) * 128], Vt[0:L, hl * 64:(hl + 1) * 64], False, True, ku, [bk(7)])
        suv = B[7][:, 0:256].rearrange("p (c h x) -> p c h x", c=2, h=2)
        GL = t["GL"]
        for h2 in range(2):
            r = slice(64 * h2, 64 * h2 + 64)
            blkv = ST[r, :, 64 * h2:64 * h2 + 64]
            self.tt(blkv, blkv, GL[r, :].unsqueeze(2).broadcast_to([64, 2, 64]), ALU.mult, ku, ku)
            self.tt(blkv, blkv, suv[r, :, h2, :], ALU.add, ku + [bk(7)], ku)

    def build(self):
        self.setup()
        self.phase_input()
        for i in range(DEPTH):
            j = i // 2
            if i % 2 == 0:
                self.s5_layer(i, j)
            else:
                self.rwkv_layer(i, j)
            self.layer_norm(i, 0)
            self.P.emit()
            self.xa_layer(i)
            self.layer_norm(i, 1)
            self.P.emit()
            self.mlp_layer(i)
            self.layer_norm(i, 2)
            self.P.emit()
        self.phase_output()
        return self.nc


WSHAPES = (("s5_w_glu_v", [2, D, D]), ("s5_w_glu_g", [2, D, D]), ("rwkv_w_r", [2, D, D]), ("rwkv_w_k", [2, D, D]),
           ("rwkv_w_v", [2, D, D]), ("rwkv_w_o", [2, D, D]), ("rwkv_w1", [2, D, 64]), ("rwkv_w2", [2, 64, D]),
           ("rwkv_a1", [2, D, 64]), ("rwkv_a2", [2, 64, D]), ("rwkv_v1", [1, D, 32]), ("rwkv_v2", [1, 32, D]),
           ("rwkv_g1", [2, D, 128]), ("rwkv_g2", [2, 128, D]), ("xa_w_q", [4, D, D]), ("xa_w_k", [4, D, D]),
           ("xa_w_v", [4, D, D]), ("xa_w_o", [4, D, D]), ("mlp_w1", [4, D, 4 * D]), ("mlp_w2", [4, 4 * D, D]))

_CACHE = {}


def kernel(**inp):
    inp = {k: np.asarray(v) for k, v in inp.items()}
    par, pcols = pack_params(inp)
    npar = par.shape[1]
    if "nc" not in _CACHE:
        kb = KB(pcols, npar)
        _CACHE["nc"] = kb.build()
        _CACHE["names"] = [k if isinstance(k, str) else f"{k[0]}{k[1]}" for k in kb.I.keys()]
    nc = _CACHE["nc"]
    consts = make_consts()
    s5m = [pack_s5_mats(inp, j) for j in range(2)]
    in_maps = []
    for cid in range(8):
        sl = slice(cid * NSEQ, (cid + 1) * NSEQ)
        m = {}
        m["xin"] = np.ascontiguousarray(np.concatenate([inp["x_prompt"][cid], inp["x_sample"][sl].reshape(TS, D)], axis=0))
        m["mem"] = np.ascontiguousarray(inp["mem_prompt"][cid])
        m["ck"] = np.ascontiguousarray(inp["cache_mem_k"][:, sl].reshape(4, NSEQ, 256, D))
        m["cv"] = np.ascontiguousarray(inp["cache_mem_v"][:, sl].reshape(4, NSEQ, 256, D))
        m["s5re0"] = np.ascontiguousarray(inp["state_s5_re"][:, sl].reshape(2, NSEQ, 4096))
        m["s5im0"] = np.ascontiguousarray(inp["state_s5_im"][:, sl].reshape(2, NSEQ, 4096))
        m["rw0"] = np.ascontiguousarray(inp["state_rwkv"][:, sl])
        m["sh0"] = np.ascontiguousarray(inp["state_shift"][:, sl])
        m["par"] = par
        m["consts"] = consts
        for j in range(2):
            for k in ("bbr", "bbi", "cpr", "cpi"):
                m[f"{k}{j}"] = s5m[j][k]
        for nm, _ in WSHAPES:
            m[nm] = inp[nm]
        in_maps.append(m)
    declared = set(_CACHE["names"])
    in_maps = [{k: v for k, v in m.items() if k in declared} for m in in_maps]
    res = run_bass_kernel_spmd(nc, in_maps, core_ids=list(range(8)))
    R = res.results
    f32 = np.float32
    y_prompt = np.stack([R[c]["y"][:TP] for c in range(8)]).astype(f32)
    y_sample = np.concatenate([R[c]["y"][TP:].reshape(NSEQ, 8, D) for c in range(8)]).astype(f32)
    memk = np.stack([R[c]["memk"] for c in range(8)], axis=1).reshape(4, 8, 256, 4, 256).astype(f32)
    memv = np.stack([R[c]["memv"] for c in range(8)], axis=1).reshape(4, 8, 256, 4, 256).astype(f32)
    s5pre = np.stack([R[c]["s5pre"] for c in range(8)], axis=1).reshape(2, 8, 64, 64).astype(f32)
    s5pim = np.stack([R[c]["s5pim"] for c in range(8)], axis=1).reshape(2, 8, 64, 64).astype(f32)
    rwp = np.stack([R[c]["rwp"] for c in range(8)], axis=1).astype(f32)
    shp = np.stack([R[c]["shp"] for c in range(8)], axis=1).astype(f32)
    s5sre = np.concatenate([R[c]["s5sre"] for c in range(8)], axis=1).reshape(2, 128, 64, 64).astype(f32)
    s5sim = np.concatenate([R[c]["s5sim"] for c in range(8)], axis=1).reshape(2, 128, 64, 64).astype(f32)
    rws = np.concatenate([R[c]["rws"] for c in range(8)], axis=1).astype(f32)
    shs = np.concatenate([R[c]["shs"] for c in range(8)], axis=1).astype(f32)
    return (y_prompt, y_sample, memk, memv, s5pre, s5pim, rwp, shp, s5sre, s5sim, rws, shs)
```

```python
import math
import numpy as np
import concourse.bass as bass
import concourse.mybir as mybir
from concourse.bass_utils import run_bass_kernel_spmd

F32 = mybir.dt.float32
BF16 = mybir.dt.bfloat16
I32 = mybir.dt.int32
AF = mybir.ActivationFunctionType
ALU = mybir.AluOpType
AX = mybir.AxisListType

D = 1024
DEPTH = 4
TP = 2048
TS = 128
TT = TP + TS
NSEQ = 16
ALPHA = (2.0 * DEPTH) ** 0.25
LN_EPS = 1e-5
GN_EPS = 64e-5
TBLK = [(0, 512), (512, 512), (1024, 512), (1536, 512), (2048, 128)]
STUB_RWKV = False

ENGS = ("pe", "act", "dve", "pool", "sp")
SEM_ROLL = 20000
SAME_ENGINE_WAIT = True
N_DMA_SEMS = 24


class Prog:
    def __init__(self, nc):
        self.nc = nc
        self.ops = {e: [] for e in ENGS}
        self.sems = {e: [nc.alloc_semaphore(f"s_{e}_0")] for e in ENGS}
        self.own_sems = {e: {id(self.sems[e][0])} for e in ENGS}
        self.cnt = {e: 0 for e in ENGS}
        self.known = {e: {} for e in ENGS}
        self.last_w = {}
        self.readers = {}
        self.dma_sems = [nc.alloc_semaphore(f"s_dma_{i}") for i in range(N_DMA_SEMS)]
        self.dma_cnt = [0] * N_DMA_SEMS
        self.dma_rr = 0
        self.n_ins = 0

    def _tok(self, eng):
        if self.cnt[eng] >= SEM_ROLL:
            self.sems[eng].append(self.nc.alloc_semaphore(f"s_{eng}_{len(self.sems[eng])}"))
            self.own_sems[eng].add(id(self.sems[eng][-1]))
            self.cnt[eng] = 0
        self.cnt[eng] += 1
        return (self.sems[eng][-1], self.cnt[eng])

    def _deps(self, eng, reads, writes):
        toks = []
        for k in reads:
            t = self.last_w.get(k)
            if t is not None:
                toks.append(t)
        for k in writes:
            t = self.last_w.get(k)
            if t is not None:
                toks.append(t)
            toks.extend(self.readers.get(k, ()))
        waits = {}
        kn = self.known[eng]
        own = self.own_sems[eng]
        for (sem, val) in toks:
            key = id(sem)
            if not SAME_ENGINE_WAIT and key in own:
                continue
            if kn.get(key, (None, 0))[1] >= val:
                continue
            if key not in waits or waits[key][1] < val:
                waits[key] = (sem, val)
        for key, sv in waits.items():
            kn[key] = sv
        return list(waits.values())

    def _commit(self, tok, reads, writes):
        for k in reads:
            self.readers.setdefault(k, []).append(tok)
        for k in writes:
            self.last_w[k] = tok
            self.readers[k] = []

    @staticmethod
    def _excl(reads, writes):
        r2 = [k for k in reads if not (isinstance(k, tuple) and k[0] == "ps")]
        w2 = list(writes) + [k for k in reads if isinstance(k, tuple) and k[0] == "ps"]
        return r2, w2

    def op(self, eng, fn, reads=(), writes=()):
        reads, writes = self._excl(reads, writes)
        waits = self._deps(eng, reads, writes)
        tok = self._tok(eng)
        self.ops[eng].append((waits, fn, tok[0], 1))
        self._commit(tok, reads, writes)
        self.n_ins += 1
        return tok

    def dma(self, eng, fn, reads=(), writes=()):
        i = self.dma_rr
        self.dma_rr = (self.dma_rr + 1) % N_DMA_SEMS
        sem = self.dma_sems[i]
        waits = self._deps(eng, reads, writes)
        prev = self.dma_cnt[i]
        if prev > 0:
            key = id(sem)
            if self.known[eng].get(key, (None, 0))[1] < prev:
                waits.append((sem, prev))
                self.known[eng][key] = (sem, prev)
        self.dma_cnt[i] += 16
        tok = (sem, self.dma_cnt[i])
        self.ops[eng].append((waits, fn, sem, 16))
        self._commit(tok, reads, writes)
        self.n_ins += 1
        return tok

    def drain_dmas(self, eng="sp"):
        waits = []
        for i, sem in enumerate(self.dma_sems):
            if self.dma_cnt[i] > 0 and self.known[eng].get(id(sem), (None, 0))[1] < self.dma_cnt[i]:
                waits.append((sem, self.dma_cnt[i]))
                self.known[eng][id(sem)] = (sem, self.dma_cnt[i])
        self.ops[eng].append((waits, None, None, 0))

    def emit(self):
        nc = self.nc
        self.drain_dmas("sp")
        emap = {"pe": "tensor", "act": "scalar", "dve": "vector", "pool": "gpsimd", "sp": "sync"}
        with nc.Block() as block:
            for e in ENGS:
                ops = self.ops[e]

                def body(engine, ops=ops):
                    for (waits, fn, sem, amt) in ops:
                        for (s, v) in waits:
                            engine.wait_ge(s, v)
                        if fn is not None:
                            fn(engine).then_inc(sem, amt)

                getattr(block, emap[e])(body)
        self.ops = {e: [] for e in ENGS}
        self.last_w = {}
        self.readers = {}


def _fm(v):
    return np.ascontiguousarray(np.asarray(v).reshape(8, 128).T)


def pack_params(inp):
    cols = {}
    mats = []

    def add(name, arr):
        cols[name] = sum(m.shape[1] for m in mats)
        mats.append(np.asarray(arr, dtype=np.float32))

    for i in range(4):
        for k in range(3):
            add(("lng", i, k), _fm(inp["ln_g"][i, k]))
            add(("lnb", i, k), _fm(inp["ln_b"][i, k]))
    for j in range(2):
        add(("s5d", j), _fm(inp["s5_d"][j]))
        add(("mu", j), np.concatenate([_fm(inp["rwkv_mu"][j, m]) for m in range(6)], axis=1))
        for nm in ("rwkv_w0", "rwkv_a0", "rwkv_k_k", "rwkv_k_a", "rwkv_lnx_g", "rwkv_lnx_b"):
            add((nm, j), _fm(inp[nm][j]))
        add(("rwkv_r_k", j), _fm(inp["rwkv_r_k"][j].reshape(-1)))
    add(("rwkv_v0", 1), _fm(inp["rwkv_v0"][0]))
    for j in range(2):
        for nm in ("s5_a_re", "s5_a_im"):
            a = inp[nm][j].reshape(32, 2, 64)
            add((nm, j), np.ascontiguousarray(a.transpose(1, 2, 0).reshape(128, 32)))
        ld = np.repeat(inp["s5_log_dt"][j].reshape(32, 2, 1), 64, axis=2)
        add(("s5_log_dt", j), np.ascontiguousarray(ld.transpose(1, 2, 0).reshape(128, 32)))
    return np.ascontiguousarray(np.concatenate(mats, axis=1)), cols


def pack_s5_mats(inp, j):
    out = {}
    for nm, key in (("s5_b_re", "bbr"), ("s5_b_im", "bbi")):
        b = inp[nm][j]
        bb = np.zeros((8, 16, 8, 4, 2, 64), np.float32)
        for k in range(8):
            for q in range(4):
                for hf in range(2):
                    g8 = 2 * q + hf
                    bb[g8, :, k, q, hf, :] = b[8 * k + g8].T
        out[key] = bb.reshape(128, 32 * 128)
    for nm, key in (("s5_c_re", "cpr"), ("s5_c_im", "cpi")):
        c = inp[nm][j]
        cp = np.zeros((2, 64, 32, 128), np.float32)
        for tile in range(32):
            for hf in range(2):
                c0 = (tile % 4) * 32 + hf * 16
                cp[hf, :, tile, c0:c0 + 16] = c[2 * tile + hf].T
        out[key] = cp.reshape(128, 32 * 128)
    return out


def make_consts():
    c = np.zeros((128, 1024), np.float32)
    c[:, 0:128] = np.eye(128)
    c[0:64, 128:192] = 1.0
    c[64:128, 192:256] = 1.0
    s = np.arange(64)
    c[0:64, 256:320] = (s[:, None] < s[None, :])
    c[0:64, 320:384] = (s[:, None] <= s[None, :])
    c[0:64, 384:448] = (s[:, None] > s[None, :])
    c[:, 448:512] = 1.0
    c[:, 448] = 0.0
    c[:, 512:576] = 1.0
    c[:, 512:576:8] = 0.0
    c[:, 576:704] = 1.0 / 1024.0
    return c


class KB:
    def __init__(self, pcols, npar):
        nc = bass.Bass("TRN2", target_bir_lowering=False)
        self.nc = nc
        self.P = Prog(nc)
        self.pc = pcols
        self.npar = npar
        self.force_auto = False
        self.regions = []
        self.stream = None

    def _tblocks(self, col, ap, width):
        t_lo = col % width
        ext = 1
        for (st, cnt) in list(ap.ap)[1:]:
            if abs(st) < width:
                ext += (cnt - 1) * abs(st)
        t_hi = t_lo + ext
        return [t0 for (t0, n) in TBLK if t0 < t_hi and t0 + n > t_lo]

    def akeys(self, ap):
        if ap is None or not hasattr(ap, "tensor"):
            return []
        t = ap.tensor
        if type(t).__name__.startswith("DRam"):
            return []
        name = t.name
        if name.startswith("ps"):
            return [("ps", int(name[2:]))]
        dims = list(ap.ap)
        pstride = dims[0][0]
        col = ap.offset % pstride if pstride > 0 else ap.offset
        if name == "X":
            return [("X", tb) for tb in self._tblocks(col, ap, TT)]
        if name == "AR":
            es = 4 if ap.dtype == F32 or ap.dtype == I32 else 2
            ext = 1
            for (st, cnt) in dims[1:]:
                ext += (cnt - 1) * abs(st)
            b0, b1 = col * es, (col + ext) * es
            if b1 <= 17408 * 2:
                if es == 2:
                    return [("XB", tb) for tb in self._tblocks(col, ap, TT)]
                return [("XB", tb) for (tb, n) in TBLK]
            ks = [k for (r0, r1, k) in self.regions if r0 < b1 and r1 > b0]
            return ks if ks else [("AR", b0 // 2048)]
        return [name]

    def _k(self, reads, writes, ins, outs):
        if not self.force_auto:
            return reads, writes
        r, w = [], []
        for a in ins:
            r.extend(self.akeys(a))
        for a in outs:
            w.extend(self.akeys(a))
        return r, w

    def mm(self, out, lhsT, rhs, start, stop, reads=None, writes=None):
        reads, writes = self._k(reads, writes, [lhsT, rhs], [out])
        self._emit("op", "pe", lambda e: e.matmul(out, lhsT=lhsT, rhs=rhs, start=start, stop=stop), reads, writes)

    def tr(self, out, in_, ident, reads=None, writes=None):
        reads, writes = self._k(reads, writes, [in_, ident], [out])
        self._emit("op", "pe", lambda e: e.transpose(out, in_, ident), reads, writes)

    def tt(self, out, in0, in1, op, reads=None, writes=None, eng="dve"):
        reads, writes = self._k(reads, writes, [in0, in1], [out])
        self._emit("op", eng, lambda e: e.tensor_tensor(out=out, in0=in0, in1=in1, op=op), reads, writes)

    def ts(self, out, in0, s1, op0, reads=None, writes=None, s2=None, op1=None, eng="dve"):
        reads, writes = self._k(reads, writes, [in0, s1, s2], [out])
        if op1 is None:
            self._emit("op", eng, lambda e: e.tensor_scalar(out=out, in0=in0, scalar1=s1, scalar2=None, op0=op0), reads, writes)
        else:
            self._emit("op", eng, lambda e: e.tensor_scalar(out=out, in0=in0, scalar1=s1, scalar2=s2, op0=op0, op1=op1), reads, writes)

    def stt(self, out, in0, scalar, in1, op0, op1, reads=None, writes=None):
        reads, writes = self._k(reads, writes, [in0, scalar, in1], [out])
        self._emit("op", "dve", lambda e: e.scalar_tensor_tensor(out=out, in0=in0, scalar=scalar, in1=in1, op0=op0, op1=op1), reads, writes)

    def act(self, out, in_, func, reads=None, writes=None, bias=None, scale=1.0, accum_out=None):
        reads, writes = self._k(reads, writes, [in_, bias], [out, accum_out])
        kw = {}
        if bias is not None:
            kw["bias"] = bias
        if accum_out is not None:
            kw["accum_out"] = accum_out
        self._emit("op", "act", lambda e: e.activation(out=out, in_=in_, func=func, scale=scale, **kw), reads, writes)

    def cp(self, eng, out, in_, reads=None, writes=None):
        reads, writes = self._k(reads, writes, [in_], [out])
        if eng == "act":
            self._emit("op", "act", lambda e: e.copy(out=out, in_=in_), reads, writes)
        else:
            self._emit("op", eng, lambda e: e.tensor_copy(out=out, in_=in_), reads, writes)

    def red(self, out, in_, op, reads=None, writes=None):
        reads, writes = self._k(reads, writes, [in_], [out])
        self._emit("op", "dve", lambda e: e.tensor_reduce(out=out, in_=in_, op=op, axis=AX.X), reads, writes)

    def rcp(self, out, in_, reads=None, writes=None):
        reads, writes = self._k(reads, writes, [in_], [out])
        self._emit("op", "dve", lambda e: e.reciprocal(out=out, in_=in_), reads, writes)

    def mset(self, out, val, reads=None, writes=None, eng="dve"):
        reads, writes = self._k(reads, writes, [], [out])
        self._emit("op", eng, lambda e: e.memset(out, val), reads, writes)

    def scan(self, out, d0, d1, reads=None, writes=None):
        reads, writes = self._k(reads, writes, [d0, d1], [out])
        self._emit("op", "dve", lambda e: e.tensor_tensor_scan(out=out, data0=d0, data1=d1, initial=0.0, op0=ALU.mult, op1=ALU.add), reads, writes)

    def dma(self, eng, out, in_, reads=None, writes=None):
        reads, writes = self._k(reads, writes, [in_], [out])
        self._emit("dma", eng, lambda e: e.dma_start(out=out, in_=in_), reads, writes)

    def _emit(self, kind, eng, fn, reads, writes):
        rec = (kind, eng, fn, reads, writes)
        if self.stream is not None:
            self.stream.append(rec)
        else:
            self._flush(rec)

    def _flush(self, rec):
        kind, eng, fn, reads, writes = rec
        if kind == "op":
            self.P.op(eng, fn, reads, writes)
        else:
            self.P.dma(eng, fn, reads, writes)

    def setup(self):
        nc = self.nc

        def din(name, shape):
            return nc.dram_tensor(name, list(shape), F32, kind="ExternalInput").ap()

        def dout(name, shape):
            return nc.dram_tensor(name, list(shape), F32, kind="ExternalOutput").ap()

        shapes = {"xin": [TT, D], "mem": [256, D], "ck": [4, NSEQ, 256, D], "cv": [4, NSEQ, 256, D], "s5re0": [2, NSEQ, 4096],
                  "s5im0": [2, NSEQ, 4096], "rw0": [2, NSEQ, 16, 64, 64], "sh0": [2, NSEQ, D], "par": [128, self.npar], "consts": [128, 1024]}
        for j in range(2):
            for k in ("bbr", "bbi", "cpr", "cpi"):
                shapes[(k, j)] = [128, 4096]
        for nm, shp in WSHAPES:
            shapes[nm] = shp

        class Lazy(dict):
            def __missing__(d, key):
                name = key if isinstance(key, str) else f"{key[0]}{key[1]}"
                d[key] = din(name, shapes[key])
                return d[key]
        I = Lazy()
        O = {}
        O["y"] = dout("y", [TT, D])
        O["memk"] = dout("memk", [4, 256, D])
        O["memv"] = dout("memv", [4, 256, D])
        O["s5pre"] = dout("s5pre", [2, 4096])
        O["s5pim"] = dout("s5pim", [2, 4096])
        O["rwp"] = dout("rwp", [2, 16, 64, 64])
        O["shp"] = dout("shp", [2, D])
        O["s5sre"] = dout("s5sre", [2, NSEQ, 4096])
        O["s5sim"] = dout("s5sim", [2, NSEQ, 4096])
        O["rws"] = dout("rws", [2, NSEQ, 16, 64, 64])
        O["shs"] = dout("shs", [2, NSEQ, D])
        self.I, self.O = I, O
        self.VF = nc.dram_tensor("vfirst", [128, 8, TT], F32, kind="Internal").ap()

        self.X = nc.alloc_sbuf_tensor("X", [128, 8, TT], F32)
        self.AR = nc.alloc_sbuf_tensor("AR", [128, 51200], BF16)
        self.MEMT = nc.alloc_sbuf_tensor("MEMT", [128, 8, 256], BF16)
        self.PAR = nc.alloc_sbuf_tensor("PAR", [128, self.npar], F32)
        self.CON = nc.alloc_sbuf_tensor("CON", [128, 1024], F32)
        self.CONB = nc.alloc_sbuf_tensor("CONB", [128, 256], BF16)
        self.T = [nc.alloc_sbuf_tensor(f"T{i}", [128, 512], F32) for i in range(3)]
        self.SQ = nc.alloc_sbuf_tensor("SQ", [128, 8, 512], BF16)
        self.RS = nc.alloc_sbuf_tensor("RS", [128, 512], F32)
        self.SM = nc.alloc_sbuf_tensor("SM", [128, 256], F32)
        self.PS = [nc.alloc_psum_tensor(f"ps{i}", [128, 512], F32) for i in range(8)]
        AR = self.AR
        self.XB = AR[:, 0:17408].rearrange("p (c t) -> p c t", c=8)
        self.A1 = AR[:, 17408:34816].rearrange("p (c t) -> p c t", c=8)
        self.WR = [AR[:, 34816 + i * 8192: 34816 + (i + 1) * 8192].rearrange("p (c n) -> p c n", c=8) for i in range(2)]
        CON, CONB = self.CON, self.CONB
        self.IDF = CON[:, 0:128]
        self.BLK = CON[:, 128:256]
        self.MLT = CON[0:64, 256:320]
        self.MLE = CON[0:64, 320:384]
        self.MGT = CON[0:64, 384:448]
        self.RST64 = CON[:, 448:512]
        self.RST8 = CON[:, 512:576]
        self.IDB = CONB[:, 0:128]
        self.ONB = CONB[:, 128:256]
        self.wr_i = 0
        self.bank_i = 0
        self.EPS = self.SM[:, 0:1]
        self.EPS2 = self.SM[:, 1:2]

    def par(self, key, n=8):
        c0 = self.pc[key]
        return self.PAR[:, c0:c0 + n]

    def prefetch(self, tag, dram2d):
        self.prefetched = (tag, self.wload(dram2d))

    def wload(self, dram2d, ncols=1024, tag=None):
        pf = getattr(self, "prefetched", None)
        if tag is not None and pf is not None and pf[0] == tag:
            self.prefetched = None
            return pf[1]
        i = self.wr_i
        self.wr_i ^= 1
        dst = self.WR[i][:, :, 0:ncols]
        src = dram2d.rearrange("(c p) n -> p c n", p=128)
        self.dma("pool", dst, src, [], [("WR", i)])
        return self.WR[i], ("WR", i)

    def next_bank(self):
        i = self.bank_i
        self.bank_i = (i + 1) % 4
        return self.PS[i], ("ps", i)

    def dense(self, w, wkey, src, srckey, evac, n_oc=8):
        big = TBLK[:4]
        for oc in range(n_oc):
            base = (oc % 2) * 4
            for c in range(8):
                for bi, (t0, n) in enumerate(big):
                    self.mm(self.PS[base + bi][:, 0:n], w[:, c, oc * 128:(oc + 1) * 128], src[:, c, t0:t0 + n], c == 0, c == 7, [wkey, (srckey, t0)], [("ps", base + bi)])
            for bi, (t0, n) in enumerate(big):
                evac(self.PS[base + bi], ("ps", base + bi), oc, t0, n)
        (t0, n) = TBLK[4]
        for oc in range(n_oc):
            ps, pk = self.next_bank()
            for c in range(8):
                self.mm(ps[:, 0:n], w[:, c, oc * 128:(oc + 1) * 128], src[:, c, t0:t0 + n], c == 0, c == 7, [wkey, (srckey, t0)], [pk])
            evac(ps, pk, oc, t0, n)

    def layer_norm(self, i, k):
        X, XB, SQ, RS, PS, ONB = self.X, self.XB, self.SQ, self.RS, self.PS, self.ONB
        g = self.par(("lng", i, k))
        b = self.par(("lnb", i, k))
        for (t0, n) in TBLK:
            xb = X[:, :, t0:t0 + n]
            kx = ("X", t0)
            self.cp("act", SQ[:, :, 0:n], xb, [kx], ["SQ"])
            for c in range(8):
                self.mm(PS[4][:, 0:n], ONB, SQ[:, c, 0:n], c == 0, c == 7, ["SQ", "CONB"], [("ps", 4)])
            self.tt(xb, xb, PS[4][:, 0:n].unsqueeze(1).broadcast_to([128, 8, n]), ALU.subtract, [kx, ("ps", 4)], [kx])
            self.act(SQ[:, :, 0:n], xb, AF.Square, [kx], ["SQ"])
            for c in range(8):
                self.mm(PS[5][:, 0:n], ONB, SQ[:, c, 0:n], c == 0, c == 7, ["SQ", "CONB"], [("ps", 5)])
            self.act(RS[:, 0:n], PS[5][:, 0:n], AF.Sqrt, [("ps", 5), "eps"], ["RS"], bias=self.EPS)
            self.rcp(RS[:, 0:n], RS[:, 0:n], ["RS"], ["RS"])
            self.tt(xb, xb, RS[:, 0:n].unsqueeze(1).broadcast_to([128, 8, n]), ALU.mult, [kx, "RS"], [kx])
            for c in range(8):
                self.ts(X[:, c, t0:t0 + n], X[:, c, t0:t0 + n], g[:, c:c + 1], ALU.mult, [kx, "PAR"], [kx], s2=b[:, c:c + 1], op1=ALU.add)
            self.cp("act", XB[:, :, t0:t0 + n], xb, [kx], [("XB", t0)])

    def phase_input(self):
        I, X, XB, AR, PS, IDF = self.I, self.X, self.XB, self.AR, self.PS, self.IDF
        self.dma("sp", self.PAR[:], I["par"], [], ["PAR"])
        self.dma("sp", self.CON[:], I["consts"], [], ["CON"])
        self.cp("dve", self.CONB[:, 0:128], self.CON[:, 0:128], ["CON"], ["CONB"])
        self.cp("dve", self.CONB[:, 128:256], self.CON[:, 576:704], ["CON"], ["CONB"])
        self.mset(self.SM[:, 0:1], LN_EPS, [], ["eps"])
        self.mset(self.SM[:, 1:2], GN_EPS, [], ["eps"])
        STG = [AR[:, 17408 + i * 2048: 17408 + (i + 1) * 2048].bitcast(F32) for i in range(2)]
        import os
        for tt in [int(x) for x in os.environ.get('PI_TILES', ','.join(str(i) for i in range(19))).split(',')]:
            st = STG[tt % 2]
            sk = ("stg", tt % 2)
            if tt < 17:
                self.dma("sp", st, I["xin"][tt * 128:(tt + 1) * 128, :], [], [sk])
            else:
                self.dma("sp", st, I["mem"][(tt - 17) * 128:(tt - 16) * 128, :], [], [sk])
            for cg in range(2):
                ps, pk = self.next_bank()
                for cc in range(4):
                    c = cg * 4 + cc
                    self.tr(ps[:, cc * 128:(cc + 1) * 128], st[:, c * 128:(c + 1) * 128], IDF, [sk, "CON"], [pk])
                psv = ps[:].rearrange("p (a b) -> p a b", a=4)
                if tt < 17:
                    self.cp("dve", X[:, cg * 4:(cg + 1) * 4, tt * 128:(tt + 1) * 128], psv, [pk], [("Xi", tt, cg)])
                    self.cp("act", XB[:, cg * 4:(cg + 1) * 4, tt * 128:(tt + 1) * 128], psv, [pk], [("XBi", tt, cg)])
                else:
                    m0 = (tt - 17) * 128
                    self.cp("act", self.MEMT[:, cg * 4:(cg + 1) * 4, m0:m0 + 128], psv, [pk], ["MEMT"])
        self.P.emit()

    def phase_output(self):
        AR, PS, IDF, X, O = self.AR, self.PS, self.IDF, self.X, self.O
        STG = [AR[:, 17408 + i * 2048: 17408 + (i + 1) * 2048].bitcast(F32) for i in range(2)]
        for tt in range(17):
            st = STG[tt % 2]
            sk = ("stg", tt % 2)
            for cg in range(2):
                ps, pk = self.next_bank()
                for cc in range(4):
                    c = cg * 4 + cc
                    self.tr(ps[:, cc * 128:(cc + 1) * 128], X[:, c, tt * 128:(tt + 1) * 128], IDF, [], [pk])
                self.cp("dve" if cg == 0 else "act", st[:, cg * 512:(cg + 1) * 512], ps[:], [pk], [sk])
            self.dma("sp", O["y"][tt * 128:(tt + 1) * 128, :], st, [sk], [("yout", tt)])
        self.P.emit()

    def mlp_layer(self, i):
        X, XB, A1, T, I = self.X, self.XB, self.A1, self.T, self.I
        cnt = [0]
        for fg in range(4):
            w1, k1 = self.wload(I["mlp_w1"][i][:, fg * 1024:(fg + 1) * 1024], tag=("w1", i) if fg == 0 else None)

            def evac1(ps, pk, oc, t0, n):
                tt_, tk = T[cnt[0] % 2], ("T", cnt[0] % 2)
                cnt[0] += 1
                self.act(tt_[:, 0:n], ps[:, 0:n], AF.Relu, [pk], [tk])
                self.tt(A1[:, oc, t0:t0 + n], tt_[:, 0:n], tt_[:, 0:n], ALU.mult, [tk], [("A1", t0)])
            self.dense(w1, k1, XB, "XB", evac1)
            w2, k2 = self.wload(I["mlp_w2"][i][fg * 1024:(fg + 1) * 1024, :])
            if fg == 0:
                def evac2(ps, pk, oc, t0, n):
                    self.stt(X[:, oc, t0:t0 + n], X[:, oc, t0:t0 + n], ALPHA, ps[:, 0:n], ALU.mult, ALU.add, [pk, ("X", t0)], [("X", t0)])
            else:
                def evac2(ps, pk, oc, t0, n):
                    self.tt(X[:, oc, t0:t0 + n], X[:, oc, t0:t0 + n], ps[:, 0:n], ALU.add, [pk, ("X", t0)], [("X", t0)])
            self.dense(w2, k2, A1, "A1", evac2)

    def xa_layer(self, i):
        X, XB, A1, T, I, O, PS, MEMT, AR, IDB, SM = self.X, self.XB, self.A1, self.T, self.I, self.O, self.PS, self.MEMT, self.AR, self.IDB, self.SM
        wq, kq = self.wload(I["xa_w_q"][i], tag=("q", i))

        def evq(ps, pk, oc, t0, n):
            self.act(A1[:, oc, t0:t0 + n], ps[:, 0:n], AF.Copy, [pk], [("A1", t0)], scale=0.0625)
        self.dense(wq, kq, XB, "XB", evq)
        self.prefetch(("k", i), I["xa_w_k"][i])
        self.P.emit()
        KT = AR[:, 0:2048].rearrange("p (c m) -> p c m", c=8)
        VB = AR[:, 2048:4096].rearrange("p (a n) -> p a n", a=2)
        PNB = AR[:, 4096:5120].rearrange("p (h m) -> p h m", h=4)
        PT = AR[:, 5120:6144].rearrange("p (a t) -> p a t", a=8)
        KS = [AR[:, 6144 + s * 2048: 8192 + s * 2048].rearrange("p (a n) -> p a n", a=2) for s in range(2)]
        VS = [AR[:, 10240 + s * 2048: 12288 + s * 2048].rearrange("p (a n) -> p a n", a=2) for s in range(2)]
        KTS = AR[:, 14336:16384].rearrange("p (c m) -> p c m", c=8)
        PTS = AR[:, 16384:16448].rearrange("p (a t) -> p a t", a=8)
        PEXP = [T[0], T[1]]
        psT = PS[4][:].bitcast(BF16)
        MX, NMX, SUM, RSM = SM[:, 8:12], SM[:, 12:16], SM[:, 16:20], SM[:, 20:24]
        SC = [PS[5], PS[6]]

        wk, kk = self.wload(I["xa_w_k"][i], tag=("k", i))
        for oc in range(8):
            ps, pk = self.next_bank()
            for c in range(8):
                self.mm(ps[:, 0:256], wk[:, c, oc * 128:(oc + 1) * 128], MEMT[:, c, :], c == 0, c == 7, [kk, "MEMT"], [pk])
            self.cp("act", KT[:, oc, :], ps[:, 0:256], [pk], ["KT"])

        def tokmajor(w, wkey, outap, vb):
            for mt in range(2):
                banks = [self.next_bank(), self.next_bank()]
                for c in range(8):
                    for nb in range(2):
                        ps, pk = banks[nb]
                        self.mm(ps[:], MEMT[:, c, mt * 128:(mt + 1) * 128], w[:, c, nb * 512:(nb + 1) * 512], c == 0, c == 7, [wkey, "MEMT"], [pk])
                for nb in range(2):
                    ps, pk = banks[nb]
                    self.cp("dve", T[2][:], ps[:], [pk], ["T2"])
                    if vb:
                        self.cp("act", VB[:, mt, nb * 512:(nb + 1) * 512], ps[:], [pk], ["VB"])
                    self.dma("sp", outap[mt * 128:(mt + 1) * 128, nb * 512:(nb + 1) * 512], T[2][:], ["T2"], [("mo", id(outap), mt, nb)])
        tokmajor(wk, kk, O["memk"][i], False)
        wv, kv = self.wload(I["xa_w_v"][i])
        tokmajor(wv, kv, O["memv"][i], True)

        def softmax(np_):
            for b in range(2):
                self.red(MX[0:np_, 2 * b:2 * b + 2], SC[b][0:np_, :].rearrange("p (h m) -> p h m", h=2), ALU.max, [("ps", 5 + b)], ["MX"])
            self.ts(NMX[0:np_, :], MX[0:np_, :], -1.0, ALU.mult, ["MX"], ["NMX"])
            for h in range(4):
                sl = slice((h % 2) * 256, (h % 2) * 256 + 256)
                self.act(PEXP[h // 2][0:np_, sl], SC[h // 2][0:np_, sl], AF.Exp, [("ps", 5 + h // 2), "NMX"], [("PEXP", h), ("SUM", h)],
                         bias=NMX[0:np_, h:h + 1], accum_out=SUM[0:np_, h:h + 1])
            self.rcp(RSM[0:np_, :], SUM[0:np_, :], [("SUM", h) for h in range(4)], ["RSM"])
            for b in range(2):
                self.tt(PNB[0:np_, 2 * b:2 * b + 2, :], PEXP[b][0:np_, :].rearrange("p (h m) -> p h m", h=2),
                        RSM[0:np_, 2 * b:2 * b + 2].unsqueeze(2).broadcast_to([np_, 2, 256]), ALU.mult,
                        [("PEXP", 2 * b), ("PEXP", 2 * b + 1), "RSM"], ["PNB"])

        for tt in range(16):
            t0 = tt * 128
            ak = ("A1", (t0 // 512) * 512)
            for h in range(4):
                for dc in range(2):
                    fc = 2 * h + dc
                    self.mm(SC[h // 2][:, (h % 2) * 256:(h % 2) * 256 + 256], A1[:, fc, t0:t0 + 128], KT[:, fc, :], dc == 0, dc == 1, [ak, "KT"], [("ps", 5 + h // 2)])
            softmax(128)
            for h in range(4):
                for mt in range(2):
                    a = h * 2 + mt
                    self.tr(psT[:, a * 128:(a + 1) * 128], PNB[:, h, mt * 128:(mt + 1) * 128], IDB, ["PNB", "CONB"], [("ps", 4)])
            self.cp("act", PT[:].rearrange("p a t -> p (a t)"), psT[:, 0:1024], [("ps", 4)], ["PT"])
            for fc in range(8):
                h, dc = fc // 2, fc % 2
                bank, bk = (PS[7], ("ps", 7)) if fc >= 4 else (PS[3], ("ps", 3))
                for mt in range(2):
                    self.mm(bank[:, (fc % 4) * 128:(fc % 4) * 128 + 128], VB[:, mt, h * 256 + dc * 128:h * 256 + dc * 128 + 128], PT[:, h * 2 + mt, :],
                            mt == 0, mt == 1, ["VB", "PT"], [bk])
            self.cp("dve", A1[:, 0:4, t0:t0 + 128], PS[3][:].rearrange("p (a t) -> p a t", a=4), [("ps", 3)], [ak])
            self.cp("act", A1[:, 4:8, t0:t0 + 128], PS[7][:].rearrange("p (a t) -> p a t", a=4), [("ps", 7)], [ak])
        for s in range(NSEQ):
            ks, vs = KS[s % 2], VS[s % 2]
            kkey, vkey = ("KS", s % 2), ("VS", s % 2)
            self.dma("pool", ks, I["ck"][i, s].rearrange("(a p) n -> p a n", p=128), [], [kkey])
            self.dma("pool", vs, I["cv"][i, s].rearrange("(a p) n -> p a n", p=128), [], [vkey])
            for mt in range(2):
                for fc in range(8):
                    self.tr(psT[:, fc * 128:(fc + 1) * 128], ks[:, mt, fc * 128:(fc + 1) * 128], IDB, [kkey, "CONB"], [("ps", 4)])
                self.cp("act", KTS[:, :, mt * 128:(mt + 1) * 128], psT[:, 0:1024].rearrange("p (c m) -> p c m", c=8), [("ps", 4)], ["KTS"])
            c0 = TP + 8 * s
            for h in range(4):
                for dc in range(2):
                    fc = 2 * h + dc
                    self.mm(SC[h // 2][0:8, (h % 2) * 256:(h % 2) * 256 + 256], A1[:, fc, c0:c0 + 8], KTS[:, fc, :], dc == 0, dc == 1, [("A1", 2048), "KTS"], [("ps", 5 + h // 2)])
            softmax(8)
            for h in range(4):
                for mt in range(2):
                    a = h * 2 + mt
                    self.tr(psT[:, a * 8:(a + 1) * 8], PNB[0:8, h, mt * 128:(mt + 1) * 128], IDB[0:8, 0:8], ["PNB", "CONB"], [("ps", 4)])
            self.cp("act", PTS[:].rearrange("p a t -> p (a t)"), psT[:, 0:64], [("ps", 4)], ["PTS"])
            for fc in range(8):
                h, dc = fc // 2, fc % 2
                for mt in range(2):
                    self.mm(PS[7][:, fc * 8:fc * 8 + 8], vs[:, mt, h * 256 + dc * 128:h * 256 + dc * 128 + 128], PTS[:, h * 2 + mt, :], mt == 0, mt == 1, [vkey, "PTS"], [("ps", 7)])
            self.cp("dve", A1[:, :, c0:c0 + 8], PS[7][:, 0:64].rearrange("p (a t) -> p a t", a=8), [("ps", 7)], [("A1", 2048)])
        wo, ko = self.wload(I["xa_w_o"][i])

        def evo(ps, pk, oc, t0, n):
            self.stt(X[:, oc, t0:t0 + n], X[:, oc, t0:t0 + n], ALPHA, ps[:, 0:n], ALU.mult, ALU.add, [pk, ("X", t0)], [("X", t0)])
        self.dense(wo, ko, A1, "A1", evo)
        self.prefetch(("w1", i), I["mlp_w1"][i][:, 0:1024])

    def s5_layer(self, i, j):
        X, XB, T, I, O, PS, AR, SM, IDF, RS = self.X, self.XB, self.T, self.I, self.O, self.PS, self.AR, self.SM, self.IDF, self.RS
        B0 = 17408
        BBR = AR[:, B0:B0 + 4096].rearrange("p (t c) -> p t c", t=32)
        BBI = AR[:, B0 + 4096:B0 + 8192].rearrange("p (t c) -> p t c", t=32)
        CRB = AR[:, B0 + 8192:B0 + 12288].rearrange("p (t c) -> p t c", t=32)
        CIN = AR[:, B0 + 12288:B0 + 16384].rearrange("p (t c) -> p t c", t=32)
        BU0 = AR[:, B0 + 16384:B0 + 20480].bitcast(F32).rearrange("p (r t s) -> p r t s", r=2, t=32)
        BU = [BU0, BU0]
        TAB = AR[:, B0 + 20480:B0 + 24576].bitcast(F32).rearrange("p (r t s) -> p r t s", r=2, t=32)
        EC, ES = TAB[:, 0], TAB[:, 1]
        SQF = self.SQ[:].rearrange("p c t -> p (c t)").bitcast(F32)
        TQ4 = SQF[:, 0:1024].rearrange("p (r t s) -> p r t s", r=2, t=32)
        tq = SQF[:, 0:1024].rearrange("p (t s) -> p t s", t=32)
        RH = SQF[:, 1024:2048].rearrange("p (t s) -> p t s", t=32)
        HH = AR[:, B0 + 24576:B0 + 28672].bitcast(F32).rearrange("p (r t s) -> p r t s", r=2, t=32)
        HHB = AR[:, B0 + 28672:B0 + 30720].rearrange("p (r t s) -> p r t s", r=2, t=32)
        SP = AR[:, B0 + 30720:B0 + 33792].bitcast(F32)
        H0 = SP[:, 0:1024].rearrange("p (r t q) -> p r t q", r=2, t=32)

        def sm(k):
            return SP[:, 1024 + 32 * k:1024 + 32 * (k + 1)]
        pair = lambda k: SP[:, 1024 + 32 * k:1024 + 32 * (k + 2)].rearrange("p (r t) -> p r t", r=2)
        C1, C2, HL, M1, M2 = pair(0), pair(2), pair(4), pair(6), pair(8)
        abre, abim = sm(0), sm(1)
        FRE, FIM = sm(10), sm(11)
        dt, ang, mag, sn = sm(12), sm(13), sm(14), sm(15)
        cs, den, nre, r_, m_, kf = [SM[:, 64 + 32 * k:96 + 32 * k] for k in range(6)]
        TI = SM[:, 32:64].bitcast(I32)
        pc = self.pc
        LRE = self.PAR[:, pc[("s5_a_re", j)]:pc[("s5_a_re", j)] + 32]
        LIM = self.PAR[:, pc[("s5_a_im", j)]:pc[("s5_a_im", j)] + 32]
        LDT = self.PAR[:, pc[("s5_log_dt", j)]:pc[("s5_log_dt", j)] + 32]
        K = ["s5p", "PAR"]
        TWO_PI = 2.0 * math.pi
        tt = lambda o, a, b, op: self.tt(o, a, b, op, K, K)
        ts = lambda o, a, s1, op0, s2=None, op1=None: self.ts(o, a, s1, op0, K, K, s2=s2, op1=op1)
        ac = lambda o, a, f, scale=1.0: self.act(o, a, f, K, K, scale=scale)
        ac(dt, LDT, AF.Exp)
        tt(mag, LRE, dt, ALU.mult)
        ac(mag, mag, AF.Exp)
        tt(ang, LIM, dt, ALU.mult)
        ts(kf, ang, 1.0 / TWO_PI, ALU.mult)
        self.cp("dve", TI, kf, K, K)
        self.cp("dve", kf, TI, K, K)
        self.stt(r_, kf, -TWO_PI, ang, ALU.mult, ALU.add, K, K)

        def wrap(x):
            ts(m_, x, math.pi, ALU.is_gt)
            self.stt(x, m_, -TWO_PI, x, ALU.mult, ALU.add, K, K)
            ts(m_, x, -math.pi, ALU.is_lt)
            self.stt(x, m_, TWO_PI, x, ALU.mult, ALU.add, K, K)
        wrap(r_)
        ac(sn, r_, AF.Sin)
        ts(r_, r_, math.pi / 2, ALU.add)
        wrap(r_)
        ac(cs, r_, AF.Sin)
        tt(abre, mag, cs, ALU.mult)
        tt(abim, mag, sn, ALU.mult)
        ts(sm(2), abim, -1.0, ALU.mult)
        self.cp("dve", sm(3), abre, K, K)
        tt(den, LRE, LRE, ALU.mult)
        tt(m_, LIM, LIM, ALU.mult)
        tt(den, den, m_, ALU.add)
        self.rcp(den, den, K, K)
        ts(nre, abre, -1.0, ALU.add)
        tt(FRE, nre, LRE, ALU.mult)
        tt(m_, abim, LIM, ALU.mult)
        tt(FRE, FRE, m_, ALU.add)
        tt(FRE, FRE, den, ALU.mult)
        tt(FIM, abim, LRE, ALU.mult)
        tt(m_, nre, LIM, ALU.mult)
        tt(FIM, FIM, m_, ALU.subtract)
        tt(FIM, FIM, den, ALU.mult)
        KT_ = K + ["TAB", "tq", "RH"]
        self.mset(EC[:, :, 0:1], 1.0, KT_, ["TAB"])
        self.mset(ES[:, :, 0:1], 0.0, KT_, ["TAB"])
        self.cp("dve", EC[:, :, 1], cs, KT_, ["TAB"])
        self.cp("dve", ES[:, :, 1], sn, KT_, ["TAB"])
        for tau in range(2, 32):
            self.tt(EC[:, :, tau], EC[:, :, tau - 1], cs, ALU.mult, KT_, ["TAB"])
            self.tt(dt, ES[:, :, tau - 1], sn, ALU.mult, KT_, K)
            self.tt(EC[:, :, tau], EC[:, :, tau], dt, ALU.subtract, KT_, ["TAB"])
            self.tt(ES[:, :, tau], ES[:, :, tau - 1], cs, ALU.mult, KT_, ["TAB"])
            self.tt(dt, EC[:, :, tau - 1], sn, ALU.mult, KT_, K)
            self.tt(ES[:, :, tau], ES[:, :, tau], dt, ALU.add, KT_, ["TAB"])
        self.cp("dve", RH, mag.unsqueeze(2).broadcast_to([128, 32, 32]), KT_, ["RH"])
        self.mset(RH[:, :, 0:1], 0.0, KT_, ["RH"])
        self.dma("pool", BBR[:].rearrange("p t c -> p (t c)"), I[("bbr", j)], [], ["BB"])
        self.dma("pool", BBI[:].rearrange("p t c -> p (t c)"), I[("bbi", j)], [], ["BB"])
        v3 = lambda a: a[:].rearrange("p (t c) -> p t c", t=4)
        for g in range(8):
            sr, si, tm, r4 = T[0], T[1], T[2], RS
            self.dma("sp", sr[:], I[("cpr", j)][:, g * 512:(g + 1) * 512], [], ["T0"])
            self.dma("sp", si[:], I[("cpi", j)][:, g * 512:(g + 1) * 512], [], ["T1"])
            fre_b = FRE[:, g * 4:(g + 1) * 4].unsqueeze(2).broadcast_to([128, 4, 128])
            fim_b = FIM[:, g * 4:(g + 1) * 4].unsqueeze(2).broadcast_to([128, 4, 128])
            kk = ["T0", "T1", "T2", "RS", "CC"] + K
            self.tt(v3(tm), v3(si), fim_b, ALU.mult, kk, ["T2"])
            self.tt(v3(si), v3(si), fre_b, ALU.mult, kk, ["T1"])
            self.tt(v3(r4), v3(sr), fre_b, ALU.mult, kk, ["RS"])
            self.tt(CRB[:, g * 4:(g + 1) * 4, :], v3(r4), v3(tm), ALU.subtract, kk, ["CC"])
            self.tt(v3(sr), v3(sr), fim_b, ALU.mult, kk, ["T0"])
            self.tt(v3(sr), v3(sr), v3(si), ALU.add, kk, ["T0"])
            self.ts(CIN[:, g * 4:(g + 1) * 4, :], v3(sr), -1.0, ALU.mult, kk, ["CC"])
        for r, nm in enumerate(("s5re0", "s5im0")):
            for half in range(8):
                st, sk = T[half % 2], "T%d" % (half % 2)
                self.dma("sp", st[0:16, :], I[nm][j][:, half * 512:(half + 1) * 512], ["CC"], [sk])
                for q in range(4):
                    tile = half * 4 + q
                    self.tr(PS[0][:, tile * 16:(tile + 1) * 16], st[0:16, q * 128:(q + 1) * 128], IDF[0:16, 0:16], [sk, "CON"], [("ps", 0)])
            self.cp("dve", H0[:, r, :, :], PS[0][:].rearrange("p (t q) -> p t q", t=32), [("ps", 0)], ["H0"])
        fn2, gre, gim = dt, ang, mag
        tt(fn2, FRE, FRE, ALU.mult)
        tt(m_, FIM, FIM, ALU.mult)
        tt(fn2, fn2, m_, ALU.add)
        self.rcp(fn2, fn2, K, K)
        tt(gre, FRE, fn2, ALU.mult)
        tt(gim, FIM, fn2, ALU.mult)
        ts(gim, gim, -1.0, ALU.mult)
        HT = TQ4
        bq = lambda f: f.unsqueeze(2).broadcast_to([128, 32, 16])
        K2 = K + ["H0", "tq"]
        ta, tb = HT[:, 0, :, 0:16], HT[:, 1, :, 0:16]
        self.tt(ta, H0[:, 0], bq(gre), ALU.mult, K2, ["tq"])
        self.tt(tb, H0[:, 1], bq(gim), ALU.mult, K2, ["tq"])
        self.tt(ta, ta, tb, ALU.subtract, K2, ["tq"])
        self.tt(tb, H0[:, 0], bq(gim), ALU.mult, K2, ["tq"])
        self.tt(H0[:, 1], H0[:, 1], bq(gre), ALU.mult, K2, ["H0"])
        self.tt(H0[:, 1], H0[:, 1], tb, ALU.add, K2, ["H0"])
        self.cp("dve", H0[:, 0], ta, K2, ["H0"])

        DPAR = self.par(("s5d", j))
        def front(b):
            bu, bk = BU0, ("BU", 0)
            c0 = b * 32
            tb0 = (c0 // 512) * 512 if c0 < 2048 else 2048
            xk, xk2 = ("XB", tb0), ("X", tb0)
            for r, BB in enumerate((BBR, BBI)):
                for half in range(2):
                    ps, pk = self.next_bank()
                    for tl in range(16):
                        tile = half * 16 + tl
                        self.mm(ps[:, tl * 32:(tl + 1) * 32], BB[:, tile, :], XB[:, tile // 4, c0:c0 + 32], True, True, ["BB", xk], [pk])
                    self.cp("act", bu[:, r, half * 16:(half + 1) * 16, :], ps[:].rearrange("p (t s) -> p t s", t=16), [pk], [bk])
            hk = ["HH", "HL", "m", "H0"] + K
            if b < 64:
                kr = hk + [bk, "TAB", "tq", "RH"]
                br, bi = bu[:, 0], bu[:, 1]
                self.tt(HH[:, 0], EC, br, ALU.mult, kr, ["HH"])
                self.tt(tq, ES, bi, ALU.mult, kr, ["tq"])
                self.tt(HH[:, 0], HH[:, 0], tq, ALU.add, kr, ["HH"])
                self.tt(HH[:, 1], EC, bi, ALU.mult, kr, ["HH"])
                self.tt(tq, ES, br, ALU.mult, kr, ["tq"])
                self.tt(HH[:, 1], HH[:, 1], tq, ALU.subtract, kr, ["HH"])
                if b > 0:
                    self.tt(M1, C1, HL[:, 0:1, :].broadcast_to([128, 2, 32]), ALU.mult, kr, ["m"])
                    self.tt(M2, C2, HL[:, 1:2, :].broadcast_to([128, 2, 32]), ALU.mult, kr, ["m"])
                    self.tt(HH[:, :, :, 0], HH[:, :, :, 0], M1, ALU.add, kr, ["HH"])
                    self.tt(HH[:, :, :, 0], HH[:, :, :, 0], M2, ALU.add, kr, ["HH"])
                for r in range(2):
                    self.scan(bu[:, r].rearrange("p t s -> p (t s)"), RH.rearrange("p t s -> p (t s)"), HH[:, r].rearrange("p t s -> p (t s)"), kr, [bk])
                self.tt(HH[:, 0], EC, br, ALU.mult, kr, ["HH"])
                self.tt(tq, ES, bi, ALU.mult, kr, ["tq"])
                self.tt(HH[:, 0], HH[:, 0], tq, ALU.subtract, kr, ["HH"])
                self.tt(HH[:, 1], ES, br, ALU.mult, kr, ["HH"])
                self.tt(tq, EC, bi, ALU.mult, kr, ["tq"])
                self.tt(HH[:, 1], HH[:, 1], tq, ALU.add, kr, ["HH"])
                self.cp("dve", HL, HH[:, :, :, 31], ["HH"], ["HL"])
            else:
                q0 = (b - 64) * 4
                buv = bu[:].rearrange("p r t (q s) -> p r t q s", q=4)
                hhv = HH[:].rearrange("p r t (q s) -> p r t q s", q=4)
                ob = TQ4
                ok = "tq"
                m1, m2 = ob[:, :, :, 0:4], ob[:, :, :, 4:8]
                c1b = C1.unsqueeze(3).broadcast_to([128, 2, 32, 4])
                c2b = C2.unsqueeze(3).broadcast_to([128, 2, 32, 4])
                for s in range(8):
                    prev = hhv[:, :, :, :, s - 1] if s > 0 else H0[:, :, :, q0:q0 + 4]
                    self.tt(m1, c1b, prev[:, 0:1].broadcast_to([128, 2, 32, 4]), ALU.mult, hk + [ok], [ok])
                    self.tt(m2, c2b, prev[:, 1:2].broadcast_to([128, 2, 32, 4]), ALU.mult, hk + [ok], [ok])
                    self.tt(m1, m1, m2, ALU.add, hk + [ok], [ok])
                    self.tt(hhv[:, :, :, :, s], m1, buv[:, :, :, :, s], ALU.add, hk + [ok, bk], ["HH"])
            if b >= 63:
                nq = 1 if b == 63 else 4
                src = HH[:, :, :, 31:32] if b == 63 else HH[:].rearrange("p r t (q s) -> p r t q s", q=4)[:, :, :, :, 7]
                FT, fk = TQ4, "tq"
                fo, ft = FT[:, :, :, 8:8 + nq], FT[:, :, :, 12:12 + nq]
                fb = lambda f: f.unsqueeze(2).broadcast_to([128, 32, nq])
                rk = ["HH", fk] + K
                self.tt(fo[:, 0], src[:, 0], fb(FRE), ALU.mult, rk, [fk])
                self.tt(ft[:, 0], src[:, 1], fb(FIM), ALU.mult, rk, [fk])
                self.tt(fo[:, 0], fo[:, 0], ft[:, 0], ALU.subtract, rk, [fk])
                self.tt(fo[:, 1], src[:, 0], fb(FIM), ALU.mult, rk, [fk])
                self.tt(ft[:, 1], src[:, 1], fb(FRE), ALU.mult, rk, [fk])
                self.tt(fo[:, 1], fo[:, 1], ft[:, 1], ALU.add, rk, [fk])
                for r in range(2):
                    if b == 63:
                        outn = ("s5pre", "s5pim")[r]
                        self.tr(PS[7][0:32, 0:128], fo[:, r, :, 0], IDF, [fk, "CON"], [("ps", 7)])
                        self.cp("act", T[2][0:32, 0:128], PS[7][0:32, 0:128], [("ps", 7)], ["T2"])
                        self.dma("sp", O[outn][j].rearrange("(t p) -> t p", p=128), T[2][0:32, 0:128], ["T2"], [("so", outn)])
                    else:
                        outn = ("s5sre", "s5sim")[r]
                        q0 = (b - 64) * 4
                        for tg in range(8):
                            for tl in range(4):
                                self.tr(PS[7][0:4, tl * 128:(tl + 1) * 128], fo[:, r, tg * 4 + tl, :], IDF, [fk, "CON"], [("ps", 7)])
                            self.cp("act", T[2][0:4, :], PS[7][0:4, :], [("ps", 7)], ["T2"])
                            self.dma("sp", O[outn][j][q0:q0 + 4, tg * 512:(tg + 1) * 512], T[2][0:4, :], ["T2"], [("so", outn, tg, q0)])

        def back(b):
            c0 = b * 32
            tb0 = (c0 // 512) * 512 if c0 < 2048 else 2048
            xk, xk2 = ("XB", tb0), ("X", tb0)
            self.cp("act", HHB[:], HH[:], ["HH"], ["HHB"])
            for k in range(8):
                n = 0
                for q in range(4):
                    tile = k * 4 + q
                    for r, CC in enumerate((CRB, CIN)):
                        self.mm(PS[6][:, k * 32:(k + 1) * 32], CC[:, tile, :], HHB[:, r, tile, :], n == 0, n == 7, ["CC", "HHB"], [("ps", 6)])
                        n += 1
            yv = PS[6][:, 0:256].rearrange("p (k s) -> p k s", k=8)
            v = T[0][:, 0:256].rearrange("p (k s) -> p k s", k=8)
            t1 = T[1][:, 0:256].rearrange("p (k s) -> p k s", k=8)
            gk = ["T0", "T1", ("ps", 6)]
            self.tt(v, X[:, :, c0:c0 + 32], DPAR.unsqueeze(2).broadcast_to([128, 8, 32]), ALU.mult, gk + [xk2, "PAR"], ["T0"])
            self.tt(v, v, yv, ALU.add, gk, ["T0"])
            self.tt(t1, v, v, ALU.mult, gk, ["T1"])
            self.ts(t1, t1, 0.044715, ALU.mult, gk, ["T1"], s2=1.0, op1=ALU.add)
            self.tt(t1, t1, v, ALU.mult, gk, ["T1"])
            self.act(t1, t1, AF.Tanh, gk, ["T1"], scale=0.7978845608028654)
            self.ts(t1, t1, 1.0, ALU.add, gk, ["T1"], s2=0.5, op1=ALU.mult)
            self.tt(XB[:, :, c0:c0 + 32], t1, v, ALU.mult, gk + [xk], [xk])

        front(0)
        for b in range(64 + 4):
            self.stream = []
            back(b)
            lt = self.stream
            self.stream = []
            if b + 1 < 68:
                front(b + 1)
            lp = self.stream
            self.stream = None
            ia = ib = 0
            while ia < len(lt) or ib < len(lp):
                if ia < len(lt):
                    self._flush(lt[ia]); ia += 1
                if ib < len(lp):
                    self._flush(lp[ib]); ib += 1
        self.P.emit()
        wv, kv = self.wload(I["s5_w_glu_v"][j])
        wg, kg = self.wload(I["s5_w_glu_g"][j])
        big = TBLK[:4]
        cnt = [0]

        def glu_evac(pv, kpv, pg, kpg, oc, t0, n):
            tt_, tk = T[cnt[0] % 2], ("T", cnt[0] % 2)
            cnt[0] += 1
            self.act(tt_[:, 0:n], pg[:, 0:n], AF.Sigmoid, [kpg], [tk])
            self.tt(tt_[:, 0:n], tt_[:, 0:n], pv[:, 0:n], ALU.mult, [kpv, tk], [tk])
            self.stt(X[:, oc, t0:t0 + n], X[:, oc, t0:t0 + n], ALPHA, tt_[:, 0:n], ALU.mult, ALU.add, [tk, ("X", t0)], [("X", t0)])
        for oc in range(8):
            for c in range(8):
                for bi, (t0, n) in enumerate(big):
                    self.mm(PS[bi][:, 0:n], wv[:, c, oc * 128:(oc + 1) * 128], XB[:, c, t0:t0 + n], c == 0, c == 7, [kv, ("XB", t0)], [("ps", bi)])
            for c in range(8):
                for bi, (t0, n) in enumerate(big):
                    self.mm(PS[4 + bi][:, 0:n], wg[:, c, oc * 128:(oc + 1) * 128], XB[:, c, t0:t0 + n], c == 0, c == 7, [kg, ("XB", t0)], [("ps", 4 + bi)])
            for bi, (t0, n) in enumerate(big):
                glu_evac(PS[bi], ("ps", bi), PS[4 + bi], ("ps", 4 + bi), oc, t0, n)
        (t0, n) = TBLK[4]
        for oc in range(8):
            pv, pg = PS[(oc % 2) * 2], PS[1 + (oc % 2) * 2]
            kpv, kpg = ("ps", (oc % 2) * 2), ("ps", 1 + (oc % 2) * 2)
            for c in range(8):
                self.mm(pv[:, 0:n], wv[:, c, oc * 128:(oc + 1) * 128], XB[:, c, t0:t0 + n], c == 0, c == 7, [kv, ("XB", t0)], [kpv])
            for c in range(8):
                self.mm(pg[:, 0:n], wg[:, c, oc * 128:(oc + 1) * 128], XB[:, c, t0:t0 + n], c == 0, c == 7, [kg, ("XB", t0)], [kpg])
            glu_evac(pv, kpv, pg, kpg, oc, t0, n)
        self.prefetch(("q", i), I["xa_w_q"][i])

    def rwkv_layer(self, i, j):
        X, XB, T, I, O, PS, AR, SM, IDF, BLK = self.X, self.XB, self.T, self.I, self.O, self.PS, self.AR, self.SM, self.IDF, self.BLK
        if STUB_RWKV:
            for (t0, n) in TBLK:
                self.ts(X[:, :, t0:t0 + n], X[:, :, t0:t0 + n], ALPHA, ALU.mult, [("X", t0)], [("X", t0)])
            return
        off = [17408]

        self.force_auto = True
        self.regions = []

        def ab(n):
            a = AR[:, off[0]:off[0] + n]
            self.regions.append((off[0] * 2, (off[0] + n) * 2, ("rk", len(self.regions))))
            off[0] += n
            assert off[0] <= 51200, off[0]
            return a

        def af(n):
            return ab(2 * n).bitcast(F32)
        q3 = lambda a: a.rearrange("p (c t) -> p c t", c=2)
        WRq, WKq, WVq = [ab(2048).rearrange("p (c n) -> p c n", c=8) for _ in range(3)]
        WOq = ab(2048).rearrange("p (c n) -> p c n", c=2)
        W1, A1w = [ab(512).rearrange("p (c n) -> p c n", c=8) for _ in range(2)]
        G1w = ab(1024).rearrange("p (c n) -> p c n", c=8)
        V1w = ab(256).rearrange("p (c n) -> p c n", c=8)
        W2q, A2q, V2q, G2q = [ab(256) for _ in range(4)]
        XM = ab(3072).rearrange("p (m c t) -> p m c t", m=6, c=8)
        XX = af(512).rearrange("p (c t) -> p c t", c=8)
        TMX = af(512).rearrange("p (c t) -> p c t", c=8)
        R_, K_, A_, KAP, LW, LG, E_, T1t, Y_, YBs, VFt, T1e, SSe = [q3(af(128)) for _ in range(13)]
        V2, G2, BON2 = [[q3(af(128)) for _ in range(2)] for _ in range(3)]
        BT2, KT2, BH2, KH2, Vb2 = [[q3(ab(128)) for _ in range(2)] for _ in range(5)]
        ATRT2 = [ab(256).rearrange("p (w c t) -> p w c t", w=2, c=2) for _ in range(2)]
        Vt, BHt, KHt, Ut = [ab(256) for _ in range(4)]
        h4 = lambda a: a.rearrange("p (h t) -> p h t", h=4)
        Xa, XTa, Xb, XTb, Rs, Pb = [h4(ab(256)) for _ in range(6)]
        Pm = h4(af(256))
        STb = ab(256).rearrange("p (c n) -> p c n", c=2)
        MBR, MKA, MKR = [h4(ab(256)) for _ in range(3)]
        STp, STs, BDin = [af(256).rearrange("p (c n) -> p c n", c=2) for _ in range(3)]
        OB = q3(ab(128))
        T1b, A1b, V1b, G1b = ab(64), ab(64), ab(64), ab(64)
        SH0 = af(128).rearrange("p (c s) -> p c s", c=8)
        XL = af(136).rearrange("p (c s) -> p c s", c=8)
        OMKA = af(8)
        RSTa, RSTb = af(128), af(128)
        GLu2 = [af(16).rearrange("p (c u) -> p c u", c=2) for _ in range(2)]
        LGL = af(16).rearrange("p (c u) -> p c u", c=2)
        SSn = q3(af(128))
        pc = self.pc
        MU = self.par(("mu", j), 48).rearrange("p (m c) -> p m c", m=6)
        W0, A0, KKp, KAp, LXG, LXB, RKp = [self.par((nm, j)) for nm in ("rwkv_w0", "rwkv_a0", "rwkv_k_k", "rwkv_k_a", "rwkv_lnx_g", "rwkv_lnx_b", "rwkv_r_k")]
        V0p = self.par(("rwkv_v0", 1))
        B = [PS[k] for k in range(8)]
        bk = lambda k: ("ps", k)
        K0 = ["rk"]

        self.dma("pool", W1, I["rwkv_w1"][j].rearrange("(c p) n -> p c n", p=128), [], ["Wl"])
        self.dma("pool", A1w, I["rwkv_a1"][j].rearrange("(c p) n -> p c n", p=128), [], ["Wl"])
        self.dma("pool", G1w, I["rwkv_g1"][j].rearrange("(c p) n -> p c n", p=128), [], ["Wl"])
        if j == 1:
            self.dma("pool", V1w, I["rwkv_v1"][0].rearrange("(c p) n -> p c n", p=128), [], ["Wl"])
        self.ts(OMKA, KAp, -1.0, ALU.mult, ["PAR"], K0, s2=1.0, op1=ALU.add)
        for a, src in ((RSTa, self.RST64), (RSTb, self.RST8)):
            self.cp("dve", a[:, 0:64], src, ["CON"], K0)
            self.cp("dve", a[:, 64:128], src, ["CON"], K0)
        self.cp("dve", XL[:, :, 0:1], X[:, :, 2047:2048], [("X", 1536)], ["XL"])
        self.cp("dve", XL[:, :, 1:17], X[:, :, 2048:2176].rearrange("p c (u t) -> p c u t", u=16)[:, :, :, 7], [("X", 2048)], ["XL"])
        for half in range(2):
            for cc in range(4):
                c = half * 4 + cc
                self.tr(B[6][0:17, cc * 128:(cc + 1) * 128], XL[:, c, :], IDF, ["XL", "CON"], [bk(6)])
            self.cp("act", T[half][0:17, :], B[6][0:17, :], [bk(6)], [("T", half)])
            self.dma("sp", O["shp"][j:j + 1, half * 512:(half + 1) * 512], T[half][0:1, :], [("T", half)], [("sho", half)])
            self.dma("sp", O["shs"][j][:, half * 512:(half + 1) * 512], T[half][1:17, :], [("T", half)], [("shso", half)])
        for half in range(2):
            self.dma("sp", T[half][0:16, :], I["sh0"][j][:, half * 512:(half + 1) * 512], [], [("T", half)])
            for cc in range(4):
                c = half * 4 + cc
                self.tr(B[7][:, c * 16:(c + 1) * 16], T[half][0:16, cc * 128:(cc + 1) * 128], IDF[0:16, 0:16], [("T", half), "CON"], [bk(7)])
        self.cp("dve", SH0, B[7][:, 0:128].rearrange("p (c s) -> p c s", c=8), [bk(7)], ["SH0"])
        self.mset(BDin, 0.0, [], ["BDin"])
        for (t0_, n_) in TBLK:
            self.ts(X[:, :, t0_:t0_ + n_], X[:, :, t0_:t0_ + n_], ALPHA, ALU.mult)

        for hq in range(4):
            cs = slice(hq * 256, (hq + 1) * 256)
            for w, nm in ((WRq, "rwkv_w_r"), (WKq, "rwkv_w_k"), (WVq, "rwkv_w_v")):
                self.dma("pool", w, I[nm][j][:, cs].rearrange("(c p) n -> p c n", p=128), [], ["Wq"])
            self.dma("pool", WOq, I["rwkv_w_o"][j][cs, :].rearrange("(c p) n -> p c n", p=128), [], ["Wq"])
            self.dma("pool", W2q[0:64, :], I["rwkv_w2"][j][:, cs], [], ["Wq"])
            self.dma("pool", A2q[0:64, :], I["rwkv_a2"][j][:, cs], [], ["Wq"])
            if j == 1:
                self.dma("pool", V2q[0:32, :], I["rwkv_v2"][0][:, cs], [], ["Wq"])
            self.dma("pool", G2q, I["rwkv_g2"][j][:, cs], [], ["Wq"])
            self.mset(STp, 0.0, [], ["STp"])
            pcs = slice(2 * hq, 2 * hq + 2)
            bc = lambda p: p[:, pcs].unsqueeze(2).broadcast_to([128, 2, 64])
            def prep(blk):
                bs = blk % 2
                V_, G_, BON, BT, KT, BH, KH, ATRT, GLu = V2[bs], G2[bs], BON2[bs], BT2[bs], KT2[bs], BH2[bs], KH2[bs], ATRT2[bs], GLu2[bs]
                AT, RT = ATRT[:, 0], ATRT[:, 1]
                samp = blk >= 32
                g0 = 64 * blk
                tb0 = (g0 // 512) * 512 if g0 < 2048 else 2048
                xbb = XB[:, :, g0:g0 + 64]
                xk = ("XB", tb0)
                kb_ = ["blk"]
                if samp:
                    sb = blk - 32
                    xbv = xbb.rearrange("p c (u t) -> p c u t", u=8)
                    xxv = XX.rearrange("p c (u t) -> p c u t", u=8)
                    self.tt(xxv[:, :, :, 1:8], xbv[:, :, :, 0:7], xbv[:, :, :, 1:8], ALU.subtract, [xk], kb_)
                    self.tt(xxv[:, :, :, 0], SH0[:, :, 8 * sb:8 * sb + 8], xbv[:, :, :, 0], ALU.subtract, [xk, "SH0"], kb_)
                elif blk == 0:
                    self.tt(XX[:, :, 1:64], XB[:, :, 0:63], XB[:, :, 1:64], ALU.subtract, [xk], kb_)
                    self.ts(XX[:, :, 0:1], XB[:, :, 0:1], -1.0, ALU.mult, [xk], kb_)
                else:
                    pk_ = ("XB", ((g0 - 1) // 512) * 512)
                    self.tt(XX, XB[:, :, g0 - 1:g0 + 63], xbb, ALU.subtract, [xk, pk_], kb_)
                for m in range(6):
                    self.tt(TMX, XX, MU[:, m, :].unsqueeze(2).broadcast_to([128, 8, 64]), ALU.mult, kb_ + ["PAR"], kb_)
                    self.tt(XM[:, m], TMX, xbb, ALU.add, kb_ + [xk], kb_)
                for n_, (w, m) in enumerate(((WRq, 0), (WKq, 2), (WVq, 3))):
                    for c2 in range(2):
                        for c in range(8):
                            self.mm(B[0][:, n_ * 128 + c2 * 64:n_ * 128 + c2 * 64 + 64], w[:, c, c2 * 128:(c2 + 1) * 128], XM[:, m, c, :], c == 0, c == 7, kb_ + ["Wq"], [bk(0)])
                self.cp("act", R_, q3(B[0][:, 0:128]), [bk(0)], kb_)
                self.cp("dve", K_, q3(B[0][:, 128:256]), [bk(0)], kb_)
                self.cp("act", V_, q3(B[0][:, 256:384]), [bk(0)], kb_)
                for c in range(8):
                    self.mm(B[1][0:64, 0:64], W1[:, c, :], XM[:, 1, c, :], c == 0, c == 7, kb_ + ["Wl"], [bk(1)])
                for c in range(8):
                    self.mm(B[1][0:64, 64:128], A1w[:, c, :], XM[:, 4, c, :], c == 0, c == 7, kb_ + ["Wl"], [bk(1)])
                if j == 1:
                    for c in range(8):
                        self.mm(B[1][0:32, 128:192], V1w[:, c, :], XM[:, 3, c, :], c == 0, c == 7, kb_ + ["Wl"], [bk(1)])
                for c in range(8):
                    self.mm(B[1][:, 192:256], G1w[:, c, :], XM[:, 5, c, :], c == 0, c == 7, kb_ + ["Wl"], [bk(1)])
                self.act(T1b[0:64, :], B[1][0:64, 0:64], AF.Tanh, [bk(1)], kb_)
                self.cp("act", A1b[0:64, :], B[1][0:64, 64:128], [bk(1)], kb_)
                if j == 1:
                    self.cp("act", V1b[0:32, :], B[1][0:32, 128:192], [bk(1)], kb_)
                self.act(G1b, B[1][:, 192:256], AF.Sigmoid, [bk(1)], kb_)
                for c2 in range(2):
                    self.mm(B[2][:, c2 * 64:c2 * 64 + 64], W2q[0:64, c2 * 128:(c2 + 1) * 128], T1b[0:64, :], True, True, kb_ + ["Wq"], [bk(2)])
                    self.mm(B[2][:, 128 + c2 * 64:128 + c2 * 64 + 64], A2q[0:64, c2 * 128:(c2 + 1) * 128], A1b[0:64, :], True, True, kb_ + ["Wq"], [bk(2)])
                    if j == 1:
                        self.mm(B[2][:, 256 + c2 * 64:256 + c2 * 64 + 64], V2q[0:32, c2 * 128:(c2 + 1) * 128], V1b[0:32, :], True, True, kb_ + ["Wq"], [bk(2)])
                    self.mm(B[2][:, 384 + c2 * 64:384 + c2 * 64 + 64], G2q[:, c2 * 128:(c2 + 1) * 128], G1b, True, True, kb_ + ["Wq"], [bk(2)])
                kp = kb_ + ["PAR"]
                self.tt(LW, q3(B[2][:, 0:128]), bc(W0), ALU.add, kp + [bk(2)], kb_)
                self.act(LW, LW, AF.Sigmoid, kb_, kb_)
                self.ts(LW, LW, -0.6065306597126334, ALU.mult, kb_, kb_)
                self.tt(A_, q3(B[2][:, 128:256]), bc(A0), ALU.add, kp + [bk(2)], kb_)
                self.act(A_, A_, AF.Sigmoid, kb_, kb_)
                self.cp("dve", G_, q3(B[2][:, 384:512]), [bk(2)], kb_)
                vfd = self.VF[:, 2 * hq:2 * hq + 2, g0:g0 + 64]
                if j == 1:
                    self.tt(T1t, q3(B[2][:, 256:384]), bc(V0p), ALU.add, kp + [bk(2)], kb_)
                    self.act(T1t, T1t, AF.Sigmoid, kb_, kb_)
                    self.dma("sp", VFt, vfd, kb_, kb_)
                    self.tt(VFt, VFt, V_, ALU.subtract, kb_, kb_)
                    self.tt(VFt, VFt, T1t, ALU.mult, kb_, kb_)
                    self.tt(V_, V_, VFt, ALU.add, kb_, kb_)
                else:
                    self.dma("sp", vfd, V_, kb_, [("vf", hq, blk)])
                self.tt(KAP, K_, bc(KKp), ALU.mult, kp, kb_)
                self.tt(T1t, KAP, KAP, ALU.mult, kb_, kb_)
                for c2 in range(2):
                    self.mm(B[3][:, c2 * 64:c2 * 64 + 64], BLK, T1t[:, c2, :], True, True, kb_ + ["CON"], [bk(3)])
                self.act(SSn, q3(B[3][:, 0:128]), AF.Sqrt, [bk(3)], kb_)
                self.ts(SSn, SSn, 1e-12, ALU.max, kb_, kb_)
                self.rcp(SSn, SSn, kb_, kb_)
                self.tt(KAP, KAP, SSn, ALU.mult, kb_, kb_)
                self.tt(T1t, A_, bc(KAp), ALU.mult, kp, kb_)
                self.tt(T1t, T1t, OMKA[:, pcs].unsqueeze(2).broadcast_to([128, 2, 64]), ALU.add, kb_ + K0, kb_)
                self.tt(K_, K_, T1t, ALU.mult, kb_, kb_)
                self.tt(T1t, R_, K_, ALU.mult, kb_, kb_)
                self.tt(T1t, T1t, bc(RKp), ALU.mult, kp, kb_)
                for c2 in range(2):
                    self.mm(B[3][:, 128 + c2 * 64:128 + c2 * 64 + 64], BLK, T1t[:, c2, :], True, True, kb_ + ["CON"], [bk(3)])
                self.tt(BON, q3(B[3][:, 128:256]), V_, ALU.mult, kb_ + [bk(3)], kb_)
                self.scan(LG.rearrange("p c t -> p (c t)"), RSTb if samp else RSTa, LW.rearrange("p c t -> p (c t)"), kb_ + K0, kb_)
                nu = 8 if samp else 1
                L = 8 if samp else 64
                lgv = LG.rearrange("p c (u t) -> p c u t", u=nu)
                self.cp("dve", LGL[:, :, 0:nu], lgv[:, :, :, L - 1], kb_, kb_)
                self.act(GLu[:, :, 0:nu], LGL[:, :, 0:nu], AF.Exp, kb_, kb_)
                self.tt(A_, KAP, A_, ALU.mult, kb_, kb_)
                self.act(E_, LG, AF.Exp, kb_, kb_)
                self.tt(RT, R_, E_, ALU.mult, kb_, kb_)
                self.tt(T1t, LG, LW, ALU.subtract, kb_, kb_)
                self.act(E_, T1t, AF.Exp, kb_, kb_)
                self.stt(AT, KAP, -1.0, E_, ALU.mult, ALU.mult, kb_, kb_)
                self.act(E_, LG, AF.Exp, kb_, kb_, scale=-1.0)
                self.tt(BT, A_, E_, ALU.mult, kb_, kb_)
                self.tt(KT, K_, E_, ALU.mult, kb_, kb_)
                t1v = T1t.rearrange("p c (u t) -> p c u t", u=nu)
                self.tt(t1v, LGL[:, :, 0:nu].unsqueeze(3).broadcast_to([128, 2, nu, L]), lgv, ALU.subtract, kb_, kb_)
                self.act(E_, T1t, AF.Exp, kb_, kb_)
                self.tt(BH, A_, E_, ALU.mult, kb_, kb_)
                self.tt(KH, K_, E_, ALU.mult, kb_, kb_)
                self.cp("act", Vb2[bs], V_, kb_, kb_)

            def tail(blk):
                bs = blk % 2
                V_, G_, BON, BT, KT, BH, KH, ATRT, GLu = V2[bs], G2[bs], BON2[bs], BT2[bs], KT2[bs], BH2[bs], KH2[bs], ATRT2[bs], GLu2[bs]
                AT, RT = ATRT[:, 0], ATRT[:, 1]
                samp = blk >= 32
                g0 = 64 * blk
                tb0 = (g0 // 512) * 512 if g0 < 2048 else 2048
                kb_ = ["blk"]
                kp = kb_ + ["PAR"]
                nu = 8 if samp else 1
                L = 8 if samp else 64
                for u in range(nu):
                    c0 = u * L
                    if samp:
                        sq = (blk - 32) * 8 + u
                        ST = STs
                        for c2 in range(2):
                            for h2 in range(2):
                                hd = 4 * hq + 2 * c2 + h2
                                self.dma("sp", BDin[64 * h2:64 * h2 + 64, c2, 64 * h2:64 * h2 + 64], I["rw0"][j, sq, hd], ["BDin"] + kb_, ["BDin"])
                        for c2 in range(2):
                            self.tr(B[7][:, c2 * 128:(c2 + 1) * 128], BDin[:, c2, :], IDF, ["BDin", "CON"], [bk(7)])
                        self.cp("dve", STs, B[7][:, 0:256].rearrange("p (c n) -> p c n", c=2), [bk(7)], ["ST"])
                    else:
                        ST = STp
                    self.rwkv_unit(c0, L, ST, dict(V_=Vb2[bs], Pb=Pb, STb=STb, BH=BH, KH=KH, AT=AT, RT=RT, ATRT=ATRT, BT=BT, KT=KT, Vt=Vt, BHt=BHt, KHt=KHt, Ut=Ut, Xa=Xa, XTa=XTa, Xb=Xb, XTb=XTb,
                                                     Pm=Pm, Rs=Rs, MBR=MBR, MKA=MKA, MKR=MKR, YBs=YBs, Y_=Y_, GL=GLu[:, :, u]), kb_)
                    last = (blk == 31) or samp
                    if last:
                        for c2 in range(2):
                            self.tr(B[7][:, c2 * 128:(c2 + 1) * 128], ST[:, c2, :], IDF, ["ST", "CON"], [bk(7)])
                        self.cp("dve", BDin, B[7][:, 0:256].rearrange("p (c n) -> p c n", c=2), [bk(7)], ["BDin"])
                        for c2 in range(2):
                            for h2 in range(2):
                                hd = 4 * hq + 2 * c2 + h2
                                dst = O["rws"][j, sq, hd] if samp else O["rwp"][j, hd]
                                self.dma("sp", dst, BDin[64 * h2:64 * h2 + 64, c2, 64 * h2:64 * h2 + 64], ["BDin"], [("rwo", hq, blk, u, c2, h2)])
                for c2 in range(2):
                    self.mm(B[6][:, c2 * 64:c2 * 64 + 64], BLK, Y_[:, c2, :], True, True, kb_ + ["CON"], [bk(6)])
                self.stt(Y_, q3(B[6][:, 0:128]), -1.0 / 64.0, Y_, ALU.mult, ALU.add, kb_ + [bk(6)], kb_)
                self.tt(T1e, Y_, Y_, ALU.mult, kb_, kb_)
                for c2 in range(2):
                    self.mm(B[6][:, 128 + c2 * 64:128 + c2 * 64 + 64], BLK, T1e[:, c2, :], True, True, kb_ + ["CON"], [bk(6)])
                self.act(SSe, q3(B[6][:, 128:256]), AF.Sqrt, [bk(6), "eps"], kb_, bias=self.EPS2, scale=1.0 / 64.0)
                self.rcp(SSe, SSe, kb_, kb_)
                self.tt(Y_, Y_, SSe, ALU.mult, kb_, kb_)
                for c2 in range(2):
                    c = 2 * hq + c2
                    self.ts(Y_[:, c2, :], Y_[:, c2, :], LXG[:, c:c + 1], ALU.mult, kp, kb_, s2=LXB[:, c:c + 1], op1=ALU.add)
                self.tt(Y_, Y_, BON, ALU.add, kb_, kb_)
                self.tt(OB, Y_, G_, ALU.mult, kb_, kb_)
                for oc in range(8):
                    for c2 in range(2):
                        self.mm(B[7][:, oc * 64:(oc + 1) * 64], WOq[:, c2, oc * 128:(oc + 1) * 128], OB[:, c2, :], c2 == 0, c2 == 1, kb_ + ["Wq"], [bk(7)])
                xg = ("X", tb0)
                pso = B[7][:].rearrange("p (c t) -> p c t", c=8)
                self.tt(X[:, :, g0:g0 + 64], X[:, :, g0:g0 + 64], pso, ALU.add, [bk(7), xg], [xg])

            import os
            NB = int(os.environ.get('RW_BLKS', '34'))
            prep(0)
            for blk in range(NB):
                self.stream = []
                tail(blk)
                lt = self.stream
                self.stream = []
                if blk + 1 < NB:
                    prep(blk + 1)
                lp = self.stream
                self.stream = None
                ia = ib = 0
                while ia < len(lt) or ib < len(lp):
                    if ia < len(lt):
                        self._flush(lt[ia]); ia += 1
                    if ib < len(lp):
                        self._flush(lp[ib]); ib += 1
        self.force_auto = False

    def rwkv_unit(self, c0, L, ST, t, kb_):
        PS, IDF = self.PS, self.IDF
        B = PS
        bk = lambda k: ("ps", k)
        ku = kb_ + ["ST"]
        cs = slice(c0, c0 + L)
        nlev = {64: 5, 8: 2}[L]
        for n_, (src, dst, bank, col) in enumerate(((t["V_"], t["Vt"], 4, 0), (t["BH"], t["BHt"], 4, 256), (t["KH"], t["KHt"], 5, 0))):
            Bb = B[bank][:].bitcast(BF16)
            for c2 in range(2):
                self.tr(Bb[0:L, col + c2 * 128:col + (c2 + 1) * 128], src[:, c2, cs], self.IDB, kb_ + ["CON"], [bk(bank)])
            self.cp("act" if n_ % 2 == 0 else "dve", dst[0:L, :], Bb[0:L, col:col + 256], [bk(bank)], ku)
        STb, Pb = t["STb"], t["Pb"]
        self.cp("act", STb, ST, ku, ku)
        ATRT, AT, BT, KT = t["ATRT"], t["AT"], t["BT"], t["KT"]
        for h2 in range(2):
            b0 = 64 * h2
            bs, bn = B[4 + h2], B[6 + h2]
            for c2 in range(2):
                rhs = ATRT[b0:b0 + 64, :, c2, cs]
                self.mm(bs[0:L, c2 * 128:c2 * 128 + 2 * L], BT[b0:b0 + 64, c2, cs], rhs, True, True, ku, [bk(4 + h2)])
                self.mm(bs[0:L, 256 + c2 * 128:256 + c2 * 128 + 2 * L], KT[b0:b0 + 64, c2, cs], rhs, True, True, ku, [bk(4 + h2)])
                self.mm(bn[0:L, c2 * 64:c2 * 64 + L], AT[b0:b0 + 64, c2, cs], BT[b0:b0 + 64, c2, cs], True, True, ku, [bk(6 + h2)])
            v4 = bs[0:L, :].rearrange("p (k x) -> p k x", k=4)
            mlt = self.MLT[0:L, 0:L].unsqueeze(1).broadcast_to([L, 2, L])
            mle = self.MLE[0:L, 0:L].unsqueeze(1).broadcast_to([L, 2, L])
            mgt = self.MGT[0:L, 0:L].unsqueeze(1).broadcast_to([L, 2, L])
            hsel = slice(h2, 4, 2)
            self.tt(t["Xa"][0:L, hsel, 0:L], v4[:, 0:2, 0:L], mlt, ALU.mult, [bk(4 + h2), "CON"], ku)
            self.tt(t["MBR"][0:L, hsel, 0:L], v4[:, 0:2, L:2 * L], mle, ALU.mult, [bk(4 + h2), "CON"], ku)
            self.tt(t["MKA"][0:L, hsel, 0:L], v4[:, 2:4, 0:L], mlt, ALU.mult, [bk(4 + h2), "CON"], ku)
            self.tt(t["MKR"][0:L, hsel, 0:L], v4[:, 2:4, L:2 * L], mle, ALU.mult, [bk(4 + h2), "CON"], ku)
            self.tt(t["XTa"][0:L, hsel, 0:L], bn[0:L, 0:128].rearrange("p (k x) -> p k x", k=2)[:, :, 0:L], mgt, ALU.mult, [bk(6 + h2), "CON"], ku)
        Xc, XTc, Xn, XTn, Pm = t["Xa"], t["XTa"], t["Xb"], t["XTb"], t["Pm"]
        self.tt(Pm[0:L, :, 0:L], Xc[0:L, :, 0:L], IDF[0:L, 0:L].unsqueeze(1).broadcast_to([L, 4, L]), ALU.add, ku + ["CON"], ku)
        self.cp("act", Pb[0:L, :, 0:L], Pm[0:L, :, 0:L], ku, ku)
        for lev in range(nlev):
            lastlev = lev == nlev - 1
            for hl in range(4):
                if not lastlev:
                    self.mm(B[4][0:L, hl * 64:hl * 64 + L], XTc[0:L, hl, 0:L], Xc[0:L, hl, 0:L], True, True, ku, [bk(4)])
                self.mm(B[5][0:L, hl * 64:hl * 64 + L], Xc[0:L, hl, 0:L], XTc[0:L, hl, 0:L], True, True, ku, [bk(5)])
            if not lastlev:
                self.cp("act", Xn[0:L, :, 0:L], B[4][0:L, 0:256].rearrange("p (h x) -> p h x", h=4)[:, :, 0:L], [bk(4)], ku)
            self.cp("dve", XTn[0:L, :, 0:L], B[5][0:L, 0:256].rearrange("p (h x) -> p h x", h=4)[:, :, 0:L], [bk(5)], ku)
            for hl in range(4):
                self.mm(B[6][0:L, hl * 64:hl * 64 + L], XTn[0:L, hl, 0:L], Pb[0:L, hl, 0:L], True, True, ku, [bk(6)])
            self.tt(Pm[0:L, :, 0:L], Pm[0:L, :, 0:L], B[6][0:L, 0:256].rearrange("p (h x) -> p h x", h=4)[:, :, 0:L], ALU.add, ku + [bk(6)], ku)
            self.cp("act", Pb[0:L, :, 0:L], Pm[0:L, :, 0:L], ku, ku)
            Xc, XTc, Xn, XTn = Xn, XTn, Xc, XTc
        Vt, BHt, KHt, Ut, Rs, MKA, MBR, MKR = t["Vt"], t["BHt"], t["KHt"], t["Ut"], t["Rs"], t["MKA"], t["MBR"], t["MKR"]
        for c2 in range(2):
            self.mm(B[7][0:L, c2 * 128:(c2 + 1) * 128], AT[:, c2, cs], STb[:, c2, :], True, False, ku, [bk(7)])
            for h2 in range(2):
                hl = 2 * c2 + h2
                self.mm(B[7][0:L, hl * 64:(hl + 1) * 64], MKA[0:L, hl, 0:L], Vt[0:L, hl * 64:(hl + 1) * 64], False, h2 == 1, ku, [bk(7)])
        self.cp("act", Rs[0:L, :, :], B[7][0:L, 0:256].rearrange("p (h x) -> p h x", h=4), [bk(7)], ku)
        for hl in range(4):
            self.mm(B[4][0:L, hl * 64:(hl + 1) * 64], Pb[0:L, hl, 0:L], Rs[0:L, hl, :], True, True, ku, [bk(4)])
        self.cp("dve", Ut[0:L, :], B[4][0:L, 0:256], [bk(4)], ku)
        for c2 in range(2):
            self.mm(B[5][:, c2 * 64:c2 * 64 + L], STb[:, c2, :], t["RT"][:, c2, cs], True, True, ku, [bk(5)])
        for hl in range(4):
            self.mm(B[6][0:64, hl * 64:hl * 64 + L], Ut[0:L, hl * 64:(hl + 1) * 64], MBR[0:L, hl, 0:L], True, False, ku, [bk(6)])
            self.mm(B[6][0:64, hl * 64:hl * 64 + L], Vt[0:L, hl * 64:(hl + 1) * 64], MKR[0:L, hl, 0:L], False, True, ku, [bk(6)])
        ybv = B[6][0:64, 0:256].rearrange("p (c h x) -> p c h x", c=2, h=2)
        YBs = t["YBs"]
        for h2 in range(2):
            self.cp("act", YBs[64 * h2:64 * h2 + 64, :, 0:L], ybv[:, :, h2, 0:L], [bk(6)], ku)
        self.tt(t["Y_"][:, :, cs], B[5][:, 0:128].rearrange("p (c x) -> p c x", c=2)[:, :, 0:L], YBs[:, :, 0:L], ALU.add, ku + [bk(5)], ku)
        for hl in range(4):
            c2 = hl // 2
            self.mm(B[7][:, hl * 64:(hl + 1) * 64], BHt[0:L, c2 * 128:(c2 + 1) * 128], Ut[0:L, hl * 64:(hl + 1) * 64], True, False, ku, [bk(7)])
            self.mm(B[7][:, hl * 64:(hl + 1) * 64], KHt[0:L, c2 * 128:(c2 + 1) * 128], Vt[0:L, hl * 64:(hl + 1) * 64], False, True, ku, [bk(7)])
        suv = B[7][:, 0:256].rearrange("p (c h x) -> p c h x", c=2, h=2)
        GL = t["GL"]
        for h2 in range(2):
            r = slice(64 * h2, 64 * h2 + 64)
            blkv = ST[r, :, 64 * h2:64 * h2 + 64]
            self.tt(blkv, blkv, GL[r, :].unsqueeze(2).broadcast_to([64, 2, 64]), ALU.mult, ku, ku)
            self.tt(blkv, blkv, suv[r, :, h2, :], ALU.add, ku + [bk(7)], ku)

    def build(self):
        self.setup()
        self.phase_input()
        for i in range(DEPTH):
            j = i // 2
            if i % 2 == 0:
                self.s5_layer(i, j)
            else:
                self.rwkv_layer(i, j)
            self.layer_norm(i, 0)
            self.P.emit()
            self.xa_layer(i)
            self.layer_norm(i, 1)
            self.P.emit()
            self.mlp_layer(i)
            self.layer_norm(i, 2)
            self.P.emit()
        self.phase_output()
        return self.nc


WSHAPES = (("s5_w_glu_v", [2, D, D]), ("s5_w_glu_g", [2, D, D]), ("rwkv_w_r", [2, D, D]), ("rwkv_w_k", [2, D, D]),
           ("rwkv_w_v", [2, D, D]), ("rwkv_w_o", [2, D, D]), ("rwkv_w1", [2, D, 64]), ("rwkv_w2", [2, 64, D]),
           ("rwkv_a1", [2, D, 64]), ("rwkv_a2", [2, 64, D]), ("rwkv_v1", [1, D, 32]), ("rwkv_v2", [1, 32, D]),
           ("rwkv_g1", [2, D, 128]), ("rwkv_g2", [2, 128, D]), ("xa_w_q", [4, D, D]), ("xa_w_k", [4, D, D]),
           ("xa_w_v", [4, D, D]), ("xa_w_o", [4, D, D]), ("mlp_w1", [4, D, 4 * D]), ("mlp_w2", [4, 4 * D, D]))

_CACHE = {}


def kernel(**inp):
    inp = {k: np.asarray(v) for k, v in inp.items()}
    par, pcols = pack_params(inp)
    npar = par.shape[1]
    if "nc" not in _CACHE:
        kb = KB(pcols, npar)
        _CACHE["nc"] = kb.build()
        _CACHE["names"] = [k if isinstance(k, str) else f"{k[0]}{k[1]}" for k in kb.I.keys()]
    nc = _CACHE["nc"]
    consts = make_consts()
    s5m = [pack_s5_mats(inp, j) for j in range(2)]
    in_maps = []
    for cid in range(8):
        sl = slice(cid * NSEQ, (cid + 1) * NSEQ)
        m = {}
        m["xin"] = np.ascontiguousarray(np.concatenate([inp["x_prompt"][cid], inp["x_sample"][sl].reshape(TS, D)], axis=0))
        m["mem"] = np.ascontiguousarray(inp["mem_prompt"][cid])
        m["ck"] = np.ascontiguousarray(inp["cache_mem_k"][:, sl].reshape(4, NSEQ, 256, D))
        m["cv"] = np.ascontiguousarray(inp["cache_mem_v"][:, sl].reshape(4, NSEQ, 256, D))
        m["s5re0"] = np.ascontiguousarray(inp["state_s5_re"][:, sl].reshape(2, NSEQ, 4096))
        m["s5im0"] = np.ascontiguousarray(inp["state_s5_im"][:, sl].reshape(2, NSEQ, 4096))
        m["rw0"] = np.ascontiguousarray(inp["state_rwkv"][:, sl])
        m["sh0"] = np.ascontiguousarray(inp["state_shift"][:, sl])
        m["par"] = par
        m["consts"] = consts
        for j in range(2):
            for k in ("bbr", "bbi", "cpr", "cpi"):
                m[f"{k}{j}"] = s5m[j][k]
        for nm, _ in WSHAPES:
            m[nm] = inp[nm]
        in_maps.append(m)
    declared = set(_CACHE["names"])
    in_maps = [{k: v for k, v in m.items() if k in declared} for m in in_maps]
    res = run_bass_kernel_spmd(nc, in_maps, core_ids=list(range(8)))
    R = res.results
    f32 = np.float32
    y_prompt = np.stack([R[c]["y"][:TP] for c in range(8)]).astype(f32)
    y_sample = np.concatenate([R[c]["y"][TP:].reshape(NSEQ, 8, D) for c in range(8)]).astype(f32)
    memk = np.stack([R[c]["memk"] for c in range(8)], axis=1).reshape(4, 8, 256, 4, 256).astype(f32)
    memv = np.stack([R[c]["memv"] for c in range(8)], axis=1).reshape(4, 8, 256, 4, 256).astype(f32)
    s5pre = np.stack([R[c]["s5pre"] for c in range(8)], axis=1).reshape(2, 8, 64, 64).astype(f32)
    s5pim = np.stack([R[c]["s5pim"] for c in range(8)], axis=1).reshape(2, 8, 64, 64).astype(f32)
    rwp = np.stack([R[c]["rwp"] for c in range(8)], axis=1).astype(f32)
    shp = np.stack([R[c]["shp"] for c in range(8)], axis=1).astype(f32)
    s5sre = np.concatenate([R[c]["s5sre"] for c in range(8)], axis=1).reshape(2, 128, 64, 64).astype(f32)
    s5sim = np.concatenate([R[c]["s5sim"] for c in range(8)], axis=1).reshape(2, 128, 64, 64).astype(f32)
    rws = np.concatenate([R[c]["rws"] for c in range(8)], axis=1).astype(f32)
    shs = np.concatenate([R[c]["shs"] for c in range(8)], axis=1).astype(f32)
    return (y_prompt, y_sample, memk, memv, s5pre, s5pim, rwp, shp, s5sre, s5sim, rws, shs)
```
